# Optimizing a Trainium2 kernel written in Bass

```python
import math
import jax, jax.numpy as jnp
from jax import lax
import numpy as np

D_MODEL = 1024
BATCH = 16
SEQ = 2048
DEPTH = 4
DEC_BATCH = 4
DEC_SEQ = 8192
PAST_LEN = 128

H_A = 8
Q_LORA = 256
KV_LORA = 128
NOPE_DIM = 64
ROPE_DIM = 32
V_DIM = 64
ROPE_THETA = 10000.0
Q_BLOCK = 128
H_M = 4
D_M = D_MODEL // 2
DH_M = D_M // H_M
CONV_M = 3
CHUNK = 128
D_S = D_MODEL // 2
GROUP_W = 16
N_GROUPS = D_S // GROUP_W
STATE_P = 64
D_FF = 2816
CONV_F = 3
N_BRANCH = 3
ALPHA = (2 * DEPTH) ** 0.25
BETA = (8 * DEPTH) ** -0.25
LN_EPS = 1e-5
IN_SPLITS = (Q_LORA, KV_LORA, ROPE_DIM, D_M, D_M, D_M, 4 * H_M, D_S, N_BRANCH * D_MODEL)
N_IN = sum(IN_SPLITS)

kernel_name = 'hybrid_mla_mlstm_s5_encoder'


def _layer_norm(x, g, b):
    xf = x.astype(jnp.float32)
    mu = xf.mean(-1, keepdims=True)
    var = jnp.square(xf - mu).mean(-1, keepdims=True)
    return ((xf - mu) * lax.rsqrt(var + LN_EPS) * g.astype(jnp.float32) + b.astype(jnp.float32)).astype(x.dtype)


def _rms_norm(x, g):
    xf = x.astype(jnp.float32)
    return (xf * lax.rsqrt(jnp.square(xf).mean(-1, keepdims=True) + LN_EPS) * g.astype(jnp.float32)).astype(x.dtype)


def _dwconv(x, w, b):
    y = lax.conv_general_dilated(x, w[:, None, :].astype(x.dtype), window_strides=(1,), padding='SAME',
                                 dimension_numbers=('NWC', 'WIO', 'NWC'), feature_group_count=x.shape[-1])
    return y + b.astype(x.dtype)


def _rope_tables(S):
    pos = jnp.arange(S, dtype=jnp.float32)
    inv = ROPE_THETA ** (-jnp.arange(0, ROPE_DIM, 2, dtype=jnp.float32) / ROPE_DIM)
    ang = pos[:, None] * inv[None, :]
    return jnp.cos(ang), jnp.sin(ang)


def _apply_rope(x, cos, sin):
    half = ROPE_DIM // 2
    xf = x.astype(jnp.float32)
    x1, x2 = xf[..., :half], xf[..., half:]
    return jnp.concatenate([x1 * cos - x2 * sin, x2 * cos + x1 * sin], axis=-1).astype(x.dtype)


def _mla(cq, ckv, kr, q_norm_g, kv_norm_g, w_uq, w_ukv):
    B, S, _ = cq.shape
    cq = _rms_norm(cq, q_norm_g)
    ckv = _rms_norm(ckv, kv_norm_g)
    q = (cq @ w_uq).reshape(B, S, H_A, NOPE_DIM + ROPE_DIM)
    kv = (ckv @ w_ukv).reshape(B, S, H_A, NOPE_DIM + V_DIM)
    q_nope, q_rope = q[..., :NOPE_DIM], q[..., NOPE_DIM:]
    k_nope, v = kv[..., :NOPE_DIM], kv[..., NOPE_DIM:]
    cos, sin = _rope_tables(S)
    q_rope = _apply_rope(q_rope, cos[:, None, :], sin[:, None, :])
    k_rope = _apply_rope(kr, cos, sin)
    scale = (NOPE_DIM + ROPE_DIM) ** -0.5
    nb = S // Q_BLOCK

    def blocks(t):
        return t.reshape(B, nb, Q_BLOCK, H_A, t.shape[-1]).swapaxes(0, 1)

    def attend(qb):
        qn, qr = qb
        s = jnp.einsum('bqhd,bkhd->bhqk', qn, k_nope) + jnp.einsum('bqhr,bkr->bhqk', qr, k_rope)
        p = jax.nn.softmax(s.astype(jnp.float32) * scale, axis=-1).astype(v.dtype)
        return jnp.einsum('bhqk,bkhd->bqhd', p, v)

    o = lax.map(attend, (blocks(q_nope), blocks(q_rope)))
    return o.swapaxes(0, 1).reshape(B, S, H_A * V_DIM)


def _mlstm_direction(q, k, v, i_pre, f_pre):
    B, S, H, dh = q.shape
    n = S // CHUNK

    def to_chunks(t):
        return t.reshape(B, n, CHUNK, H, dh).transpose(1, 0, 3, 2, 4)

    def gate_chunks(t):
        return t.reshape(B, n, CHUNK, H).transpose(1, 0, 3, 2)

    qc, kc, vc = to_chunks(q), to_chunks(k * dh ** -0.5), to_chunks(v)
    ic = gate_chunks(i_pre)
    lfc = gate_chunks(jax.nn.log_sigmoid(f_pre))
    mask = jnp.tril(jnp.ones((CHUNK, CHUNK), dtype=bool))

    def step(carry, inp):
        C, nv, m = carry
        qb, kb, vb, ib, lfb = inp
        a = jnp.cumsum(lfb, axis=-1)
        a_last = a[..., -1]
        D = jnp.where(mask, a[..., :, None] - a[..., None, :] + ib[..., None, :], -jnp.inf)
        inter = a + m[..., None]
        m_t = jnp.maximum(inter, D.max(-1))
        w_inter = jnp.exp(inter - m_t)
        s = jnp.einsum('bhtd,bhsd->bhts', qb, kb) * jnp.exp(D - m_t[..., None])
        num = jnp.einsum('bhts,bhsd->bhtd', s, vb) + w_inter[..., None] * jnp.einsum('bhtd,bhde->bhte', qb, C)
        den = s.sum(-1) + w_inter * jnp.einsum('bhtd,bhd->bht', qb, nv)
        h = num / jnp.maximum(jnp.abs(den), jnp.exp(-m_t))[..., None]
        g = a_last[..., None] - a + ib
        m_new = jnp.maximum(a_last + m, g.max(-1))
        wg = jnp.exp(g - m_new[..., None])
        decay = jnp.exp(a_last + m - m_new)
        C_new = decay[..., None, None] * C + jnp.einsum('bhs,bhsd,bhse->bhde', wg, kb, vb)
        n_new = decay[..., None] * nv + jnp.einsum('bhs,bhsd->bhd', wg, kb)
        return (C_new, n_new, m_new), h

    init = (jnp.zeros((B, H, dh, dh), jnp.float32), jnp.zeros((B, H, dh), jnp.float32),
            jnp.zeros((B, H), jnp.float32))
    _, h = lax.scan(step, init, (qc, kc, vc, ic, lfc))
    return h.transpose(1, 0, 3, 2, 4).reshape(B, S, H, dh)


def _flip(t):
    return jnp.flip(t, axis=1)


def _mlstm_branch(xm, vm, om, gm, conv_w, conv_b, w_q, w_k, b_gate, norm_g):
    B, S, _ = xm.shape
    xc = jax.nn.silu(_dwconv(xm, conv_w, conv_b)).reshape(B, S, H_M, DH_M)
    q = jnp.einsum('bshd,hde->bshe', xc, w_q).astype(jnp.float32)
    k = jnp.einsum('bshd,hde->bshe', xc, w_k).astype(jnp.float32)
    v = vm.reshape(B, S, H_M, DH_M).astype(jnp.float32)
    g = (gm.reshape(B, S, 2, 2, H_M) + b_gate).astype(jnp.float32)
    h_fwd = _mlstm_direction(q, k, v, g[:, :, 0, 0], g[:, :, 0, 1])
    h_bwd = _flip(_mlstm_direction(_flip(q), _flip(k), _flip(v), _flip(g[:, :, 1, 0]), _flip(g[:, :, 1, 1])))
    h = h_fwd + h_bwd
    mu = h.mean(-1, keepdims=True)
    var = jnp.square(h - mu).mean(-1, keepdims=True)
    h = (h - mu) * lax.rsqrt(var + LN_EPS) * norm_g.reshape(H_M, DH_M).astype(jnp.float32)
    return (jax.nn.sigmoid(om.astype(jnp.float32)) * h.reshape(B, S, D_M)).astype(xm.dtype)


def _ssm_combine(e1, e2):
    a1, b1 = e1
    a2, b2 = e2
    return (a1 * a2, a2 * b1 + b2)


def _s5_branch(u, a_re, a_im, log_dt, b_re, b_im, c_re, c_im, d_skip, w_glu):
    B, S, _ = u.shape
    uf = u.astype(jnp.float32).reshape(B, S, N_GROUPS, GROUP_W)
    b_c = lax.complex(b_re.astype(jnp.float32), b_im.astype(jnp.float32))
    c_c = lax.complex(c_re.astype(jnp.float32), c_im.astype(jnp.float32))
    bu = jnp.einsum('bsgc,gpc->bsgp', uf.astype(jnp.complex64), b_c)
    y = d_skip.astype(jnp.float32) * uf
    for direction in range(2):
        lam = lax.complex(a_re[direction].astype(jnp.float32), a_im[direction].astype(jnp.float32))
        dt = jnp.exp(log_dt[direction].astype(jnp.float32))[:, None]
        a_bar = jnp.exp(lam * dt)
        b_scale = (a_bar - 1.0) / lam
        _, states = lax.associative_scan(_ssm_combine, (jnp.broadcast_to(a_bar, bu.shape), b_scale * bu),
                                         axis=1, reverse=(direction == 1))
        y = y + jnp.einsum('bsgp,gcp->bsgc', states, c_c).real
    y = jax.nn.gelu(y).reshape(B, S, D_S).astype(u.dtype)
    val, gate = jnp.split(y @ w_glu, 2, axis=-1)
    return val * jax.nn.sigmoid(gate)


def _token_mixer(x, p, l):
    B, S, _ = x.shape
    proj = x @ p['w_in'][l]
    cq, ckv, kr, xm, vm, om, gm, us, gpre = jnp.split(proj, np.cumsum(IN_SPLITS)[:-1].tolist(), axis=-1)
    y_a = _mla(cq, ckv, kr, p['q_norm_g'][l], p['kv_norm_g'][l], p['w_uq'][l], p['w_ukv'][l]) @ p['w_proj_a'][l]
    y_b = _mlstm_branch(xm, vm, om, gm, p['conv_m_w'][l], p['conv_m_b'][l], p['w_q_m'][l], p['w_k_m'][l],
                        p['b_mlstm_gate'][l], p['mh_norm_g'][l]) @ p['w_proj_b'][l]
    y_c = _s5_branch(us, p['s5_a_re'][l], p['s5_a_im'][l], p['s5_log_dt'][l], p['s5_b_re'][l], p['s5_b_im'][l],
                     p['s5_c_re'][l], p['s5_c_im'][l], p['s5_d'][l], p['w_glu'][l])
    gates = jax.nn.sigmoid(gpre.reshape(B, S, N_BRANCH, D_MODEL) + p['b_merge'][l])
    merged = gates[:, :, 0] * y_a + gates[:, :, 1] * y_b + gates[:, :, 2] * y_c
    return merged @ p['w_o'][l]


def _conv_ffn(x, w_up, conv_w, conv_b, w_down):
    a, b = jnp.split(x @ w_up, 2, axis=-1)
    return (jax.nn.gelu(_dwconv(a, conv_w, conv_b)) * b) @ w_down


def _encoder(x, p):
    x = _layer_norm(x, p['ln0_g'], p['ln0_b'])
    for l in range(DEPTH):
        x = _layer_norm(ALPHA * x + _token_mixer(x, p, l), p['ln1_g'][l], p['ln1_b'][l])
        x = _layer_norm(ALPHA * x + _conv_ffn(x, p['w_up'][l], p['conv_f_w'][l], p['conv_f_b'][l], p['w_down'][l]),
                        p['ln2_g'][l], p['ln2_b'][l])
    return x


def setup_inputs(seed: int = 0) -> dict:
    key = jax.random.key(seed)
    ks = iter(jax.random.split(key, 48))
    L = DEPTH

    def nrm(shape, scale):
        return jax.random.normal(next(ks), shape, jnp.float32) * scale

    def gain(shape):
        return 1.0 + nrm(shape, 0.02)

    i_bias = nrm((L, 2, H_M), 0.1)
    f_bias = jnp.linspace(3.0, 6.0, H_M, dtype=jnp.float32) + nrm((L, 2, H_M), 0.1)
    return {
        'x_prompt': nrm((BATCH, SEQ, D_MODEL), 1.0),
        'x_sample': nrm((DEC_BATCH, DEC_SEQ, D_MODEL), 1.0),
        'ln0_g': gain((D_MODEL,)),
        'ln0_b': nrm((D_MODEL,), 0.02),
        'w_in': nrm((L, D_MODEL, N_IN), D_MODEL ** -0.5),
        'b_mlstm_gate': jnp.stack([i_bias, f_bias], axis=2),
        'b_merge': nrm((L, N_BRANCH, D_MODEL), 0.1),
        'q_norm_g': gain((L, Q_LORA)),
        'kv_norm_g': gain((L, KV_LORA)),
        'w_uq': nrm((L, Q_LORA, H_A * (NOPE_DIM + ROPE_DIM)), Q_LORA ** -0.5),
        'w_ukv': nrm((L, KV_LORA, H_A * (NOPE_DIM + V_DIM)), KV_LORA ** -0.5),
        'w_proj_a': nrm((L, H_A * V_DIM, D_MODEL), (H_A * V_DIM) ** -0.5),
        'conv_m_w': nrm((L, CONV_M, D_M), CONV_M ** -0.5),
        'conv_m_b': nrm((L, D_M), 0.02),
        'w_q_m': nrm((L, H_M, DH_M, DH_M), DH_M ** -0.5),
        'w_k_m': nrm((L, H_M, DH_M, DH_M), DH_M ** -0.5),
        'mh_norm_g': gain((L, D_M)),
        'w_proj_b': nrm((L, D_M, D_MODEL), D_M ** -0.5),
        's5_a_re': -0.5 + nrm((L, 2, N_GROUPS, STATE_P), 0.01),
        's5_a_im': jnp.pi * jnp.arange(STATE_P, dtype=jnp.float32) + nrm((L, 2, N_GROUPS, STATE_P), 0.01),
        's5_log_dt': jax.random.uniform(next(ks), (L, 2, N_GROUPS), jnp.float32,
                                        minval=math.log(1e-3), maxval=math.log(1e-1)),
        's5_b_re': nrm((L, N_GROUPS, STATE_P, GROUP_W), (2 * GROUP_W) ** -0.5),
        's5_b_im': nrm((L, N_GROUPS, STATE_P, GROUP_W), (2 * GROUP_W) ** -0.5),
        's5_c_re': nrm((L, N_GROUPS, GROUP_W, STATE_P), (2 * STATE_P) ** -0.5),
        's5_c_im': nrm((L, N_GROUPS, GROUP_W, STATE_P), (2 * STATE_P) ** -0.5),
        's5_d': nrm((L, N_GROUPS, GROUP_W), 1.0),
        'w_glu': nrm((L, D_S, 2 * D_MODEL), D_S ** -0.5),
        'w_o': nrm((L, D_MODEL, D_MODEL), BETA * D_MODEL ** -0.5),
        'ln1_g': gain((L, D_MODEL)),
        'ln1_b': nrm((L, D_MODEL), 0.02),
        'w_up': nrm((L, D_MODEL, 2 * D_FF), D_MODEL ** -0.5),
        'conv_f_w': nrm((L, CONV_F, D_FF), CONV_F ** -0.5),
        'conv_f_b': nrm((L, D_FF), 0.02),
        'w_down': nrm((L, D_FF, D_MODEL), BETA * D_FF ** -0.5),
        'ln2_g': gain((L, D_MODEL)),
        'ln2_b': nrm((L, D_MODEL), 0.02),
    }


def reference(x_prompt, x_sample, ln0_g, ln0_b, w_in, b_mlstm_gate, b_merge, q_norm_g, kv_norm_g, w_uq, w_ukv,
              w_proj_a, conv_m_w, conv_m_b, w_q_m, w_k_m, mh_norm_g, w_proj_b, s5_a_re, s5_a_im, s5_log_dt,
              s5_b_re, s5_b_im, s5_c_re, s5_c_im, s5_d, w_glu, w_o, ln1_g, ln1_b, w_up, conv_f_w, conv_f_b,
              w_down, ln2_g, ln2_b):
    params = dict(ln0_g=ln0_g, ln0_b=ln0_b, w_in=w_in, b_mlstm_gate=b_mlstm_gate, b_merge=b_merge,
                  q_norm_g=q_norm_g, kv_norm_g=kv_norm_g, w_uq=w_uq, w_ukv=w_ukv, w_proj_a=w_proj_a,
                  conv_m_w=conv_m_w, conv_m_b=conv_m_b, w_q_m=w_q_m, w_k_m=w_k_m, mh_norm_g=mh_norm_g,
                  w_proj_b=w_proj_b, s5_a_re=s5_a_re, s5_a_im=s5_a_im, s5_log_dt=s5_log_dt, s5_b_re=s5_b_re,
                  s5_b_im=s5_b_im, s5_c_re=s5_c_re, s5_c_im=s5_c_im, s5_d=s5_d, w_glu=w_glu, w_o=w_o,
                  ln1_g=ln1_g, ln1_b=ln1_b, w_up=w_up, conv_f_w=conv_f_w, conv_f_b=conv_f_b, w_down=w_down,
                  ln2_g=ln2_g, ln2_b=ln2_b)
    y_prompt = _encoder(x_prompt, params)
    y_sample = _encoder(x_sample, params)
    return (y_prompt, y_sample)
```

```python
import math
import numpy as np
import concourse.bass as bass
import concourse.mybir as mybir
from concourse.bass_utils import run_bass_kernel_spmd
from contextlib import ExitStack

F32 = mybir.dt.float32
BF16 = mybir.dt.bfloat16
AF = mybir.ActivationFunctionType
ALU = mybir.AluOpType
AX = mybir.AxisListType

D = 1024
KD = 8
H_A = 8
NIN = 5552
DFF = 2816
NFC = 22
ALPHA = 8 ** 0.25
LN_EPS = 1e-5
TT = 512
O_CQ, O_CKV, O_KR, O_XM, O_VM, O_OM, O_GM, O_US, O_GP = 0, 256, 384, 416, 928, 1440, 1952, 1968, 2480
WNAMES = ['ln0_g', 'ln0_b', 'w_in', 'b_mlstm_gate', 'b_merge', 'q_norm_g', 'kv_norm_g', 'w_uq', 'w_ukv', 'w_proj_a',
          'conv_m_w', 'conv_m_b', 'w_q_m', 'w_k_m', 'mh_norm_g', 'w_proj_b', 's5_a_re', 's5_a_im', 's5_log_dt',
          's5_b_re', 's5_b_im', 's5_c_re', 's5_c_im', 's5_d', 'w_glu', 'w_o', 'ln1_g', 'ln1_b', 'w_up', 'conv_f_w',
          'conv_f_b', 'w_down', 'ln2_g', 'ln2_b']


class Buf:
    __slots__ = ("name", "lw", "rd")

    def __init__(self, name=""):
        self.name = name
        self.lw = None
        self.rd = {}


class Tl:
    def __init__(self, t, name=""):
        self.t = t
        self.b = Buf(name)

    def __getitem__(self, k):
        return self.t[k]


class DT:
    def __init__(self, ap, name, ntile):
        self.ap = ap
        self.bs = [Buf(name + str(i)) for i in range(ntile)]

    def __getitem__(self, k):
        return self.ap[k]


def _bl(xs):
    out = []
    for x in xs:
        if isinstance(x, Buf):
            out.append(x)
        elif isinstance(x, (list, tuple)):
            out.extend(_bl(x))
        else:
            out.append(x.b)
    return out


class Prog:
    COMPUTE = ("pe", "act", "dve", "pool")

    def __init__(self, nc, ndma=12):
        self.nc = nc
        self.es = ExitStack()
        self.eng = {"pe": nc.tensor, "act": nc.scalar, "dve": nc.vector, "pool": nc.gpsimd, "sp": nc.sync}
        self.sem = {}
        for e in self.COMPUTE:
            self.sem[e] = self.es.enter_context(nc.semaphore("s_" + e))
        self.cnt = {e: 0 for e in self.COMPUTE}
        self.dq = {}
        for q in ("sp", "pool"):
            sems = [self.es.enter_context(nc.semaphore("d_%s%d" % (q, i))) for i in range(ndma)]
            self.dq[q] = {"sems": sems, "tgt": [0] * ndma, "i": 0}
        self.waited = {e: {} for e in self.eng}
        self.semobj = {}
        self.ninstr = 0

    def _semkey(self, s):
        k = id(s)
        self.semobj[k] = s
        return k

    def _need(self, e, tok, deps):
        if tok is None:
            return
        k, v = tok
        if self.waited[e].get(k, 0) >= v:
            return
        if deps.get(k, 0) < v:
            deps[k] = v

    def _collect(self, e, reads, writes, is_dma=False):
        deps = {}
        own = self._semkey(self.sem[e]) if (e in self.COMPUTE and not is_dma) else None
        for b in reads:
            self._need(e, b.lw, deps)
        for b in writes:
            if b.lw is not None and b.lw[0] != own:
                self._need(e, b.lw, deps)
            for key, tok in b.rd.items():
                if tok[0] != own:
                    self._need(e, tok, deps)
        if e == "pe" and own in deps:
            del deps[own]
        for k, v in deps.items():
            self.eng[e].wait_ge(self.semobj[k], v)
            self.waited[e][k] = v
            self.ninstr += 1

    def _update(self, tok, reads, writes, rkey):
        for b in writes:
            b.lw = tok
            b.rd = {}
        for b in reads:
            b.rd[rkey] = tok

    def op(self, e, fn, reads=(), writes=()):
        reads = _bl(reads)
        writes = _bl(writes)
        self._collect(e, reads, writes)
        ins = fn(self.eng[e])
        self.cnt[e] += 1
        ins.then_inc(self.sem[e], 1)
        tok = (self._semkey(self.sem[e]), self.cnt[e])
        self._update(tok, reads, writes, e)
        self.ninstr += 1
        return tok

    def dma(self, q, out, in_, reads=(), writes=(), **kw):
        reads = _bl(reads)
        writes = _bl(writes)
        q = "pool" if type(out.tensor).__name__.startswith("DRam") else "sp"
        d = self.dq[q]
        i = d["i"]
        d["i"] = (i + 1) % len(d["sems"])
        s = d["sems"][i]
        k = self._semkey(s)
        if d["tgt"][i] > 0 and self.waited[q].get(k, 0) < d["tgt"][i]:
            self.eng[q].wait_ge(s, d["tgt"][i])
            self.waited[q][k] = d["tgt"][i]
        self._collect(q, reads, writes, is_dma=True)
        ins = self.eng[q].dma_start(out=out, in_=in_, **kw)
        d["tgt"][i] += 16
        ins.then_inc(s, 16)
        tok = (k, d["tgt"][i])
        self._update(tok, reads, writes, ("dma", k))
        self.ninstr += 1
        return tok

    def barrier(self):
        toks = [(self._semkey(self.sem[e]), self.cnt[e]) for e in self.COMPUTE if self.cnt[e] > 0]
        for q, d in self.dq.items():
            for s, t in zip(d["sems"], d["tgt"]):
                if t > 0:
                    toks.append((self._semkey(s), t))
        for e in self.eng:
            for k, v in toks:
                if e in self.COMPUTE and k == self._semkey(self.sem[e]):
                    continue
                if self.waited[e].get(k, 0) < v:
                    self.eng[e].wait_ge(self.semobj[k], v)
                    self.waited[e][k] = v
                    self.ninstr += 1

    def close(self):
        self.barrier()
        self.es.close()


def wshapes(L):
    return {
        'ln0_g': (D,), 'ln0_b': (D,), 'w_in': (L, D, NIN), 'b_mlstm_gate': (L, 2, 2, 4), 'b_merge': (L, 3, D),
        'q_norm_g': (L, 256), 'kv_norm_g': (L, 128), 'w_uq': (L, 256, 768), 'w_ukv': (L, 128, 1024),
        'w_proj_a': (L, 512, D), 'conv_m_w': (L, 3, 512), 'conv_m_b': (L, 512), 'w_q_m': (L, 4, 128, 128),
        'w_k_m': (L, 4, 128, 128), 'mh_norm_g': (L, 512), 'w_proj_b': (L, 512, D), 's5_a_re': (L, 2, 32, 64),
        's5_a_im': (L, 2, 32, 64), 's5_log_dt': (L, 2, 32), 's5_b_re': (L, 32, 64, 16), 's5_b_im': (L, 32, 64, 16),
        's5_c_re': (L, 32, 16, 64), 's5_c_im': (L, 32, 16, 64), 's5_d': (L, 32, 16), 'w_glu': (L, 512, 2 * D),
        'w_o': (L, D, D), 'ln1_g': (L, D), 'ln1_b': (L, D), 'w_up': (L, D, 2 * DFF), 'conv_f_w': (L, 3, DFF),
        'conv_f_b': (L, DFF), 'w_down': (L, DFF, D), 'ln2_g': (L, D), 'ln2_b': (L, D)}


def build(T, SL, DEPTH, dbg=(), branches=("a", "b", "c"), stop_after=None):
    NSEG = T // SL
    NT = T // TT
    NCH = T // 128
    nc = bass.Bass("TRN2", target_bir_lowering=False)
    x_in = nc.dram_tensor("x", [T, D], F32, kind="ExternalInput").ap()
    link_in = nc.dram_tensor("link", [1, 1], F32, kind="ExternalInput").ap()
    W = {n: nc.dram_tensor(n, list(s), F32, kind="ExternalInput").ap() for n, s in wshapes(DEPTH).items()}
    y_out = nc.dram_tensor("y", [T, D], F32, kind="ExternalOutput").ap()
    P = Prog(nc)
    es = P.es
    NSL = "allow_slow_non_contiguous"

    def scratch(name, shape, dt):
        kind = "ExternalOutput" if name in dbg else "Internal"
        return DT(nc.dram_tensor(name, list(shape), dt, kind=kind).ap(), name, NT)

    XT = scratch("XT", [D, T], F32)
    XTB = scratch("XTB", [D, T], BF16)
    AT = scratch("AT", [DFF, T], F32)
    OT = scratch("OT", [512, T], BF16)
    HBT = scratch("HBT", [512, T], BF16)
    YC = scratch("YC", [D, T], F32)
    QT = scratch("QT", [8, 98, T], BF16)
    KT = scratch("KT", [8, 98, T], BF16)
    VA = scratch("VA", [T, 8, 128], BF16)
    XM = scratch("XM", [512, T], F32)
    XCB = scratch("XCB", [512, T], BF16)
    VM = scratch("VM", [T, 512], BF16)
    OMS = scratch("OMS", [T, 512], F32)
    GM = scratch("GM", [16, T], F32)
    US = scratch("US", [512, T], F32)
    USB = scratch("USB", [512, T], BF16)
    HF = scratch("HF", [T, 512], F32)
    Y1 = scratch("Y1", [512, T], F32)
    RC = scratch("RC", [32, T], F32)
    RS = scratch("RS", [32, T], F32)

    uid = [0]

    def sbt(stack, name, shape, dt=F32):
        uid[0] += 1
        name = "%s_%d" % (name, uid[0])
        return Tl(stack.enter_context(nc.sbuf_tensor(name, list(shape), dt)), name)

    ps = [Tl(es.enter_context(nc.psum_tensor("ps%d" % i, [128, 512], F32)), "ps%d" % i) for i in range(8)]
    ident = sbt(es, "ident", [128, 128])
    identb = sbt(es, "identb", [128, 128], BF16)
    ones32 = sbt(es, "ones32", [128, 128])
    linkc = sbt(es, "linkc", [128, 1])
    lng = sbt(es, "lng", [128, 2 * DEPTH + 1, 8])
    lnb = sbt(es, "lnb", [128, 2 * DEPTH + 1, 8])
    stg = [sbt(es, "stg%d" % i, [128, 1024]) for i in range(2)]
    stgi = [0]

    def mm(out, lhsT, rhs, start, stop, reads, writes):
        return P.op("pe", lambda e: e.matmul(out, lhsT=lhsT, rhs=rhs, start=start, stop=stop), reads, writes)

    def act(out, in_, func, reads, writes, **kw):
        return P.op("act", lambda e: e.activation(out=out, in_=in_, func=func, **kw), reads, writes)

    def tt(out, in0, in1, op, reads, writes, eng="dve"):
        return P.op(eng, lambda e: e.tensor_tensor(out=out, in0=in0, in1=in1, op=op), reads, writes)

    def ts(out, in0, s1, s2, op0, op1, reads, writes, eng="dve"):
        if s2 is None:
            return P.op(eng, lambda e: e.tensor_scalar(out=out, in0=in0, scalar1=s1, scalar2=None, op0=op0), reads, writes)
        return P.op(eng, lambda e: e.tensor_scalar(out=out, in0=in0, scalar1=s1, scalar2=s2, op0=op0, op1=op1), reads, writes)

    def stt(out, in0, scalar, in1, op0, op1, reads, writes):
        return P.op("dve", lambda e: e.scalar_tensor_tensor(out=out, in0=in0, scalar=scalar, in1=in1, op0=op0, op1=op1),
                    reads, writes)

    def load_cast(dst, src, dtl, n, scale=1.0, sreads=()):
        o = 0
        while o < n:
            w = min(1024, n - o)
            s = stg[stgi[0] % 2]
            stgi[0] += 1
            np_ = dst.shape[0]
            P.dma("sp", s[0:np_, 0:w], src[:, o:o + w], [], [s])
            act(dst[:, o:o + w], s[0:np_, 0:w], AF.Copy if isinstance(scale, float) else AF.Identity,
                [s] + list(sreads), [dtl], scale=scale)
            o += w

    P.op("pool", lambda e: e.memset(ident[:], 0.0), [], [ident])
    P.op("pool", lambda e: e.affine_select(out=ident[:], in_=ident[:], compare_op=ALU.not_equal, fill=1.0, base=0,
                                           pattern=[[-1, 128]], channel_multiplier=1), [ident], [ident])
    P.op("dve", lambda e: e.tensor_copy(out=identb[:], in_=ident[:]), [ident], [identb])
    P.op("dve", lambda e: e.memset(ones32[:], 1.0), [], [ones32])
    P.dma("sp", linkc[:], link_in.partition_broadcast(128), [], [linkc])
    with nc.allow_non_contiguous_dma(reason="tiny param vectors"):
        P.dma("sp", lng[:, 0, :], W['ln0_g'].rearrange("(c p) -> p c", p=128), [], [lng])
        P.dma("sp", lnb[:, 0, :], W['ln0_b'].rearrange("(c p) -> p c", p=128), [], [lnb])
        for l in range(DEPTH):
            for j, nm in ((1, 'ln1'), (2, 'ln2')):
                P.dma("sp", lng[:, j + 2 * l, :], W[nm + '_g'][l].rearrange("(c p) -> p c", p=128), [], [lng])
                P.dma("sp", lnb[:, j + 2 * l, :], W[nm + '_b'][l].rearrange("(c p) -> p c", p=128), [], [lnb])

    def ln_tile(y, sq, xb, sm, li, ti, pA, pB):
        mean, var = sm
        t0 = ti * TT
        act(sq[:], y[:], AF.Square, [y], [sq])
        for c in range(8):
            mm(pA[:, :], ones32[:], y[:, c, :], c == 0, c == 7, [ones32, y], [pA])
        for c in range(8):
            mm(pB[:, :], ones32[:], sq[:, c, :], c == 0, c == 7, [ones32, sq], [pB])
        P.op("act", lambda e: e.mul(out=mean[:], in_=pA[:, :], mul=1.0 / D), [pA], [mean])
        tt(var[:], mean[:], mean[:], ALU.mult, [mean], [var])
        stt(var[:], pB[:, :], 1.0 / D, var[:], ALU.mult, ALU.subtract, [pB, var], [var])
        ts(var[:], var[:], LN_EPS, None, ALU.add, None, [var], [var])
        act(var[:], var[:], AF.Sqrt, [var], [var])
        P.op("dve", lambda e: e.reciprocal(out=var[:], in_=var[:]), [var], [var])
        bc = lambda a: a[:].unsqueeze(1).to_broadcast([128, 8, TT])
        tt(y[:], y[:], bc(mean), ALU.subtract, [y, mean], [y])
        tt(y[:], y[:], bc(var), ALU.mult, [y, var], [y])
        tt(y[:], y[:], lng[:, li, :].unsqueeze(2).to_broadcast([128, 8, TT]), ALU.mult, [y, lng], [y])
        tt(y[:], y[:], lnb[:, li, :].unsqueeze(2).to_broadcast([128, 8, TT]), ALU.add, [y, lnb], [y])
        act(xb[:], y[:], AF.Copy, [y], [xb])
        P.dma("sp", XT[:, t0:t0 + TT].rearrange("(c p) t -> p c t", p=128), y[:], [y], [XT.bs[ti]])
        P.dma("pool", XTB[:, t0:t0 + TT].rearrange("(c p) t -> p c t", p=128), xb[:], [xb], [XTB.bs[ti]])

    def phase0():
        with ExitStack() as st:
            xin = [sbt(st, "p0x%d" % i, [128, D]) for i in range(2)]
            y = sbt(st, "p0y", [128, 8, TT])
            sq = sbt(st, "p0sq", [128, 8, TT])
            xb = sbt(st, "p0xb", [128, 8, TT], BF16)
            sm = (sbt(st, "p0m", [128, TT]), sbt(st, "p0v", [128, TT]))
            k = 0
            for ti in range(NT):
                for b in range(4):
                    xi = xin[k % 2]
                    k += 1
                    r0 = ti * TT + b * 128
                    P.dma("sp", xi[:], x_in[r0:r0 + 128, :], [], [xi])
                    for c in range(8):
                        mm(ps[c][:, b * 128:(b + 1) * 128], xi[:, c * 128:(c + 1) * 128], ident[:], True, True,
                           [xi, ident], [ps[c]])
                for c in range(8):
                    if c % 2 == 0:
                        P.op("dve", lambda e: e.tensor_copy(out=y[:, c, :], in_=ps[c][:, :]), [ps[c]], [y])
                    else:
                        act(y[:, c, :], ps[c][:, :], AF.Copy, [ps[c]], [y])
                ln_tile(y, sq, xb, sm, 0, ti, ps[0], ps[1])
            P.barrier()

    def phase_out():
        with ExitStack() as st:
            xs = [sbt(st, "pox%d" % i, [128, 8, TT]) for i in range(2)]
            yo = [sbt(st, "poy%d" % i, [128, D]) for i in range(2)]
            k = 0
            for ti in range(NT):
                t0 = ti * TT
                x = xs[ti % 2]
                P.dma("sp", x[:], XT[:, t0:t0 + TT].rearrange("(c p) t -> p c t", p=128), [XT.bs[ti]], [x])
                for b in range(4):
                    o = yo[k % 2]
                    k += 1
                    for c in range(8):
                        pb = ps[(c // 4) + 2 * (b % 2)]
                        mm(pb[:, (c % 4) * 128:(c % 4 + 1) * 128], x[:, c, b * 128:(b + 1) * 128], ident[:], True, True,
                           [x, ident], [pb])
                    for hlf in range(2):
                        pb = ps[hlf + 2 * (b % 2)]
                        if hlf == 0:
                            P.op("dve", lambda e: e.tensor_copy(out=o[:, 0:512], in_=pb[:, :]), [pb], [o])
                        else:
                            act(o[:, 512:1024], pb[:, :], AF.Copy, [pb], [o])
                    r0 = t0 + b * 128
                    P.dma("pool", y_out[r0:r0 + 128, :], o[:], [o], [])
            P.barrier()

    def seg_edge(t):
        if t <= 0 or t >= T:
            return 2
        return 1 if t % SL == 0 else 0

    def load_halo(q, tl, src, r0, nrow, ti, dtbufs):
        t0 = ti * TT
        le, re = seg_edge(t0), seg_edge(t0 + TT)
        lo = t0 - 1 if le != 2 else t0
        hi = t0 + TT + 1 if re != 2 else t0 + TT
        rd = [dtbufs[j] for j in (ti - 1, ti, ti + 1) if 0 <= j < NT]
        P.dma(q, tl[0:nrow, (lo - t0 + 1):(hi - t0 + 1)], src[r0:r0 + nrow, lo:hi], rd, [tl])
        if le == 2:
            P.op("dve", lambda e: e.memset(tl[0:nrow, 0:1], 0.0), [], [tl])
        elif le == 1:
            ts(tl[0:nrow, 0:1], tl[0:nrow, 0:1], linkc[0:nrow, 0:1], None, ALU.mult, None, [tl, linkc], [tl])
        if re == 2:
            P.op("dve", lambda e: e.memset(tl[0:nrow, TT + 1:TT + 2], 0.0), [], [tl])
        elif re == 1:
            ts(tl[0:nrow, TT + 1:TT + 2], tl[0:nrow, TT + 1:TT + 2], linkc[0:nrow, 0:1], None, ALU.mult, None,
               [tl, linkc], [tl])

    def phase_f1(l):
        with ExitStack() as st:
            wa = sbt(st, "f1wa", [128, 8, DFF], BF16)
            for k in range(8):
                load_cast(wa[:, k, :], W['w_up'][l, k * 128:(k + 1) * 128, 0:DFF], wa, DFF)
            xbs = [sbt(st, "f1xb%d" % i, [128, 8, TT], BF16) for i in range(2)]
            ao = [sbt(st, "f1ao%d" % i, [128, TT]) for i in range(4)]
            n = 0
            for ti in range(NT):
                t0 = ti * TT
                xb = xbs[ti % 2]
                P.dma("sp", xb[:], XTB[:, t0:t0 + TT].rearrange("(c p) t -> p c t", p=128), [XTB.bs[ti]], [xb])
                for j in range(NFC):
                    pb = ps[n % 4]
                    a = ao[n % 4]
                    for k in range(8):
                        mm(pb[:, :], wa[:, k, j * 128:(j + 1) * 128], xb[:, k, :], k == 0, k == 7, [wa, xb], [pb])
                    if n % 2 == 0:
                        act(a[:], pb[:, :], AF.Copy, [pb], [a])
                    else:
                        P.op("dve", lambda e: e.tensor_copy(out=a[:], in_=pb[:, :]), [pb], [a])
                    P.dma("pool" if n % 2 else "sp", AT[j * 128:(j + 1) * 128, t0:t0 + TT], a[:], [a], [AT.bs[ti]])
                    n += 1
            P.barrier()

    def phase_f2(l):
        with ExitStack() as st:
            wb = sbt(st, "f2wb", [128, 8, DFF], BF16)
            wd = sbt(st, "f2wd", [128, NFC, D], BF16)
            cw = sbt(st, "f2cw", [128, 3, NFC])
            cb = sbt(st, "f2cb", [128, NFC])
            for k in range(8):
                load_cast(wb[:, k, :], W['w_up'][l, k * 128:(k + 1) * 128, DFF:2 * DFF], wb, DFF)
            for j in range(NFC):
                load_cast(wd[:, j, :], W['w_down'][l, j * 128:(j + 1) * 128, :], wd, D)
            with nc.allow_non_contiguous_dma(reason="tiny param vectors"):
                for w in range(3):
                    P.dma("sp", cw[:, w, :], W['conv_f_w'][l, w].rearrange("(c p) -> p c", p=128), [], [cw])
                P.dma("sp", cb[:, :], W['conv_f_b'][l].rearrange("(c p) -> p c", p=128), [], [cb])
            xb1 = sbt(st, "f2xb", [128, 8, TT], BF16)
            y = sbt(st, "f2y", [128, 8, TT])
            hh = sbt(st, "f2hh", [128, NFC, TT], BF16)
            sq = sbt(st, "f2sq", [128, 8, TT])
            xbo = sbt(st, "f2xbo", [128, 8, TT], BF16)
            sm = (sbt(st, "f2m", [128, TT]), sbt(st, "f2v", [128, TT]))
            ats = [sbt(st, "f2at%d" % i, [128, TT + 2]) for i in range(3)]
            accs = [sbt(st, "f2ac%d" % i, [128, TT]) for i in range(2)]
            n = 0
            for ti in range(NT):
                t0 = ti * TT
                P.dma("sp", xb1[:], XTB[:, t0:t0 + TT].rearrange("(c p) t -> p c t", p=128), [XTB.bs[ti]], [xb1])
                for j in range(NFC):
                    at = ats[n % 3]
                    acc = accs[n % 2]
                    pb = ps[n % 3]
                    n += 1
                    load_halo("sp" if j % 2 else "pool", at, AT, j * 128, 128, ti, AT.bs)
                    for k in range(8):
                        mm(pb[:, :], wb[:, k, j * 128:(j + 1) * 128], xb1[:, k, :], k == 0, k == 7, [wb, xb1], [pb])
                    ts(acc[:], at[:, 0:TT], cw[:, 0, j:j + 1], None, ALU.mult, None, [at, cw], [acc])
                    stt(acc[:], at[:, 1:TT + 1], cw[:, 1, j:j + 1], acc[:], ALU.mult, ALU.add, [at, cw, acc], [acc])
                    stt(acc[:], at[:, 2:TT + 2], cw[:, 2, j:j + 1], acc[:], ALU.mult, ALU.add, [at, cw, acc], [acc])
                    act(acc[:], acc[:], AF.Gelu_apprx_tanh, [acc, cb], [acc], bias=cb[:, j:j + 1])
                    tt(hh[:, j, :], acc[:], pb[:, :], ALU.mult, [acc, pb], [hh])
                P.dma("sp", y[:], XT[:, t0:t0 + TT].rearrange("(c p) t -> p c t", p=128), [XT.bs[ti]], [y])
                for o in range(8):
                    pb = ps[4 + o % 2]
                    for j in range(NFC):
                        mm(pb[:, :], wd[:, j, o * 128:(o + 1) * 128], hh[:, j, :], j == 0, j == NFC - 1, [wd, hh], [pb])
                    stt(y[:, o, :], y[:, o, :], ALPHA, pb[:, :], ALU.mult, ALU.add, [y, pb], [y])
                ln_tile(y, sq, xbo, sm, 2 + 2 * l, ti, ps[6], ps[7])
            P.barrier()

    def phase_mrg(l):
        with ExitStack() as st:
            wg = sbt(st, "mgwg", [128, 8, 3 * D], BF16)
            wo = sbt(st, "mgwo", [128, 8, D], BF16)
            wpa = sbt(st, "mgwpa", [128, 4, D], BF16)
            wpb = sbt(st, "mgwpb", [128, 4, D], BF16)
            bm = sbt(st, "mgbm", [128, 3, 8])
            mhg = sbt(st, "mgmhg", [128, 4])
            with nc.allow_non_contiguous_dma(reason="tiny param vectors"):
                for br in range(3):
                    P.dma("sp", bm[:, br, :], W['b_merge'][l, br].rearrange("(c p) -> p c", p=128), [], [bm])
                P.dma("sp", mhg[:, :], W['mh_norm_g'][l].rearrange("(c p) -> p c", p=128), [], [mhg])
            for k in range(8):
                load_cast(wg[:, k, :], W['w_in'][l, k * 128:(k + 1) * 128, O_GP:NIN], wg, 3 * D)
                load_cast(wo[:, k, :], W['w_o'][l, k * 128:(k + 1) * 128, :], wo, D)
            for k in range(4):
                load_cast(wpa[:, k, :], W['w_proj_a'][l, k * 128:(k + 1) * 128, :], wpa, D)
                load_cast(wpb[:, k, :], W['w_proj_b'][l, k * 128:(k + 1) * 128, :], wpb, D, scale=mhg[:, k:k + 1],
                          sreads=[mhg])
            xb = sbt(st, "mgxb", [128, 8, TT], BF16)
            y = sbt(st, "mgy", [128, 8, TT])
            ot = sbt(st, "mgot", [128, 4, TT], BF16)
            hb = sbt(st, "mghb", [128, 4, TT], BF16)
            yc = sbt(st, "mgyc", [128, 8, TT])
            mg = sbt(st, "mgmg", [128, 8, TT], BF16)
            sq = sbt(st, "mgsq", [128, 8, TT])
            xbo = sbt(st, "mgxbo", [128, 8, TT], BF16)
            sm = (sbt(st, "mgm", [128, TT]), sbt(st, "mgv", [128, TT]))
            g = [sbt(st, "mgg%d" % i, [128, TT]) for i in range(3)]
            t1 = sbt(st, "mgt1", [128, TT])
            t2 = sbt(st, "mgt2", [128, TT])
            fm = lambda a: a.rearrange("(c p) t -> p c t", p=128)
            for ti in range(NT):
                t0 = ti * TT
                P.dma("sp", xb[:], fm(XTB[:, t0:t0 + TT]), [XTB.bs[ti]], [xb])
                if "a" in branches:
                    P.dma("sp", ot[:], fm(OT[:, t0:t0 + TT]), [OT.bs[ti]], [ot])
                if "b" in branches:
                    P.dma("pool", hb[:], fm(HBT[:, t0:t0 + TT]), [HBT.bs[ti]], [hb])
                if "c" in branches:
                    P.dma("sp", yc[:], fm(YC[:, t0:t0 + TT]), [YC.bs[ti]], [yc])
                for c in range(8):
                    cs = slice(c * 128, (c + 1) * 128)
                    terms = []
                    for bi, br in enumerate("abc"):
                        if br not in branches:
                            continue
                        pb = ps[bi]
                        for k in range(8):
                            mm(pb[:, :], wg[:, k, bi * D + c * 128: bi * D + (c + 1) * 128], xb[:, k, :], k == 0, k == 7,
                               [wg, xb], [pb])
                        act(g[bi][:], pb[:, :], AF.Sigmoid, [pb, bm], [g[bi]], bias=bm[:, bi, c:c + 1])
                        if br == "a":
                            for k in range(4):
                                mm(ps[3][:, :], wpa[:, k, cs], ot[:, k, :], k == 0, k == 3, [wpa, ot], [ps[3]])
                            terms.append((g[bi], ps[3], ps[3][:, :]))
                        elif br == "b":
                            for k in range(4):
                                mm(ps[4][:, :], wpb[:, k, cs], hb[:, k, :], k == 0, k == 3, [wpb, hb], [ps[4]])
                            terms.append((g[bi], ps[4], ps[4][:, :]))
                        else:
                            terms.append((g[bi], yc, yc[:, c, :]))
                    if not terms:
                        P.op("dve", lambda e: e.memset(mg[:, c, :], 0.0), [], [mg])
                    for i, (gt, src, sap) in enumerate(terms):
                        last = i == len(terms) - 1
                        if i == 0:
                            tt(mg[:, c, :] if last else t1[:], gt[:], sap, ALU.mult, [gt, src], [mg if last else t1])
                        else:
                            tt(t2[:], gt[:], sap, ALU.mult, [gt, src], [t2])
                            tt(mg[:, c, :] if last else t1[:], t1[:], t2[:], ALU.add, [t1, t2], [mg if last else t1])
                P.dma("sp", y[:], fm(XT[:, t0:t0 + TT]), [XT.bs[ti]], [y])
                for o in range(8):
                    pb = ps[5 + o % 2]
                    for k in range(8):
                        mm(pb[:, :], wo[:, k, o * 128:(o + 1) * 128], mg[:, k, :], k == 0, k == 7, [wo, mg], [pb])
                    stt(y[:, o, :], y[:, o, :], ALPHA, pb[:, :], ALU.mult, ALU.add, [y, pb], [y])
                ln_tile(y, sq, xbo, sm, 1 + 2 * l, ti, ps[6], ps[7])
            P.barrier()


    I32 = mybir.dt.int32
    TWO_PI = 2.0 * math.pi

    def phase_init():
        with ExitStack() as st:
            CB = min(T, 2048)
            ji = sbt(st, "inji", [32, 1], I32)
            jf = sbt(st, "injf", [32, 1])
            jt = sbt(st, "injt", [32, 1])
            invf = sbt(st, "ininvf", [32, 1])
            sgn = sbt(st, "insgn", [32, 1])
            oml = sbt(st, "inoml", [32, 1])
            ti32 = sbt(st, "inti", [32, CB], I32)
            si32 = sbt(st, "insi", [32, CB], I32)
            tf = sbt(st, "intf", [32, CB])
            sf = sbt(st, "insf", [32, CB])
            ang = sbt(st, "inang", [32, CB])
            q = sbt(st, "inq", [32, CB])
            red = sbt(st, "inred", [32, CB])
            P.op("pool", lambda e: e.iota(ji[:], pattern=[[0, 1]], base=0, channel_multiplier=1), [], [ji])
            P.op("dve", lambda e: e.tensor_copy(out=jf[:], in_=ji[:]), [ji], [jf])
            ts(jt[:], jf[:], 16.0, 16.0, ALU.is_ge, ALU.mult, [jf], [jt])
            tt(invf[:], jf[:], jt[:], ALU.subtract, [jf, jt], [invf])
            act(invf[:], invf[:], AF.Exp, [invf], [invf], scale=-math.log(10000.0) / 16.0)
            ts(sgn[:], jt[:], 1.0 / 8.0, -1.0, ALU.mult, ALU.add, [jt], [sgn])
            ts(oml[:], linkc[0:32, :], -1.0, 1.0, ALU.mult, ALU.add, [linkc], [oml])
            for b0 in range(0, T, CB):
                P.op("pool", lambda e: e.iota(ti32[:], pattern=[[1, CB]], base=b0, channel_multiplier=0), [], [ti32])
                if SL >= CB:
                    P.op("pool", lambda e: e.iota(si32[:], pattern=[[0, CB]], base=(b0 // SL) * SL, channel_multiplier=0),
                         [], [si32])
                else:
                    P.op("pool", lambda e: e.iota(si32[:], pattern=[[SL, CB // SL], [0, SL]], base=b0,
                                                  channel_multiplier=0), [], [si32])
                P.op("dve", lambda e: e.tensor_copy(out=tf[:], in_=ti32[:]), [ti32], [tf])
                P.op("dve", lambda e: e.tensor_copy(out=sf[:], in_=si32[:]), [si32], [sf])
                stt(tf[:], sf[:], oml[:, 0:1], tf[:], ALU.mult, ALU.subtract, [sf, oml, tf], [tf])
                ts(ang[:], tf[:], invf[:, 0:1], -1.0, ALU.mult, ALU.mult, [tf, invf], [ang])
                for which, dst in ((0, RS), (1, RC)):
                    if which == 1:
                        ts(ang[:], ang[:], math.pi / 2.0, None, ALU.add, None, [ang], [ang])
                    ts(q[:], ang[:], 1.0 / TWO_PI, None, ALU.mult, None, [ang], [q])
                    ts(q[:], q[:], 12582912.0, 12582912.0, ALU.add, ALU.subtract, [q], [q])
                    stt(red[:], q[:], -6.28125, ang[:], ALU.mult, ALU.add, [q, ang], [red])
                    stt(red[:], q[:], -(TWO_PI - 6.28125), red[:], ALU.mult, ALU.add, [q, red], [red])
                    ts(red[:], red[:], -3.1415925, 3.1415925, ALU.max, ALU.min, [red], [red])
                    act(red[:], red[:], AF.Sin, [red], [red])
                    if which == 0:
                        ts(red[:], red[:], sgn[:, 0:1], None, ALU.mult, None, [red, sgn], [red])
                    P.dma("sp", dst[:, b0:b0 + CB], red[:], [red], dst.bs[b0 // TT:(b0 + CB) // TT])
            onesr = sbt(st, "inones", [2, 8, TT], BF16)
            mrow = sbt(st, "inmrow", [8, TT], BF16)
            lk8 = sbt(st, "inlk8", [8, 1])
            P.op("dve", lambda e: e.memset(onesr[:], 1.0), [], [onesr])
            ts(lk8[:], linkc[0:8, :], -1.0, 30000.0, ALU.add, ALU.mult, [linkc], [lk8])
            P.op("dve", lambda e: e.memset(mrow[:], 1.0), [], [mrow])
            ts(mrow[:], mrow[:], lk8[:, 0:1], None, ALU.mult, None, [mrow, lk8], [mrow])
            for ti in range(NT):
                t0 = ti * TT
                P.dma("sp", KT[:, 96:98, t0:t0 + TT].rearrange("h r t -> r h t"), onesr[:], [onesr], [KT.bs[ti]])
                P.dma("sp", QT[:, 97, t0:t0 + TT], mrow[:], [mrow], [QT.bs[ti]])
            P.barrier()

    def phase_a(l):
        with ExitStack() as st:
            wA = sbt(st, "awA", [128, 8, O_GP], BF16)
            wkrs = sbt(st, "awkrs", [128, 8, 32], BF16)
            wq = sbt(st, "awq", [128, 2, 8, 96], BF16)
            wqs = sbt(st, "awqs", [128, 2, 8, 96], BF16)
            wkn = sbt(st, "awkn", [128, 8, 64], BF16)
            wv = sbt(st, "awv", [128, 8, 64], BF16)
            qg = sbt(st, "aqg", [128, 2])
            kvg = sbt(st, "akvg", [128, 1])
            gmb = sbt(st, "agmb", [16, 1])
            indq = sbt(st, "aindq", [96, 8, 8], BF16)
            indk = sbt(st, "aindk", [128, 4, 8], BF16)
            onr = sbt(st, "aonr", [32, 8], BF16)
            with nc.allow_non_contiguous_dma(reason="tiny param vectors"):
                P.dma("sp", qg[:, :], W['q_norm_g'][l].rearrange("(c p) -> p c", p=128), [], [qg])
                P.dma("sp", kvg[:, :], W['kv_norm_g'][l].rearrange("(c p) -> p c", p=128), [], [kvg])
                P.dma("sp", gmb[:, :], W['b_mlstm_gate'][l].rearrange("a b (c o) -> (a b c) o", o=1), [], [gmb])
            for k in range(8):
                load_cast(wA[:, k, :], W['w_in'][l, k * 128:(k + 1) * 128, 0:O_GP], wA, O_GP)
                act(wkrs[:, k, 0:16], wA[:, k, O_KR + 16:O_KR + 32], AF.Copy, [wA], [wkrs])
                act(wkrs[:, k, 16:32], wA[:, k, O_KR:O_KR + 16], AF.Copy, [wA], [wkrs])
            for c2 in range(2):
                s = stg[stgi[0] % 2]
                stgi[0] += 1
                P.dma("sp", s[:, 0:768], W['w_uq'][l, c2 * 128:(c2 + 1) * 128, :], [], [s])
                sv = s[:, 0:768].rearrange("p (h d) -> p h d", h=8)
                sc = qg[:, c2:c2 + 1]
                act(wq[:, c2, :, :], sv[:, :, :], AF.Identity, [s, qg], [wq], scale=sc)
                P.op("dve", lambda e: e.memset(wqs[:, c2, :, 0:64], 0.0), [], [wqs])
                act(wqs[:, c2, :, 64:80], sv[:, :, 80:96], AF.Identity, [s, qg], [wqs], scale=sc)
                act(wqs[:, c2, :, 80:96], sv[:, :, 64:80], AF.Identity, [s, qg], [wqs], scale=sc)
            s = stg[stgi[0] % 2]
            stgi[0] += 1
            P.dma("sp", s[:, 0:1024], W['w_ukv'][l, :, :], [], [s])
            sv = s[:, 0:1024].rearrange("p (h d) -> p h d", h=8)
            act(wkn[:, :, :], sv[:, :, 0:64], AF.Identity, [s, kvg], [wkn], scale=kvg[:, 0:1])
            act(wv[:, :, :], sv[:, :, 64:128], AF.Identity, [s, kvg], [wv], scale=kvg[:, 0:1])
            P.op("dve", lambda e: e.memset(indq[:], 0.0), [], [indq])
            P.op("dve", lambda e: e.memset(indk[:], 0.0), [], [indk])
            P.op("dve", lambda e: e.memset(onr[:], 1.0), [], [onr])
            for h in range(8):
                P.op("dve", lambda e: e.memset(indq[:, h, h:h + 1], 1.0), [], [indq])
                P.op("dve", lambda e: e.memset(indk[(h % 2) * 64:(h % 2) * 64 + 64, h // 2, h:h + 1], 1.0), [], [indk])
            xbs = [sbt(st, "axb%d" % i, [128, 8, TT], BF16) for i in range(2)]
            cqf = sbt(st, "acqf", [128, 2, TT])
            sqt = sbt(st, "asq", [128, 2, TT])
            rstd = sbt(st, "arstd", [128, TT])
            cqn = sbt(st, "acqn", [128, 2, TT], BF16)
            ckf = sbt(st, "ackf", [128, TT])
            ckvn = sbt(st, "ackvn", [128, TT], BF16)
            rct = sbt(st, "arc", [96, TT])
            rst = sbt(st, "ars", [96, TT])
            r1 = sbt(st, "ar1", [96, TT])
            r2 = sbt(st, "ar2", [96, TT])
            krr = sbt(st, "akrr", [32, TT], BF16)
            fst = [sbt(st, "afst%d" % i, [128, TT]) for i in range(4)]
            bst = [sbt(st, "abst%d" % i, [128, TT], BF16) for i in range(4)]
            qos = [sbt(st, "aqo%d" % i, [96, TT], BF16) for i in range(2)]
            sqb = [sbt(st, "asqb%d" % i, [128, TT], BF16) for i in range(2)]
            vas = [sbt(st, "ava%d" % i, [128, 8, 128], BF16) for i in range(2)]
            qn2 = sbt(st, "aqn2", [8, T])
            km2 = sbt(st, "akm2", [8, 1])
            kmx = sbt(st, "akmx", [8, 1])
            mrw = sbt(st, "amrw", [8, T], BF16)
            for v in vas:
                P.op("dve", lambda e: e.memset(v[:], 1.0), [], [v])
            P.op("dve", lambda e: e.memset(km2[:], 0.0), [], [km2])
            cnt = {"b": 0, "f": 0, "s": 0, "q": 0}

            def nb():
                cnt["b"] += 1
                return ps[cnt["b"] % 6]

            def nf():
                cnt["f"] += 1
                return fst[cnt["f"] % 4]

            def nbs():
                cnt["s"] += 1
                return bst[cnt["s"] % 4]

            def dq():
                cnt["q"] += 1
                return "sp" if cnt["q"] % 2 else "pool"

            def rstd_from(pss, n):
                ts(rstd[:], pss[:, :], 1.0 / n, LN_EPS, ALU.mult, ALU.add, [pss], [rstd])
                act(rstd[:], rstd[:], AF.Sqrt, [rstd], [rstd])
                P.op("dve", lambda e: e.reciprocal(out=rstd[:], in_=rstd[:]), [rstd], [rstd])

            for ti in range(NT):
                t0 = ti * TT
                tsl = slice(t0, t0 + TT)
                xb = xbs[ti % 2]
                P.dma("sp", xb[:], XTB[:, tsl].rearrange("(c p) t -> p c t", p=128), [XTB.bs[ti]], [xb])
                P.dma("pool", rct[0:32, :], RC[:, tsl], [RC.bs[ti]], [rct])
                P.dma("pool", rst[0:32, :], RS[:, tsl], [RS.bs[ti]], [rst])
                P.dma("pool", rct[64:96, :], RC[:, tsl], [RC.bs[ti]], [rct])
                P.dma("pool", rst[64:96, :], RS[:, tsl], [RS.bs[ti]], [rst])

                def fm_out(c0, m):
                    pb = nb()
                    for k in range(8):
                        mm(pb[0:m, :], wA[:, k, c0:c0 + m], xb[:, k, :], k == 0, k == 7, [wA, xb], [pb])
                    return pb
                for c2 in range(2):
                    pb = fm_out(O_CQ + c2 * 128, 128)
                    act(cqf[:, c2, :], pb[:, :], AF.Copy, [pb], [cqf])
                    act(sqt[:, c2, :], pb[:, :], AF.Square, [pb], [sqt])
                pss = nb()
                for c2 in range(2):
                    mm(pss[:, :], ones32[:], sqt[:, c2, :], c2 == 0, c2 == 1, [ones32, sqt], [pss])
                rstd_from(pss, 256.0)
                tt(cqn[:], cqf[:], rstd[:].unsqueeze(1).to_broadcast([128, 2, TT]), ALU.mult, [cqf, rstd], [cqn])
                pb = fm_out(O_CKV, 128)
                act(ckf[:], pb[:, :], AF.Copy, [pb], [ckf])
                act(sqt[:, 0, :], pb[:, :], AF.Square, [pb], [sqt])
                pss = nb()
                mm(pss[:, :], ones32[:], sqt[:, 0, :], True, True, [ones32, sqt], [pss])
                rstd_from(pss, 128.0)
                tt(ckvn[:], ckf[:], rstd[:], ALU.mult, [ckf, rstd], [ckvn])
                pb = fm_out(O_KR, 32)
                pb2 = nb()
                for k in range(8):
                    mm(pb2[0:32, :], wkrs[:, k, :], xb[:, k, :], k == 0, k == 7, [wkrs, xb], [pb2])
                tt(r1[0:32, :], pb2[0:32, :], rst[0:32, :], ALU.mult, [pb2, rst], [r1])
                tt(r2[0:32, :], pb[0:32, :], rct[0:32, :], ALU.mult, [pb, rct], [r2])
                tt(krr[:], r1[0:32, :], r2[0:32, :], ALU.add, [r1, r2], [krr])
                for h in range(8):
                    P.dma(dq(), KT[h, 64:96, tsl], krr[:], [krr], [KT.bs[ti]])
                for (c0, dst) in ((O_XM, XM), (O_US, US)):
                    for c in range(4):
                        pb = fm_out(c0 + c * 128, 128)
                        f = nf()
                        if c % 2:
                            act(f[:], pb[:, :], AF.Copy, [pb], [f])
                        else:
                            P.op("dve", lambda e: e.tensor_copy(out=f[:], in_=pb[:, :]), [pb], [f])
                        P.dma(dq(), dst[c * 128:(c + 1) * 128, tsl], f[:], [f], [dst.bs[ti]])
                        if dst is US:
                            bs_ = nbs()
                            act(bs_[:], pb[:, :], AF.Copy, [pb], [bs_])
                            P.dma(dq(), USB[c * 128:(c + 1) * 128, tsl], bs_[:], [bs_], [USB.bs[ti]])
                pb = fm_out(O_GM, 16)
                f = nf()
                act(f[0:16, :], pb[0:16, :], AF.Identity, [pb, gmb], [f], bias=gmb[:, 0:1])
                P.dma(dq(), GM[:, tsl], f[0:16, :], [f], [GM.bs[ti]])
                for b in range(4):
                    r0 = t0 + b * 128
                    pb = nb()
                    for k in range(8):
                        mm(pb[:, :], xb[:, k, b * 128:(b + 1) * 128], wA[:, k, O_VM:O_VM + 512], k == 0, k == 7, [wA, xb], [pb])
                    bs_ = nbs()
                    P.op("dve", lambda e: e.tensor_copy(out=bs_[:], in_=pb[:, :]), [pb], [bs_])
                    P.dma(dq(), VM[r0:r0 + 128, :], bs_[:], [bs_], [VM.bs[ti]])
                    pb = nb()
                    for k in range(8):
                        mm(pb[:, :], xb[:, k, b * 128:(b + 1) * 128], wA[:, k, O_OM:O_OM + 512], k == 0, k == 7, [wA, xb], [pb])
                    f = nf()
                    act(f[:], pb[:, :], AF.Sigmoid, [pb], [f])
                    P.dma(dq(), OMS[r0:r0 + 128, :], f[:], [f], [OMS.bs[ti]])
                for h in range(8):
                    pq = nb()
                    for c2 in range(2):
                        mm(pq[0:96, :], wq[:, c2, h, :], cqn[:, c2, :], c2 == 0, c2 == 1, [wq, cqn], [pq])
                    pqs = nb()
                    for c2 in range(2):
                        mm(pqs[0:96, :], wqs[:, c2, h, :], cqn[:, c2, :], c2 == 0, c2 == 1, [wqs, cqn], [pqs])
                    qo = qos[h % 2]
                    tt(r1[64:96, :], pqs[64:96, :], rst[64:96, :], ALU.mult, [pqs, rst], [r1])
                    tt(r2[64:96, :], pq[64:96, :], rct[64:96, :], ALU.mult, [pq, rct], [r2])
                    tt(qo[64:96, :], r1[64:96, :], r2[64:96, :], ALU.add, [r1, r2], [qo])
                    act(qo[0:64, :], pq[0:64, :], AF.Copy, [pq], [qo])
                    sb_ = sqb[h % 2]
                    act(sb_[0:96, :], qo[:, :], AF.Square, [qo], [sb_])
                    mm(ps[6][0:8, :], indq[:, h, :], sb_[0:96, :], h == 0, h == 7, [indq, sb_], [ps[6]])
                    P.dma(dq(), QT[h, 0:96, tsl], qo[:, :], [qo], [QT.bs[ti]])
                P.op("dve", lambda e: e.tensor_copy(out=qn2[:, tsl], in_=ps[6][0:8, :]), [ps[6]], [qn2])
                sb_ = sqb[0]
                act(sb_[0:32, :], krr[:, :], AF.Square, [krr], [sb_])
                mm(ps[7][0:8, :], onr[:, :], sb_[0:32, :], True, False, [onr, sb_], [ps[7]])
                for j in range(4):
                    pb = nb()
                    mm(pb[:, :], wkn[:, 2 * j:2 * j + 2, :].rearrange("p h d -> p (h d)"), ckvn[:], True, True, [wkn, ckvn], [pb])
                    ko = nbs()
                    P.op("dve", lambda e: e.tensor_copy(out=ko[:], in_=pb[:, :]), [pb], [ko])
                    sb_ = sqb[1 - j % 2]
                    act(sb_[:, :], ko[:, :], AF.Square, [ko], [sb_])
                    mm(ps[7][0:8, :], indk[:, j, :], sb_[:, :], False, j == 3, [indk, sb_], [ps[7]])
                    P.dma(dq(), KT[2 * j, 0:64, tsl], ko[0:64, :], [ko], [KT.bs[ti]])
                    P.dma(dq(), KT[2 * j + 1, 0:64, tsl], ko[64:128, :], [ko], [KT.bs[ti]])
                P.op("dve", lambda e: e.reduce_max(out=kmx[:], in_=ps[7][0:8, :], axis=AX.X), [ps[7]], [kmx])
                tt(km2[:], km2[:], kmx[:], ALU.max, [km2, kmx], [km2])
                for b in range(4):
                    r0 = t0 + b * 128
                    pb = nb()
                    mm(pb[:, :], ckvn[:, b * 128:(b + 1) * 128], wv[:].rearrange("p h d -> p (h d)"), True, True, [ckvn, wv], [pb])
                    va = vas[b % 2]
                    P.op("dve", lambda e: e.tensor_copy(out=va[:, :, 0:64], in_=pb[:, :].rearrange("p (h d) -> p h d", h=8)),
                         [pb], [va])
                    P.dma(dq(), VA[r0:r0 + 128, :, :], va[:], [va], [VA.bs[ti]])
            act(qn2[:], qn2[:], AF.Sqrt, [qn2, km2], [qn2], scale=km2[:, 0:1])
            ts(mrw[:], qn2[:], -1.0, None, ALU.mult, None, [qn2], [mrw])
            P.dma("sp", QT[:, 96, :], mrw[:], [mrw], QT.bs)
            P.barrier()


    def phase_att(l):
        SCALE = 96.0 ** -0.5
        with ExitStack() as st:
            kTs = [sbt(st, "tk%d" % i, [98, T], BF16) for i in range(2)]
            qTs = [sbt(st, "tq%d" % i, [98, T], BF16) for i in range(2)]
            vhs = [sbt(st, "tv%d" % i, [128, NCH, 128], BF16) for i in range(2)]
            pts = [sbt(st, "tp%d" % i, [128, TT], BF16) for i in range(3)]
            rlt = sbt(st, "trl", [128, TT])
            osb = [sbt(st, "to%d" % i, [64, TT], BF16) for i in range(2)]
            n = 0
            for h in range(8):
                kT, qT, vh = kTs[h % 2], qTs[h % 2], vhs[h % 2]
                P.dma("sp", kT[:], KT[h, :, :], KT.bs, [kT])
                P.dma("pool", qT[:], QT[h, :, :], QT.bs, [qT])
                P.dma("sp", vh[:], VA[:, h, :].rearrange("(n p) d -> p n d", p=128), VA.bs, [vh])
                for i in range(NT):
                    qs = slice(i * TT, (i + 1) * TT)
                    acc = ps[4 + i % 2]
                    segq = (i * TT) // SL

                    def score(kb):
                        R = 97 if (kb * 128) // SL == segq else 98
                        pb = ps[(n + kb) % 3]
                        mm(pb[:, :], kT[0:R, kb * 128:(kb + 1) * 128], qT[0:R, qs], True, True, [kT, qT], [pb])
                        return pb
                    pbn = score(0)
                    for kb in range(NCH):
                        pb = pbn
                        pt = pts[(n + kb) % 3]
                        if kb + 1 < NCH:
                            pbn = score(kb + 1)
                        act(pt[:], pb[:, :], AF.Exp, [pb], [pt], scale=SCALE)
                        mm(acc[:, :], vh[:, kb, :], pt[:], kb == 0, kb == NCH - 1, [vh, pt], [acc])
                    n += NCH
                    P.op("dve", lambda e: e.reciprocal(out=rlt[64:128, :], in_=acc[64:128, :]), [acc], [rlt])
                    o = osb[i % 2]
                    tt(o[:], acc[0:64, :], rlt[64:128, :], ALU.mult, [acc, rlt], [o])
                    P.dma("pool" if i % 2 else "sp", OT[h * 64:(h + 1) * 64, qs], o[:], [o], [OT.bs[i]])
            P.barrier()


    def emit_sin(dst_t, dst, ang_t, ang, q_t, q, shift):
        ts(q, ang, 1.0 / TWO_PI, shift / TWO_PI, ALU.mult, ALU.add, [ang_t], [q_t])
        ts(q, q, 12582912.0, 12582912.0, ALU.add, ALU.subtract, [q_t], [q_t])
        stt(dst, q, -6.28125, ang, ALU.mult, ALU.add, [q_t, ang_t], [dst_t])
        stt(dst, q, -(TWO_PI - 6.28125), dst, ALU.mult, ALU.add, [q_t, dst_t], [dst_t])
        if shift:
            ts(dst, dst, shift, None, ALU.add, None, [dst_t], [dst_t])
        ts(dst, dst, -3.1415925, 3.1415925, ALU.max, ALU.min, [dst_t], [dst_t])
        act(dst, dst, AF.Sin, [dst_t], [dst_t])

    def phase_s5(l):
        NSC = "tiny param vectors"
        with ExitStack() as lst:
            CLr = sbt(lst, "sCLr", [128, 16, 128], BF16)
            CLi = sbt(lst, "sCLi", [128, 16, 128], BF16)
            wglu = sbt(lst, "swglu", [128, 4, 2 * D], BF16)
            dsk = sbt(lst, "sdsk", [128, 4])
            for k in range(4):
                load_cast(wglu[:, k, :], W['w_glu'][l, k * 128:(k + 1) * 128, :], wglu, 2 * D)
            with nc.allow_non_contiguous_dma(reason=NSC):
                P.dma("sp", dsk[:, :], W['s5_d'][l].rearrange("(c g) w -> (g w) c", c=4), [], [dsk])
            with ExitStack() as st:
                Z = [sbt(st, "sZ%d" % i, [128, 4, 128]) for i in range(2)]
                for i, nm in enumerate(('s5_c_re', 's5_c_im')):
                    P.op("dve", lambda e: e.memset(Z[i][:], 0.0), [], [Z[i]])
                    for ch in range(4):
                        for jj in range(4):
                            for two in range(2):
                                g = 2 * (4 * ch + jj) + two
                                P.dma("sp" if two else "pool", Z[i][32 * jj + 16 * two:32 * jj + 16 * two + 16, ch, 64 * two:64 * two + 64],
                                      W[nm][l, g], [], [Z[i]])
                P.op("dve", lambda e: e.memset(CLr[:], 0.0), [], [CLr])
                P.op("dve", lambda e: e.memset(CLi[:], 0.0), [], [CLi])
                for i, CL in enumerate((CLr, CLi)):
                    for ch in range(4):
                        pb = ps[(2 * i + ch) % 4]
                        mm(pb[:, 0:128], Z[i][:, ch, :], ident[:], True, True, [Z[i], ident], [pb])
                        for jj in range(4):
                            act(CL[:, 4 * ch + jj, 32 * jj:32 * jj + 32], pb[:, 32 * jj:32 * jj + 32], AF.Copy, [pb], [CL],
                                scale=(1.0 if i == 0 else -1.0))
                P.barrier()
            for d in (0, 1):
                with ExitStack() as st:
                    sm = {n: sbt(st, "s5" + n, [128, 16]) for n in
                          ("are", "aim", "dt", "rmag", "th", "cs", "sn", "q", "abr", "abi", "nr", "ni", "den", "bsr", "bsi", "t")}
                    with nc.allow_non_contiguous_dma(reason=NSC):
                        for two in range(2):
                            prt = slice(two * 64, two * 64 + 64)
                            P.dma("sp", sm["are"][prt, :], W['s5_a_re'][l, d].rearrange("(j two) p -> two p j", two=2)[two], [], [sm["are"]])
                            P.dma("sp", sm["aim"][prt, :], W['s5_a_im'][l, d].rearrange("(j two) p -> two p j", two=2)[two], [], [sm["aim"]])
                            P.dma("sp", sm["dt"][prt, :], W['s5_log_dt'][l, d].rearrange("(j two) -> two j", two=2)[two].partition_broadcast(64),
                                  [], [sm["dt"]])
                    A = lambda n: sm[n][:, :]
                    act(A("dt"), A("dt"), AF.Exp, [sm["dt"]], [sm["dt"]])
                    tt(A("rmag"), A("are"), A("dt"), ALU.mult, [sm["are"], sm["dt"]], [sm["rmag"]])
                    act(A("rmag"), A("rmag"), AF.Exp, [sm["rmag"]], [sm["rmag"]])
                    tt(A("th"), A("aim"), A("dt"), ALU.mult, [sm["aim"], sm["dt"]], [sm["th"]])
                    emit_sin(sm["sn"], A("sn"), sm["th"], A("th"), sm["q"], A("q"), 0.0)
                    emit_sin(sm["cs"], A("cs"), sm["th"], A("th"), sm["q"], A("q"), math.pi / 2.0)
                    tt(A("abr"), A("rmag"), A("cs"), ALU.mult, [sm["rmag"], sm["cs"]], [sm["abr"]])
                    tt(A("abi"), A("rmag"), A("sn"), ALU.mult, [sm["rmag"], sm["sn"]], [sm["abi"]])
                    ts(A("abr"), A("abr"), -1.0, None, ALU.add, None, [sm["abr"]], [sm["abr"]])
                    tt(A("nr"), A("abr"), A("are"), ALU.mult, [sm["abr"], sm["are"]], [sm["nr"]])
                    tt(A("t"), A("abi"), A("aim"), ALU.mult, [sm["abi"], sm["aim"]], [sm["t"]])
                    tt(A("nr"), A("nr"), A("t"), ALU.add, [sm["nr"], sm["t"]], [sm["nr"]])
                    tt(A("ni"), A("abi"), A("are"), ALU.mult, [sm["abi"], sm["are"]], [sm["ni"]])
                    tt(A("t"), A("abr"), A("aim"), ALU.mult, [sm["abr"], sm["aim"]], [sm["t"]])
                    tt(A("ni"), A("ni"), A("t"), ALU.subtract, [sm["ni"], sm["t"]], [sm["ni"]])
                    tt(A("den"), A("are"), A("are"), ALU.mult, [sm["are"]], [sm["den"]])
                    tt(A("t"), A("aim"), A("aim"), ALU.mult, [sm["aim"]], [sm["t"]])
                    tt(A("den"), A("den"), A("t"), ALU.add, [sm["den"], sm["t"]], [sm["den"]])
                    P.op("dve", lambda e: e.reciprocal(out=A("den"), in_=A("den")), [sm["den"]], [sm["den"]])
                    tt(A("bsr"), A("nr"), A("den"), ALU.mult, [sm["nr"], sm["den"]], [sm["bsr"]])
                    tt(A("bsi"), A("ni"), A("den"), ALU.mult, [sm["ni"], sm["den"]], [sm["bsi"]])
                    BLr = sbt(st, "sBLr", [128, 16, 128], BF16)
                    BLi = sbt(st, "sBLi", [128, 16, 128], BF16)
                    cosT = sbt(st, "scosT", [128, 16, TT])
                    sinT = sbt(st, "ssinT", [128, 16, TT])
                    with ExitStack() as st2:
                        Bt = [sbt(st2, "sBt%d" % i, [128, 16, 16]) for i in range(2)]
                        Bp = [sbt(st2, "sBp%d" % i, [128, 16, 16]) for i in range(2)]
                        tmp = sbt(st2, "sBtmp", [128, 16, 16])
                        X = sbt(st2, "sX", [128, 16, 2, 16])
                        for i, nm in enumerate(('s5_b_re', 's5_b_im')):
                            for two in range(2):
                                P.dma("sp", Bt[i][two * 64:two * 64 + 64, :, :],
                                      W[nm][l].rearrange("(j two) p c -> two p j c", two=2)[two], [], [Bt[i]])
                        bc = lambda n: sm[n][:, :].unsqueeze(2).to_broadcast([128, 16, 16])
                        tt(Bp[0][:], Bt[0][:], bc("bsr"), ALU.mult, [Bt[0], sm["bsr"]], [Bp[0]])
                        tt(tmp[:], Bt[1][:], bc("bsi"), ALU.mult, [Bt[1], sm["bsi"]], [tmp])
                        tt(Bp[0][:], Bp[0][:], tmp[:], ALU.subtract, [Bp[0], tmp], [Bp[0]])
                        tt(Bp[1][:], Bt[1][:], bc("bsr"), ALU.mult, [Bt[1], sm["bsr"]], [Bp[1]])
                        tt(tmp[:], Bt[0][:], bc("bsi"), ALU.mult, [Bt[0], sm["bsi"]], [tmp])
                        tt(Bp[1][:], Bp[1][:], tmp[:], ALU.add, [Bp[1], tmp], [Bp[1]])
                        for i, BL in enumerate((BLr, BLi)):
                            P.op("dve", lambda e: e.memset(BL[:], 0.0), [], [BL])
                            P.op("dve", lambda e: e.memset(X[:], 0.0), [], [X])
                            P.op("dve", lambda e: e.tensor_copy(out=X[0:64, :, 0, :], in_=Bp[i][0:64, :, :]), [Bp[i]], [X])
                            P.op("dve", lambda e: e.tensor_copy(out=X[64:128, :, 1, :], in_=Bp[i][64:128, :, :]), [Bp[i]], [X])
                            for ch in range(4):
                                pb = ps[ch]
                                mm(pb[:, 0:128], X[:, 4 * ch:4 * ch + 4, :, :].rearrange("p a b c -> p (a b c)"), ident[:], True, True,
                                   [X, ident], [pb])
                                for jj in range(4):
                                    act(BL[32 * jj:32 * jj + 32, 4 * ch + jj, :], pb[32 * jj:32 * jj + 32, 0:128], AF.Copy, [pb], [BL])
                        ti32 = sbt(st2, "sti", [128, TT], I32)
                        tau = sbt(st2, "stau", [128, TT])
                        ang = sbt(st2, "sang", [128, 16, TT])
                        qq = sbt(st2, "sqq", [128, 16, TT])
                        if d == 0:
                            P.op("pool", lambda e: e.iota(ti32[:], pattern=[[1, TT]], base=1, channel_multiplier=0), [], [ti32])
                        else:
                            P.op("pool", lambda e: e.iota(ti32[:], pattern=[[-1, TT]], base=TT, channel_multiplier=0), [], [ti32])
                        P.op("dve", lambda e: e.tensor_copy(out=tau[:], in_=ti32[:]), [ti32], [tau])
                        tt(ang[:], sm["th"][:, :].unsqueeze(2).to_broadcast([128, 16, TT]),
                           tau[:].unsqueeze(1).to_broadcast([128, 16, TT]), ALU.mult, [sm["th"], tau], [ang])
                        fl = lambda t: t[:].rearrange("p a b -> p (a b)")
                        emit_sin(sinT, fl(sinT), ang, fl(ang), qq, fl(qq), 0.0)
                        emit_sin(cosT, fl(cosT), ang, fl(ang), qq, fl(qq), math.pi / 2.0)
                        P.barrier()
                    ufs = [sbt(st, "suf%d" % i, [128, 4, TT]) for i in range(2)]
                    ubs = [sbt(st, "sub%d" % i, [128, 4, TT], BF16) for i in range(2)]
                    wk = [[sbt(st, "sw%s%d" % (n, i), [128, TT]) for i in range(2)] for n in "abcdefgh"]
                    xb_ = [[sbt(st, "sx%s%d" % (n, i), [128, TT], BF16) for i in range(2)] for n in "ri"]
                    car = [sbt(st, "scar%d" % i, [128, 16]) for i in range(2)]
                    y1t = [sbt(st, "sy1%d" % i, [128, TT]) for i in range(2)]
                    yg = sbt(st, "syg", [128, 4, TT], BF16)
                    sg = [sbt(st, "ssg%d" % i, [128, TT]) for i in range(2)]
                    P.op("dve", lambda e: e.memset(car[0][:], 0.0), [], [car[0]])
                    P.op("dve", lambda e: e.memset(car[1][:], 0.0), [], [car[1]])
                    order = list(range(NT)) if d == 0 else list(range(NT - 1, -1, -1))
                    nblk = 0
                    for it, ti in enumerate(order):
                        t0 = ti * TT
                        tsl = slice(t0, t0 + TT)
                        uf, ub = ufs[it % 2], ubs[it % 2]
                        P.dma("sp", uf[:], US[:, tsl].rearrange("(c p) t -> p c t", p=128), [US.bs[ti]], [uf])
                        act(ub[:], uf[:], AF.Copy, [uf], [ub])
                        if it > 0:
                            bnd = t0 if d == 0 else t0 + TT
                            if bnd % SL == 0:
                                for cc in car:
                                    ts(cc[:], cc[:], linkc[:, 0:1], None, ALU.mult, None, [cc, linkc], [cc])
                        for ch in range(4):
                            psy = ps[4 + ch % 2]
                            for jj in range(4):
                                j = 4 * ch + jj
                                pr_ = slice(32 * jj, 32 * jj + 32)
                                pvr, pvi = ps[(2 * nblk) % 4], ps[(2 * nblk + 1) % 4]
                                w = [wk[i][nblk % 2] for i in range(8)]
                                xr_b, xi_b = xb_[0][nblk % 2], xb_[1][nblk % 2]
                                nblk += 1
                                mm(pvr[:, :], BLr[:, j, :], ub[:, ch, :], True, True, [BLr, ub], [pvr])
                                mm(pvi[:, :], BLi[:, j, :], ub[:, ch, :], True, True, [BLi, ub], [pvi])
                                c_, s_ = cosT[:, j, :], sinT[:, j, :]
                                tt(w[0][:], pvr[:, :], c_, ALU.mult, [pvr, cosT], [w[0]])
                                tt(w[1][:], pvi[:, :], s_, ALU.mult, [pvi, sinT], [w[1]])
                                tt(w[0][:], w[0][:], w[1][:], ALU.add, [w[0], w[1]], [w[0]])
                                tt(w[2][:], pvi[:, :], c_, ALU.mult, [pvi, cosT], [w[2]])
                                tt(w[3][:], pvr[:, :], s_, ALU.mult, [pvr, sinT], [w[3]])
                                tt(w[2][:], w[2][:], w[3][:], ALU.subtract, [w[2], w[3]], [w[2]])
                                rb = sm["rmag"][:, j:j + 1].to_broadcast([128, TT])
                                for (src, dst, cc) in ((w[0], w[4], car[0]), (w[2], w[5], car[1])):
                                    if d == 0:
                                        P.op("dve", lambda e: e.tensor_tensor_scan(out=dst[:], data0=rb, data1=src[:],
                                                                                   initial=cc[:, j:j + 1], op0=ALU.mult, op1=ALU.add),
                                             [src, cc, sm["rmag"]], [dst])
                                    else:
                                        P.op("dve", lambda e: e.tensor_tensor_scan(out=dst[:, ::-1], data0=rb, data1=src[:, ::-1],
                                                                                   initial=cc[:, j:j + 1], op0=ALU.mult, op1=ALU.add),
                                             [src, cc, sm["rmag"]], [dst])
                                tt(w[6][:], w[4][:], c_, ALU.mult, [w[4], cosT], [w[6]])
                                tt(w[1][:], w[5][:], s_, ALU.mult, [w[5], sinT], [w[1]])
                                tt(w[6][:], w[6][:], w[1][:], ALU.subtract, [w[6], w[1]], [w[6]])
                                tt(w[7][:], w[4][:], s_, ALU.mult, [w[4], sinT], [w[7]])
                                tt(w[3][:], w[5][:], c_, ALU.mult, [w[5], cosT], [w[3]])
                                tt(w[7][:], w[7][:], w[3][:], ALU.add, [w[7], w[3]], [w[7]])
                                lastc = slice(TT - 1, TT) if d == 0 else slice(0, 1)
                                act(car[0][:, j:j + 1], w[6][:, lastc], AF.Copy, [w[6]], [car[0]])
                                act(car[1][:, j:j + 1], w[7][:, lastc], AF.Copy, [w[7]], [car[1]])
                                act(xr_b[:], w[6][:], AF.Copy, [w[6]], [xr_b])
                                act(xi_b[:], w[7][:], AF.Copy, [w[7]], [xi_b])
                                mm(psy[:, :], CLr[:, j, :], xr_b[:], jj == 0, False, [CLr, xr_b], [psy])
                                mm(psy[:, :], CLi[:, j, :], xi_b[:], False, jj == 3, [CLi, xi_b], [psy])
                            y1 = y1t[ch % 2]
                            if d == 0:
                                stt(y1[:], uf[:, ch, :], dsk[:, ch:ch + 1], psy[:, :], ALU.mult, ALU.add, [uf, dsk, psy], [y1])
                                P.dma("pool", Y1[ch * 128:(ch + 1) * 128, tsl], y1[:], [y1], [Y1.bs[ti]])
                            else:
                                P.dma("pool", y1[:], Y1[ch * 128:(ch + 1) * 128, tsl], [Y1.bs[ti]], [y1])
                                tt(y1[:], y1[:], psy[:, :], ALU.add, [y1, psy], [y1])
                                act(yg[:, ch, :], y1[:], AF.Gelu_apprx_tanh, [y1], [yg])
                        if d == 1:
                            for o in range(8):
                                pv, pg = ps[6], ps[7]
                                for k in range(4):
                                    mm(pv[:, :], wglu[:, k, o * 128:(o + 1) * 128], yg[:, k, :], k == 0, k == 3, [wglu, yg], [pv])
                                for k in range(4):
                                    mm(pg[:, :], wglu[:, k, D + o * 128:D + (o + 1) * 128], yg[:, k, :], k == 0, k == 3, [wglu, yg], [pg])
                                s1, s2 = sg[0], sg[1]
                                act(s1[:], pg[:, :], AF.Sigmoid, [pg], [s1])
                                s3 = y1t[o % 2]
                                tt(s2[:], s1[:], pv[:, :], ALU.mult, [s1, pv], [s2])
                                P.dma("sp" if o % 2 else "pool", YC[o * 128:(o + 1) * 128, tsl], s2[:], [s2], [YC.bs[ti]])
                    P.barrier()


    MS = nc.dram_tensor("MSscr", [4, 4, NCH], F32).ap()
    MSb = [Buf("ms%d" % i) for i in range(4)]

    def phase_m(l):
        CW = max(1, NCH // 32)
        NBLK = NCH // CW
        R = 4 * NBLK
        WD = CW * 128
        NCS = SL // 128
        with ExitStack() as st:
            cw = sbt(st, "m0cw", [128, 3, 4])
            cb = sbt(st, "m0cb", [128, 4])
            with nc.allow_non_contiguous_dma(reason="tiny param vectors"):
                for w in range(3):
                    P.dma("sp", cw[:, w, :], W['conv_m_w'][l, w].rearrange("(c p) -> p c", p=128), [], [cw])
                P.dma("sp", cb[:, :], W['conv_m_b'][l].rearrange("(c p) -> p c", p=128), [], [cb])
            xts = [sbt(st, "m0x%d" % i, [128, TT + 2]) for i in range(3)]
            accs = [sbt(st, "m0a%d" % i, [128, TT]) for i in range(2)]
            xos = [sbt(st, "m0o%d" % i, [128, TT], BF16) for i in range(2)]
            n = 0
            for ti in range(NT):
                t0 = ti * TT
                for c in range(4):
                    xt, acc, xo = xts[n % 3], accs[n % 2], xos[n % 2]
                    n += 1
                    load_halo("sp" if c % 2 else "pool", xt, XM, c * 128, 128, ti, XM.bs)
                    ts(acc[:], xt[:, 0:TT], cw[:, 0, c:c + 1], None, ALU.mult, None, [xt, cw], [acc])
                    stt(acc[:], xt[:, 1:TT + 1], cw[:, 1, c:c + 1], acc[:], ALU.mult, ALU.add, [xt, cw, acc], [acc])
                    stt(acc[:], xt[:, 2:TT + 2], cw[:, 2, c:c + 1], acc[:], ALU.mult, ALU.add, [xt, cw, acc], [acc])
                    act(xo[:], acc[:], AF.Silu, [acc, cb], [xo], bias=cb[:, c:c + 1])
                    P.dma("sp", XCB[c * 128:(c + 1) * 128, t0:t0 + TT], xo[:], [xo], [XCB.bs[ti]])
            P.barrier()
        import os as _os
        MSTOP = _os.environ.get('MSTOP', '')
        if MSTOP == 'm0':
            return
        with ExitStack() as lst:
            wq = sbt(lst, "mwq", [128, 4, 128], BF16)
            wk = sbt(lst, "mwk", [128, 4, 128], BF16)
            for h in range(4):
                load_cast(wq[:, h, :], W['w_q_m'][l, h], wq, 128)
                load_cast(wk[:, h, :], W['w_k_m'][l, h], wk, 128, scale=128.0 ** -0.5)
            maskF = sbt(lst, "mmaskF", [128, 128])
            maskB = sbt(lst, "mmaskB", [128, 128])
            for mk, cm, stp in ((maskF, -1, 1), (maskB, 1, -1)):
                P.op("pool", lambda e: e.memset(mk[:], 1.0), [], [mk])
                P.op("pool", lambda e: e.affine_select(out=mk[:], in_=mk[:], compare_op=ALU.is_ge, fill=0.0, base=0,
                                                       pattern=[[stp, 128]], channel_multiplier=cm), [mk], [mk])
            rmask = sbt(lst, "mrmask", [128, CW, 128])
            nmask = sbt(lst, "mnmask", [128, CW, 128])
            P.op("dve", lambda e: e.memset(rmask[:], 1.0), [], [rmask])
            P.op("dve", lambda e: e.memset(rmask[:, :, 0:1], 0.0), [], [rmask])
            P.op("dve", lambda e: e.memset(nmask[:], 0.0), [], [nmask])
            P.op("dve", lambda e: e.memset(nmask[:, :, 0:1], -1.0e30), [], [nmask])
            for d in (0, 1):
                with ExitStack() as st:
                    rev = (lambda ap: ap) if d == 0 else (lambda ap: ap[:, ::-1])
                    g = {n: sbt(st, "mg" + n, [R, WD]) for n in ("I", "A", "b", "cb", "M", "nM", "wi", "en", "wg", "t")}
                    cl = {n: sbt(st, "mc" + n, [R, CW]) for n in ("ms", "Ml", "t")}
                    ch4 = {n: sbt(st, "m4" + n, [4, NCH]) for n in ("al", "bm", "mn", "ms")}
                    mini = sbt(st, "mmini", [4, 1])
                    G = lambda n: g[n][:, :]
                    G3 = lambda n: g[n][:, :].rearrange("r (c w) -> r c w", w=128)
                    for h in range(4):
                        rs_ = slice(h * NBLK, (h + 1) * NBLK)
                        P.dma("sp", g["I"][rs_, :], GM[d * 8 + h, :].rearrange("(b w) -> b w", w=WD), GM.bs, [g["I"]])
                        P.dma("pool", g["A"][rs_, :], GM[d * 8 + 4 + h, :].rearrange("(b w) -> b w", w=WD), GM.bs, [g["A"]])
                    act(G("A"), G("A"), AF.Exp, [g["A"]], [g["A"]], scale=-1.0)
                    act(G("A"), G("A"), AF.Ln, [g["A"]], [g["A"]], bias=1.0)
                    rm2 = rmask[0:R, :, :].rearrange("r c w -> r (c w)")
                    nm2 = nmask[0:R, :, :].rearrange("r c w -> r (c w)")
                    P.op("dve", lambda e: e.tensor_tensor_scan(out=rev(G("t")), data0=rm2, data1=rev(G("A")), initial=0.0,
                                                               op0=ALU.mult, op1=ALU.add), [g["A"], rmask], [g["t"]])
                    tt(G("b"), G("I"), G("t"), ALU.add, [g["I"], g["t"]], [g["b"]])
                    P.op("dve", lambda e: e.tensor_tensor_scan(out=rev(G("cb")), data0=nm2, data1=rev(G("b")), initial=-1.0e30,
                                                               op0=ALU.add, op1=ALU.max), [g["b"], nmask], [g["cb"]])
                    lastw = 127 if d == 0 else 0
                    ts(cl["t"][:, :], G3("t")[:, :, lastw], -1.0, None, ALU.mult, None, [g["t"]], [cl["t"]])
                    P.dma("sp", MS[0].rearrange("h (b c) -> (h b) c", c=CW), cl["t"][:, :], [cl["t"]], [MSb[0]])
                    P.dma("sp", MS[1].rearrange("h (b c) -> (h b) c", c=CW), G3("cb")[:, :, lastw], [g["cb"]], [MSb[1]])
                    P.dma("sp", ch4["al"][:, :], MS[0], [MSb[0]], [ch4["al"]])
                    P.dma("sp", ch4["bm"][:, :], MS[1], [MSb[1]], [ch4["bm"]])
                    P.op("dve", lambda e: e.memset(mini[:], 0.0), [], [mini])
                    for sg_ in (range(NSEG) if d == 0 else range(NSEG - 1, -1, -1)):
                        c0, c1 = sg_ * NCS, (sg_ + 1) * NCS
                        P.op("dve", lambda e: e.tensor_tensor_scan(out=rev(ch4["mn"][:, c0:c1]), data0=rev(ch4["bm"][:, c0:c1]),
                                                                   data1=rev(ch4["al"][:, c0:c1]), initial=mini[:, 0:1],
                                                                   op0=ALU.max, op1=ALU.add), [ch4["bm"], ch4["al"], mini], [ch4["mn"]])
                        if d == 0:
                            P.op("dve", lambda e: e.tensor_copy(out=ch4["ms"][:, c0:c0 + 1], in_=mini[:, :]), [mini], [ch4["ms"]])
                            if NCS > 1:
                                P.op("dve", lambda e: e.tensor_copy(out=ch4["ms"][:, c0 + 1:c1], in_=ch4["mn"][:, c0:c1 - 1]),
                                     [ch4["mn"]], [ch4["ms"]])
                            lastm = ch4["mn"][:, c1 - 1:c1]
                        else:
                            P.op("dve", lambda e: e.tensor_copy(out=ch4["ms"][:, c1 - 1:c1], in_=mini[:, :]), [mini], [ch4["ms"]])
                            if NCS > 1:
                                P.op("dve", lambda e: e.tensor_copy(out=ch4["ms"][:, c0:c1 - 1], in_=ch4["mn"][:, c0 + 1:c1]),
                                     [ch4["mn"]], [ch4["ms"]])
                            lastm = ch4["mn"][:, c0:c0 + 1]
                        ts(mini[:], lastm, linkc[0:4, 0:1], None, ALU.mult, None, [ch4["mn"], linkc], [mini])
                    P.dma("sp", MS[2], ch4["ms"][:, :], [ch4["ms"]], [MSb[2]])
                    P.dma("sp", cl["ms"][:, :], MS[2].rearrange("h (b c) -> (h b) c", c=CW), [MSb[2]], [cl["ms"]])
                    msb = cl["ms"][:, :].unsqueeze(2).to_broadcast([R, CW, 128])
                    tt(G3("M"), G3("cb"), msb, ALU.max, [g["cb"], cl["ms"]], [g["M"]])
                    ts(G("nM"), G("M"), -1.0, None, ALU.mult, None, [g["M"]], [g["nM"]])
                    tt(G3("wi"), G3("nM"), msb, ALU.add, [g["nM"], cl["ms"]], [g["wi"]])
                    act(G("wi"), G("wi"), AF.Exp, [g["wi"]], [g["wi"]])
                    tt(G("en"), G("t"), G("M"), ALU.subtract, [g["t"], g["M"]], [g["en"]])
                    act(G("en"), G("en"), AF.Exp, [g["en"]], [g["en"]])
                    P.op("dve", lambda e: e.tensor_copy(out=cl["Ml"][:, :], in_=G3("M")[:, :, lastw]), [g["M"]], [cl["Ml"]])
                    tt(G3("wg"), G3("b"), cl["Ml"][:, :].unsqueeze(2).to_broadcast([R, CW, 128]), ALU.subtract,
                       [g["b"], cl["Ml"]], [g["wg"]])
                    act(G("wg"), G("wg"), AF.Exp, [g["wg"]], [g["wg"]])
                    if MSTOP == 'm1':
                        P.barrier()
                        continue
                    xcs = [sbt(st, "mxc%d" % i, [128, 4, 128], BF16) for i in range(2)]
                    vas = [sbt(st, "mva%d" % i, [128, 4, 130], BF16) for i in range(2)]
                    for v in vas:
                        P.op("dve", lambda e: e.memset(v[:], 1.0), [], [v])
                    cols = [sbt(st, "mcol%d" % i, [128, 12]) for i in range(2)]
                    C32 = [sbt(st, "mC%d" % i, [128, 130]) for i in range(4)]
                    Cb = [sbt(st, "mCb%d" % i, [128, 130], BF16) for i in range(4)]
                    for h in range(4):
                        P.op("dve", lambda e: e.memset(C32[h][:], 0.0), [], [C32[h]])
                        P.op("dve", lambda e: e.memset(Cb[h][:], 0.0), [], [Cb[h]])
                    W2 = lambda nm, i: [sbt(st, "m%s%d" % (nm, k), [128, 128], BF16 if i else F32) for k in range(2)]
                    qTs, kTs, kts, STs, qts = W2("qT", 1), W2("kT", 1), W2("kt", 1), W2("ST", 1), W2("qt", 1)
                    ETs, wbs, sfs = W2("ET", 0), W2("wb", 0), W2("sf", 0)
                    dn = [sbt(st, "mdn%d" % k, [128, 2]) for k in range(2)]
                    hst = [sbt(st, "mhst%d" % k, [128, 512]) for k in range(2)]
                    hfs = [sbt(st, "mhf%d" % k, [128, 512]) for k in range(2)]
                    oms = [sbt(st, "moms%d" % k, [128, 512]) for k in range(2)]
                    bst6 = sbt(st, "mbst", [128, 4, 6])
                    mv = sbt(st, "mmv", [128, 4, 2])
                    rsd = sbt(st, "mrsd", [128, 4])
                    hbn = [sbt(st, "mhbn%d" % k, [128, 512], BF16) for k in range(2)]
                    hbt = [sbt(st, "mhbt%d" % k, [128, 4, 128], BF16) for k in range(2)]
                    mask = maskF if d == 0 else maskB
                    order = list(range(NCH)) if d == 0 else list(range(NCH - 1, -1, -1))
                    n = 0
                    for it, c in enumerate(order):
                        blk, cwi = c // CW, c % CW
                        csl = slice(cwi * 128, (cwi + 1) * 128)
                        tsl = slice(c * 128, (c + 1) * 128)
                        ti = (c * 128) // TT
                        xc, va, col = xcs[it % 2], vas[it % 2], cols[it % 2]
                        P.dma("sp", xc[:], XCB[:, tsl].rearrange("(k p) t -> p k t", p=128), [XCB.bs[ti]], [xc])
                        P.dma("pool", va[:, :, 0:128], VM[tsl, :].rearrange("t (h e) -> t h e", h=4), [VM.bs[ti]], [va])
                        if d == 1:
                            hf, om_ = hfs[it % 2], oms[it % 2]
                            P.dma("sp", hf[:], HF[tsl, :], [HF.bs[ti]], [hf])
                            P.dma("pool", om_[:], OMS[tsl, :], [OMS.bs[ti]], [om_])
                        bnd = c * 128 if d == 0 else (c + 1) * 128
                        if it > 0 and bnd % SL == 0:
                            for h in range(4):
                                ts(C32[h][:], C32[h][:], linkc[:, 0:1], None, ALU.mult, None, [C32[h], linkc], [C32[h]])
                                act(Cb[h][:], C32[h][:], AF.Copy, [C32[h]], [Cb[h]])
                        pc = ps[7]
                        selc = ident[0:R, blk:blk + 3 * NBLK + 1:NBLK]
                        for qi, nm in enumerate(("b", "wg", "en")):
                            mm(pc[:, 4 * qi:4 * qi + 4], g[nm][:, csl], selc, True, True, [g[nm], ident], [pc])
                        P.op("dve", lambda e: e.tensor_copy(out=col[:], in_=pc[:, 0:12]), [pc], [col])
                        hs = hst[it % 2]
                        for h in range(4):
                            r = h * NBLK + blk
                            k2 = n % 2
                            n += 1
                            qT, kT, kt, ST, qt, ET, wb, sf = qTs[k2], kTs[k2], kts[k2], STs[k2], qts[k2], ETs[k2], wbs[k2], sfs[k2]
                            pq, pk, pkt, pm, pw, pS = ps[0], ps[1], ps[2], ps[3], ps[4], ps[5]
                            pn = ps[6]
                            sel = ident[0:R, r:r + 1].to_broadcast([R, 128])
                            mm(pq[:, 0:128], wq[:, h, :], xc[:, h, :], True, True, [wq, xc], [pq])
                            mm(pk[:, 0:128], wk[:, h, :], xc[:, h, :], True, True, [wk, xc], [pk])
                            mm(pkt[:, 0:128], xc[:, h, :], wk[:, h, :], True, True, [wk, xc], [pkt])
                            mm(pm[:, 0:128], sel, g["nM"][:, csl], True, True, [ident, g["nM"]], [pm])
                            mm(pw[:, 0:128], sel, g["wi"][:, csl], True, True, [ident, g["wi"]], [pw])
                            act(qT[:], pq[:, 0:128], AF.Copy, [pq], [qT])
                            act(kT[:], pk[:, 0:128], AF.Copy, [pk], [kT])
                            ts(kt[:], pkt[:, 0:128], col[:, 4 + h:5 + h], None, ALU.mult, None, [pkt, col], [kt])
                            ts(ET[:], pm[:, 0:128], col[:, h:h + 1], 0.0, ALU.add, ALU.min, [pm, col], [ET])
                            act(ET[:], ET[:], AF.Exp, [ET], [ET])
                            tt(ET[:], ET[:], mask[:], ALU.mult, [ET, mask], [ET])
                            mm(pS[:, 0:128], kT[:], qT[:], True, True, [kT, qT], [pS])
                            tt(ST[:], pS[:, 0:128], ET[:], ALU.mult, [pS, ET], [ST])
                            act(wb[:], pw[:, 0:128], AF.Copy, [pw], [wb])
                            tt(qt[:], pq[:, 0:128], wb[:], ALU.mult, [pq, wb], [qt])
                            mm(pn[:, 0:130], ST[:], va[:, h, :], True, False, [ST, va], [pn])
                            mm(pn[:, 0:130], qt[:], Cb[h][:], False, True, [qt, Cb[h]], [pn])
                            mm(pkt[:, 0:130], kt[:], va[:, h, :], True, True, [kt, va], [pkt])
                            dcol = slice(127, 128) if d == 0 else slice(0, 1)
                            stt(C32[h][:], C32[h][:], wb[:, dcol], pkt[:, 0:130], ALU.mult, ALU.add, [C32[h], wb, pkt], [C32[h]])
                            act(Cb[h][:], C32[h][:], AF.Copy, [C32[h]], [Cb[h]])
                            dd = dn[k2]
                            act(dd[:, 0:1], pn[:, 128:129], AF.Abs, [pn], [dd])
                            ts(dd[:, 0:1], dd[:, 0:1], col[:, 8 + h:9 + h], None, ALU.max, None, [dd, col], [dd])
                            P.op("dve", lambda e: e.reciprocal(out=dd[:, 1:2], in_=dd[:, 0:1]), [dd], [dd])
                            hsl = slice(h * 128, (h + 1) * 128)
                            if d == 0:
                                ts(hs[:, hsl], pn[:, 0:128], dd[:, 1:2], None, ALU.mult, None, [pn, dd], [hs])
                            else:
                                stt(hs[:, hsl], pn[:, 0:128], dd[:, 1:2], hf[:, hsl], ALU.mult, ALU.add, [pn, dd, hf], [hs])
                        if d == 0:
                            P.dma("sp", HF[tsl, :], hs[:], [hs], [HF.bs[ti]])
                        else:
                            for h in range(4):
                                P.op("dve", lambda e: e.bn_stats(out=bst6[:, h, :], in_=hs[:, h * 128:(h + 1) * 128]), [hs], [bst6])
                                P.op("dve", lambda e: e.bn_aggr(out=mv[:, h, :], in_=bst6[:, h, :]), [bst6], [mv])
                            ts(rsd[:], mv[:, :, 1], LN_EPS, None, ALU.add, None, [mv], [rsd])
                            act(rsd[:], rsd[:], AF.Sqrt, [rsd], [rsd])
                            P.op("dve", lambda e: e.reciprocal(out=rsd[:], in_=rsd[:]), [rsd], [rsd])
                            hb = hbn[it % 2]
                            for h in range(4):
                                hsl = slice(h * 128, (h + 1) * 128)
                                ts(hs[:, hsl], hs[:, hsl], mv[:, h, 0:1], rsd[:, h:h + 1], ALU.subtract, ALU.mult, [hs, mv, rsd], [hs])
                            tt(hb[:], hs[:], om_[:], ALU.mult, [hs, om_], [hb])
                            ht = hbt[it % 2]
                            pt_ = ps[7]
                            for h in range(4):
                                mm(pt_[:, h * 128:(h + 1) * 128], hb[:, h * 128:(h + 1) * 128], identb[:], True, True, [hb, identb], [pt_])
                            act(ht[:], pt_[:, :].rearrange("p (k t) -> p k t", k=4), AF.Copy, [pt_], [ht])
                            P.dma("sp", HBT[:, tsl].rearrange("(k p) t -> p k t", p=128), ht[:], [ht], [HBT.bs[ti]])
                    P.barrier()


    def gen_att(l, st):
        SCALE = 96.0 ** -0.5
        kT = sbt(st, "xk", [98, T], BF16)
        vh = sbt(st, "xv", [128, NCH, 128], BF16)
        qTs = [sbt(st, "xq%d" % i, [98, TT], BF16) for i in range(2)]
        pts = [sbt(st, "xp%d" % i, [128, TT], BF16) for i in range(3)]
        rlt = sbt(st, "xrl", [128, TT])
        osb = [sbt(st, "xo%d" % i, [64, TT], BF16) for i in range(2)]
        sbank = [ps[0], ps[1], ps[2]]
        abank = [ps[3], ps[4]]
        n = 0
        nq = 0
        for h in range(8):
            P.dma("sp", kT[:], KT[h, :, :], KT.bs, [kT])
            P.dma("pool", vh[:], VA[:, h, :].rearrange("(n p) d -> p n d", p=128), VA.bs, [vh])
            for i in range(NT):
                qs = slice(i * TT, (i + 1) * TT)
                qT = qTs[nq % 2]
                acc = abank[nq % 2]
                nq += 1
                P.dma("sp", qT[:], QT[h, :, qs], [QT.bs[i]], [qT])
                segq = (i * TT) // SL

                def score(kb):
                    R = 97 if (kb * 128) // SL == segq else 98
                    pb = sbank[(n + kb) % 3]
                    mm(pb[:, :], kT[0:R, kb * 128:(kb + 1) * 128], qT[0:R, :], True, True, [kT, qT], [pb])
                    return pb
                pend = [score(0)]
                if NCH > 1:
                    pend.append(score(1))
                for kb in range(NCH):
                    pb = pend.pop(0)
                    pt = pts[(n + kb) % 3]
                    act(pt[:], pb[:, :], AF.Exp, [pb], [pt], scale=SCALE)
                    if kb + 2 < NCH:
                        pend.append(score(kb + 2))
                    mm(acc[:, :], vh[:, kb, :], pt[:], kb == 0, kb == NCH - 1, [vh, pt], [acc])
                    yield
                n += NCH
                P.op("dve", lambda e: e.reciprocal(out=rlt[64:128, :], in_=acc[64:128, :]), [acc], [rlt])
                o = osb[i % 2]
                tt(o[:], acc[0:64, :], rlt[64:128, :], ALU.mult, [acc, rlt], [o])
                P.dma("pool", OT[h * 64:(h + 1) * 64, qs], o[:], [o], [OT.bs[i]])

    import os as _os2
    PENG = _os2.environ.get('PENG', 'dve')

    def gen_s5(l, st):
        pending = []
        rnd = [0]

        def later(k, fn):
            pending.append((rnd[0] + k, fn))

        def tick():
            rnd[0] += 1
            due = [p for p in pending if p[0] <= rnd[0]]
            for p in due:
                pending.remove(p)
            for p in due:
                p[1]()

        CLr = sbt(st, "sCLr", [128, 16, 128], BF16)
        CLi = sbt(st, "sCLi", [128, 16, 128], BF16)
        wglu = sbt(st, "swglu", [128, 4, 2 * D], BF16)
        dsk = sbt(st, "sdsk", [128, 4])
        pvr, pvi, psy = ps[5], ps[6], ps[7]
        NW = 2
        wk = [[sbt(st, "sw%s%d" % (n, i), [128, TT]) for i in range(NW)] for n in "abcdef"]

        class Al:
            def __init__(self, par, view):
                self.t = view
                self.b = par.b

            def __getitem__(self, k):
                return self.t[k]
        v3 = lambda tl: tl[:, 0:256].rearrange("p (a b) -> p a b", a=16)
        Zt = Al(wk[0][0], wk[0][0][:, :].rearrange("p (a b) -> p a b", a=4))
        Bt = [Al(wk[1][i], v3(wk[1][i])) for i in range(2)]
        Bp = [Al(wk[2][i], v3(wk[2][i])) for i in range(2)]
        tmp = Al(wk[3][0], v3(wk[3][0]))
        X = Al(wk[4][0], wk[4][0][:, :].rearrange("p (a b c) -> p a b c", a=16, b=2))
        for k in range(4):
            load_cast(wglu[:, k, :], W['w_glu'][l, k * 128:(k + 1) * 128, :], wglu, 2 * D)
            yield 2.0
        P.dma("sp", dsk[:, :], W['s5_d'][l].rearrange("(c g) w -> (g w) c", c=4), [], [dsk])
        for i, (nm, CL) in enumerate((('s5_c_re', CLr), ('s5_c_im', CLi))):
            P.op("dve", lambda e: e.memset(Zt[:], 0.0), [], [Zt])
            P.op("dve", lambda e: e.memset(CL[:], 0.0), [], [CL])
            for ch in range(4):
                for jj in range(4):
                    for two in range(2):
                        g = 2 * (4 * ch + jj) + two
                        P.dma("sp" if two else "pool", Zt[32 * jj + 16 * two:32 * jj + 16 * two + 16, ch, 64 * two:64 * two + 64],
                              W[nm][l, g], [], [Zt])
            yield 2.0
            for ch in range(4):
                mm(psy[:, 0:128], Zt[:, ch, :], ident[:], True, True, [Zt, ident], [psy])
                for jj in range(4):
                    P.op("dve", lambda e: e.tensor_scalar(out=CL[:, 4 * ch + jj, 32 * jj:32 * jj + 32], in0=psy[:, 32 * jj:32 * jj + 32],
                                                          scalar1=(1.0 if i == 0 else -1.0), scalar2=None, op0=ALU.mult), [psy], [CL])
                yield 2.0
        sm = {n: sbt(st, "s5" + n, [128, 16]) for n in
              ("are", "aim", "dt", "rmag", "th", "cs", "sn", "q", "abr", "abi", "nr", "ni", "den", "bsr", "bsi", "t")}
        A = lambda n: sm[n][:, :]
        BLr = sbt(st, "sBLr", [128, 16, 128], BF16)
        BLi = sbt(st, "sBLi", [128, 16, 128], BF16)
        cosT = sbt(st, "scosT", [128, 16, TT])
        sinT = sbt(st, "ssinT", [128, 16, TT])
        ti32 = sbt(st, "sti", [128, TT], I32)
        tau = sbt(st, "stau", [128, TT])
        ang = sbt(st, "sang", [128, TT])
        qq = sbt(st, "sqq", [128, TT])
        ub = sbt(st, "sub", [128, 4, TT], BF16)
        xb_ = [[sbt(st, "sx%s%d" % (n, i), [128, TT], BF16) for i in range(4)] for n in "ri"]
        tn = [sbt(st, "stn%d" % i, [128, 4]) for i in range(2)]
        car = [sbt(st, "scar%d" % i, [128, 16]) for i in range(2)]
        y1t = [sbt(st, "sy1%d" % i, [128, TT]) for i in range(2)]
        yg = sbt(st, "syg", [128, 4, TT], BF16)
        sg = y1t
        for d in (0, 1):
            for two in range(2):
                prt = slice(two * 64, two * 64 + 64)
                P.dma("sp", sm["are"][prt, :], W['s5_a_re'][l, d].rearrange("(j two) p -> two p j", two=2)[two], [], [sm["are"]])
                P.dma("sp", sm["aim"][prt, :], W['s5_a_im'][l, d].rearrange("(j two) p -> two p j", two=2)[two], [], [sm["aim"]])
                P.dma("sp", sm["dt"][prt, :], W['s5_log_dt'][l, d].rearrange("(j two) -> two j", two=2)[two].partition_broadcast(64),
                      [], [sm["dt"]])
            if True:
                for i, nm in enumerate(('s5_b_re', 's5_b_im')):
                    for two in range(2):
                        P.dma("pool", Bt[i][two * 64:two * 64 + 64, :, :],
                              W[nm][l].rearrange("(j two) p c -> two p j c", two=2)[two], [], [Bt[i]])
            yield 2.0
            act(A("dt"), A("dt"), AF.Exp, [sm["dt"]], [sm["dt"]])
            tt(A("rmag"), A("are"), A("dt"), ALU.mult, [sm["are"], sm["dt"]], [sm["rmag"]])
            act(A("rmag"), A("rmag"), AF.Exp, [sm["rmag"]], [sm["rmag"]])
            tt(A("th"), A("aim"), A("dt"), ALU.mult, [sm["aim"], sm["dt"]], [sm["th"]])
            yield 2.0
            emit_sin(sm["sn"], A("sn"), sm["th"], A("th"), sm["q"], A("q"), 0.0)
            yield 2.0
            emit_sin(sm["cs"], A("cs"), sm["th"], A("th"), sm["q"], A("q"), math.pi / 2.0)
            yield 2.0
            tt(A("abr"), A("rmag"), A("cs"), ALU.mult, [sm["rmag"], sm["cs"]], [sm["abr"]])
            tt(A("abi"), A("rmag"), A("sn"), ALU.mult, [sm["rmag"], sm["sn"]], [sm["abi"]])
            ts(A("abr"), A("abr"), -1.0, None, ALU.add, None, [sm["abr"]], [sm["abr"]])
            tt(A("nr"), A("abr"), A("are"), ALU.mult, [sm["abr"], sm["are"]], [sm["nr"]])
            tt(A("t"), A("abi"), A("aim"), ALU.mult, [sm["abi"], sm["aim"]], [sm["t"]])
            tt(A("nr"), A("nr"), A("t"), ALU.add, [sm["nr"], sm["t"]], [sm["nr"]])
            tt(A("ni"), A("abi"), A("are"), ALU.mult, [sm["abi"], sm["are"]], [sm["ni"]])
            tt(A("t"), A("abr"), A("aim"), ALU.mult, [sm["abr"], sm["aim"]], [sm["t"]])
            tt(A("ni"), A("ni"), A("t"), ALU.subtract, [sm["ni"], sm["t"]], [sm["ni"]])
            yield 2.0
            tt(A("den"), A("are"), A("are"), ALU.mult, [sm["are"]], [sm["den"]])
            tt(A("t"), A("aim"), A("aim"), ALU.mult, [sm["aim"]], [sm["t"]])
            tt(A("den"), A("den"), A("t"), ALU.add, [sm["den"], sm["t"]], [sm["den"]])
            P.op("dve", lambda e: e.reciprocal(out=A("den"), in_=A("den")), [sm["den"]], [sm["den"]])
            tt(A("bsr"), A("nr"), A("den"), ALU.mult, [sm["nr"], sm["den"]], [sm["bsr"]])
            tt(A("bsi"), A("ni"), A("den"), ALU.mult, [sm["ni"], sm["den"]], [sm["bsi"]])
            yield 2.0
            bc = lambda n: sm[n][:, :].unsqueeze(2).to_broadcast([128, 16, 16])
            tt(Bp[0][:], Bt[0][:], bc("bsr"), ALU.mult, [Bt[0], sm["bsr"]], [Bp[0]])
            tt(tmp[:], Bt[1][:], bc("bsi"), ALU.mult, [Bt[1], sm["bsi"]], [tmp])
            tt(Bp[0][:], Bp[0][:], tmp[:], ALU.subtract, [Bp[0], tmp], [Bp[0]])
            tt(Bp[1][:], Bt[1][:], bc("bsr"), ALU.mult, [Bt[1], sm["bsr"]], [Bp[1]])
            tt(tmp[:], Bt[0][:], bc("bsi"), ALU.mult, [Bt[0], sm["bsi"]], [tmp])
            tt(Bp[1][:], Bp[1][:], tmp[:], ALU.add, [Bp[1], tmp], [Bp[1]])
            yield 2.0
            for i, BL in enumerate((BLr, BLi)):
                P.op("dve", lambda e: e.memset(BL[:], 0.0), [], [BL])
                P.op("dve", lambda e: e.memset(X[:], 0.0), [], [X])
                P.op("dve", lambda e: e.tensor_copy(out=X[0:64, :, 0, :], in_=Bp[i][0:64, :, :]), [Bp[i]], [X])
                P.op("dve", lambda e: e.tensor_copy(out=X[64:128, :, 1, :], in_=Bp[i][64:128, :, :]), [Bp[i]], [X])
                yield 2.0
                for ch in range(4):
                    mm(psy[:, 0:128], X[:, 4 * ch:4 * ch + 4, :, :].rearrange("p a b c -> p (a b c)"), ident[:], True, True,
                       [X, ident], [psy])
                    yield 2.0
                    for jj in range(4):
                        P.op("dve", lambda e: e.tensor_copy(out=BL[32 * jj:32 * jj + 32, 4 * ch + jj, :], in_=psy[32 * jj:32 * jj + 32, 0:128]),
                             [psy], [BL])
            if d == 0:
                P.op("pool", lambda e: e.iota(ti32[:], pattern=[[1, TT]], base=1, channel_multiplier=0), [], [ti32])
            else:
                P.op("pool", lambda e: e.iota(ti32[:], pattern=[[-1, TT]], base=TT, channel_multiplier=0), [], [ti32])
            P.op("dve", lambda e: e.tensor_copy(out=tau[:], in_=ti32[:]), [ti32], [tau])
            for j in range(16):
                ts(ang[:], tau[:], sm["th"][:, j:j + 1], None, ALU.mult, None, [tau, sm["th"]], [ang])
                emit_sin(sinT, sinT[:, j, :], ang, ang[:], qq, qq[:], 0.0)
                yield 5.0
                emit_sin(cosT, cosT[:, j, :], ang, ang[:], qq, qq[:], math.pi / 2.0)
                yield 5.0
            P.op("dve", lambda e: e.memset(car[0][:], 0.0), [], [car[0]])
            P.op("dve", lambda e: e.memset(car[1][:], 0.0), [], [car[1]])
            order = list(range(NT)) if d == 0 else list(range(NT - 1, -1, -1))
            nblk = 0
            lastc = TT - 1 if d == 0 else 0
            for it, ti in enumerate(order):
                t0 = ti * TT
                tsl = slice(t0, t0 + TT)
                P.dma("sp", ub[:], USB[:, tsl].rearrange("(c p) t -> p c t", p=128), [USB.bs[ti]], [ub])
                if it > 0:
                    bnd = t0 if d == 0 else t0 + TT
                    if bnd % SL == 0:
                        for cc in car:
                            ts(cc[:], cc[:], linkc[:, 0:1], None, ALU.mult, None, [cc, linkc], [cc])
                for ch in range(4):
                    for jj in range(4):
                        j = 4 * ch + jj
                        w = [wk[i][nblk % NW] for i in range(6)]
                        xr_b, xi_b = xb_[0][nblk % 4], xb_[1][nblk % 4]
                        tnn = tn[nblk % 2]
                        nblk += 1
                        mm(pvr[:, :], BLr[:, j, :], ub[:, ch, :], True, True, [BLr, ub], [pvr])
                        mm(pvi[:, :], BLi[:, j, :], ub[:, ch, :], True, True, [BLi, ub], [pvi])
                        c_, s_ = cosT[:, j, :], sinT[:, j, :]
                        cl_, sl_ = cosT[:, j, lastc:lastc + 1], sinT[:, j, lastc:lastc + 1]
                        tt(w[0][:], pvr[:, :], c_, ALU.mult, [pvr, cosT], [w[0]])
                        tt(w[1][:], pvi[:, :], s_, ALU.mult, [pvi, sinT], [w[1]])
                        tt(w[2][:], pvi[:, :], c_, ALU.mult, [pvi, cosT], [w[2]])
                        tt(w[3][:], pvr[:, :], s_, ALU.mult, [pvr, sinT], [w[3]])
                        tt(w[0][:], w[0][:], w[1][:], ALU.add, [w[0], w[1]], [w[0]])
                        tt(w[2][:], w[2][:], w[3][:], ALU.subtract, [w[2], w[3]], [w[2]])
                        rb = sm["rmag"][:, j:j + 1].to_broadcast([128, TT])
                        for (src, dst, cc) in ((w[0], w[4], car[0]), (w[2], w[5], car[1])):
                            if d == 0:
                                P.op("dve", lambda e: e.tensor_tensor_scan(out=dst[:], data0=rb, data1=src[:],
                                                                           initial=cc[:, j:j + 1], op0=ALU.mult, op1=ALU.add),
                                     [src, cc, sm["rmag"]], [dst])
                            else:
                                P.op("dve", lambda e: e.tensor_tensor_scan(out=dst[:, ::-1], data0=rb, data1=src[:, ::-1],
                                                                           initial=cc[:, j:j + 1], op0=ALU.mult, op1=ALU.add),
                                     [src, cc, sm["rmag"]], [dst])
                        tt(w[0][:], w[4][:], c_, ALU.mult, [w[4], cosT], [w[0]], eng=PENG)
                        tt(w[1][:], w[4][:], s_, ALU.mult, [w[4], sinT], [w[1]], eng=PENG)
                        tt(w[2][:], w[5][:], s_, ALU.mult, [w[5], sinT], [w[2]], eng=PENG)
                        tt(w[3][:], w[5][:], c_, ALU.mult, [w[5], cosT], [w[3]], eng=PENG)
                        tt(xr_b[:], w[0][:], w[2][:], ALU.subtract, [w[0], w[2]], [xr_b])
                        tt(xi_b[:], w[1][:], w[3][:], ALU.add, [w[1], w[3]], [xi_b])
                        lc = slice(lastc, lastc + 1)
                        tt(car[0][:, j:j + 1], w[0][:, lc], w[2][:, lc], ALU.subtract, [w[0], w[2]], [car[0]])
                        tt(car[1][:, j:j + 1], w[1][:, lc], w[3][:, lc], ALU.add, [w[1], w[3]], [car[1]])

                        def cmm(j=j, jj=jj, xr_b=xr_b, xi_b=xi_b):
                            mm(psy[:, :], CLr[:, j, :], xr_b[:], jj == 0, False, [CLr, xr_b], [psy])
                            mm(psy[:, :], CLi[:, j, :], xi_b[:], False, jj == 3, [CLi, xi_b], [psy])
                        later(3, cmm)
                        if jj == 3:
                            y1 = y1t[ch % 2]
                            if d == 0:
                                P.dma("pool", y1[:], US[ch * 128:(ch + 1) * 128, tsl], [US.bs[ti]], [y1])

                                def fin(ch=ch, y1=y1, tsl=tsl, ti=ti):
                                    stt(y1[:], y1[:], dsk[:, ch:ch + 1], psy[:, :], ALU.mult, ALU.add, [y1, dsk, psy], [y1])
                                    P.dma("pool", Y1[ch * 128:(ch + 1) * 128, tsl], y1[:], [y1], [Y1.bs[ti]])
                                later(4, fin)
                            else:
                                P.dma("pool", y1[:], Y1[ch * 128:(ch + 1) * 128, tsl], [Y1.bs[ti]], [y1])

                                def fin(ch=ch, y1=y1):
                                    tt(y1[:], y1[:], psy[:, :], ALU.add, [y1, psy], [y1])
                                later(4, fin)

                                def fin2(ch=ch, y1=y1):
                                    act(yg[:, ch, :], y1[:], AF.Gelu_apprx_tanh, [y1], [yg])
                                later(5, fin2)
                        yield 11.0
                        tick()
                if d == 1:
                    for _ in range(6):
                        yield 0.3
                        tick()
                    for o in range(8):
                        for k in range(4):
                            mm(pvr[:, :], wglu[:, k, o * 128:(o + 1) * 128], yg[:, k, :], k == 0, k == 3, [wglu, yg], [pvr])
                        for k in range(4):
                            mm(pvi[:, :], wglu[:, k, D + o * 128:D + (o + 1) * 128], yg[:, k, :], k == 0, k == 3, [wglu, yg], [pvi])
                        s1, s2 = sg[0], sg[1]
                        yield 0.8
                        tick()
                        act(s1[:], pvi[:, :], AF.Sigmoid, [pvi], [s1])
                        yield 0.8
                        tick()
                        tt(s2[:], s1[:], pvr[:, :], ALU.mult, [s1, pvr], [s2])
                        P.dma("sp" if o % 2 else "pool", YC[o * 128:(o + 1) * 128, tsl], s2[:], [s2], [YC.bs[ti]])
                else:
                    for _ in range(6):
                        yield 0.3
                        tick()
            for _ in range(6):
                yield 0.3
                tick()

    def phase_att_s5(l):
        with ExitStack() as st:
            ga = gen_att(l, st)
            gs = gen_s5(l, st)
            n_att = 8 * NT * NCH
            total_cost = 2 * NT * 16 * 11.0 + NT * 16 * 0.8 + 2 * NT * 4 * 0.3 + 2 * (32 * 5.0 + 40 * 2.0) + 30.0
            per_us = n_att / total_cost
            acc_ = 0.0
            a_done = s_done = False
            while not (a_done and s_done):
                if not s_done:
                    try:
                        c = next(gs)
                        acc_ += (c if c else 1.0) * per_us
                    except StopIteration:
                        s_done = True
                if s_done:
                    acc_ += 64
                while acc_ >= 1.0 and not a_done:
                    acc_ -= 1.0
                    try:
                        next(ga)
                    except StopIteration:
                        a_done = True
                if a_done:
                    acc_ = 0.0
            P.barrier()

    with nc.allow_non_contiguous_dma(reason="small strided parameter / layout-conversion DMAs"):
        phase0()
        if branches:
            phase_init()
        for l in range(DEPTH):
            if branches:
                phase_a(l)
            if stop_after == 'a':
                break
            if "b" in branches:
                phase_m(l)
            if "a" in branches and "c" in branches:
                phase_att_s5(l)
            else:
                if "a" in branches:
                    phase_att(l)
                if "c" in branches:
                    phase_s5(l)
            phase_mrg(l)
            phase_f1(l)
            phase_f2(l)
        phase_out()
    P.close()
    return nc, P


NCORES = 8
T_CORE = 8192
SEGLEN = 2048


def kernel(**inputs):
    xp = np.ascontiguousarray(inputs['x_prompt'], dtype=np.float32)
    xs = np.ascontiguousarray(inputs['x_sample'], dtype=np.float32)
    nc, _ = build(T_CORE, SEGLEN, 4)
    wts = {n: np.ascontiguousarray(inputs[n], dtype=np.float32) for n in WNAMES}
    in_maps = []
    for c in range(NCORES):
        if c < 4:
            x = xp[4 * c:4 * c + 4].reshape(T_CORE, D)
            link = np.zeros((1, 1), np.float32)
        else:
            x = xs[c - 4].reshape(T_CORE, D)
            link = np.ones((1, 1), np.float32)
        m = {"x": x, "link": link}
        m.update(wts)
        in_maps.append(m)
    res = run_bass_kernel_spmd(nc, in_maps, core_ids=list(range(NCORES)))
    ys = [np.asarray(r["y"], dtype=np.float32) for r in res.results]
    y_prompt = np.concatenate([ys[c].reshape(4, SEGLEN, D) for c in range(4)], axis=0)
    y_sample = np.stack([ys[c].reshape(T_CORE, D) for c in range(4, 8)], axis=0)
    return (y_prompt, y_sample)
```

```python
import math
import numpy as np
import concourse.bass as bass
import concourse.mybir as mybir
from concourse.bass_utils import run_bass_kernel_spmd
from contextlib import ExitStack

F32 = mybir.dt.float32
BF16 = mybir.dt.bfloat16
AF = mybir.ActivationFunctionType
ALU = mybir.AluOpType
AX = mybir.AxisListType

D = 1024
KD = 8
H_A = 8
NIN = 5552
DFF = 2816
NFC = 22
ALPHA = 8 ** 0.25
LN_EPS = 1e-5
TT = 512
O_CQ, O_CKV, O_KR, O_XM, O_VM, O_OM, O_GM, O_US, O_GP = 0, 256, 384, 416, 928, 1440, 1952, 1968, 2480
WNAMES = ['ln0_g', 'ln0_b', 'w_in', 'b_mlstm_gate', 'b_merge', 'q_norm_g', 'kv_norm_g', 'w_uq', 'w_ukv', 'w_proj_a',
          'conv_m_w', 'conv_m_b', 'w_q_m', 'w_k_m', 'mh_norm_g', 'w_proj_b', 's5_a_re', 's5_a_im', 's5_log_dt',
          's5_b_re', 's5_b_im', 's5_c_re', 's5_c_im', 's5_d', 'w_glu', 'w_o', 'ln1_g', 'ln1_b', 'w_up', 'conv_f_w',
          'conv_f_b', 'w_down', 'ln2_g', 'ln2_b']


class Buf:
    __slots__ = ("name", "lw", "rd")

    def __init__(self, name=""):
        self.name = name
        self.lw = None
        self.rd = {}


class Tl:
    def __init__(self, t, name=""):
        self.t = t
        self.b = Buf(name)

    def __getitem__(self, k):
        return self.t[k]


class DT:
    def __init__(self, ap, name, ntile):
        self.ap = ap
        self.bs = [Buf(name + str(i)) for i in range(ntile)]

    def __getitem__(self, k):
        return self.ap[k]


def _bl(xs):
    out = []
    for x in xs:
        if isinstance(x, Buf):
            out.append(x)
        elif isinstance(x, (list, tuple)):
            out.extend(_bl(x))
        else:
            out.append(x.b)
    return out


class Prog:
    COMPUTE = ("pe", "act", "dve", "pool")

    def __init__(self, nc, ndma=12):
        self.nc = nc
        self.es = ExitStack()
        self.eng = {"pe": nc.tensor, "act": nc.scalar, "dve": nc.vector, "pool": nc.gpsimd, "sp": nc.sync}
        self.sem = {}
        for e in self.COMPUTE:
            self.sem[e] = self.es.enter_context(nc.semaphore("s_" + e))
        self.cnt = {e: 0 for e in self.COMPUTE}
        self.dq = {}
        for q in ("sp", "pool"):
            sems = [self.es.enter_context(nc.semaphore("d_%s%d" % (q, i))) for i in range(ndma)]
            self.dq[q] = {"sems": sems, "tgt": [0] * ndma, "i": 0}
        self.waited = {e: {} for e in self.eng}
        self.semobj = {}
        self.ninstr = 0

    def _semkey(self, s):
        k = id(s)
        self.semobj[k] = s
        return k

    def _need(self, e, tok, deps):
        if tok is None:
            return
        k, v = tok
        if self.waited[e].get(k, 0) >= v:
            return
        if deps.get(k, 0) < v:
            deps[k] = v

    def _collect(self, e, reads, writes, is_dma=False):
        deps = {}
        own = self._semkey(self.sem[e]) if (e in self.COMPUTE and not is_dma) else None
        for b in reads:
            self._need(e, b.lw, deps)
        for b in writes:
            if b.lw is not None and b.lw[0] != own:
                self._need(e, b.lw, deps)
            for key, tok in b.rd.items():
                if tok[0] != own:
                    self._need(e, tok, deps)
        if e == "pe" and own in deps:
            del deps[own]
        for k, v in deps.items():
            self.eng[e].wait_ge(self.semobj[k], v)
            self.waited[e][k] = v
            self.ninstr += 1

    def _update(self, tok, reads, writes, rkey):
        for b in writes:
            b.lw = tok
            b.rd = {}
        for b in reads:
            b.rd[rkey] = tok

    def op(self, e, fn, reads=(), writes=()):
        reads = _bl(reads)
        writes = _bl(writes)
        self._collect(e, reads, writes)
        ins = fn(self.eng[e])
        self.cnt[e] += 1
        ins.then_inc(self.sem[e], 1)
        tok = (self._semkey(self.sem[e]), self.cnt[e])
        self._update(tok, reads, writes, e)
        self.ninstr += 1
        return tok

    def dma(self, q, out, in_, reads=(), writes=(), **kw):
        reads = _bl(reads)
        writes = _bl(writes)
        q = "pool" if type(out.tensor).__name__.startswith("DRam") else "sp"
        d = self.dq[q]
        i = d["i"]
        d["i"] = (i + 1) % len(d["sems"])
        s = d["sems"][i]
        k = self._semkey(s)
        if d["tgt"][i] > 0 and self.waited[q].get(k, 0) < d["tgt"][i]:
            self.eng[q].wait_ge(s, d["tgt"][i])
            self.waited[q][k] = d["tgt"][i]
        self._collect(q, reads, writes, is_dma=True)
        ins = self.eng[q].dma_start(out=out, in_=in_, **kw)
        d["tgt"][i] += 16
        ins.then_inc(s, 16)
        tok = (k, d["tgt"][i])
        self._update(tok, reads, writes, ("dma", k))
        self.ninstr += 1
        return tok

    def barrier(self):
        toks = [(self._semkey(self.sem[e]), self.cnt[e]) for e in self.COMPUTE if self.cnt[e] > 0]
        for q, d in self.dq.items():
            for s, t in zip(d["sems"], d["tgt"]):
                if t > 0:
                    toks.append((self._semkey(s), t))
        for e in self.eng:
            for k, v in toks:
                if e in self.COMPUTE and k == self._semkey(self.sem[e]):
                    continue
                if self.waited[e].get(k, 0) < v:
                    self.eng[e].wait_ge(self.semobj[k], v)
                    self.waited[e][k] = v
                    self.ninstr += 1

    def close(self):
        self.barrier()
        self.es.close()


def wshapes(L):
    return {
        'ln0_g': (D,), 'ln0_b': (D,), 'w_in': (L, D, NIN), 'b_mlstm_gate': (L, 2, 2, 4), 'b_merge': (L, 3, D),
        'q_norm_g': (L, 256), 'kv_norm_g': (L, 128), 'w_uq': (L, 256, 768), 'w_ukv': (L, 128, 1024),
        'w_proj_a': (L, 512, D), 'conv_m_w': (L, 3, 512), 'conv_m_b': (L, 512), 'w_q_m': (L, 4, 128, 128),
        'w_k_m': (L, 4, 128, 128), 'mh_norm_g': (L, 512), 'w_proj_b': (L, 512, D), 's5_a_re': (L, 2, 32, 64),
        's5_a_im': (L, 2, 32, 64), 's5_log_dt': (L, 2, 32), 's5_b_re': (L, 32, 64, 16), 's5_b_im': (L, 32, 64, 16),
        's5_c_re': (L, 32, 16, 64), 's5_c_im': (L, 32, 16, 64), 's5_d': (L, 32, 16), 'w_glu': (L, 512, 2 * D),
        'w_o': (L, D, D), 'ln1_g': (L, D), 'ln1_b': (L, D), 'w_up': (L, D, 2 * DFF), 'conv_f_w': (L, 3, DFF),
        'conv_f_b': (L, DFF), 'w_down': (L, DFF, D), 'ln2_g': (L, D), 'ln2_b': (L, D)}


def build(T, SL, DEPTH, dbg=(), branches=("a", "b", "c"), stop_after=None):
    NSEG = T // SL
    NT = T // TT
    NCH = T // 128
    nc = bass.Bass("TRN2", target_bir_lowering=False)
    x_in = nc.dram_tensor("x", [T, D], F32, kind="ExternalInput").ap()
    link_in = nc.dram_tensor("link", [1, 1], F32, kind="ExternalInput").ap()
    W = {n: nc.dram_tensor(n, list(s), F32, kind="ExternalInput").ap() for n, s in wshapes(DEPTH).items()}
    y_out = nc.dram_tensor("y", [T, D], F32, kind="ExternalOutput").ap()
    P = Prog(nc)
    es = P.es
    NSL = "allow_slow_non_contiguous"

    def scratch(name, shape, dt):
        kind = "ExternalOutput" if name in dbg else "Internal"
        return DT(nc.dram_tensor(name, list(shape), dt, kind=kind).ap(), name, NT)

    XT = scratch("XT", [D, T], F32)
    XTB = scratch("XTB", [D, T], BF16)
    AT = scratch("AT", [DFF, T], F32)
    OT = scratch("OT", [512, T], BF16)
    HBT = scratch("HBT", [512, T], BF16)
    YC = scratch("YC", [D, T], F32)
    QT = scratch("QT", [8, 98, T], BF16)
    KT = scratch("KT", [8, 98, T], BF16)
    VA = scratch("VA", [T, 8, 128], BF16)
    XM = scratch("XM", [512, T], F32)
    XCB = scratch("XCB", [512, T], BF16)
    VM = scratch("VM", [T, 512], BF16)
    OMS = scratch("OMS", [T, 512], F32)
    GM = scratch("GM", [16, T], F32)
    US = scratch("US", [512, T], F32)
    USB = scratch("USB", [512, T], BF16)
    HF = scratch("HF", [T, 512], F32)
    Y1 = scratch("Y1", [512, T], F32)
    RC = scratch("RC", [32, T], F32)
    RS = scratch("RS", [32, T], F32)

    uid = [0]

    def sbt(stack, name, shape, dt=F32):
        uid[0] += 1
        name = "%s_%d" % (name, uid[0])
        return Tl(stack.enter_context(nc.sbuf_tensor(name, list(shape), dt)), name)

    ps = [Tl(es.enter_context(nc.psum_tensor("ps%d" % i, [128, 512], F32)), "ps%d" % i) for i in range(8)]
    ident = sbt(es, "ident", [128, 128])
    identb = sbt(es, "identb", [128, 128], BF16)
    ones32 = sbt(es, "ones32", [128, 128])
    linkc = sbt(es, "linkc", [128, 1])
    lng = sbt(es, "lng", [128, 2 * DEPTH + 1, 8])
    lnb = sbt(es, "lnb", [128, 2 * DEPTH + 1, 8])
    stg = [sbt(es, "stg%d" % i, [128, 1024]) for i in range(2)]
    stgi = [0]

    def mm(out, lhsT, rhs, start, stop, reads, writes):
        return P.op("pe", lambda e: e.matmul(out, lhsT=lhsT, rhs=rhs, start=start, stop=stop), reads, writes)

    def act(out, in_, func, reads, writes, **kw):
        return P.op("act", lambda e: e.activation(out=out, in_=in_, func=func, **kw), reads, writes)

    def tt(out, in0, in1, op, reads, writes, eng="dve"):
        return P.op(eng, lambda e: e.tensor_tensor(out=out, in0=in0, in1=in1, op=op), reads, writes)

    def ts(out, in0, s1, s2, op0, op1, reads, writes, eng="dve"):
        if s2 is None:
            return P.op(eng, lambda e: e.tensor_scalar(out=out, in0=in0, scalar1=s1, scalar2=None, op0=op0), reads, writes)
        return P.op(eng, lambda e: e.tensor_scalar(out=out, in0=in0, scalar1=s1, scalar2=s2, op0=op0, op1=op1), reads, writes)

    def stt(out, in0, scalar, in1, op0, op1, reads, writes):
        return P.op("dve", lambda e: e.scalar_tensor_tensor(out=out, in0=in0, scalar=scalar, in1=in1, op0=op0, op1=op1),
                    reads, writes)

    def load_cast(dst, src, dtl, n, scale=1.0, sreads=()):
        o = 0
        while o < n:
            w = min(1024, n - o)
            s = stg[stgi[0] % 2]
            stgi[0] += 1
            np_ = dst.shape[0]
            P.dma("sp", s[0:np_, 0:w], src[:, o:o + w], [], [s])
            act(dst[:, o:o + w], s[0:np_, 0:w], AF.Copy if isinstance(scale, float) else AF.Identity,
                [s] + list(sreads), [dtl], scale=scale)
            o += w

    P.op("pool", lambda e: e.memset(ident[:], 0.0), [], [ident])
    P.op("pool", lambda e: e.affine_select(out=ident[:], in_=ident[:], compare_op=ALU.not_equal, fill=1.0, base=0,
                                           pattern=[[-1, 128]], channel_multiplier=1), [ident], [ident])
    P.op("dve", lambda e: e.tensor_copy(out=identb[:], in_=ident[:]), [ident], [identb])
    P.op("dve", lambda e: e.memset(ones32[:], 1.0), [], [ones32])
    P.dma("sp", linkc[:], link_in.partition_broadcast(128), [], [linkc])
    with nc.allow_non_contiguous_dma(reason="tiny param vectors"):
        P.dma("sp", lng[:, 0, :], W['ln0_g'].rearrange("(c p) -> p c", p=128), [], [lng])
        P.dma("sp", lnb[:, 0, :], W['ln0_b'].rearrange("(c p) -> p c", p=128), [], [lnb])
        for l in range(DEPTH):
            for j, nm in ((1, 'ln1'), (2, 'ln2')):
                P.dma("sp", lng[:, j + 2 * l, :], W[nm + '_g'][l].rearrange("(c p) -> p c", p=128), [], [lng])
                P.dma("sp", lnb[:, j + 2 * l, :], W[nm + '_b'][l].rearrange("(c p) -> p c", p=128), [], [lnb])

    def ln_tile(y, sq, xb, sm, li, ti, pA, pB):
        mean, var = sm
        t0 = ti * TT
        act(sq[:], y[:], AF.Square, [y], [sq])
        for c in range(8):
            mm(pA[:, :], ones32[:], y[:, c, :], c == 0, c == 7, [ones32, y], [pA])
        for c in range(8):
            mm(pB[:, :], ones32[:], sq[:, c, :], c == 0, c == 7, [ones32, sq], [pB])
        P.op("act", lambda e: e.mul(out=mean[:], in_=pA[:, :], mul=1.0 / D), [pA], [mean])
        tt(var[:], mean[:], mean[:], ALU.mult, [mean], [var])
        stt(var[:], pB[:, :], 1.0 / D, var[:], ALU.mult, ALU.subtract, [pB, var], [var])
        ts(var[:], var[:], LN_EPS, None, ALU.add, None, [var], [var])
        act(var[:], var[:], AF.Sqrt, [var], [var])
        P.op("dve", lambda e: e.reciprocal(out=var[:], in_=var[:]), [var], [var])
        bc = lambda a: a[:].unsqueeze(1).to_broadcast([128, 8, TT])
        tt(y[:], y[:], bc(mean), ALU.subtract, [y, mean], [y])
        tt(y[:], y[:], bc(var), ALU.mult, [y, var], [y])
        tt(y[:], y[:], lng[:, li, :].unsqueeze(2).to_broadcast([128, 8, TT]), ALU.mult, [y, lng], [y])
        tt(y[:], y[:], lnb[:, li, :].unsqueeze(2).to_broadcast([128, 8, TT]), ALU.add, [y, lnb], [y])
        act(xb[:], y[:], AF.Copy, [y], [xb])
        P.dma("sp", XT[:, t0:t0 + TT].rearrange("(c p) t -> p c t", p=128), y[:], [y], [XT.bs[ti]])
        P.dma("pool", XTB[:, t0:t0 + TT].rearrange("(c p) t -> p c t", p=128), xb[:], [xb], [XTB.bs[ti]])

    def phase0():
        with ExitStack() as st:
            xin = [sbt(st, "p0x%d" % i, [128, D]) for i in range(2)]
            y = sbt(st, "p0y", [128, 8, TT])
            sq = sbt(st, "p0sq", [128, 8, TT])
            xb = sbt(st, "p0xb", [128, 8, TT], BF16)
            sm = (sbt(st, "p0m", [128, TT]), sbt(st, "p0v", [128, TT]))
            k = 0
            for ti in range(NT):
                for b in range(4):
                    xi = xin[k % 2]
                    k += 1
                    r0 = ti * TT + b * 128
                    P.dma("sp", xi[:], x_in[r0:r0 + 128, :], [], [xi])
                    for c in range(8):
                        mm(ps[c][:, b * 128:(b + 1) * 128], xi[:, c * 128:(c + 1) * 128], ident[:], True, True,
                           [xi, ident], [ps[c]])
                for c in range(8):
                    if c % 2 == 0:
                        P.op("dve", lambda e: e.tensor_copy(out=y[:, c, :], in_=ps[c][:, :]), [ps[c]], [y])
                    else:
                        act(y[:, c, :], ps[c][:, :], AF.Copy, [ps[c]], [y])
                ln_tile(y, sq, xb, sm, 0, ti, ps[0], ps[1])
            P.barrier()

    def phase_out():
        with ExitStack() as st:
            xs = [sbt(st, "pox%d" % i, [128, 8, TT]) for i in range(2)]
            yo = [sbt(st, "poy%d" % i, [128, D]) for i in range(2)]
            k = 0
            for ti in range(NT):
                t0 = ti * TT
                x = xs[ti % 2]
                P.dma("sp", x[:], XT[:, t0:t0 + TT].rearrange("(c p) t -> p c t", p=128), [XT.bs[ti]], [x])
                for b in range(4):
                    o = yo[k % 2]
                    k += 1
                    for c in range(8):
                        pb = ps[(c // 4) + 2 * (b % 2)]
                        mm(pb[:, (c % 4) * 128:(c % 4 + 1) * 128], x[:, c, b * 128:(b + 1) * 128], ident[:], True, True,
                           [x, ident], [pb])
                    for hlf in range(2):
                        pb = ps[hlf + 2 * (b % 2)]
                        if hlf == 0:
                            P.op("dve", lambda e: e.tensor_copy(out=o[:, 0:512], in_=pb[:, :]), [pb], [o])
                        else:
                            act(o[:, 512:1024], pb[:, :], AF.Copy, [pb], [o])
                    r0 = t0 + b * 128
                    P.dma("pool", y_out[r0:r0 + 128, :], o[:], [o], [])
            P.barrier()

    def seg_edge(t):
        if t <= 0 or t >= T:
            return 2
        return 1 if t % SL == 0 else 0

    def load_halo(q, tl, src, r0, nrow, ti, dtbufs):
        t0 = ti * TT
        le, re = seg_edge(t0), seg_edge(t0 + TT)
        lo = t0 - 1 if le != 2 else t0
        hi = t0 + TT + 1 if re != 2 else t0 + TT
        rd = [dtbufs[j] for j in (ti - 1, ti, ti + 1) if 0 <= j < NT]
        P.dma(q, tl[0:nrow, (lo - t0 + 1):(hi - t0 + 1)], src[r0:r0 + nrow, lo:hi], rd, [tl])
        if le == 2:
            P.op("dve", lambda e: e.memset(tl[0:nrow, 0:1], 0.0), [], [tl])
        elif le == 1:
            ts(tl[0:nrow, 0:1], tl[0:nrow, 0:1], linkc[0:nrow, 0:1], None, ALU.mult, None, [tl, linkc], [tl])
        if re == 2:
            P.op("dve", lambda e: e.memset(tl[0:nrow, TT + 1:TT + 2], 0.0), [], [tl])
        elif re == 1:
            ts(tl[0:nrow, TT + 1:TT + 2], tl[0:nrow, TT + 1:TT + 2], linkc[0:nrow, 0:1], None, ALU.mult, None,
               [tl, linkc], [tl])

    def phase_f1(l):
        with ExitStack() as st:
            wa = sbt(st, "f1wa", [128, 8, DFF], BF16)
            for k in range(8):
                load_cast(wa[:, k, :], W['w_up'][l, k * 128:(k + 1) * 128, 0:DFF], wa, DFF)
            xbs = [sbt(st, "f1xb%d" % i, [128, 8, TT], BF16) for i in range(2)]
            ao = [sbt(st, "f1ao%d" % i, [128, TT]) for i in range(4)]
            n = 0
            for ti in range(NT):
                t0 = ti * TT
                xb = xbs[ti % 2]
                P.dma("sp", xb[:], XTB[:, t0:t0 + TT].rearrange("(c p) t -> p c t", p=128), [XTB.bs[ti]], [xb])
                for j in range(NFC):
                    pb = ps[n % 4]
                    a = ao[n % 4]
                    for k in range(8):
                        mm(pb[:, :], wa[:, k, j * 128:(j + 1) * 128], xb[:, k, :], k == 0, k == 7, [wa, xb], [pb])
                    if n % 2 == 0:
                        act(a[:], pb[:, :], AF.Copy, [pb], [a])
                    else:
                        P.op("dve", lambda e: e.tensor_copy(out=a[:], in_=pb[:, :]), [pb], [a])
                    P.dma("pool" if n % 2 else "sp", AT[j * 128:(j + 1) * 128, t0:t0 + TT], a[:], [a], [AT.bs[ti]])
                    n += 1
            P.barrier()

    def phase_f2(l):
        with ExitStack() as st:
            wb = sbt(st, "f2wb", [128, 8, DFF], BF16)
            wd = sbt(st, "f2wd", [128, NFC, D], BF16)
            cw = sbt(st, "f2cw", [128, 3, NFC])
            cb = sbt(st, "f2cb", [128, NFC])
            for k in range(8):
                load_cast(wb[:, k, :], W['w_up'][l, k * 128:(k + 1) * 128, DFF:2 * DFF], wb, DFF)
            for j in range(NFC):
                load_cast(wd[:, j, :], W['w_down'][l, j * 128:(j + 1) * 128, :], wd, D)
            with nc.allow_non_contiguous_dma(reason="tiny param vectors"):
                for w in range(3):
                    P.dma("sp", cw[:, w, :], W['conv_f_w'][l, w].rearrange("(c p) -> p c", p=128), [], [cw])
                P.dma("sp", cb[:, :], W['conv_f_b'][l].rearrange("(c p) -> p c", p=128), [], [cb])
            xb1 = sbt(st, "f2xb", [128, 8, TT], BF16)
            y = sbt(st, "f2y", [128, 8, TT])
            hh = sbt(st, "f2hh", [128, NFC, TT], BF16)
            sq = sbt(st, "f2sq", [128, 8, TT])
            xbo = sbt(st, "f2xbo", [128, 8, TT], BF16)
            sm = (sbt(st, "f2m", [128, TT]), sbt(st, "f2v", [128, TT]))
            ats = [sbt(st, "f2at%d" % i, [128, TT + 2]) for i in range(3)]
            accs = [sbt(st, "f2ac%d" % i, [128, TT]) for i in range(2)]
            n = 0
            for ti in range(NT):
                t0 = ti * TT
                P.dma("sp", xb1[:], XTB[:, t0:t0 + TT].rearrange("(c p) t -> p c t", p=128), [XTB.bs[ti]], [xb1])
                for j in range(NFC):
                    at = ats[n % 3]
                    acc = accs[n % 2]
                    pb = ps[n % 3]
                    n += 1
                    load_halo("sp" if j % 2 else "pool", at, AT, j * 128, 128, ti, AT.bs)
                    for k in range(8):
                        mm(pb[:, :], wb[:, k, j * 128:(j + 1) * 128], xb1[:, k, :], k == 0, k == 7, [wb, xb1], [pb])
                    ts(acc[:], at[:, 0:TT], cw[:, 0, j:j + 1], None, ALU.mult, None, [at, cw], [acc])
                    stt(acc[:], at[:, 1:TT + 1], cw[:, 1, j:j + 1], acc[:], ALU.mult, ALU.add, [at, cw, acc], [acc])
                    stt(acc[:], at[:, 2:TT + 2], cw[:, 2, j:j + 1], acc[:], ALU.mult, ALU.add, [at, cw, acc], [acc])
                    act(acc[:], acc[:], AF.Gelu_apprx_tanh, [acc, cb], [acc], bias=cb[:, j:j + 1])
                    tt(hh[:, j, :], acc[:], pb[:, :], ALU.mult, [acc, pb], [hh])
                P.dma("sp", y[:], XT[:, t0:t0 + TT].rearrange("(c p) t -> p c t", p=128), [XT.bs[ti]], [y])
                for o in range(8):
                    pb = ps[4 + o % 2]
                    for j in range(NFC):
                        mm(pb[:, :], wd[:, j, o * 128:(o + 1) * 128], hh[:, j, :], j == 0, j == NFC - 1, [wd, hh], [pb])
                    stt(y[:, o, :], y[:, o, :], ALPHA, pb[:, :], ALU.mult, ALU.add, [y, pb], [y])
                ln_tile(y, sq, xbo, sm, 2 + 2 * l, ti, ps[6], ps[7])
            P.barrier()

    def phase_mrg(l):
        with ExitStack() as st:
            wg = sbt(st, "mgwg", [128, 8, 3 * D], BF16)
            wo = sbt(st, "mgwo", [128, 8, D], BF16)
            wpa = sbt(st, "mgwpa", [128, 4, D], BF16)
            wpb = sbt(st, "mgwpb", [128, 4, D], BF16)
            bm = sbt(st, "mgbm", [128, 3, 8])
            mhg = sbt(st, "mgmhg", [128, 4])
            with nc.allow_non_contiguous_dma(reason="tiny param vectors"):
                for br in range(3):
                    P.dma("sp", bm[:, br, :], W['b_merge'][l, br].rearrange("(c p) -> p c", p=128), [], [bm])
                P.dma("sp", mhg[:, :], W['mh_norm_g'][l].rearrange("(c p) -> p c", p=128), [], [mhg])
            for k in range(8):
                load_cast(wg[:, k, :], W['w_in'][l, k * 128:(k + 1) * 128, O_GP:NIN], wg, 3 * D)
                load_cast(wo[:, k, :], W['w_o'][l, k * 128:(k + 1) * 128, :], wo, D)
            for k in range(4):
                load_cast(wpa[:, k, :], W['w_proj_a'][l, k * 128:(k + 1) * 128, :], wpa, D)
                load_cast(wpb[:, k, :], W['w_proj_b'][l, k * 128:(k + 1) * 128, :], wpb, D, scale=mhg[:, k:k + 1],
                          sreads=[mhg])
            xb = sbt(st, "mgxb", [128, 8, TT], BF16)
            y = sbt(st, "mgy", [128, 8, TT])
            ot = sbt(st, "mgot", [128, 4, TT], BF16)
            hb = sbt(st, "mghb", [128, 4, TT], BF16)
            yc = sbt(st, "mgyc", [128, 8, TT])
            mg = sbt(st, "mgmg", [128, 8, TT], BF16)
            sq = sbt(st, "mgsq", [128, 8, TT])
            xbo = sbt(st, "mgxbo", [128, 8, TT], BF16)
            sm = (sbt(st, "mgm", [128, TT]), sbt(st, "mgv", [128, TT]))
            g = [sbt(st, "mgg%d" % i, [128, TT]) for i in range(3)]
            t1 = sbt(st, "mgt1", [128, TT])
            t2 = sbt(st, "mgt2", [128, TT])
            fm = lambda a: a.rearrange("(c p) t -> p c t", p=128)
            for ti in range(NT):
                t0 = ti * TT
                P.dma("sp", xb[:], fm(XTB[:, t0:t0 + TT]), [XTB.bs[ti]], [xb])
                if "a" in branches:
                    P.dma("sp", ot[:], fm(OT[:, t0:t0 + TT]), [OT.bs[ti]], [ot])
                if "b" in branches:
                    P.dma("pool", hb[:], fm(HBT[:, t0:t0 + TT]), [HBT.bs[ti]], [hb])
                if "c" in branches:
                    P.dma("sp", yc[:], fm(YC[:, t0:t0 + TT]), [YC.bs[ti]], [yc])
                for c in range(8):
                    cs = slice(c * 128, (c + 1) * 128)
                    terms = []
                    for bi, br in enumerate("abc"):
                        if br not in branches:
                            continue
                        pb = ps[bi]
                        for k in range(8):
                            mm(pb[:, :], wg[:, k, bi * D + c * 128: bi * D + (c + 1) * 128], xb[:, k, :], k == 0, k == 7,
                               [wg, xb], [pb])
                        act(g[bi][:], pb[:, :], AF.Sigmoid, [pb, bm], [g[bi]], bias=bm[:, bi, c:c + 1])
                        if br == "a":
                            for k in range(4):
                                mm(ps[3][:, :], wpa[:, k, cs], ot[:, k, :], k == 0, k == 3, [wpa, ot], [ps[3]])
                            terms.append((g[bi], ps[3], ps[3][:, :]))
                        elif br == "b":
                            for k in range(4):
                                mm(ps[4][:, :], wpb[:, k, cs], hb[:, k, :], k == 0, k == 3, [wpb, hb], [ps[4]])
                            terms.append((g[bi], ps[4], ps[4][:, :]))
                        else:
                            terms.append((g[bi], yc, yc[:, c, :]))
                    if not terms:
                        P.op("dve", lambda e: e.memset(mg[:, c, :], 0.0), [], [mg])
                    for i, (gt, src, sap) in enumerate(terms):
                        last = i == len(terms) - 1
                        if i == 0:
                            tt(mg[:, c, :] if last else t1[:], gt[:], sap, ALU.mult, [gt, src], [mg if last else t1])
                        else:
                            tt(t2[:], gt[:], sap, ALU.mult, [gt, src], [t2])
                            tt(mg[:, c, :] if last else t1[:], t1[:], t2[:], ALU.add, [t1, t2], [mg if last else t1])
                P.dma("sp", y[:], fm(XT[:, t0:t0 + TT]), [XT.bs[ti]], [y])
                for o in range(8):
                    pb = ps[5 + o % 2]
                    for k in range(8):
                        mm(pb[:, :], wo[:, k, o * 128:(o + 1) * 128], mg[:, k, :], k == 0, k == 7, [wo, mg], [pb])
                    stt(y[:, o, :], y[:, o, :], ALPHA, pb[:, :], ALU.mult, ALU.add, [y, pb], [y])
                ln_tile(y, sq, xbo, sm, 1 + 2 * l, ti, ps[6], ps[7])
            P.barrier()


    I32 = mybir.dt.int32
    TWO_PI = 2.0 * math.pi

    def phase_init():
        with ExitStack() as st:
            CB = min(T, 2048)
            ji = sbt(st, "inji", [32, 1], I32)
            jf = sbt(st, "injf", [32, 1])
            jt = sbt(st, "injt", [32, 1])
            invf = sbt(st, "ininvf", [32, 1])
            sgn = sbt(st, "insgn", [32, 1])
            oml = sbt(st, "inoml", [32, 1])
            ti32 = sbt(st, "inti", [32, CB], I32)
            si32 = sbt(st, "insi", [32, CB], I32)
            tf = sbt(st, "intf", [32, CB])
            sf = sbt(st, "insf", [32, CB])
            ang = sbt(st, "inang", [32, CB])
            q = sbt(st, "inq", [32, CB])
            red = sbt(st, "inred", [32, CB])
            P.op("pool", lambda e: e.iota(ji[:], pattern=[[0, 1]], base=0, channel_multiplier=1), [], [ji])
            P.op("dve", lambda e: e.tensor_copy(out=jf[:], in_=ji[:]), [ji], [jf])
            ts(jt[:], jf[:], 16.0, 16.0, ALU.is_ge, ALU.mult, [jf], [jt])
            tt(invf[:], jf[:], jt[:], ALU.subtract, [jf, jt], [invf])
            act(invf[:], invf[:], AF.Exp, [invf], [invf], scale=-math.log(10000.0) / 16.0)
            ts(sgn[:], jt[:], 1.0 / 8.0, -1.0, ALU.mult, ALU.add, [jt], [sgn])
            ts(oml[:], linkc[0:32, :], -1.0, 1.0, ALU.mult, ALU.add, [linkc], [oml])
            for b0 in range(0, T, CB):
                P.op("pool", lambda e: e.iota(ti32[:], pattern=[[1, CB]], base=b0, channel_multiplier=0), [], [ti32])
                if SL >= CB:
                    P.op("pool", lambda e: e.iota(si32[:], pattern=[[0, CB]], base=(b0 // SL) * SL, channel_multiplier=0),
                         [], [si32])
                else:
                    P.op("pool", lambda e: e.iota(si32[:], pattern=[[SL, CB // SL], [0, SL]], base=b0,
                                                  channel_multiplier=0), [], [si32])
                P.op("dve", lambda e: e.tensor_copy(out=tf[:], in_=ti32[:]), [ti32], [tf])
                P.op("dve", lambda e: e.tensor_copy(out=sf[:], in_=si32[:]), [si32], [sf])
                stt(tf[:], sf[:], oml[:, 0:1], tf[:], ALU.mult, ALU.subtract, [sf, oml, tf], [tf])
                ts(ang[:], tf[:], invf[:, 0:1], -1.0, ALU.mult, ALU.mult, [tf, invf], [ang])
                for which, dst in ((0, RS), (1, RC)):
                    if which == 1:
                        ts(ang[:], ang[:], math.pi / 2.0, None, ALU.add, None, [ang], [ang])
                    ts(q[:], ang[:], 1.0 / TWO_PI, None, ALU.mult, None, [ang], [q])
                    ts(q[:], q[:], 12582912.0, 12582912.0, ALU.add, ALU.subtract, [q], [q])
                    stt(red[:], q[:], -6.28125, ang[:], ALU.mult, ALU.add, [q, ang], [red])
                    stt(red[:], q[:], -(TWO_PI - 6.28125), red[:], ALU.mult, ALU.add, [q, red], [red])
                    ts(red[:], red[:], -3.1415925, 3.1415925, ALU.max, ALU.min, [red], [red])
                    act(red[:], red[:], AF.Sin, [red], [red])
                    if which == 0:
                        ts(red[:], red[:], sgn[:, 0:1], None, ALU.mult, None, [red, sgn], [red])
                    P.dma("sp", dst[:, b0:b0 + CB], red[:], [red], dst.bs[b0 // TT:(b0 + CB) // TT])
            onesr = sbt(st, "inones", [2, 8, TT], BF16)
            mrow = sbt(st, "inmrow", [8, TT], BF16)
            lk8 = sbt(st, "inlk8", [8, 1])
            P.op("dve", lambda e: e.memset(onesr[:], 1.0), [], [onesr])
            ts(lk8[:], linkc[0:8, :], -1.0, 30000.0, ALU.add, ALU.mult, [linkc], [lk8])
            P.op("dve", lambda e: e.memset(mrow[:], 1.0), [], [mrow])
            ts(mrow[:], mrow[:], lk8[:, 0:1], None, ALU.mult, None, [mrow, lk8], [mrow])
            for ti in range(NT):
                t0 = ti * TT
                P.dma("sp", KT[:, 96:98, t0:t0 + TT].rearrange("h r t -> r h t"), onesr[:], [onesr], [KT.bs[ti]])
                P.dma("sp", QT[:, 97, t0:t0 + TT], mrow[:], [mrow], [QT.bs[ti]])
            P.barrier()

    def phase_a(l):
        with ExitStack() as st:
            wA = sbt(st, "awA", [128, 8, O_GP], BF16)
            wkrs = sbt(st, "awkrs", [128, 8, 32], BF16)
            wq = sbt(st, "awq", [128, 2, 8, 96], BF16)
            wqs = sbt(st, "awqs", [128, 2, 8, 96], BF16)
            wkn = sbt(st, "awkn", [128, 8, 64], BF16)
            wv = sbt(st, "awv", [128, 8, 64], BF16)
            qg = sbt(st, "aqg", [128, 2])
            kvg = sbt(st, "akvg", [128, 1])
            gmb = sbt(st, "agmb", [16, 1])
            indq = sbt(st, "aindq", [96, 8, 8], BF16)
            indk = sbt(st, "aindk", [128, 4, 8], BF16)
            onr = sbt(st, "aonr", [32, 8], BF16)
            with nc.allow_non_contiguous_dma(reason="tiny param vectors"):
                P.dma("sp", qg[:, :], W['q_norm_g'][l].rearrange("(c p) -> p c", p=128), [], [qg])
                P.dma("sp", kvg[:, :], W['kv_norm_g'][l].rearrange("(c p) -> p c", p=128), [], [kvg])
                P.dma("sp", gmb[:, :], W['b_mlstm_gate'][l].rearrange("a b (c o) -> (a b c) o", o=1), [], [gmb])
            for k in range(8):
                load_cast(wA[:, k, :], W['w_in'][l, k * 128:(k + 1) * 128, 0:O_GP], wA, O_GP)
                act(wkrs[:, k, 0:16], wA[:, k, O_KR + 16:O_KR + 32], AF.Copy, [wA], [wkrs])
                act(wkrs[:, k, 16:32], wA[:, k, O_KR:O_KR + 16], AF.Copy, [wA], [wkrs])
            for c2 in range(2):
                s = stg[stgi[0] % 2]
                stgi[0] += 1
                P.dma("sp", s[:, 0:768], W['w_uq'][l, c2 * 128:(c2 + 1) * 128, :], [], [s])
                sv = s[:, 0:768].rearrange("p (h d) -> p h d", h=8)
                sc = qg[:, c2:c2 + 1]
                act(wq[:, c2, :, :], sv[:, :, :], AF.Identity, [s, qg], [wq], scale=sc)
                P.op("dve", lambda e: e.memset(wqs[:, c2, :, 0:64], 0.0), [], [wqs])
                act(wqs[:, c2, :, 64:80], sv[:, :, 80:96], AF.Identity, [s, qg], [wqs], scale=sc)
                act(wqs[:, c2, :, 80:96], sv[:, :, 64:80], AF.Identity, [s, qg], [wqs], scale=sc)
            s = stg[stgi[0] % 2]
            stgi[0] += 1
            P.dma("sp", s[:, 0:1024], W['w_ukv'][l, :, :], [], [s])
            sv = s[:, 0:1024].rearrange("p (h d) -> p h d", h=8)
            act(wkn[:, :, :], sv[:, :, 0:64], AF.Identity, [s, kvg], [wkn], scale=kvg[:, 0:1])
            act(wv[:, :, :], sv[:, :, 64:128], AF.Identity, [s, kvg], [wv], scale=kvg[:, 0:1])
            P.op("dve", lambda e: e.memset(indq[:], 0.0), [], [indq])
            P.op("dve", lambda e: e.memset(indk[:], 0.0), [], [indk])
            P.op("dve", lambda e: e.memset(onr[:], 1.0), [], [onr])
            for h in range(8):
                P.op("dve", lambda e: e.memset(indq[:, h, h:h + 1], 1.0), [], [indq])
                P.op("dve", lambda e: e.memset(indk[(h % 2) * 64:(h % 2) * 64 + 64, h // 2, h:h + 1], 1.0), [], [indk])
            xbs = [sbt(st, "axb%d" % i, [128, 8, TT], BF16) for i in range(2)]
            cqf = sbt(st, "acqf", [128, 2, TT])
            sqt = sbt(st, "asq", [128, 2, TT])
            rstd = sbt(st, "arstd", [128, TT])
            cqn = sbt(st, "acqn", [128, 2, TT], BF16)
            ckf = sbt(st, "ackf", [128, TT])
            ckvn = sbt(st, "ackvn", [128, TT], BF16)
            rct = sbt(st, "arc", [96, TT])
            rst = sbt(st, "ars", [96, TT])
            r1 = sbt(st, "ar1", [96, TT])
            r2 = sbt(st, "ar2", [96, TT])
            krr = sbt(st, "akrr", [32, TT], BF16)
            fst = [sbt(st, "afst%d" % i, [128, TT]) for i in range(4)]
            bst = [sbt(st, "abst%d" % i, [128, TT], BF16) for i in range(4)]
            qos = [sbt(st, "aqo%d" % i, [96, TT], BF16) for i in range(2)]
            sqb = [sbt(st, "asqb%d" % i, [128, TT], BF16) for i in range(2)]
            sqb3 = sbt(st, "asqb3", [32, TT], BF16)
            vas = [sbt(st, "ava%d" % i, [128, 8, 128], BF16) for i in range(2)]
            qn2 = sbt(st, "aqn2", [8, T])
            km2 = sbt(st, "akm2", [8, 1])
            kmx = sbt(st, "akmx", [8, 1])
            mrw = sbt(st, "amrw", [8, T], BF16)
            for v in vas:
                P.op("dve", lambda e: e.memset(v[:], 1.0), [], [v])
            P.op("dve", lambda e: e.memset(km2[:], 0.0), [], [km2])
            cnt = {"b": 0, "f": 0, "s": 0, "q": 0}

            def nb():
                cnt["b"] += 1
                return ps[cnt["b"] % 6]

            def nf():
                cnt["f"] += 1
                return fst[cnt["f"] % 4]

            def nbs():
                cnt["s"] += 1
                return bst[cnt["s"] % 4]

            def dq():
                cnt["q"] += 1
                return "sp" if cnt["q"] % 2 else "pool"

            def rstd_from(pss, n):
                ts(rstd[:], pss[:, :], 1.0 / n, LN_EPS, ALU.mult, ALU.add, [pss], [rstd])
                act(rstd[:], rstd[:], AF.Ln, [rstd], [rstd])
                act(rstd[:], rstd[:], AF.Exp, [rstd], [rstd], scale=-0.5)

            for ti in range(NT):
                t0 = ti * TT
                tsl = slice(t0, t0 + TT)
                xb = xbs[ti % 2]
                P.dma("sp", xb[:], XTB[:, tsl].rearrange("(c p) t -> p c t", p=128), [XTB.bs[ti]], [xb])
                P.dma("pool", rct[0:32, :], RC[:, tsl], [RC.bs[ti]], [rct])
                P.dma("pool", rst[0:32, :], RS[:, tsl], [RS.bs[ti]], [rst])
                P.dma("pool", rct[64:96, :], RC[:, tsl], [RC.bs[ti]], [rct])
                P.dma("pool", rst[64:96, :], RS[:, tsl], [RS.bs[ti]], [rst])

                def fm_out(c0, m):
                    pb = nb()
                    for k in range(8):
                        mm(pb[0:m, :], wA[:, k, c0:c0 + m], xb[:, k, :], k == 0, k == 7, [wA, xb], [pb])
                    return pb
                for c2 in range(2):
                    pb = fm_out(O_CQ + c2 * 128, 128)
                    act(cqf[:, c2, :], pb[:, :], AF.Copy, [pb], [cqf])
                    act(sqt[:, c2, :], pb[:, :], AF.Square, [pb], [sqt])
                pss = nb()
                for c2 in range(2):
                    mm(pss[:, :], ones32[:], sqt[:, c2, :], c2 == 0, c2 == 1, [ones32, sqt], [pss])
                rstd_from(pss, 256.0)
                tt(cqn[:], cqf[:], rstd[:].unsqueeze(1).to_broadcast([128, 2, TT]), ALU.mult, [cqf, rstd], [cqn])
                pb = fm_out(O_CKV, 128)
                act(ckf[:], pb[:, :], AF.Copy, [pb], [ckf])
                act(sqt[:, 0, :], pb[:, :], AF.Square, [pb], [sqt])
                pss = nb()
                mm(pss[:, :], ones32[:], sqt[:, 0, :], True, True, [ones32, sqt], [pss])
                rstd_from(pss, 128.0)
                tt(ckvn[:], ckf[:], rstd[:], ALU.mult, [ckf, rstd], [ckvn])
                pb = fm_out(O_KR, 32)
                pb2 = nb()
                for k in range(8):
                    mm(pb2[0:32, :], wkrs[:, k, :], xb[:, k, :], k == 0, k == 7, [wkrs, xb], [pb2])
                tt(r1[0:32, :], pb2[0:32, :], rst[0:32, :], ALU.mult, [pb2, rst], [r1])
                tt(r2[0:32, :], pb[0:32, :], rct[0:32, :], ALU.mult, [pb, rct], [r2])
                tt(krr[:], r1[0:32, :], r2[0:32, :], ALU.add, [r1, r2], [krr])
                for h in range(8):
                    P.dma(dq(), KT[h, 64:96, tsl], krr[:], [krr], [KT.bs[ti]])
                for (c0, dst) in ((O_XM, XM), (O_US, US)):
                    for c in range(4):
                        pb = fm_out(c0 + c * 128, 128)
                        f = nf()
                        if c % 2:
                            act(f[:], pb[:, :], AF.Copy, [pb], [f])
                        else:
                            P.op("dve", lambda e: e.tensor_copy(out=f[:], in_=pb[:, :]), [pb], [f])
                        P.dma(dq(), dst[c * 128:(c + 1) * 128, tsl], f[:], [f], [dst.bs[ti]])
                        if dst is US:
                            bs_ = nbs()
                            act(bs_[:], pb[:, :], AF.Copy, [pb], [bs_])
                            P.dma(dq(), USB[c * 128:(c + 1) * 128, tsl], bs_[:], [bs_], [USB.bs[ti]])
                pb = fm_out(O_GM, 16)
                f = nf()
                act(f[0:16, :], pb[0:16, :], AF.Identity, [pb, gmb], [f], bias=gmb[:, 0:1])
                P.dma(dq(), GM[:, tsl], f[0:16, :], [f], [GM.bs[ti]])
                for b in range(4):
                    r0 = t0 + b * 128
                    pb = nb()
                    for k in range(8):
                        mm(pb[:, :], xb[:, k, b * 128:(b + 1) * 128], wA[:, k, O_VM:O_VM + 512], k == 0, k == 7, [wA, xb], [pb])
                    bs_ = nbs()
                    P.op("dve", lambda e: e.tensor_copy(out=bs_[:], in_=pb[:, :]), [pb], [bs_])
                    P.dma(dq(), VM[r0:r0 + 128, :], bs_[:], [bs_], [VM.bs[ti]])
                    pb = nb()
                    for k in range(8):
                        mm(pb[:, :], xb[:, k, b * 128:(b + 1) * 128], wA[:, k, O_OM:O_OM + 512], k == 0, k == 7, [wA, xb], [pb])
                    f = nf()
                    act(f[:], pb[:, :], AF.Sigmoid, [pb], [f])
                    P.dma(dq(), OMS[r0:r0 + 128, :], f[:], [f], [OMS.bs[ti]])
                pend_q = []
                for h in range(8):
                    pq = nb()
                    for c2 in range(2):
                        mm(pq[0:96, :], wq[:, c2, h, :], cqn[:, c2, :], c2 == 0, c2 == 1, [wq, cqn], [pq])
                    pqs = nb()
                    for c2 in range(2):
                        mm(pqs[0:96, :], wqs[:, c2, h, :], cqn[:, c2, :], c2 == 0, c2 == 1, [wqs, cqn], [pqs])
                    while pend_q:
                        pend_q.pop(0)()
                    qo = qos[h % 2]
                    tt(r1[64:96, :], pqs[64:96, :], rst[64:96, :], ALU.mult, [pqs, rst], [r1])
                    tt(r2[64:96, :], pq[64:96, :], rct[64:96, :], ALU.mult, [pq, rct], [r2])
                    tt(qo[64:96, :], r1[64:96, :], r2[64:96, :], ALU.add, [r1, r2], [qo])
                    act(qo[0:64, :], pq[0:64, :], AF.Copy, [pq], [qo])
                    sb_ = sqb[h % 2]
                    act(sb_[0:96, :], qo[:, :], AF.Square, [qo], [sb_])
                    pend_q.append(lambda h=h, sb_=sb_: mm(ps[6][0:8, :], indq[:, h, :], sb_[0:96, :], h == 0, h == 7, [indq, sb_], [ps[6]]))
                    P.dma(dq(), QT[h, 0:96, tsl], qo[:, :], [qo], [QT.bs[ti]])
                pend_k = list(pend_q)
                sb_ = sqb3
                act(sb_[0:32, :], krr[:, :], AF.Square, [krr], [sb_])
                pend_k.append(lambda sb_=sb_: mm(ps[7][0:8, :], onr[:, :], sb_[0:32, :], True, False, [onr, sb_], [ps[7]]))
                for j in range(4):
                    pb = nb()
                    mm(pb[:, :], wkn[:, 2 * j:2 * j + 2, :].rearrange("p h d -> p (h d)"), ckvn[:], True, True, [wkn, ckvn], [pb])
                    while pend_k:
                        pend_k.pop(0)()
                    if j == 0:
                        P.op("dve", lambda e: e.tensor_copy(out=qn2[:, tsl], in_=ps[6][0:8, :]), [ps[6]], [qn2])
                    ko = nbs()
                    P.op("dve", lambda e: e.tensor_copy(out=ko[:], in_=pb[:, :]), [pb], [ko])
                    sb_ = sqb[1 - j % 2]
                    act(sb_[:, :], ko[:, :], AF.Square, [ko], [sb_])
                    pend_k.append(lambda j=j, sb_=sb_: mm(ps[7][0:8, :], indk[:, j, :], sb_[:, :], False, j == 3, [indk, sb_], [ps[7]]))
                    P.dma(dq(), KT[2 * j, 0:64, tsl], ko[0:64, :], [ko], [KT.bs[ti]])
                    P.dma(dq(), KT[2 * j + 1, 0:64, tsl], ko[64:128, :], [ko], [KT.bs[ti]])
                pend_v = list(pend_k)
                for b in range(4):
                    r0 = t0 + b * 128
                    pb = nb()
                    mm(pb[:, :], ckvn[:, b * 128:(b + 1) * 128], wv[:].rearrange("p h d -> p (h d)"), True, True, [ckvn, wv], [pb])
                    while pend_v:
                        pend_v.pop(0)()
                    if b == 0:
                        P.op("dve", lambda e: e.reduce_max(out=kmx[:], in_=ps[7][0:8, :], axis=AX.X), [ps[7]], [kmx])
                        tt(km2[:], km2[:], kmx[:], ALU.max, [km2, kmx], [km2])
                    va = vas[b % 2]
                    P.op("dve", lambda e: e.tensor_copy(out=va[:, :, 0:64], in_=pb[:, :].rearrange("p (h d) -> p h d", h=8)),
                         [pb], [va])
                    P.dma(dq(), VA[r0:r0 + 128, :, :], va[:], [va], [VA.bs[ti]])
            act(qn2[:], qn2[:], AF.Sqrt, [qn2, km2], [qn2], scale=km2[:, 0:1])
            ts(mrw[:], qn2[:], -1.0, None, ALU.mult, None, [qn2], [mrw])
            P.dma("sp", QT[:, 96, :], mrw[:], [mrw], QT.bs)
            P.barrier()


    def phase_att(l):
        SCALE = 96.0 ** -0.5
        with ExitStack() as st:
            kTs = [sbt(st, "tk%d" % i, [98, T], BF16) for i in range(2)]
            qTs = [sbt(st, "tq%d" % i, [98, T], BF16) for i in range(2)]
            vhs = [sbt(st, "tv%d" % i, [128, NCH, 128], BF16) for i in range(2)]
            pts = [sbt(st, "tp%d" % i, [128, TT], BF16) for i in range(3)]
            rlt = sbt(st, "trl", [128, TT])
            osb = [sbt(st, "to%d" % i, [64, TT], BF16) for i in range(2)]
            n = 0
            for h in range(8):
                kT, qT, vh = kTs[h % 2], qTs[h % 2], vhs[h % 2]
                P.dma("sp", kT[:], KT[h, :, :], KT.bs, [kT])
                P.dma("pool", qT[:], QT[h, :, :], QT.bs, [qT])
                P.dma("sp", vh[:], VA[:, h, :].rearrange("(n p) d -> p n d", p=128), VA.bs, [vh])
                for i in range(NT):
                    qs = slice(i * TT, (i + 1) * TT)
                    acc = ps[4 + i % 2]
                    segq = (i * TT) // SL

                    def score(kb):
                        R = 97 if (kb * 128) // SL == segq else 98
                        pb = ps[(n + kb) % 3]
                        mm(pb[:, :], kT[0:R, kb * 128:(kb + 1) * 128], qT[0:R, qs], True, True, [kT, qT], [pb])
                        return pb
                    pbn = score(0)
                    for kb in range(NCH):
                        pb = pbn
                        pt = pts[(n + kb) % 3]
                        if kb + 1 < NCH:
                            pbn = score(kb + 1)
                        act(pt[:], pb[:, :], AF.Exp, [pb], [pt], scale=SCALE)
                        mm(acc[:, :], vh[:, kb, :], pt[:], kb == 0, kb == NCH - 1, [vh, pt], [acc])
                    n += NCH
                    P.op("dve", lambda e: e.reciprocal(out=rlt[64:128, :], in_=acc[64:128, :]), [acc], [rlt])
                    o = osb[i % 2]
                    tt(o[:], acc[0:64, :], rlt[64:128, :], ALU.mult, [acc, rlt], [o])
                    P.dma("pool" if i % 2 else "sp", OT[h * 64:(h + 1) * 64, qs], o[:], [o], [OT.bs[i]])
            P.barrier()


    def emit_sin(dst_t, dst, ang_t, ang, q_t, q, shift):
        ts(q, ang, 1.0 / TWO_PI, shift / TWO_PI, ALU.mult, ALU.add, [ang_t], [q_t])
        ts(q, q, 12582912.0, 12582912.0, ALU.add, ALU.subtract, [q_t], [q_t])
        stt(dst, q, -6.28125, ang, ALU.mult, ALU.add, [q_t, ang_t], [dst_t])
        stt(dst, q, -(TWO_PI - 6.28125), dst, ALU.mult, ALU.add, [q_t, dst_t], [dst_t])
        if shift:
            ts(dst, dst, shift, None, ALU.add, None, [dst_t], [dst_t])
        ts(dst, dst, -3.1415925, 3.1415925, ALU.max, ALU.min, [dst_t], [dst_t])
        act(dst, dst, AF.Sin, [dst_t], [dst_t])

    def phase_s5(l):
        NSC = "tiny param vectors"
        with ExitStack() as lst:
            CLr = sbt(lst, "sCLr", [128, 16, 128], BF16)
            CLi = sbt(lst, "sCLi", [128, 16, 128], BF16)
            wglu = sbt(lst, "swglu", [128, 4, 2 * D], BF16)
            dsk = sbt(lst, "sdsk", [128, 4])
            for k in range(4):
                load_cast(wglu[:, k, :], W['w_glu'][l, k * 128:(k + 1) * 128, :], wglu, 2 * D)
            with nc.allow_non_contiguous_dma(reason=NSC):
                P.dma("sp", dsk[:, :], W['s5_d'][l].rearrange("(c g) w -> (g w) c", c=4), [], [dsk])
            with ExitStack() as st:
                Z = [sbt(st, "sZ%d" % i, [128, 4, 128]) for i in range(2)]
                for i, nm in enumerate(('s5_c_re', 's5_c_im')):
                    P.op("dve", lambda e: e.memset(Z[i][:], 0.0), [], [Z[i]])
                    for ch in range(4):
                        for jj in range(4):
                            for two in range(2):
                                g = 2 * (4 * ch + jj) + two
                                P.dma("sp" if two else "pool", Z[i][32 * jj + 16 * two:32 * jj + 16 * two + 16, ch, 64 * two:64 * two + 64],
                                      W[nm][l, g], [], [Z[i]])
                P.op("dve", lambda e: e.memset(CLr[:], 0.0), [], [CLr])
                P.op("dve", lambda e: e.memset(CLi[:], 0.0), [], [CLi])
                for i, CL in enumerate((CLr, CLi)):
                    for ch in range(4):
                        pb = ps[(2 * i + ch) % 4]
                        mm(pb[:, 0:128], Z[i][:, ch, :], ident[:], True, True, [Z[i], ident], [pb])
                        for jj in range(4):
                            act(CL[:, 4 * ch + jj, 32 * jj:32 * jj + 32], pb[:, 32 * jj:32 * jj + 32], AF.Copy, [pb], [CL],
                                scale=(1.0 if i == 0 else -1.0))
                P.barrier()
            for d in (0, 1):
                with ExitStack() as st:
                    sm = {n: sbt(st, "s5" + n, [128, 16]) for n in
                          ("are", "aim", "dt", "rmag", "th", "cs", "sn", "q", "abr", "abi", "nr", "ni", "den", "bsr", "bsi", "t")}
                    with nc.allow_non_contiguous_dma(reason=NSC):
                        for two in range(2):
                            prt = slice(two * 64, two * 64 + 64)
                            P.dma("sp", sm["are"][prt, :], W['s5_a_re'][l, d].rearrange("(j two) p -> two p j", two=2)[two], [], [sm["are"]])
                            P.dma("sp", sm["aim"][prt, :], W['s5_a_im'][l, d].rearrange("(j two) p -> two p j", two=2)[two], [], [sm["aim"]])
                            P.dma("sp", sm["dt"][prt, :], W['s5_log_dt'][l, d].rearrange("(j two) -> two j", two=2)[two].partition_broadcast(64),
                                  [], [sm["dt"]])
                    A = lambda n: sm[n][:, :]
                    act(A("dt"), A("dt"), AF.Exp, [sm["dt"]], [sm["dt"]])
                    tt(A("rmag"), A("are"), A("dt"), ALU.mult, [sm["are"], sm["dt"]], [sm["rmag"]])
                    act(A("rmag"), A("rmag"), AF.Exp, [sm["rmag"]], [sm["rmag"]])
                    tt(A("th"), A("aim"), A("dt"), ALU.mult, [sm["aim"], sm["dt"]], [sm["th"]])
                    emit_sin(sm["sn"], A("sn"), sm["th"], A("th"), sm["q"], A("q"), 0.0)
                    emit_sin(sm["cs"], A("cs"), sm["th"], A("th"), sm["q"], A("q"), math.pi / 2.0)
                    tt(A("abr"), A("rmag"), A("cs"), ALU.mult, [sm["rmag"], sm["cs"]], [sm["abr"]])
                    tt(A("abi"), A("rmag"), A("sn"), ALU.mult, [sm["rmag"], sm["sn"]], [sm["abi"]])
                    ts(A("abr"), A("abr"), -1.0, None, ALU.add, None, [sm["abr"]], [sm["abr"]])
                    tt(A("nr"), A("abr"), A("are"), ALU.mult, [sm["abr"], sm["are"]], [sm["nr"]])
                    tt(A("t"), A("abi"), A("aim"), ALU.mult, [sm["abi"], sm["aim"]], [sm["t"]])
                    tt(A("nr"), A("nr"), A("t"), ALU.add, [sm["nr"], sm["t"]], [sm["nr"]])
                    tt(A("ni"), A("abi"), A("are"), ALU.mult, [sm["abi"], sm["are"]], [sm["ni"]])
                    tt(A("t"), A("abr"), A("aim"), ALU.mult, [sm["abr"], sm["aim"]], [sm["t"]])
                    tt(A("ni"), A("ni"), A("t"), ALU.subtract, [sm["ni"], sm["t"]], [sm["ni"]])
                    tt(A("den"), A("are"), A("are"), ALU.mult, [sm["are"]], [sm["den"]])
                    tt(A("t"), A("aim"), A("aim"), ALU.mult, [sm["aim"]], [sm["t"]])
                    tt(A("den"), A("den"), A("t"), ALU.add, [sm["den"], sm["t"]], [sm["den"]])
                    P.op("dve", lambda e: e.reciprocal(out=A("den"), in_=A("den")), [sm["den"]], [sm["den"]])
                    tt(A("bsr"), A("nr"), A("den"), ALU.mult, [sm["nr"], sm["den"]], [sm["bsr"]])
                    tt(A("bsi"), A("ni"), A("den"), ALU.mult, [sm["ni"], sm["den"]], [sm["bsi"]])
                    BLr = sbt(st, "sBLr", [128, 16, 128], BF16)
                    BLi = sbt(st, "sBLi", [128, 16, 128], BF16)
                    cosT = sbt(st, "scosT", [128, 16, TT])
                    sinT = sbt(st, "ssinT", [128, 16, TT])
                    with ExitStack() as st2:
                        Bt = [sbt(st2, "sBt%d" % i, [128, 16, 16]) for i in range(2)]
                        Bp = [sbt(st2, "sBp%d" % i, [128, 16, 16]) for i in range(2)]
                        tmp = sbt(st2, "sBtmp", [128, 16, 16])
                        X = sbt(st2, "sX", [128, 16, 2, 16])
                        for i, nm in enumerate(('s5_b_re', 's5_b_im')):
                            for two in range(2):
                                P.dma("sp", Bt[i][two * 64:two * 64 + 64, :, :],
                                      W[nm][l].rearrange("(j two) p c -> two p j c", two=2)[two], [], [Bt[i]])
                        bc = lambda n: sm[n][:, :].unsqueeze(2).to_broadcast([128, 16, 16])
                        tt(Bp[0][:], Bt[0][:], bc("bsr"), ALU.mult, [Bt[0], sm["bsr"]], [Bp[0]])
                        tt(tmp[:], Bt[1][:], bc("bsi"), ALU.mult, [Bt[1], sm["bsi"]], [tmp])
                        tt(Bp[0][:], Bp[0][:], tmp[:], ALU.subtract, [Bp[0], tmp], [Bp[0]])
                        tt(Bp[1][:], Bt[1][:], bc("bsr"), ALU.mult, [Bt[1], sm["bsr"]], [Bp[1]])
                        tt(tmp[:], Bt[0][:], bc("bsi"), ALU.mult, [Bt[0], sm["bsi"]], [tmp])
                        tt(Bp[1][:], Bp[1][:], tmp[:], ALU.add, [Bp[1], tmp], [Bp[1]])
                        for i, BL in enumerate((BLr, BLi)):
                            P.op("dve", lambda e: e.memset(BL[:], 0.0), [], [BL])
                            P.op("dve", lambda e: e.memset(X[:], 0.0), [], [X])
                            P.op("dve", lambda e: e.tensor_copy(out=X[0:64, :, 0, :], in_=Bp[i][0:64, :, :]), [Bp[i]], [X])
                            P.op("dve", lambda e: e.tensor_copy(out=X[64:128, :, 1, :], in_=Bp[i][64:128, :, :]), [Bp[i]], [X])
                            for ch in range(4):
                                pb = ps[ch]
                                mm(pb[:, 0:128], X[:, 4 * ch:4 * ch + 4, :, :].rearrange("p a b c -> p (a b c)"), ident[:], True, True,
                                   [X, ident], [pb])
                                for jj in range(4):
                                    act(BL[32 * jj:32 * jj + 32, 4 * ch + jj, :], pb[32 * jj:32 * jj + 32, 0:128], AF.Copy, [pb], [BL])
                        ti32 = sbt(st2, "sti", [128, TT], I32)
                        tau = sbt(st2, "stau", [128, TT])
                        ang = sbt(st2, "sang", [128, 16, TT])
                        qq = sbt(st2, "sqq", [128, 16, TT])
                        if d == 0:
                            P.op("pool", lambda e: e.iota(ti32[:], pattern=[[1, TT]], base=1, channel_multiplier=0), [], [ti32])
                        else:
                            P.op("pool", lambda e: e.iota(ti32[:], pattern=[[-1, TT]], base=TT, channel_multiplier=0), [], [ti32])
                        P.op("dve", lambda e: e.tensor_copy(out=tau[:], in_=ti32[:]), [ti32], [tau])
                        tt(ang[:], sm["th"][:, :].unsqueeze(2).to_broadcast([128, 16, TT]),
                           tau[:].unsqueeze(1).to_broadcast([128, 16, TT]), ALU.mult, [sm["th"], tau], [ang])
                        fl = lambda t: t[:].rearrange("p a b -> p (a b)")
                        emit_sin(sinT, fl(sinT), ang, fl(ang), qq, fl(qq), 0.0)
                        emit_sin(cosT, fl(cosT), ang, fl(ang), qq, fl(qq), math.pi / 2.0)
                        P.barrier()
                    ufs = [sbt(st, "suf%d" % i, [128, 4, TT]) for i in range(2)]
                    ubs = [sbt(st, "sub%d" % i, [128, 4, TT], BF16) for i in range(2)]
                    wk = [[sbt(st, "sw%s%d" % (n, i), [128, TT]) for i in range(2)] for n in "abcdefgh"]
                    xb_ = [[sbt(st, "sx%s%d" % (n, i), [128, TT], BF16) for i in range(2)] for n in "ri"]
                    car = [sbt(st, "scar%d" % i, [128, 16]) for i in range(2)]
                    y1t = [sbt(st, "sy1%d" % i, [128, TT]) for i in range(2)]
                    yg = sbt(st, "syg", [128, 4, TT], BF16)
                    sg = [sbt(st, "ssg%d" % i, [128, TT]) for i in range(2)]
                    P.op("dve", lambda e: e.memset(car[0][:], 0.0), [], [car[0]])
                    P.op("dve", lambda e: e.memset(car[1][:], 0.0), [], [car[1]])
                    order = list(range(NT)) if d == 0 else list(range(NT - 1, -1, -1))
                    nblk = 0
                    for it, ti in enumerate(order):
                        t0 = ti * TT
                        tsl = slice(t0, t0 + TT)
                        uf, ub = ufs[it % 2], ubs[it % 2]
                        P.dma("sp", uf[:], US[:, tsl].rearrange("(c p) t -> p c t", p=128), [US.bs[ti]], [uf])
                        act(ub[:], uf[:], AF.Copy, [uf], [ub])
                        if it > 0:
                            bnd = t0 if d == 0 else t0 + TT
                            if bnd % SL == 0:
                                for cc in car:
                                    ts(cc[:], cc[:], linkc[:, 0:1], None, ALU.mult, None, [cc, linkc], [cc])
                        for ch in range(4):
                            psy = ps[4 + ch % 2]
                            for jj in range(4):
                                j = 4 * ch + jj
                                pr_ = slice(32 * jj, 32 * jj + 32)
                                pvr, pvi = ps[(2 * nblk) % 4], ps[(2 * nblk + 1) % 4]
                                w = [wk[i][nblk % 2] for i in range(8)]
                                xr_b, xi_b = xb_[0][nblk % 2], xb_[1][nblk % 2]
                                nblk += 1
                                mm(pvr[:, :], BLr[:, j, :], ub[:, ch, :], True, True, [BLr, ub], [pvr])
                                mm(pvi[:, :], BLi[:, j, :], ub[:, ch, :], True, True, [BLi, ub], [pvi])
                                c_, s_ = cosT[:, j, :], sinT[:, j, :]
                                tt(w[0][:], pvr[:, :], c_, ALU.mult, [pvr, cosT], [w[0]])
                                tt(w[1][:], pvi[:, :], s_, ALU.mult, [pvi, sinT], [w[1]])
                                tt(w[0][:], w[0][:], w[1][:], ALU.add, [w[0], w[1]], [w[0]])
                                tt(w[2][:], pvi[:, :], c_, ALU.mult, [pvi, cosT], [w[2]])
                                tt(w[3][:], pvr[:, :], s_, ALU.mult, [pvr, sinT], [w[3]])
                                tt(w[2][:], w[2][:], w[3][:], ALU.subtract, [w[2], w[3]], [w[2]])
                                rb = sm["rmag"][:, j:j + 1].to_broadcast([128, TT])
                                for (src, dst, cc) in ((w[0], w[4], car[0]), (w[2], w[5], car[1])):
                                    if d == 0:
                                        P.op("dve", lambda e: e.tensor_tensor_scan(out=dst[:], data0=rb, data1=src[:],
                                                                                   initial=cc[:, j:j + 1], op0=ALU.mult, op1=ALU.add),
                                             [src, cc, sm["rmag"]], [dst])
                                    else:
                                        P.op("dve", lambda e: e.tensor_tensor_scan(out=dst[:, ::-1], data0=rb, data1=src[:, ::-1],
                                                                                   initial=cc[:, j:j + 1], op0=ALU.mult, op1=ALU.add),
                                             [src, cc, sm["rmag"]], [dst])
                                tt(w[6][:], w[4][:], c_, ALU.mult, [w[4], cosT], [w[6]])
                                tt(w[1][:], w[5][:], s_, ALU.mult, [w[5], sinT], [w[1]])
                                tt(w[6][:], w[6][:], w[1][:], ALU.subtract, [w[6], w[1]], [w[6]])
                                tt(w[7][:], w[4][:], s_, ALU.mult, [w[4], sinT], [w[7]])
                                tt(w[3][:], w[5][:], c_, ALU.mult, [w[5], cosT], [w[3]])
                                tt(w[7][:], w[7][:], w[3][:], ALU.add, [w[7], w[3]], [w[7]])
                                lastc = slice(TT - 1, TT) if d == 0 else slice(0, 1)
                                act(car[0][:, j:j + 1], w[6][:, lastc], AF.Copy, [w[6]], [car[0]])
                                act(car[1][:, j:j + 1], w[7][:, lastc], AF.Copy, [w[7]], [car[1]])
                                act(xr_b[:], w[6][:], AF.Copy, [w[6]], [xr_b])
                                act(xi_b[:], w[7][:], AF.Copy, [w[7]], [xi_b])
                                mm(psy[:, :], CLr[:, j, :], xr_b[:], jj == 0, False, [CLr, xr_b], [psy])
                                mm(psy[:, :], CLi[:, j, :], xi_b[:], False, jj == 3, [CLi, xi_b], [psy])
                            y1 = y1t[ch % 2]
                            if d == 0:
                                stt(y1[:], uf[:, ch, :], dsk[:, ch:ch + 1], psy[:, :], ALU.mult, ALU.add, [uf, dsk, psy], [y1])
                                P.dma("pool", Y1[ch * 128:(ch + 1) * 128, tsl], y1[:], [y1], [Y1.bs[ti]])
                            else:
                                P.dma("pool", y1[:], Y1[ch * 128:(ch + 1) * 128, tsl], [Y1.bs[ti]], [y1])
                                tt(y1[:], y1[:], psy[:, :], ALU.add, [y1, psy], [y1])
                                act(yg[:, ch, :], y1[:], AF.Gelu_apprx_tanh, [y1], [yg])
                        if d == 1:
                            for o in range(8):
                                pv, pg = ps[6], ps[7]
                                for k in range(4):
                                    mm(pv[:, :], wglu[:, k, o * 128:(o + 1) * 128], yg[:, k, :], k == 0, k == 3, [wglu, yg], [pv])
                                for k in range(4):
                                    mm(pg[:, :], wglu[:, k, D + o * 128:D + (o + 1) * 128], yg[:, k, :], k == 0, k == 3, [wglu, yg], [pg])
                                s1, s2 = sg[0], sg[1]
                                act(s1[:], pg[:, :], AF.Sigmoid, [pg], [s1])
                                s3 = y1t[o % 2]
                                tt(s2[:], s1[:], pv[:, :], ALU.mult, [s1, pv], [s2])
                                P.dma("sp" if o % 2 else "pool", YC[o * 128:(o + 1) * 128, tsl], s2[:], [s2], [YC.bs[ti]])
                    P.barrier()


    MS = nc.dram_tensor("MSscr", [4, 4, NCH], F32).ap()
    MSb = [Buf("ms%d" % i) for i in range(4)]

    def phase_m(l):
        CW = max(1, NCH // 32)
        NBLK = NCH // CW
        R = 4 * NBLK
        WD = CW * 128
        NCS = SL // 128
        with ExitStack() as st:
            cw = sbt(st, "m0cw", [128, 3, 4])
            cb = sbt(st, "m0cb", [128, 4])
            with nc.allow_non_contiguous_dma(reason="tiny param vectors"):
                for w in range(3):
                    P.dma("sp", cw[:, w, :], W['conv_m_w'][l, w].rearrange("(c p) -> p c", p=128), [], [cw])
                P.dma("sp", cb[:, :], W['conv_m_b'][l].rearrange("(c p) -> p c", p=128), [], [cb])
            xts = [sbt(st, "m0x%d" % i, [128, TT + 2]) for i in range(3)]
            accs = [sbt(st, "m0a%d" % i, [128, TT]) for i in range(2)]
            xos = [sbt(st, "m0o%d" % i, [128, TT], BF16) for i in range(2)]
            n = 0
            for ti in range(NT):
                t0 = ti * TT
                for c in range(4):
                    xt, acc, xo = xts[n % 3], accs[n % 2], xos[n % 2]
                    n += 1
                    load_halo("sp" if c % 2 else "pool", xt, XM, c * 128, 128, ti, XM.bs)
                    ts(acc[:], xt[:, 0:TT], cw[:, 0, c:c + 1], None, ALU.mult, None, [xt, cw], [acc])
                    stt(acc[:], xt[:, 1:TT + 1], cw[:, 1, c:c + 1], acc[:], ALU.mult, ALU.add, [xt, cw, acc], [acc])
                    stt(acc[:], xt[:, 2:TT + 2], cw[:, 2, c:c + 1], acc[:], ALU.mult, ALU.add, [xt, cw, acc], [acc])
                    act(xo[:], acc[:], AF.Silu, [acc, cb], [xo], bias=cb[:, c:c + 1])
                    P.dma("sp", XCB[c * 128:(c + 1) * 128, t0:t0 + TT], xo[:], [xo], [XCB.bs[ti]])
            P.barrier()
        import os as _os
        MSTOP = _os.environ.get('MSTOP', '')
        if MSTOP == 'm0':
            return
        with ExitStack() as lst:
            wq = sbt(lst, "mwq", [128, 4, 128], BF16)
            wk = sbt(lst, "mwk", [128, 4, 128], BF16)
            for h in range(4):
                load_cast(wq[:, h, :], W['w_q_m'][l, h], wq, 128)
                load_cast(wk[:, h, :], W['w_k_m'][l, h], wk, 128, scale=128.0 ** -0.5)
            maskF = sbt(lst, "mmaskF", [128, 128])
            maskB = sbt(lst, "mmaskB", [128, 128])
            for mk, cm, stp in ((maskF, -1, 1), (maskB, 1, -1)):
                P.op("pool", lambda e: e.memset(mk[:], 1.0), [], [mk])
                P.op("pool", lambda e: e.affine_select(out=mk[:], in_=mk[:], compare_op=ALU.is_ge, fill=0.0, base=0,
                                                       pattern=[[stp, 128]], channel_multiplier=cm), [mk], [mk])
            rmask = sbt(lst, "mrmask", [128, CW, 128])
            nmask = sbt(lst, "mnmask", [128, CW, 128])
            P.op("dve", lambda e: e.memset(rmask[:], 1.0), [], [rmask])
            P.op("dve", lambda e: e.memset(rmask[:, :, 0:1], 0.0), [], [rmask])
            P.op("dve", lambda e: e.memset(nmask[:], 0.0), [], [nmask])
            P.op("dve", lambda e: e.memset(nmask[:, :, 0:1], -1.0e30), [], [nmask])
            for d in (0, 1):
                with ExitStack() as st:
                    rev = (lambda ap: ap) if d == 0 else (lambda ap: ap[:, ::-1])
                    g = {n: sbt(st, "mg" + n, [R, WD]) for n in ("I", "A", "b", "cb", "M", "nM", "wi", "en", "wg", "t")}
                    cl = {n: sbt(st, "mc" + n, [R, CW]) for n in ("ms", "Ml", "t")}
                    ch4 = {n: sbt(st, "m4" + n, [4, NCH]) for n in ("al", "bm", "mn", "ms")}
                    mini = sbt(st, "mmini", [4, 1])
                    G = lambda n: g[n][:, :]
                    G3 = lambda n: g[n][:, :].rearrange("r (c w) -> r c w", w=128)
                    for h in range(4):
                        rs_ = slice(h * NBLK, (h + 1) * NBLK)
                        P.dma("sp", g["I"][rs_, :], GM[d * 8 + h, :].rearrange("(b w) -> b w", w=WD), GM.bs, [g["I"]])
                        P.dma("pool", g["A"][rs_, :], GM[d * 8 + 4 + h, :].rearrange("(b w) -> b w", w=WD), GM.bs, [g["A"]])
                    act(G("A"), G("A"), AF.Exp, [g["A"]], [g["A"]], scale=-1.0)
                    act(G("A"), G("A"), AF.Ln, [g["A"]], [g["A"]], bias=1.0)
                    rm2 = rmask[0:R, :, :].rearrange("r c w -> r (c w)")
                    nm2 = nmask[0:R, :, :].rearrange("r c w -> r (c w)")
                    P.op("dve", lambda e: e.tensor_tensor_scan(out=rev(G("t")), data0=rm2, data1=rev(G("A")), initial=0.0,
                                                               op0=ALU.mult, op1=ALU.add), [g["A"], rmask], [g["t"]])
                    tt(G("b"), G("I"), G("t"), ALU.add, [g["I"], g["t"]], [g["b"]])
                    P.op("dve", lambda e: e.tensor_tensor_scan(out=rev(G("cb")), data0=nm2, data1=rev(G("b")), initial=-1.0e30,
                                                               op0=ALU.add, op1=ALU.max), [g["b"], nmask], [g["cb"]])
                    lastw = 127 if d == 0 else 0
                    ts(cl["t"][:, :], G3("t")[:, :, lastw], -1.0, None, ALU.mult, None, [g["t"]], [cl["t"]])
                    P.dma("sp", MS[0].rearrange("h (b c) -> (h b) c", c=CW), cl["t"][:, :], [cl["t"]], [MSb[0]])
                    P.dma("sp", MS[1].rearrange("h (b c) -> (h b) c", c=CW), G3("cb")[:, :, lastw], [g["cb"]], [MSb[1]])
                    P.dma("sp", ch4["al"][:, :], MS[0], [MSb[0]], [ch4["al"]])
                    P.dma("sp", ch4["bm"][:, :], MS[1], [MSb[1]], [ch4["bm"]])
                    P.op("dve", lambda e: e.memset(mini[:], 0.0), [], [mini])
                    for sg_ in (range(NSEG) if d == 0 else range(NSEG - 1, -1, -1)):
                        c0, c1 = sg_ * NCS, (sg_ + 1) * NCS
                        P.op("dve", lambda e: e.tensor_tensor_scan(out=rev(ch4["mn"][:, c0:c1]), data0=rev(ch4["bm"][:, c0:c1]),
                                                                   data1=rev(ch4["al"][:, c0:c1]), initial=mini[:, 0:1],
                                                                   op0=ALU.max, op1=ALU.add), [ch4["bm"], ch4["al"], mini], [ch4["mn"]])
                        if d == 0:
                            P.op("dve", lambda e: e.tensor_copy(out=ch4["ms"][:, c0:c0 + 1], in_=mini[:, :]), [mini], [ch4["ms"]])
                            if NCS > 1:
                                P.op("dve", lambda e: e.tensor_copy(out=ch4["ms"][:, c0 + 1:c1], in_=ch4["mn"][:, c0:c1 - 1]),
                                     [ch4["mn"]], [ch4["ms"]])
                            lastm = ch4["mn"][:, c1 - 1:c1]
                        else:
                            P.op("dve", lambda e: e.tensor_copy(out=ch4["ms"][:, c1 - 1:c1], in_=mini[:, :]), [mini], [ch4["ms"]])
                            if NCS > 1:
                                P.op("dve", lambda e: e.tensor_copy(out=ch4["ms"][:, c0:c1 - 1], in_=ch4["mn"][:, c0 + 1:c1]),
                                     [ch4["mn"]], [ch4["ms"]])
                            lastm = ch4["mn"][:, c0:c0 + 1]
                        ts(mini[:], lastm, linkc[0:4, 0:1], None, ALU.mult, None, [ch4["mn"], linkc], [mini])
                    P.dma("sp", MS[2], ch4["ms"][:, :], [ch4["ms"]], [MSb[2]])
                    P.dma("sp", cl["ms"][:, :], MS[2].rearrange("h (b c) -> (h b) c", c=CW), [MSb[2]], [cl["ms"]])
                    msb = cl["ms"][:, :].unsqueeze(2).to_broadcast([R, CW, 128])
                    tt(G3("M"), G3("cb"), msb, ALU.max, [g["cb"], cl["ms"]], [g["M"]])
                    ts(G("nM"), G("M"), -1.0, None, ALU.mult, None, [g["M"]], [g["nM"]])
                    tt(G3("wi"), G3("nM"), msb, ALU.add, [g["nM"], cl["ms"]], [g["wi"]])
                    act(G("wi"), G("wi"), AF.Exp, [g["wi"]], [g["wi"]])
                    tt(G("en"), G("t"), G("M"), ALU.subtract, [g["t"], g["M"]], [g["en"]])
                    act(G("en"), G("en"), AF.Exp, [g["en"]], [g["en"]])
                    P.op("dve", lambda e: e.tensor_copy(out=cl["Ml"][:, :], in_=G3("M")[:, :, lastw]), [g["M"]], [cl["Ml"]])
                    tt(G3("wg"), G3("b"), cl["Ml"][:, :].unsqueeze(2).to_broadcast([R, CW, 128]), ALU.subtract,
                       [g["b"], cl["Ml"]], [g["wg"]])
                    act(G("wg"), G("wg"), AF.Exp, [g["wg"]], [g["wg"]])
                    if MSTOP == 'm1':
                        P.barrier()
                        continue
                    xcs = [sbt(st, "mxc%d" % i, [128, 4, 128], BF16) for i in range(2)]
                    vas = [sbt(st, "mva%d" % i, [128, 4, 130], BF16) for i in range(2)]
                    for v in vas:
                        P.op("dve", lambda e: e.memset(v[:], 1.0), [], [v])
                    cols = [sbt(st, "mcol%d" % i, [128, 12]) for i in range(2)]
                    C32 = [sbt(st, "mC%d" % i, [128, 130]) for i in range(4)]
                    Cb = [sbt(st, "mCb%d" % i, [128, 130], BF16) for i in range(4)]
                    for h in range(4):
                        P.op("dve", lambda e: e.memset(C32[h][:], 0.0), [], [C32[h]])
                        P.op("dve", lambda e: e.memset(Cb[h][:], 0.0), [], [Cb[h]])
                    W2 = lambda nm, i: [sbt(st, "m%s%d" % (nm, k), [128, 128], BF16 if i else F32) for k in range(2)]
                    qTs, kTs, kts, STs, qts = W2("qT", 1), W2("kT", 1), W2("kt", 1), W2("ST", 1), W2("qt", 1)
                    ETs, wbs, sfs = W2("ET", 0), W2("wb", 0), W2("sf", 0)
                    dn = [sbt(st, "mdn%d" % k, [128, 2]) for k in range(2)]
                    hst = [sbt(st, "mhst%d" % k, [128, 512]) for k in range(2)]
                    hfs = [sbt(st, "mhf%d" % k, [128, 512]) for k in range(2)]
                    oms = [sbt(st, "moms%d" % k, [128, 512]) for k in range(2)]
                    bst6 = sbt(st, "mbst", [128, 4, 6])
                    mv = sbt(st, "mmv", [128, 4, 2])
                    rsd = sbt(st, "mrsd", [128, 4])
                    hbn = [sbt(st, "mhbn%d" % k, [128, 512], BF16) for k in range(2)]
                    hbt = [sbt(st, "mhbt%d" % k, [128, 4, 128], BF16) for k in range(2)]
                    mask = maskF if d == 0 else maskB
                    order = list(range(NCH)) if d == 0 else list(range(NCH - 1, -1, -1))
                    n = 0
                    for it, c in enumerate(order):
                        blk, cwi = c // CW, c % CW
                        csl = slice(cwi * 128, (cwi + 1) * 128)
                        tsl = slice(c * 128, (c + 1) * 128)
                        ti = (c * 128) // TT
                        xc, va, col = xcs[it % 2], vas[it % 2], cols[it % 2]
                        P.dma("sp", xc[:], XCB[:, tsl].rearrange("(k p) t -> p k t", p=128), [XCB.bs[ti]], [xc])
                        P.dma("pool", va[:, :, 0:128], VM[tsl, :].rearrange("t (h e) -> t h e", h=4), [VM.bs[ti]], [va])
                        if d == 1:
                            hf, om_ = hfs[it % 2], oms[it % 2]
                            P.dma("sp", hf[:], HF[tsl, :], [HF.bs[ti]], [hf])
                            P.dma("pool", om_[:], OMS[tsl, :], [OMS.bs[ti]], [om_])
                        bnd = c * 128 if d == 0 else (c + 1) * 128
                        if it > 0 and bnd % SL == 0:
                            for h in range(4):
                                ts(C32[h][:], C32[h][:], linkc[:, 0:1], None, ALU.mult, None, [C32[h], linkc], [C32[h]])
                                act(Cb[h][:], C32[h][:], AF.Copy, [C32[h]], [Cb[h]])
                        pc = ps[7]
                        selc = ident[0:R, blk:blk + 3 * NBLK + 1:NBLK]
                        for qi, nm in enumerate(("b", "wg", "en")):
                            mm(pc[:, 4 * qi:4 * qi + 4], g[nm][:, csl], selc, True, True, [g[nm], ident], [pc])
                        P.op("dve", lambda e: e.tensor_copy(out=col[:], in_=pc[:, 0:12]), [pc], [col])
                        hs = hst[it % 2]
                        for h in range(4):
                            r = h * NBLK + blk
                            k2 = n % 2
                            n += 1
                            qT, kT, kt, ST, qt, ET, wb, sf = qTs[k2], kTs[k2], kts[k2], STs[k2], qts[k2], ETs[k2], wbs[k2], sfs[k2]
                            pq, pk, pkt, pm, pw, pS = ps[0], ps[1], ps[2], ps[3], ps[4], ps[5]
                            pn = ps[6]
                            sel = ident[0:R, r:r + 1].to_broadcast([R, 128])
                            mm(pq[:, 0:128], wq[:, h, :], xc[:, h, :], True, True, [wq, xc], [pq])
                            mm(pk[:, 0:128], wk[:, h, :], xc[:, h, :], True, True, [wk, xc], [pk])
                            mm(pkt[:, 0:128], xc[:, h, :], wk[:, h, :], True, True, [wk, xc], [pkt])
                            mm(pm[:, 0:128], sel, g["nM"][:, csl], True, True, [ident, g["nM"]], [pm])
                            mm(pw[:, 0:128], sel, g["wi"][:, csl], True, True, [ident, g["wi"]], [pw])
                            act(qT[:], pq[:, 0:128], AF.Copy, [pq], [qT])
                            act(kT[:], pk[:, 0:128], AF.Copy, [pk], [kT])
                            ts(kt[:], pkt[:, 0:128], col[:, 4 + h:5 + h], None, ALU.mult, None, [pkt, col], [kt])
                            ts(ET[:], pm[:, 0:128], col[:, h:h + 1], 0.0, ALU.add, ALU.min, [pm, col], [ET])
                            act(ET[:], ET[:], AF.Exp, [ET], [ET])
                            tt(ET[:], ET[:], mask[:], ALU.mult, [ET, mask], [ET])
                            mm(pS[:, 0:128], kT[:], qT[:], True, True, [kT, qT], [pS])
                            tt(ST[:], pS[:, 0:128], ET[:], ALU.mult, [pS, ET], [ST])
                            act(wb[:], pw[:, 0:128], AF.Copy, [pw], [wb])
                            tt(qt[:], pq[:, 0:128], wb[:], ALU.mult, [pq, wb], [qt])
                            mm(pn[:, 0:130], ST[:], va[:, h, :], True, False, [ST, va], [pn])
                            mm(pn[:, 0:130], qt[:], Cb[h][:], False, True, [qt, Cb[h]], [pn])
                            mm(pkt[:, 0:130], kt[:], va[:, h, :], True, True, [kt, va], [pkt])
                            dcol = slice(127, 128) if d == 0 else slice(0, 1)
                            stt(C32[h][:], C32[h][:], wb[:, dcol], pkt[:, 0:130], ALU.mult, ALU.add, [C32[h], wb, pkt], [C32[h]])
                            act(Cb[h][:], C32[h][:], AF.Copy, [C32[h]], [Cb[h]])
                            dd = dn[k2]
                            act(dd[:, 0:1], pn[:, 128:129], AF.Abs, [pn], [dd])
                            ts(dd[:, 0:1], dd[:, 0:1], col[:, 8 + h:9 + h], None, ALU.max, None, [dd, col], [dd])
                            P.op("dve", lambda e: e.reciprocal(out=dd[:, 1:2], in_=dd[:, 0:1]), [dd], [dd])
                            hsl = slice(h * 128, (h + 1) * 128)
                            if d == 0:
                                ts(hs[:, hsl], pn[:, 0:128], dd[:, 1:2], None, ALU.mult, None, [pn, dd], [hs])
                            else:
                                stt(hs[:, hsl], pn[:, 0:128], dd[:, 1:2], hf[:, hsl], ALU.mult, ALU.add, [pn, dd, hf], [hs])
                        if d == 0:
                            P.dma("sp", HF[tsl, :], hs[:], [hs], [HF.bs[ti]])
                        else:
                            for h in range(4):
                                P.op("dve", lambda e: e.bn_stats(out=bst6[:, h, :], in_=hs[:, h * 128:(h + 1) * 128]), [hs], [bst6])
                                P.op("dve", lambda e: e.bn_aggr(out=mv[:, h, :], in_=bst6[:, h, :]), [bst6], [mv])
                            ts(rsd[:], mv[:, :, 1], LN_EPS, None, ALU.add, None, [mv], [rsd])
                            act(rsd[:], rsd[:], AF.Ln, [rsd], [rsd])
                            act(rsd[:], rsd[:], AF.Exp, [rsd], [rsd], scale=-0.5)
                            hb = hbn[it % 2]
                            for h in range(4):
                                hsl = slice(h * 128, (h + 1) * 128)
                                ts(hs[:, hsl], hs[:, hsl], mv[:, h, 0:1], rsd[:, h:h + 1], ALU.subtract, ALU.mult, [hs, mv, rsd], [hs])
                            tt(hb[:], hs[:], om_[:], ALU.mult, [hs, om_], [hb])
                            ht = hbt[it % 2]
                            pt_ = ps[7]
                            for h in range(4):
                                mm(pt_[:, h * 128:(h + 1) * 128], hb[:, h * 128:(h + 1) * 128], identb[:], True, True, [hb, identb], [pt_])
                            act(ht[:], pt_[:, :].rearrange("p (k t) -> p k t", k=4), AF.Copy, [pt_], [ht])
                            P.dma("sp", HBT[:, tsl].rearrange("(k p) t -> p k t", p=128), ht[:], [ht], [HBT.bs[ti]])
                    P.barrier()


    def gen_att(l, st):
        SCALE = 96.0 ** -0.5
        kT = sbt(st, "xk", [98, T], BF16)
        vh = sbt(st, "xv", [128, NCH, 128], BF16)
        qTs = [sbt(st, "xq%d" % i, [98, TT], BF16) for i in range(2)]
        pts = [sbt(st, "xp%d" % i, [128, TT], BF16) for i in range(3)]
        rlt = sbt(st, "xrl", [128, TT])
        osb = [sbt(st, "xo%d" % i, [64, TT], BF16) for i in range(2)]
        sbank = [ps[0], ps[1], ps[2]]
        abank = [ps[3], ps[4]]
        n = 0
        nq = 0
        for h in range(8):
            P.dma("sp", kT[:], KT[h, :, :], KT.bs, [kT])
            P.dma("pool", vh[:], VA[:, h, :].rearrange("(n p) d -> p n d", p=128), VA.bs, [vh])
            for i in range(NT):
                qs = slice(i * TT, (i + 1) * TT)
                qT = qTs[nq % 2]
                acc = abank[nq % 2]
                nq += 1
                P.dma("sp", qT[:], QT[h, :, qs], [QT.bs[i]], [qT])
                segq = (i * TT) // SL

                def score(kb):
                    R = 97 if (kb * 128) // SL == segq else 98
                    pb = sbank[(n + kb) % 3]
                    mm(pb[:, :], kT[0:R, kb * 128:(kb + 1) * 128], qT[0:R, :], True, True, [kT, qT], [pb])
                    return pb
                pend = [score(0)]
                if NCH > 1:
                    pend.append(score(1))
                for kb in range(NCH):
                    pb = pend.pop(0)
                    pt = pts[(n + kb) % 3]
                    act(pt[:], pb[:, :], AF.Exp, [pb], [pt], scale=SCALE)
                    if kb + 2 < NCH:
                        pend.append(score(kb + 2))
                    mm(acc[:, :], vh[:, kb, :], pt[:], kb == 0, kb == NCH - 1, [vh, pt], [acc])
                    yield
                n += NCH
                P.op("dve", lambda e: e.reciprocal(out=rlt[64:128, :], in_=acc[64:128, :]), [acc], [rlt])
                o = osb[i % 2]
                tt(o[:], acc[0:64, :], rlt[64:128, :], ALU.mult, [acc, rlt], [o])
                P.dma("pool", OT[h * 64:(h + 1) * 64, qs], o[:], [o], [OT.bs[i]])

    import os as _os2
    PENG = _os2.environ.get('PENG', 'dve')

    def gen_s5(l, st):
        pending = []
        rnd = [0]

        def later(k, fn):
            pending.append((rnd[0] + k, fn))

        def tick():
            rnd[0] += 1
            due = [p for p in pending if p[0] <= rnd[0]]
            for p in due:
                pending.remove(p)
            for p in due:
                p[1]()

        CLr = sbt(st, "sCLr", [128, 16, 128], BF16)
        CLi = sbt(st, "sCLi", [128, 16, 128], BF16)
        wglu = sbt(st, "swglu", [128, 4, 2 * D], BF16)
        dsk = sbt(st, "sdsk", [128, 4])
        pvr, pvi, psy = ps[5], ps[6], ps[7]
        NW = 2
        wk = [[sbt(st, "sw%s%d" % (n, i), [128, TT]) for i in range(NW)] for n in "abcdef"]

        class Al:
            def __init__(self, par, view):
                self.t = view
                self.b = par.b

            def __getitem__(self, k):
                return self.t[k]
        v3 = lambda tl: tl[:, 0:256].rearrange("p (a b) -> p a b", a=16)
        Zt = Al(wk[0][0], wk[0][0][:, :].rearrange("p (a b) -> p a b", a=4))
        Bt = [Al(wk[1][i], v3(wk[1][i])) for i in range(2)]
        Bp = [Al(wk[2][i], v3(wk[2][i])) for i in range(2)]
        tmp = Al(wk[3][0], v3(wk[3][0]))
        X = Al(wk[4][0], wk[4][0][:, :].rearrange("p (a b c) -> p a b c", a=16, b=2))
        for k in range(4):
            load_cast(wglu[:, k, :], W['w_glu'][l, k * 128:(k + 1) * 128, :], wglu, 2 * D)
            yield 2.0
        P.dma("sp", dsk[:, :], W['s5_d'][l].rearrange("(c g) w -> (g w) c", c=4), [], [dsk])
        for i, (nm, CL) in enumerate((('s5_c_re', CLr), ('s5_c_im', CLi))):
            P.op("dve", lambda e: e.memset(Zt[:], 0.0), [], [Zt])
            P.op("dve", lambda e: e.memset(CL[:], 0.0), [], [CL])
            for ch in range(4):
                for jj in range(4):
                    for two in range(2):
                        g = 2 * (4 * ch + jj) + two
                        P.dma("sp" if two else "pool", Zt[32 * jj + 16 * two:32 * jj + 16 * two + 16, ch, 64 * two:64 * two + 64],
                              W[nm][l, g], [], [Zt])
            yield 2.0
            for ch in range(4):
                mm(psy[:, 0:128], Zt[:, ch, :], ident[:], True, True, [Zt, ident], [psy])
                for jj in range(4):
                    P.op("dve", lambda e: e.tensor_scalar(out=CL[:, 4 * ch + jj, 32 * jj:32 * jj + 32], in0=psy[:, 32 * jj:32 * jj + 32],
                                                          scalar1=(1.0 if i == 0 else -1.0), scalar2=None, op0=ALU.mult), [psy], [CL])
                yield 2.0
        sm = {n: sbt(st, "s5" + n, [128, 16]) for n in
              ("are", "aim", "dt", "rmag", "th", "cs", "sn", "q", "abr", "abi", "nr", "ni", "den", "bsr", "bsi", "t")}
        A = lambda n: sm[n][:, :]
        BLr = sbt(st, "sBLr", [128, 16, 128], BF16)
        BLi = sbt(st, "sBLi", [128, 16, 128], BF16)
        cosT = sbt(st, "scosT", [128, 16, TT])
        sinT = sbt(st, "ssinT", [128, 16, TT])
        ti32 = sbt(st, "sti", [128, TT], I32)
        tau = sbt(st, "stau", [128, TT])
        ang = sbt(st, "sang", [128, TT])
        qq = sbt(st, "sqq", [128, TT])
        ub = sbt(st, "sub", [128, 4, TT], BF16)
        xb_ = [[sbt(st, "sx%s%d" % (n, i), [128, TT], BF16) for i in range(4)] for n in "ri"]
        tn = [sbt(st, "stn%d" % i, [128, 4]) for i in range(2)]
        car = [sbt(st, "scar%d" % i, [128, 16]) for i in range(2)]
        y1t = [sbt(st, "sy1%d" % i, [128, TT]) for i in range(2)]
        yg = sbt(st, "syg", [128, 4, TT], BF16)
        sg = y1t
        for d in (0, 1):
            for two in range(2):
                prt = slice(two * 64, two * 64 + 64)
                P.dma("sp", sm["are"][prt, :], W['s5_a_re'][l, d].rearrange("(j two) p -> two p j", two=2)[two], [], [sm["are"]])
                P.dma("sp", sm["aim"][prt, :], W['s5_a_im'][l, d].rearrange("(j two) p -> two p j", two=2)[two], [], [sm["aim"]])
                P.dma("sp", sm["dt"][prt, :], W['s5_log_dt'][l, d].rearrange("(j two) -> two j", two=2)[two].partition_broadcast(64),
                      [], [sm["dt"]])
            if True:
                for i, nm in enumerate(('s5_b_re', 's5_b_im')):
                    for two in range(2):
                        P.dma("pool", Bt[i][two * 64:two * 64 + 64, :, :],
                              W[nm][l].rearrange("(j two) p c -> two p j c", two=2)[two], [], [Bt[i]])
            yield 2.0
            act(A("dt"), A("dt"), AF.Exp, [sm["dt"]], [sm["dt"]])
            tt(A("rmag"), A("are"), A("dt"), ALU.mult, [sm["are"], sm["dt"]], [sm["rmag"]])
            act(A("rmag"), A("rmag"), AF.Exp, [sm["rmag"]], [sm["rmag"]])
            tt(A("th"), A("aim"), A("dt"), ALU.mult, [sm["aim"], sm["dt"]], [sm["th"]])
            yield 2.0
            emit_sin(sm["sn"], A("sn"), sm["th"], A("th"), sm["q"], A("q"), 0.0)
            yield 2.0
            emit_sin(sm["cs"], A("cs"), sm["th"], A("th"), sm["q"], A("q"), math.pi / 2.0)
            yield 2.0
            tt(A("abr"), A("rmag"), A("cs"), ALU.mult, [sm["rmag"], sm["cs"]], [sm["abr"]])
            tt(A("abi"), A("rmag"), A("sn"), ALU.mult, [sm["rmag"], sm["sn"]], [sm["abi"]])
            ts(A("abr"), A("abr"), -1.0, None, ALU.add, None, [sm["abr"]], [sm["abr"]])
            tt(A("nr"), A("abr"), A("are"), ALU.mult, [sm["abr"], sm["are"]], [sm["nr"]])
            tt(A("t"), A("abi"), A("aim"), ALU.mult, [sm["abi"], sm["aim"]], [sm["t"]])
            tt(A("nr"), A("nr"), A("t"), ALU.add, [sm["nr"], sm["t"]], [sm["nr"]])
            tt(A("ni"), A("abi"), A("are"), ALU.mult, [sm["abi"], sm["are"]], [sm["ni"]])
            tt(A("t"), A("abr"), A("aim"), ALU.mult, [sm["abr"], sm["aim"]], [sm["t"]])
            tt(A("ni"), A("ni"), A("t"), ALU.subtract, [sm["ni"], sm["t"]], [sm["ni"]])
            yield 2.0
            tt(A("den"), A("are"), A("are"), ALU.mult, [sm["are"]], [sm["den"]])
            tt(A("t"), A("aim"), A("aim"), ALU.mult, [sm["aim"]], [sm["t"]])
            tt(A("den"), A("den"), A("t"), ALU.add, [sm["den"], sm["t"]], [sm["den"]])
            P.op("dve", lambda e: e.reciprocal(out=A("den"), in_=A("den")), [sm["den"]], [sm["den"]])
            tt(A("bsr"), A("nr"), A("den"), ALU.mult, [sm["nr"], sm["den"]], [sm["bsr"]])
            tt(A("bsi"), A("ni"), A("den"), ALU.mult, [sm["ni"], sm["den"]], [sm["bsi"]])
            yield 2.0
            bc = lambda n: sm[n][:, :].unsqueeze(2).to_broadcast([128, 16, 16])
            tt(Bp[0][:], Bt[0][:], bc("bsr"), ALU.mult, [Bt[0], sm["bsr"]], [Bp[0]])
            tt(tmp[:], Bt[1][:], bc("bsi"), ALU.mult, [Bt[1], sm["bsi"]], [tmp])
            tt(Bp[0][:], Bp[0][:], tmp[:], ALU.subtract, [Bp[0], tmp], [Bp[0]])
            tt(Bp[1][:], Bt[1][:], bc("bsr"), ALU.mult, [Bt[1], sm["bsr"]], [Bp[1]])
            tt(tmp[:], Bt[0][:], bc("bsi"), ALU.mult, [Bt[0], sm["bsi"]], [tmp])
            tt(Bp[1][:], Bp[1][:], tmp[:], ALU.add, [Bp[1], tmp], [Bp[1]])
            yield 2.0
            for i, BL in enumerate((BLr, BLi)):
                P.op("dve", lambda e: e.memset(BL[:], 0.0), [], [BL])
                P.op("dve", lambda e: e.memset(X[:], 0.0), [], [X])
                P.op("dve", lambda e: e.tensor_copy(out=X[0:64, :, 0, :], in_=Bp[i][0:64, :, :]), [Bp[i]], [X])
                P.op("dve", lambda e: e.tensor_copy(out=X[64:128, :, 1, :], in_=Bp[i][64:128, :, :]), [Bp[i]], [X])
                yield 2.0
                for ch in range(4):
                    mm(psy[:, 0:128], X[:, 4 * ch:4 * ch + 4, :, :].rearrange("p a b c -> p (a b c)"), ident[:], True, True,
                       [X, ident], [psy])
                    yield 2.0
                    for jj in range(4):
                        P.op("dve", lambda e: e.tensor_copy(out=BL[32 * jj:32 * jj + 32, 4 * ch + jj, :], in_=psy[32 * jj:32 * jj + 32, 0:128]),
                             [psy], [BL])
            if d == 0:
                P.op("pool", lambda e: e.iota(ti32[:], pattern=[[1, TT]], base=1, channel_multiplier=0), [], [ti32])
            else:
                P.op("pool", lambda e: e.iota(ti32[:], pattern=[[-1, TT]], base=TT, channel_multiplier=0), [], [ti32])
            P.op("dve", lambda e: e.tensor_copy(out=tau[:], in_=ti32[:]), [ti32], [tau])
            for j in range(16):
                ts(ang[:], tau[:], sm["th"][:, j:j + 1], None, ALU.mult, None, [tau, sm["th"]], [ang])
                emit_sin(sinT, sinT[:, j, :], ang, ang[:], qq, qq[:], 0.0)
                yield 5.0
                emit_sin(cosT, cosT[:, j, :], ang, ang[:], qq, qq[:], math.pi / 2.0)
                yield 5.0
            P.op("dve", lambda e: e.memset(car[0][:], 0.0), [], [car[0]])
            P.op("dve", lambda e: e.memset(car[1][:], 0.0), [], [car[1]])
            order = list(range(NT)) if d == 0 else list(range(NT - 1, -1, -1))
            nblk = 0
            lastc = TT - 1 if d == 0 else 0
            for it, ti in enumerate(order):
                t0 = ti * TT
                tsl = slice(t0, t0 + TT)
                P.dma("sp", ub[:], USB[:, tsl].rearrange("(c p) t -> p c t", p=128), [USB.bs[ti]], [ub])
                if it > 0:
                    bnd = t0 if d == 0 else t0 + TT
                    if bnd % SL == 0:
                        for cc in car:
                            ts(cc[:], cc[:], linkc[:, 0:1], None, ALU.mult, None, [cc, linkc], [cc])
                for ch in range(4):
                    for jj in range(4):
                        j = 4 * ch + jj
                        w = [wk[i][nblk % NW] for i in range(6)]
                        xr_b, xi_b = xb_[0][nblk % 4], xb_[1][nblk % 4]
                        tnn = tn[nblk % 2]
                        nblk += 1
                        mm(pvr[:, :], BLr[:, j, :], ub[:, ch, :], True, True, [BLr, ub], [pvr])
                        mm(pvi[:, :], BLi[:, j, :], ub[:, ch, :], True, True, [BLi, ub], [pvi])
                        c_, s_ = cosT[:, j, :], sinT[:, j, :]
                        cl_, sl_ = cosT[:, j, lastc:lastc + 1], sinT[:, j, lastc:lastc + 1]
                        tt(w[0][:], pvr[:, :], c_, ALU.mult, [pvr, cosT], [w[0]])
                        tt(w[1][:], pvi[:, :], s_, ALU.mult, [pvi, sinT], [w[1]])
                        tt(w[2][:], pvi[:, :], c_, ALU.mult, [pvi, cosT], [w[2]])
                        tt(w[3][:], pvr[:, :], s_, ALU.mult, [pvr, sinT], [w[3]])
                        tt(w[0][:], w[0][:], w[1][:], ALU.add, [w[0], w[1]], [w[0]])
                        tt(w[2][:], w[2][:], w[3][:], ALU.subtract, [w[2], w[3]], [w[2]])
                        rb = sm["rmag"][:, j:j + 1].to_broadcast([128, TT])
                        for (src, dst, cc) in ((w[0], w[4], car[0]), (w[2], w[5], car[1])):
                            if d == 0:
                                P.op("dve", lambda e: e.tensor_tensor_scan(out=dst[:], data0=rb, data1=src[:],
                                                                           initial=cc[:, j:j + 1], op0=ALU.mult, op1=ALU.add),
                                     [src, cc, sm["rmag"]], [dst])
                            else:
                                P.op("dve", lambda e: e.tensor_tensor_scan(out=dst[:, ::-1], data0=rb, data1=src[:, ::-1],
                                                                           initial=cc[:, j:j + 1], op0=ALU.mult, op1=ALU.add),
                                     [src, cc, sm["rmag"]], [dst])
                        tt(w[0][:], w[4][:], c_, ALU.mult, [w[4], cosT], [w[0]], eng=PENG)
                        tt(w[1][:], w[4][:], s_, ALU.mult, [w[4], sinT], [w[1]], eng=PENG)
                        tt(w[2][:], w[5][:], s_, ALU.mult, [w[5], sinT], [w[2]], eng=PENG)
                        tt(w[3][:], w[5][:], c_, ALU.mult, [w[5], cosT], [w[3]], eng=PENG)
                        tt(xr_b[:], w[0][:], w[2][:], ALU.subtract, [w[0], w[2]], [xr_b])
                        tt(xi_b[:], w[1][:], w[3][:], ALU.add, [w[1], w[3]], [xi_b])
                        lc = slice(lastc, lastc + 1)
                        tt(car[0][:, j:j + 1], w[0][:, lc], w[2][:, lc], ALU.subtract, [w[0], w[2]], [car[0]])
                        tt(car[1][:, j:j + 1], w[1][:, lc], w[3][:, lc], ALU.add, [w[1], w[3]], [car[1]])

                        def cmm(j=j, jj=jj, xr_b=xr_b, xi_b=xi_b):
                            mm(psy[:, :], CLr[:, j, :], xr_b[:], jj == 0, False, [CLr, xr_b], [psy])
                            mm(psy[:, :], CLi[:, j, :], xi_b[:], False, jj == 3, [CLi, xi_b], [psy])
                        later(3, cmm)
                        if jj == 3:
                            y1 = y1t[ch % 2]
                            if d == 0:
                                P.dma("pool", y1[:], US[ch * 128:(ch + 1) * 128, tsl], [US.bs[ti]], [y1])

                                def fin(ch=ch, y1=y1, tsl=tsl, ti=ti):
                                    stt(y1[:], y1[:], dsk[:, ch:ch + 1], psy[:, :], ALU.mult, ALU.add, [y1, dsk, psy], [y1])
                                    P.dma("pool", Y1[ch * 128:(ch + 1) * 128, tsl], y1[:], [y1], [Y1.bs[ti]])
                                later(4, fin)
                            else:
                                P.dma("pool", y1[:], Y1[ch * 128:(ch + 1) * 128, tsl], [Y1.bs[ti]], [y1])

                                def fin(ch=ch, y1=y1):
                                    tt(y1[:], y1[:], psy[:, :], ALU.add, [y1, psy], [y1])
                                later(4, fin)

                                def fin2(ch=ch, y1=y1):
                                    act(yg[:, ch, :], y1[:], AF.Gelu_apprx_tanh, [y1], [yg])
                                later(5, fin2)
                        yield 11.0
                        tick()
                if d == 1:
                    for _ in range(6):
                        yield 0.3
                        tick()
                    for o in range(8):
                        for k in range(4):
                            mm(pvr[:, :], wglu[:, k, o * 128:(o + 1) * 128], yg[:, k, :], k == 0, k == 3, [wglu, yg], [pvr])
                        for k in range(4):
                            mm(pvi[:, :], wglu[:, k, D + o * 128:D + (o + 1) * 128], yg[:, k, :], k == 0, k == 3, [wglu, yg], [pvi])
                        s1, s2 = sg[0], sg[1]
                        yield 0.8
                        tick()
                        act(s1[:], pvi[:, :], AF.Sigmoid, [pvi], [s1])
                        yield 0.8
                        tick()
                        tt(s2[:], s1[:], pvr[:, :], ALU.mult, [s1, pvr], [s2])
                        P.dma("sp" if o % 2 else "pool", YC[o * 128:(o + 1) * 128, tsl], s2[:], [s2], [YC.bs[ti]])
                else:
                    for _ in range(6):
                        yield 0.3
                        tick()
            for _ in range(6):
                yield 0.3
                tick()

    def phase_att_s5(l):
        with ExitStack() as st:
            ga = gen_att(l, st)
            gs = gen_s5(l, st)
            n_att = 8 * NT * NCH
            total_cost = 2 * NT * 16 * 11.0 + NT * 16 * 0.8 + 2 * NT * 4 * 0.3 + 2 * (32 * 5.0 + 40 * 2.0) + 30.0
            per_us = n_att / total_cost
            acc_ = 0.0
            a_done = s_done = False
            while not (a_done and s_done):
                if not s_done:
                    try:
                        c = next(gs)
                        acc_ += (c if c else 1.0) * per_us
                    except StopIteration:
                        s_done = True
                if s_done:
                    acc_ += 64
                while acc_ >= 1.0 and not a_done:
                    acc_ -= 1.0
                    try:
                        next(ga)
                    except StopIteration:
                        a_done = True
                if a_done:
                    acc_ = 0.0
            P.barrier()

    with nc.allow_non_contiguous_dma(reason="small strided parameter / layout-conversion DMAs"):
        phase0()
        if branches:
            phase_init()
        for l in range(DEPTH):
            if branches:
                phase_a(l)
            if stop_after == 'a':
                break
            if "b" in branches:
                phase_m(l)
            if "a" in branches and "c" in branches:
                phase_att_s5(l)
            else:
                if "a" in branches:
                    phase_att(l)
                if "c" in branches:
                    phase_s5(l)
            phase_mrg(l)
            phase_f1(l)
            phase_f2(l)
        phase_out()
    P.close()
    return nc, P


NCORES = 8
T_CORE = 8192
SEGLEN = 2048


def kernel(**inputs):
    xp = np.ascontiguousarray(inputs['x_prompt'], dtype=np.float32)
    xs = np.ascontiguousarray(inputs['x_sample'], dtype=np.float32)
    nc, _ = build(T_CORE, SEGLEN, 4)
    wts = {n: np.ascontiguousarray(inputs[n], dtype=np.float32) for n in WNAMES}
    in_maps = []
    for c in range(NCORES):
        if c < 4:
            x = xp[4 * c:4 * c + 4].reshape(T_CORE, D)
            link = np.zeros((1, 1), np.float32)
        else:
            x = xs[c - 4].reshape(T_CORE, D)
            link = np.ones((1, 1), np.float32)
        m = {"x": x, "link": link}
        m.update(wts)
        in_maps.append(m)
    res = run_bass_kernel_spmd(nc, in_maps, core_ids=list(range(NCORES)))
    ys = [np.asarray(r["y"], dtype=np.float32) for r in res.results]
    y_prompt = np.concatenate([ys[c].reshape(4, SEGLEN, D) for c in range(4)], axis=0)
    y_sample = np.stack([ys[c].reshape(T_CORE, D) for c in range(4, 8)], axis=0)
    return (y_prompt, y_sample)
```

```python
import math
import numpy as np
import concourse.bass as bass
import concourse.mybir as mybir
from concourse.bass_utils import run_bass_kernel_spmd
from contextlib import ExitStack

F32 = mybir.dt.float32
BF16 = mybir.dt.bfloat16
AF = mybir.ActivationFunctionType
ALU = mybir.AluOpType
AX = mybir.AxisListType

D = 1024
KD = 8
H_A = 8
NIN = 5552
DFF = 2816
NFC = 22
ALPHA = 8 ** 0.25
LN_EPS = 1e-5
TT = 512
O_CQ, O_CKV, O_KR, O_XM, O_VM, O_OM, O_GM, O_US, O_GP = 0, 256, 384, 416, 928, 1440, 1952, 1968, 2480
WNAMES = ['ln0_g', 'ln0_b', 'w_in', 'b_mlstm_gate', 'b_merge', 'q_norm_g', 'kv_norm_g', 'w_uq', 'w_ukv', 'w_proj_a',
          'conv_m_w', 'conv_m_b', 'w_q_m', 'w_k_m', 'mh_norm_g', 'w_proj_b', 's5_a_re', 's5_a_im', 's5_log_dt',
          's5_b_re', 's5_b_im', 's5_c_re', 's5_c_im', 's5_d', 'w_glu', 'w_o', 'ln1_g', 'ln1_b', 'w_up', 'conv_f_w',
          'conv_f_b', 'w_down', 'ln2_g', 'ln2_b']


class Buf:
    __slots__ = ("name", "lw", "rd")

    def __init__(self, name=""):
        self.name = name
        self.lw = None
        self.rd = {}


class Tl:
    def __init__(self, t, name=""):
        self.t = t
        self.b = Buf(name)

    def __getitem__(self, k):
        return self.t[k]


class DT:
    def __init__(self, ap, name, ntile):
        self.ap = ap
        self.bs = [Buf(name + str(i)) for i in range(ntile)]

    def __getitem__(self, k):
        return self.ap[k]


def _bl(xs):
    out = []
    for x in xs:
        if isinstance(x, Buf):
            out.append(x)
        elif isinstance(x, (list, tuple)):
            out.extend(_bl(x))
        else:
            out.append(x.b)
    return out


class Prog:
    COMPUTE = ("pe", "act", "dve", "pool")

    def __init__(self, nc, ndma=12):
        self.nc = nc
        self.es = ExitStack()
        self.eng = {"pe": nc.tensor, "act": nc.scalar, "dve": nc.vector, "pool": nc.gpsimd, "sp": nc.sync}
        self.sem = {}
        for e in self.COMPUTE:
            self.sem[e] = self.es.enter_context(nc.semaphore("s_" + e))
        self.cnt = {e: 0 for e in self.COMPUTE}
        self.dq = {}
        for q in ("sp", "pool"):
            sems = [self.es.enter_context(nc.semaphore("d_%s%d" % (q, i))) for i in range(ndma)]
            self.dq[q] = {"sems": sems, "tgt": [0] * ndma, "i": 0}
        self.waited = {e: {} for e in self.eng}
        self.semobj = {}
        self.ninstr = 0

    def _semkey(self, s):
        k = id(s)
        self.semobj[k] = s
        return k

    def _need(self, e, tok, deps):
        if tok is None:
            return
        k, v = tok
        if self.waited[e].get(k, 0) >= v:
            return
        if deps.get(k, 0) < v:
            deps[k] = v

    def _collect(self, e, reads, writes, is_dma=False):
        deps = {}
        own = self._semkey(self.sem[e]) if (e in self.COMPUTE and not is_dma) else None
        for b in reads:
            self._need(e, b.lw, deps)
        for b in writes:
            if b.lw is not None and b.lw[0] != own:
                self._need(e, b.lw, deps)
            for key, tok in b.rd.items():
                if tok[0] != own:
                    self._need(e, tok, deps)
        if e == "pe" and own in deps:
            del deps[own]
        for k, v in deps.items():
            self.eng[e].wait_ge(self.semobj[k], v)
            self.waited[e][k] = v
            self.ninstr += 1

    def _update(self, tok, reads, writes, rkey):
        for b in writes:
            b.lw = tok
            b.rd = {}
        for b in reads:
            b.rd[rkey] = tok

    def op(self, e, fn, reads=(), writes=()):
        reads = _bl(reads)
        writes = _bl(writes)
        self._collect(e, reads, writes)
        ins = fn(self.eng[e])
        self.cnt[e] += 1
        ins.then_inc(self.sem[e], 1)
        tok = (self._semkey(self.sem[e]), self.cnt[e])
        self._update(tok, reads, writes, e)
        self.ninstr += 1
        return tok

    def dma(self, q, out, in_, reads=(), writes=(), **kw):
        reads = _bl(reads)
        writes = _bl(writes)
        q = "pool" if type(out.tensor).__name__.startswith("DRam") else "sp"
        d = self.dq[q]
        i = d["i"]
        d["i"] = (i + 1) % len(d["sems"])
        s = d["sems"][i]
        k = self._semkey(s)
        if d["tgt"][i] > 0 and self.waited[q].get(k, 0) < d["tgt"][i]:
            self.eng[q].wait_ge(s, d["tgt"][i])
            self.waited[q][k] = d["tgt"][i]
        self._collect(q, reads, writes, is_dma=True)
        ins = self.eng[q].dma_start(out=out, in_=in_, **kw)
        d["tgt"][i] += 16
        ins.then_inc(s, 16)
        tok = (k, d["tgt"][i])
        self._update(tok, reads, writes, ("dma", k))
        self.ninstr += 1
        return tok

    def barrier(self):
        toks = [(self._semkey(self.sem[e]), self.cnt[e]) for e in self.COMPUTE if self.cnt[e] > 0]
        for q, d in self.dq.items():
            for s, t in zip(d["sems"], d["tgt"]):
                if t > 0:
                    toks.append((self._semkey(s), t))
        for e in self.eng:
            for k, v in toks:
                if e in self.COMPUTE and k == self._semkey(self.sem[e]):
                    continue
                if self.waited[e].get(k, 0) < v:
                    self.eng[e].wait_ge(self.semobj[k], v)
                    self.waited[e][k] = v
                    self.ninstr += 1

    def close(self):
        self.barrier()
        self.es.close()


def wshapes(L):
    return {
        'ln0_g': (D,), 'ln0_b': (D,), 'w_in': (L, D, NIN), 'b_mlstm_gate': (L, 2, 2, 4), 'b_merge': (L, 3, D),
        'q_norm_g': (L, 256), 'kv_norm_g': (L, 128), 'w_uq': (L, 256, 768), 'w_ukv': (L, 128, 1024),
        'w_proj_a': (L, 512, D), 'conv_m_w': (L, 3, 512), 'conv_m_b': (L, 512), 'w_q_m': (L, 4, 128, 128),
        'w_k_m': (L, 4, 128, 128), 'mh_norm_g': (L, 512), 'w_proj_b': (L, 512, D), 's5_a_re': (L, 2, 32, 64),
        's5_a_im': (L, 2, 32, 64), 's5_log_dt': (L, 2, 32), 's5_b_re': (L, 32, 64, 16), 's5_b_im': (L, 32, 64, 16),
        's5_c_re': (L, 32, 16, 64), 's5_c_im': (L, 32, 16, 64), 's5_d': (L, 32, 16), 'w_glu': (L, 512, 2 * D),
        'w_o': (L, D, D), 'ln1_g': (L, D), 'ln1_b': (L, D), 'w_up': (L, D, 2 * DFF), 'conv_f_w': (L, 3, DFF),
        'conv_f_b': (L, DFF), 'w_down': (L, DFF, D), 'ln2_g': (L, D), 'ln2_b': (L, D)}


def build(T, SL, DEPTH, dbg=(), branches=("a", "b", "c"), stop_after=None):
    NSEG = T // SL
    NT = T // TT
    NCH = T // 128
    nc = bass.Bass("TRN2", target_bir_lowering=False)
    x_in = nc.dram_tensor("x", [T, D], F32, kind="ExternalInput").ap()
    link_in = nc.dram_tensor("link", [1, 1], F32, kind="ExternalInput").ap()
    W = {n: nc.dram_tensor(n, list(s), F32, kind="ExternalInput").ap() for n, s in wshapes(DEPTH).items()}
    y_out = nc.dram_tensor("y", [T, D], F32, kind="ExternalOutput").ap()
    P = Prog(nc)
    es = P.es
    NSL = "allow_slow_non_contiguous"

    def scratch(name, shape, dt):
        kind = "ExternalOutput" if name in dbg else "Internal"
        return DT(nc.dram_tensor(name, list(shape), dt, kind=kind).ap(), name, NT)

    XT = scratch("XT", [D, T], F32)
    XTB = scratch("XTB", [D, T], BF16)
    AT = scratch("AT", [DFF, T], F32)
    OT = scratch("OT", [512, T], BF16)
    HBT = scratch("HBT", [512, T], BF16)
    YC = scratch("YC", [D, T], F32)
    QT = scratch("QT", [8, 98, T], BF16)
    KT = scratch("KT", [8, 98, T], BF16)
    VA = scratch("VA", [T, 8, 128], BF16)
    XM = scratch("XM", [512, T], F32)
    XCB = scratch("XCB", [512, T], BF16)
    VM = scratch("VM", [T, 512], BF16)
    OMS = scratch("OMS", [T, 512], F32)
    GM = scratch("GM", [16, T], F32)
    US = scratch("US", [512, T], F32)
    USB = scratch("USB", [512, T], BF16)
    HF = scratch("HF", [T, 512], F32)
    Y1 = scratch("Y1", [512, T], F32)
    RC = scratch("RC", [32, T], F32)
    RS = scratch("RS", [32, T], F32)

    uid = [0]

    def sbt(stack, name, shape, dt=F32):
        uid[0] += 1
        name = "%s_%d" % (name, uid[0])
        return Tl(stack.enter_context(nc.sbuf_tensor(name, list(shape), dt)), name)

    ps = [Tl(es.enter_context(nc.psum_tensor("ps%d" % i, [128, 512], F32)), "ps%d" % i) for i in range(8)]
    ident = sbt(es, "ident", [128, 128])
    identb = sbt(es, "identb", [128, 128], BF16)
    ones32 = sbt(es, "ones32", [128, 128])
    linkc = sbt(es, "linkc", [128, 1])
    lng = sbt(es, "lng", [128, 2 * DEPTH + 1, 8])
    lnb = sbt(es, "lnb", [128, 2 * DEPTH + 1, 8])
    stg = [sbt(es, "stg%d" % i, [128, 1024]) for i in range(2)]
    stgi = [0]

    def mm(out, lhsT, rhs, start, stop, reads, writes):
        return P.op("pe", lambda e: e.matmul(out, lhsT=lhsT, rhs=rhs, start=start, stop=stop), reads, writes)

    def act(out, in_, func, reads, writes, **kw):
        return P.op("act", lambda e: e.activation(out=out, in_=in_, func=func, **kw), reads, writes)

    def tt(out, in0, in1, op, reads, writes, eng="dve"):
        return P.op(eng, lambda e: e.tensor_tensor(out=out, in0=in0, in1=in1, op=op), reads, writes)

    def ts(out, in0, s1, s2, op0, op1, reads, writes, eng="dve"):
        if s2 is None:
            return P.op(eng, lambda e: e.tensor_scalar(out=out, in0=in0, scalar1=s1, scalar2=None, op0=op0), reads, writes)
        return P.op(eng, lambda e: e.tensor_scalar(out=out, in0=in0, scalar1=s1, scalar2=s2, op0=op0, op1=op1), reads, writes)

    def stt(out, in0, scalar, in1, op0, op1, reads, writes):
        return P.op("dve", lambda e: e.scalar_tensor_tensor(out=out, in0=in0, scalar=scalar, in1=in1, op0=op0, op1=op1),
                    reads, writes)

    def load_cast(dst, src, dtl, n, scale=1.0, sreads=()):
        o = 0
        while o < n:
            w = min(1024, n - o)
            s = stg[stgi[0] % 2]
            stgi[0] += 1
            np_ = dst.shape[0]
            P.dma("sp", s[0:np_, 0:w], src[:, o:o + w], [], [s])
            act(dst[:, o:o + w], s[0:np_, 0:w], AF.Copy if isinstance(scale, float) else AF.Identity,
                [s] + list(sreads), [dtl], scale=scale)
            o += w

    P.op("pool", lambda e: e.memset(ident[:], 0.0), [], [ident])
    P.op("pool", lambda e: e.affine_select(out=ident[:], in_=ident[:], compare_op=ALU.not_equal, fill=1.0, base=0,
                                           pattern=[[-1, 128]], channel_multiplier=1), [ident], [ident])
    P.op("dve", lambda e: e.tensor_copy(out=identb[:], in_=ident[:]), [ident], [identb])
    P.op("dve", lambda e: e.memset(ones32[:], 1.0), [], [ones32])
    P.dma("sp", linkc[:], link_in.partition_broadcast(128), [], [linkc])
    with nc.allow_non_contiguous_dma(reason="tiny param vectors"):
        P.dma("sp", lng[:, 0, :], W['ln0_g'].rearrange("(c p) -> p c", p=128), [], [lng])
        P.dma("sp", lnb[:, 0, :], W['ln0_b'].rearrange("(c p) -> p c", p=128), [], [lnb])
        for l in range(DEPTH):
            for j, nm in ((1, 'ln1'), (2, 'ln2')):
                P.dma("sp", lng[:, j + 2 * l, :], W[nm + '_g'][l].rearrange("(c p) -> p c", p=128), [], [lng])
                P.dma("sp", lnb[:, j + 2 * l, :], W[nm + '_b'][l].rearrange("(c p) -> p c", p=128), [], [lnb])

    def ln_tile(y, sq, xb, sm, li, ti, pA, pB):
        mean, var = sm
        t0 = ti * TT
        act(sq[:], y[:], AF.Square, [y], [sq])
        for c in range(8):
            mm(pA[:, :], ones32[:], y[:, c, :], c == 0, c == 7, [ones32, y], [pA])
        for c in range(8):
            mm(pB[:, :], ones32[:], sq[:, c, :], c == 0, c == 7, [ones32, sq], [pB])
        P.op("act", lambda e: e.mul(out=mean[:], in_=pA[:, :], mul=1.0 / D), [pA], [mean])
        tt(var[:], mean[:], mean[:], ALU.mult, [mean], [var])
        stt(var[:], pB[:, :], 1.0 / D, var[:], ALU.mult, ALU.subtract, [pB, var], [var])
        ts(var[:], var[:], LN_EPS, None, ALU.add, None, [var], [var])
        act(var[:], var[:], AF.Sqrt, [var], [var])
        P.op("dve", lambda e: e.reciprocal(out=var[:], in_=var[:]), [var], [var])
        bc = lambda a: a[:].unsqueeze(1).to_broadcast([128, 8, TT])
        tt(y[:], y[:], bc(mean), ALU.subtract, [y, mean], [y])
        tt(y[:], y[:], bc(var), ALU.mult, [y, var], [y])
        tt(y[:], y[:], lng[:, li, :].unsqueeze(2).to_broadcast([128, 8, TT]), ALU.mult, [y, lng], [y])
        tt(y[:], y[:], lnb[:, li, :].unsqueeze(2).to_broadcast([128, 8, TT]), ALU.add, [y, lnb], [y])
        act(xb[:], y[:], AF.Copy, [y], [xb])
        P.dma("sp", XT[:, t0:t0 + TT].rearrange("(c p) t -> p c t", p=128), y[:], [y], [XT.bs[ti]])
        P.dma("pool", XTB[:, t0:t0 + TT].rearrange("(c p) t -> p c t", p=128), xb[:], [xb], [XTB.bs[ti]])

    def phase0():
        with ExitStack() as st:
            xin = [sbt(st, "p0x%d" % i, [128, D]) for i in range(2)]
            y = sbt(st, "p0y", [128, 8, TT])
            sq = sbt(st, "p0sq", [128, 8, TT])
            xb = sbt(st, "p0xb", [128, 8, TT], BF16)
            sm = (sbt(st, "p0m", [128, TT]), sbt(st, "p0v", [128, TT]))
            k = 0
            for ti in range(NT):
                for b in range(4):
                    xi = xin[k % 2]
                    k += 1
                    r0 = ti * TT + b * 128
                    P.dma("sp", xi[:], x_in[r0:r0 + 128, :], [], [xi])
                    for c in range(8):
                        mm(ps[c][:, b * 128:(b + 1) * 128], xi[:, c * 128:(c + 1) * 128], ident[:], True, True,
                           [xi, ident], [ps[c]])
                for c in range(8):
                    if c % 2 == 0:
                        P.op("dve", lambda e: e.tensor_copy(out=y[:, c, :], in_=ps[c][:, :]), [ps[c]], [y])
                    else:
                        act(y[:, c, :], ps[c][:, :], AF.Copy, [ps[c]], [y])
                ln_tile(y, sq, xb, sm, 0, ti, ps[0], ps[1])
            P.barrier()

    def phase_out():
        with ExitStack() as st:
            xs = [sbt(st, "pox%d" % i, [128, 8, TT]) for i in range(2)]
            yo = [sbt(st, "poy%d" % i, [128, D]) for i in range(2)]
            k = 0
            for ti in range(NT):
                t0 = ti * TT
                x = xs[ti % 2]
                P.dma("sp", x[:], XT[:, t0:t0 + TT].rearrange("(c p) t -> p c t", p=128), [XT.bs[ti]], [x])
                for b in range(4):
                    o = yo[k % 2]
                    k += 1
                    for c in range(8):
                        pb = ps[(c // 4) + 2 * (b % 2)]
                        mm(pb[:, (c % 4) * 128:(c % 4 + 1) * 128], x[:, c, b * 128:(b + 1) * 128], ident[:], True, True,
                           [x, ident], [pb])
                    for hlf in range(2):
                        pb = ps[hlf + 2 * (b % 2)]
                        if hlf == 0:
                            P.op("dve", lambda e: e.tensor_copy(out=o[:, 0:512], in_=pb[:, :]), [pb], [o])
                        else:
                            act(o[:, 512:1024], pb[:, :], AF.Copy, [pb], [o])
                    r0 = t0 + b * 128
                    P.dma("pool", y_out[r0:r0 + 128, :], o[:], [o], [])
            P.barrier()

    def seg_edge(t):
        if t <= 0 or t >= T:
            return 2
        return 1 if t % SL == 0 else 0

    def load_halo(q, tl, src, r0, nrow, ti, dtbufs):
        t0 = ti * TT
        le, re = seg_edge(t0), seg_edge(t0 + TT)
        lo = t0 - 1 if le != 2 else t0
        hi = t0 + TT + 1 if re != 2 else t0 + TT
        rd = [dtbufs[j] for j in (ti - 1, ti, ti + 1) if 0 <= j < NT]
        P.dma(q, tl[0:nrow, (lo - t0 + 1):(hi - t0 + 1)], src[r0:r0 + nrow, lo:hi], rd, [tl])
        if le == 2:
            P.op("dve", lambda e: e.memset(tl[0:nrow, 0:1], 0.0), [], [tl])
        elif le == 1:
            ts(tl[0:nrow, 0:1], tl[0:nrow, 0:1], linkc[0:nrow, 0:1], None, ALU.mult, None, [tl, linkc], [tl])
        if re == 2:
            P.op("dve", lambda e: e.memset(tl[0:nrow, TT + 1:TT + 2], 0.0), [], [tl])
        elif re == 1:
            ts(tl[0:nrow, TT + 1:TT + 2], tl[0:nrow, TT + 1:TT + 2], linkc[0:nrow, 0:1], None, ALU.mult, None,
               [tl, linkc], [tl])

    def phase_f1(l):
        with ExitStack() as st:
            wa = sbt(st, "f1wa", [128, 8, DFF], BF16)
            for k in range(8):
                load_cast(wa[:, k, :], W['w_up'][l, k * 128:(k + 1) * 128, 0:DFF], wa, DFF)
            xbs = [sbt(st, "f1xb%d" % i, [128, 8, TT], BF16) for i in range(2)]
            ao = [sbt(st, "f1ao%d" % i, [128, TT]) for i in range(4)]
            n = 0
            for ti in range(NT):
                t0 = ti * TT
                xb = xbs[ti % 2]
                P.dma("sp", xb[:], XTB[:, t0:t0 + TT].rearrange("(c p) t -> p c t", p=128), [XTB.bs[ti]], [xb])
                for j in range(NFC):
                    pb = ps[n % 4]
                    a = ao[n % 4]
                    for k in range(8):
                        mm(pb[:, :], wa[:, k, j * 128:(j + 1) * 128], xb[:, k, :], k == 0, k == 7, [wa, xb], [pb])
                    if n % 2 == 0:
                        act(a[:], pb[:, :], AF.Copy, [pb], [a])
                    else:
                        P.op("dve", lambda e: e.tensor_copy(out=a[:], in_=pb[:, :]), [pb], [a])
                    P.dma("pool" if n % 2 else "sp", AT[j * 128:(j + 1) * 128, t0:t0 + TT], a[:], [a], [AT.bs[ti]])
                    n += 1
            P.barrier()

    def phase_f2(l):
        with ExitStack() as st:
            wb = sbt(st, "f2wb", [128, 8, DFF], BF16)
            wd = sbt(st, "f2wd", [128, NFC, D], BF16)
            cw = sbt(st, "f2cw", [128, 3, NFC])
            cb = sbt(st, "f2cb", [128, NFC])
            for k in range(8):
                load_cast(wb[:, k, :], W['w_up'][l, k * 128:(k + 1) * 128, DFF:2 * DFF], wb, DFF)
            for j in range(NFC):
                load_cast(wd[:, j, :], W['w_down'][l, j * 128:(j + 1) * 128, :], wd, D)
            with nc.allow_non_contiguous_dma(reason="tiny param vectors"):
                for w in range(3):
                    P.dma("sp", cw[:, w, :], W['conv_f_w'][l, w].rearrange("(c p) -> p c", p=128), [], [cw])
                P.dma("sp", cb[:, :], W['conv_f_b'][l].rearrange("(c p) -> p c", p=128), [], [cb])
            xb1 = sbt(st, "f2xb", [128, 8, TT], BF16)
            y = sbt(st, "f2y", [128, 8, TT])
            hh = sbt(st, "f2hh", [128, NFC, TT], BF16)
            sq = sbt(st, "f2sq", [128, 8, TT])
            xbo = sbt(st, "f2xbo", [128, 8, TT], BF16)
            sm = (sbt(st, "f2m", [128, TT]), sbt(st, "f2v", [128, TT]))
            ats = [sbt(st, "f2at%d" % i, [128, TT + 2]) for i in range(3)]
            accs = [sbt(st, "f2ac%d" % i, [128, TT]) for i in range(2)]
            n = 0
            for ti in range(NT):
                t0 = ti * TT
                P.dma("sp", xb1[:], XTB[:, t0:t0 + TT].rearrange("(c p) t -> p c t", p=128), [XTB.bs[ti]], [xb1])
                for j in range(NFC):
                    at = ats[n % 3]
                    acc = accs[n % 2]
                    pb = ps[n % 3]
                    n += 1
                    load_halo("sp" if j % 2 else "pool", at, AT, j * 128, 128, ti, AT.bs)
                    for k in range(8):
                        mm(pb[:, :], wb[:, k, j * 128:(j + 1) * 128], xb1[:, k, :], k == 0, k == 7, [wb, xb1], [pb])
                    ts(acc[:], at[:, 0:TT], cw[:, 0, j:j + 1], None, ALU.mult, None, [at, cw], [acc])
                    stt(acc[:], at[:, 1:TT + 1], cw[:, 1, j:j + 1], acc[:], ALU.mult, ALU.add, [at, cw, acc], [acc])
                    stt(acc[:], at[:, 2:TT + 2], cw[:, 2, j:j + 1], acc[:], ALU.mult, ALU.add, [at, cw, acc], [acc])
                    act(acc[:], acc[:], AF.Gelu_apprx_tanh, [acc, cb], [acc], bias=cb[:, j:j + 1])
                    tt(hh[:, j, :], acc[:], pb[:, :], ALU.mult, [acc, pb], [hh])
                P.dma("sp", y[:], XT[:, t0:t0 + TT].rearrange("(c p) t -> p c t", p=128), [XT.bs[ti]], [y])
                for o in range(8):
                    pb = ps[4 + o % 2]
                    for j in range(NFC):
                        mm(pb[:, :], wd[:, j, o * 128:(o + 1) * 128], hh[:, j, :], j == 0, j == NFC - 1, [wd, hh], [pb])
                    stt(y[:, o, :], y[:, o, :], ALPHA, pb[:, :], ALU.mult, ALU.add, [y, pb], [y])
                ln_tile(y, sq, xbo, sm, 2 + 2 * l, ti, ps[6], ps[7])
            P.barrier()

    def phase_mrg(l):
        with ExitStack() as st:
            wg = sbt(st, "mgwg", [128, 8, 3 * D], BF16)
            wo = sbt(st, "mgwo", [128, 8, D], BF16)
            wpa = sbt(st, "mgwpa", [128, 4, D], BF16)
            wpb = sbt(st, "mgwpb", [128, 4, D], BF16)
            bm = sbt(st, "mgbm", [128, 3, 8])
            mhg = sbt(st, "mgmhg", [128, 4])
            with nc.allow_non_contiguous_dma(reason="tiny param vectors"):
                for br in range(3):
                    P.dma("sp", bm[:, br, :], W['b_merge'][l, br].rearrange("(c p) -> p c", p=128), [], [bm])
                P.dma("sp", mhg[:, :], W['mh_norm_g'][l].rearrange("(c p) -> p c", p=128), [], [mhg])
            for k in range(8):
                load_cast(wg[:, k, :], W['w_in'][l, k * 128:(k + 1) * 128, O_GP:NIN], wg, 3 * D)
                load_cast(wo[:, k, :], W['w_o'][l, k * 128:(k + 1) * 128, :], wo, D)
            for k in range(4):
                load_cast(wpa[:, k, :], W['w_proj_a'][l, k * 128:(k + 1) * 128, :], wpa, D)
                load_cast(wpb[:, k, :], W['w_proj_b'][l, k * 128:(k + 1) * 128, :], wpb, D, scale=mhg[:, k:k + 1],
                          sreads=[mhg])
            xb = sbt(st, "mgxb", [128, 8, TT], BF16)
            y = sbt(st, "mgy", [128, 8, TT])
            ot = sbt(st, "mgot", [128, 4, TT], BF16)
            hb = sbt(st, "mghb", [128, 4, TT], BF16)
            yc = sbt(st, "mgyc", [128, 8, TT])
            mg = sbt(st, "mgmg", [128, 8, TT], BF16)
            sq = sbt(st, "mgsq", [128, 8, TT])
            xbo = sbt(st, "mgxbo", [128, 8, TT], BF16)
            sm = (sbt(st, "mgm", [128, TT]), sbt(st, "mgv", [128, TT]))
            g = [sbt(st, "mgg%d" % i, [128, TT]) for i in range(3)]
            t1 = sbt(st, "mgt1", [128, TT])
            t2 = sbt(st, "mgt2", [128, TT])
            fm = lambda a: a.rearrange("(c p) t -> p c t", p=128)
            for ti in range(NT):
                t0 = ti * TT
                P.dma("sp", xb[:], fm(XTB[:, t0:t0 + TT]), [XTB.bs[ti]], [xb])
                if "a" in branches:
                    P.dma("sp", ot[:], fm(OT[:, t0:t0 + TT]), [OT.bs[ti]], [ot])
                if "b" in branches:
                    P.dma("pool", hb[:], fm(HBT[:, t0:t0 + TT]), [HBT.bs[ti]], [hb])
                if "c" in branches:
                    P.dma("sp", yc[:], fm(YC[:, t0:t0 + TT]), [YC.bs[ti]], [yc])
                for c in range(8):
                    cs = slice(c * 128, (c + 1) * 128)
                    terms = []
                    for bi, br in enumerate("abc"):
                        if br not in branches:
                            continue
                        pb = ps[bi]
                        for k in range(8):
                            mm(pb[:, :], wg[:, k, bi * D + c * 128: bi * D + (c + 1) * 128], xb[:, k, :], k == 0, k == 7,
                               [wg, xb], [pb])
                        act(g[bi][:], pb[:, :], AF.Sigmoid, [pb, bm], [g[bi]], bias=bm[:, bi, c:c + 1])
                        if br == "a":
                            for k in range(4):
                                mm(ps[3][:, :], wpa[:, k, cs], ot[:, k, :], k == 0, k == 3, [wpa, ot], [ps[3]])
                            terms.append((g[bi], ps[3], ps[3][:, :]))
                        elif br == "b":
                            for k in range(4):
                                mm(ps[4][:, :], wpb[:, k, cs], hb[:, k, :], k == 0, k == 3, [wpb, hb], [ps[4]])
                            terms.append((g[bi], ps[4], ps[4][:, :]))
                        else:
                            terms.append((g[bi], yc, yc[:, c, :]))
                    if not terms:
                        P.op("dve", lambda e: e.memset(mg[:, c, :], 0.0), [], [mg])
                    for i, (gt, src, sap) in enumerate(terms):
                        last = i == len(terms) - 1
                        if i == 0:
                            tt(mg[:, c, :] if last else t1[:], gt[:], sap, ALU.mult, [gt, src], [mg if last else t1])
                        else:
                            tt(t2[:], gt[:], sap, ALU.mult, [gt, src], [t2])
                            tt(mg[:, c, :] if last else t1[:], t1[:], t2[:], ALU.add, [t1, t2], [mg if last else t1])
                P.dma("sp", y[:], fm(XT[:, t0:t0 + TT]), [XT.bs[ti]], [y])
                for o in range(8):
                    pb = ps[5 + o % 2]
                    for k in range(8):
                        mm(pb[:, :], wo[:, k, o * 128:(o + 1) * 128], mg[:, k, :], k == 0, k == 7, [wo, mg], [pb])
                    stt(y[:, o, :], y[:, o, :], ALPHA, pb[:, :], ALU.mult, ALU.add, [y, pb], [y])
                ln_tile(y, sq, xbo, sm, 1 + 2 * l, ti, ps[6], ps[7])
            P.barrier()


    I32 = mybir.dt.int32
    TWO_PI = 2.0 * math.pi

    def phase_init():
        with ExitStack() as st:
            CB = min(T, 2048)
            ji = sbt(st, "inji", [32, 1], I32)
            jf = sbt(st, "injf", [32, 1])
            jt = sbt(st, "injt", [32, 1])
            invf = sbt(st, "ininvf", [32, 1])
            sgn = sbt(st, "insgn", [32, 1])
            oml = sbt(st, "inoml", [32, 1])
            ti32 = sbt(st, "inti", [32, CB], I32)
            si32 = sbt(st, "insi", [32, CB], I32)
            tf = sbt(st, "intf", [32, CB])
            sf = sbt(st, "insf", [32, CB])
            ang = sbt(st, "inang", [32, CB])
            q = sbt(st, "inq", [32, CB])
            red = sbt(st, "inred", [32, CB])
            P.op("pool", lambda e: e.iota(ji[:], pattern=[[0, 1]], base=0, channel_multiplier=1), [], [ji])
            P.op("dve", lambda e: e.tensor_copy(out=jf[:], in_=ji[:]), [ji], [jf])
            ts(jt[:], jf[:], 16.0, 16.0, ALU.is_ge, ALU.mult, [jf], [jt])
            tt(invf[:], jf[:], jt[:], ALU.subtract, [jf, jt], [invf])
            act(invf[:], invf[:], AF.Exp, [invf], [invf], scale=-math.log(10000.0) / 16.0)
            ts(sgn[:], jt[:], 1.0 / 8.0, -1.0, ALU.mult, ALU.add, [jt], [sgn])
            ts(oml[:], linkc[0:32, :], -1.0, 1.0, ALU.mult, ALU.add, [linkc], [oml])
            for b0 in range(0, T, CB):
                P.op("pool", lambda e: e.iota(ti32[:], pattern=[[1, CB]], base=b0, channel_multiplier=0), [], [ti32])
                if SL >= CB:
                    P.op("pool", lambda e: e.iota(si32[:], pattern=[[0, CB]], base=(b0 // SL) * SL, channel_multiplier=0),
                         [], [si32])
                else:
                    P.op("pool", lambda e: e.iota(si32[:], pattern=[[SL, CB // SL], [0, SL]], base=b0,
                                                  channel_multiplier=0), [], [si32])
                P.op("dve", lambda e: e.tensor_copy(out=tf[:], in_=ti32[:]), [ti32], [tf])
                P.op("dve", lambda e: e.tensor_copy(out=sf[:], in_=si32[:]), [si32], [sf])
                stt(tf[:], sf[:], oml[:, 0:1], tf[:], ALU.mult, ALU.subtract, [sf, oml, tf], [tf])
                ts(ang[:], tf[:], invf[:, 0:1], -1.0, ALU.mult, ALU.mult, [tf, invf], [ang])
                for which, dst in ((0, RS), (1, RC)):
                    if which == 1:
                        ts(ang[:], ang[:], math.pi / 2.0, None, ALU.add, None, [ang], [ang])
                    ts(q[:], ang[:], 1.0 / TWO_PI, None, ALU.mult, None, [ang], [q])
                    ts(q[:], q[:], 12582912.0, 12582912.0, ALU.add, ALU.subtract, [q], [q])
                    stt(red[:], q[:], -6.28125, ang[:], ALU.mult, ALU.add, [q, ang], [red])
                    stt(red[:], q[:], -(TWO_PI - 6.28125), red[:], ALU.mult, ALU.add, [q, red], [red])
                    ts(red[:], red[:], -3.1415925, 3.1415925, ALU.max, ALU.min, [red], [red])
                    act(red[:], red[:], AF.Sin, [red], [red])
                    if which == 0:
                        ts(red[:], red[:], sgn[:, 0:1], None, ALU.mult, None, [red, sgn], [red])
                    P.dma("sp", dst[:, b0:b0 + CB], red[:], [red], dst.bs[b0 // TT:(b0 + CB) // TT])
            onesr = sbt(st, "inones", [2, 8, TT], BF16)
            mrow = sbt(st, "inmrow", [8, TT], BF16)
            lk8 = sbt(st, "inlk8", [8, 1])
            P.op("dve", lambda e: e.memset(onesr[:], 1.0), [], [onesr])
            ts(lk8[:], linkc[0:8, :], -1.0, 30000.0, ALU.add, ALU.mult, [linkc], [lk8])
            P.op("dve", lambda e: e.memset(mrow[:], 1.0), [], [mrow])
            ts(mrow[:], mrow[:], lk8[:, 0:1], None, ALU.mult, None, [mrow, lk8], [mrow])
            for ti in range(NT):
                t0 = ti * TT
                P.dma("sp", KT[:, 96:98, t0:t0 + TT].rearrange("h r t -> r h t"), onesr[:], [onesr], [KT.bs[ti]])
                P.dma("sp", QT[:, 97, t0:t0 + TT], mrow[:], [mrow], [QT.bs[ti]])
            P.barrier()

    def phase_a(l):
        with ExitStack() as st:
            wA = sbt(st, "awA", [128, 8, O_GP], BF16)
            wkrs = sbt(st, "awkrs", [128, 8, 32], BF16)
            wq = sbt(st, "awq", [128, 2, 8, 96], BF16)
            wqs = sbt(st, "awqs", [128, 2, 8, 96], BF16)
            wkn = sbt(st, "awkn", [128, 8, 64], BF16)
            wv = sbt(st, "awv", [128, 8, 64], BF16)
            qg = sbt(st, "aqg", [128, 2])
            kvg = sbt(st, "akvg", [128, 1])
            gmb = sbt(st, "agmb", [16, 1])
            indq = sbt(st, "aindq", [96, 8, 8], BF16)
            indk = sbt(st, "aindk", [128, 4, 8], BF16)
            onr = sbt(st, "aonr", [32, 8], BF16)
            with nc.allow_non_contiguous_dma(reason="tiny param vectors"):
                P.dma("sp", qg[:, :], W['q_norm_g'][l].rearrange("(c p) -> p c", p=128), [], [qg])
                P.dma("sp", kvg[:, :], W['kv_norm_g'][l].rearrange("(c p) -> p c", p=128), [], [kvg])
                P.dma("sp", gmb[:, :], W['b_mlstm_gate'][l].rearrange("a b (c o) -> (a b c) o", o=1), [], [gmb])
            for k in range(8):
                load_cast(wA[:, k, :], W['w_in'][l, k * 128:(k + 1) * 128, 0:O_GP], wA, O_GP)
                act(wkrs[:, k, 0:16], wA[:, k, O_KR + 16:O_KR + 32], AF.Copy, [wA], [wkrs])
                act(wkrs[:, k, 16:32], wA[:, k, O_KR:O_KR + 16], AF.Copy, [wA], [wkrs])
            for c2 in range(2):
                s = stg[stgi[0] % 2]
                stgi[0] += 1
                P.dma("sp", s[:, 0:768], W['w_uq'][l, c2 * 128:(c2 + 1) * 128, :], [], [s])
                sv = s[:, 0:768].rearrange("p (h d) -> p h d", h=8)
                sc = qg[:, c2:c2 + 1]
                act(wq[:, c2, :, :], sv[:, :, :], AF.Identity, [s, qg], [wq], scale=sc)
                P.op("dve", lambda e: e.memset(wqs[:, c2, :, 0:64], 0.0), [], [wqs])
                act(wqs[:, c2, :, 64:80], sv[:, :, 80:96], AF.Identity, [s, qg], [wqs], scale=sc)
                act(wqs[:, c2, :, 80:96], sv[:, :, 64:80], AF.Identity, [s, qg], [wqs], scale=sc)
            s = stg[stgi[0] % 2]
            stgi[0] += 1
            P.dma("sp", s[:, 0:1024], W['w_ukv'][l, :, :], [], [s])
            sv = s[:, 0:1024].rearrange("p (h d) -> p h d", h=8)
            act(wkn[:, :, :], sv[:, :, 0:64], AF.Identity, [s, kvg], [wkn], scale=kvg[:, 0:1])
            act(wv[:, :, :], sv[:, :, 64:128], AF.Identity, [s, kvg], [wv], scale=kvg[:, 0:1])
            P.op("dve", lambda e: e.memset(indq[:], 0.0), [], [indq])
            P.op("dve", lambda e: e.memset(indk[:], 0.0), [], [indk])
            P.op("dve", lambda e: e.memset(onr[:], 1.0), [], [onr])
            for h in range(8):
                P.op("dve", lambda e: e.memset(indq[:, h, h:h + 1], 1.0), [], [indq])
                P.op("dve", lambda e: e.memset(indk[(h % 2) * 64:(h % 2) * 64 + 64, h // 2, h:h + 1], 1.0), [], [indk])
            xbs = [sbt(st, "axb%d" % i, [128, 8, TT], BF16) for i in range(2)]
            cqf = sbt(st, "acqf", [128, 2, TT])
            sqt = sbt(st, "asq", [128, 2, TT])
            rstd = sbt(st, "arstd", [128, TT])
            cqn = sbt(st, "acqn", [128, 2, TT], BF16)
            ckf = sbt(st, "ackf", [128, TT])
            ckvn = sbt(st, "ackvn", [128, TT], BF16)
            rct = sbt(st, "arc", [96, TT])
            rst = sbt(st, "ars", [96, TT])
            r1 = sbt(st, "ar1", [96, TT])
            r2 = sbt(st, "ar2", [96, TT])
            krr = sbt(st, "akrr", [32, TT], BF16)
            fst = [sbt(st, "afst%d" % i, [128, TT]) for i in range(4)]
            bst = [sbt(st, "abst%d" % i, [128, TT], BF16) for i in range(4)]
            qos = [sbt(st, "aqo%d" % i, [96, TT], BF16) for i in range(2)]
            sqb = [sbt(st, "asqb%d" % i, [128, TT], BF16) for i in range(2)]
            sqb3 = sbt(st, "asqb3", [32, TT], BF16)
            vas = [sbt(st, "ava%d" % i, [128, 8, 128], BF16) for i in range(2)]
            qn2 = sbt(st, "aqn2", [8, T])
            km2 = sbt(st, "akm2", [8, 1])
            kmx = sbt(st, "akmx", [8, 1])
            mrw = sbt(st, "amrw", [8, T], BF16)
            for v in vas:
                P.op("dve", lambda e: e.memset(v[:], 1.0), [], [v])
            P.op("dve", lambda e: e.memset(km2[:], 0.0), [], [km2])
            cnt = {"b": 0, "f": 0, "s": 0, "q": 0}

            def nb():
                cnt["b"] += 1
                return ps[cnt["b"] % 6]

            def nf():
                cnt["f"] += 1
                return fst[cnt["f"] % 4]

            def nbs():
                cnt["s"] += 1
                return bst[cnt["s"] % 4]

            def dq():
                cnt["q"] += 1
                return "sp" if cnt["q"] % 2 else "pool"

            def rstd_from(pss, n):
                ts(rstd[:], pss[:, :], 1.0 / n, LN_EPS, ALU.mult, ALU.add, [pss], [rstd])
                act(rstd[:], rstd[:], AF.Ln, [rstd], [rstd])
                act(rstd[:], rstd[:], AF.Exp, [rstd], [rstd], scale=-0.5)

            for ti in range(NT):
                t0 = ti * TT
                tsl = slice(t0, t0 + TT)
                xb = xbs[ti % 2]
                P.dma("sp", xb[:], XTB[:, tsl].rearrange("(c p) t -> p c t", p=128), [XTB.bs[ti]], [xb])
                P.dma("pool", rct[0:32, :], RC[:, tsl], [RC.bs[ti]], [rct])
                P.dma("pool", rst[0:32, :], RS[:, tsl], [RS.bs[ti]], [rst])
                P.dma("pool", rct[64:96, :], RC[:, tsl], [RC.bs[ti]], [rct])
                P.dma("pool", rst[64:96, :], RS[:, tsl], [RS.bs[ti]], [rst])

                def fm_out(c0, m):
                    pb = nb()
                    for k in range(8):
                        mm(pb[0:m, :], wA[:, k, c0:c0 + m], xb[:, k, :], k == 0, k == 7, [wA, xb], [pb])
                    return pb
                for c2 in range(2):
                    pb = fm_out(O_CQ + c2 * 128, 128)
                    act(cqf[:, c2, :], pb[:, :], AF.Copy, [pb], [cqf])
                    act(sqt[:, c2, :], pb[:, :], AF.Square, [pb], [sqt])
                pss = nb()
                for c2 in range(2):
                    mm(pss[:, :], ones32[:], sqt[:, c2, :], c2 == 0, c2 == 1, [ones32, sqt], [pss])
                rstd_from(pss, 256.0)
                tt(cqn[:], cqf[:], rstd[:].unsqueeze(1).to_broadcast([128, 2, TT]), ALU.mult, [cqf, rstd], [cqn])
                pb = fm_out(O_CKV, 128)
                act(ckf[:], pb[:, :], AF.Copy, [pb], [ckf])
                act(sqt[:, 0, :], pb[:, :], AF.Square, [pb], [sqt])
                pss = nb()
                mm(pss[:, :], ones32[:], sqt[:, 0, :], True, True, [ones32, sqt], [pss])
                rstd_from(pss, 128.0)
                tt(ckvn[:], ckf[:], rstd[:], ALU.mult, [ckf, rstd], [ckvn])
                pb = fm_out(O_KR, 32)
                pb2 = nb()
                for k in range(8):
                    mm(pb2[0:32, :], wkrs[:, k, :], xb[:, k, :], k == 0, k == 7, [wkrs, xb], [pb2])
                tt(r1[0:32, :], pb2[0:32, :], rst[0:32, :], ALU.mult, [pb2, rst], [r1])
                tt(r2[0:32, :], pb[0:32, :], rct[0:32, :], ALU.mult, [pb, rct], [r2])
                tt(krr[:], r1[0:32, :], r2[0:32, :], ALU.add, [r1, r2], [krr])
                for h in range(8):
                    P.dma(dq(), KT[h, 64:96, tsl], krr[:], [krr], [KT.bs[ti]])
                for (c0, dst) in ((O_XM, XM), (O_US, US)):
                    for c in range(4):
                        pb = fm_out(c0 + c * 128, 128)
                        f = nf()
                        if c % 2:
                            act(f[:], pb[:, :], AF.Copy, [pb], [f])
                        else:
                            P.op("dve", lambda e: e.tensor_copy(out=f[:], in_=pb[:, :]), [pb], [f])
                        P.dma(dq(), dst[c * 128:(c + 1) * 128, tsl], f[:], [f], [dst.bs[ti]])
                        if dst is US:
                            bs_ = nbs()
                            act(bs_[:], pb[:, :], AF.Copy, [pb], [bs_])
                            P.dma(dq(), USB[c * 128:(c + 1) * 128, tsl], bs_[:], [bs_], [USB.bs[ti]])
                pb = fm_out(O_GM, 16)
                f = nf()
                act(f[0:16, :], pb[0:16, :], AF.Identity, [pb, gmb], [f], bias=gmb[:, 0:1])
                P.dma(dq(), GM[:, tsl], f[0:16, :], [f], [GM.bs[ti]])
                for b in range(4):
                    r0 = t0 + b * 128
                    pb = nb()
                    for k in range(8):
                        mm(pb[:, :], xb[:, k, b * 128:(b + 1) * 128], wA[:, k, O_VM:O_VM + 512], k == 0, k == 7, [wA, xb], [pb])
                    bs_ = nbs()
                    P.op("dve", lambda e: e.tensor_copy(out=bs_[:], in_=pb[:, :]), [pb], [bs_])
                    P.dma(dq(), VM[r0:r0 + 128, :], bs_[:], [bs_], [VM.bs[ti]])
                    pb = nb()
                    for k in range(8):
                        mm(pb[:, :], xb[:, k, b * 128:(b + 1) * 128], wA[:, k, O_OM:O_OM + 512], k == 0, k == 7, [wA, xb], [pb])
                    f = nf()
                    act(f[:], pb[:, :], AF.Sigmoid, [pb], [f])
                    P.dma(dq(), OMS[r0:r0 + 128, :], f[:], [f], [OMS.bs[ti]])
                pend_q = []
                for h in range(8):
                    pq = nb()
                    for c2 in range(2):
                        mm(pq[0:96, :], wq[:, c2, h, :], cqn[:, c2, :], c2 == 0, c2 == 1, [wq, cqn], [pq])
                    pqs = nb()
                    for c2 in range(2):
                        mm(pqs[0:96, :], wqs[:, c2, h, :], cqn[:, c2, :], c2 == 0, c2 == 1, [wqs, cqn], [pqs])
                    while pend_q:
                        pend_q.pop(0)()
                    qo = qos[h % 2]
                    tt(r1[64:96, :], pqs[64:96, :], rst[64:96, :], ALU.mult, [pqs, rst], [r1])
                    tt(r2[64:96, :], pq[64:96, :], rct[64:96, :], ALU.mult, [pq, rct], [r2])
                    tt(qo[64:96, :], r1[64:96, :], r2[64:96, :], ALU.add, [r1, r2], [qo])
                    act(qo[0:64, :], pq[0:64, :], AF.Copy, [pq], [qo])
                    sb_ = sqb[h % 2]
                    act(sb_[0:96, :], qo[:, :], AF.Square, [qo], [sb_])
                    pend_q.append(lambda h=h, sb_=sb_: mm(ps[6][0:8, :], indq[:, h, :], sb_[0:96, :], h == 0, h == 7, [indq, sb_], [ps[6]]))
                    P.dma(dq(), QT[h, 0:96, tsl], qo[:, :], [qo], [QT.bs[ti]])
                pend_k = list(pend_q)
                sb_ = sqb3
                act(sb_[0:32, :], krr[:, :], AF.Square, [krr], [sb_])
                pend_k.append(lambda sb_=sb_: mm(ps[7][0:8, :], onr[:, :], sb_[0:32, :], True, False, [onr, sb_], [ps[7]]))
                for j in range(4):
                    pb = nb()
                    mm(pb[:, :], wkn[:, 2 * j:2 * j + 2, :].rearrange("p h d -> p (h d)"), ckvn[:], True, True, [wkn, ckvn], [pb])
                    while pend_k:
                        pend_k.pop(0)()
                    if j == 0:
                        P.op("dve", lambda e: e.tensor_copy(out=qn2[:, tsl], in_=ps[6][0:8, :]), [ps[6]], [qn2])
                    ko = nbs()
                    P.op("dve", lambda e: e.tensor_copy(out=ko[:], in_=pb[:, :]), [pb], [ko])
                    sb_ = sqb[1 - j % 2]
                    act(sb_[:, :], ko[:, :], AF.Square, [ko], [sb_])
                    pend_k.append(lambda j=j, sb_=sb_: mm(ps[7][0:8, :], indk[:, j, :], sb_[:, :], False, j == 3, [indk, sb_], [ps[7]]))
                    P.dma(dq(), KT[2 * j, 0:64, tsl], ko[0:64, :], [ko], [KT.bs[ti]])
                    P.dma(dq(), KT[2 * j + 1, 0:64, tsl], ko[64:128, :], [ko], [KT.bs[ti]])
                pend_v = list(pend_k)
                for b in range(4):
                    r0 = t0 + b * 128
                    pb = nb()
                    mm(pb[:, :], ckvn[:, b * 128:(b + 1) * 128], wv[:].rearrange("p h d -> p (h d)"), True, True, [ckvn, wv], [pb])
                    while pend_v:
                        pend_v.pop(0)()
                    if b == 0:
                        P.op("dve", lambda e: e.reduce_max(out=kmx[:], in_=ps[7][0:8, :], axis=AX.X), [ps[7]], [kmx])
                        tt(km2[:], km2[:], kmx[:], ALU.max, [km2, kmx], [km2])
                    va = vas[b % 2]
                    P.op("dve", lambda e: e.tensor_copy(out=va[:, :, 0:64], in_=pb[:, :].rearrange("p (h d) -> p h d", h=8)),
                         [pb], [va])
                    P.dma(dq(), VA[r0:r0 + 128, :, :], va[:], [va], [VA.bs[ti]])
            act(qn2[:], qn2[:], AF.Sqrt, [qn2, km2], [qn2], scale=km2[:, 0:1])
            ts(mrw[:], qn2[:], -1.0, None, ALU.mult, None, [qn2], [mrw])
            P.dma("sp", QT[:, 96, :], mrw[:], [mrw], QT.bs)
            P.barrier()


    def phase_att(l):
        SCALE = 96.0 ** -0.5
        with ExitStack() as st:
            kTs = [sbt(st, "tk%d" % i, [98, T], BF16) for i in range(2)]
            qTs = [sbt(st, "tq%d" % i, [98, T], BF16) for i in range(2)]
            vhs = [sbt(st, "tv%d" % i, [128, NCH, 128], BF16) for i in range(2)]
            pts = [sbt(st, "tp%d" % i, [128, TT], BF16) for i in range(3)]
            rlt = sbt(st, "trl", [128, TT])
            osb = [sbt(st, "to%d" % i, [64, TT], BF16) for i in range(2)]
            n = 0
            for h in range(8):
                kT, qT, vh = kTs[h % 2], qTs[h % 2], vhs[h % 2]
                P.dma("sp", kT[:], KT[h, :, :], KT.bs, [kT])
                P.dma("pool", qT[:], QT[h, :, :], QT.bs, [qT])
                P.dma("sp", vh[:], VA[:, h, :].rearrange("(n p) d -> p n d", p=128), VA.bs, [vh])
                for i in range(NT):
                    qs = slice(i * TT, (i + 1) * TT)
                    acc = ps[4 + i % 2]
                    segq = (i * TT) // SL

                    def score(kb):
                        R = 97 if (kb * 128) // SL == segq else 98
                        pb = ps[(n + kb) % 3]
                        mm(pb[:, :], kT[0:R, kb * 128:(kb + 1) * 128], qT[0:R, qs], True, True, [kT, qT], [pb])
                        return pb
                    pbn = score(0)
                    for kb in range(NCH):
                        pb = pbn
                        pt = pts[(n + kb) % 3]
                        if kb + 1 < NCH:
                            pbn = score(kb + 1)
                        act(pt[:], pb[:, :], AF.Exp, [pb], [pt], scale=SCALE)
                        mm(acc[:, :], vh[:, kb, :], pt[:], kb == 0, kb == NCH - 1, [vh, pt], [acc])
                    n += NCH
                    P.op("dve", lambda e: e.reciprocal(out=rlt[64:128, :], in_=acc[64:128, :]), [acc], [rlt])
                    o = osb[i % 2]
                    tt(o[:], acc[0:64, :], rlt[64:128, :], ALU.mult, [acc, rlt], [o])
                    P.dma("pool" if i % 2 else "sp", OT[h * 64:(h + 1) * 64, qs], o[:], [o], [OT.bs[i]])
            P.barrier()


    def emit_sin(dst_t, dst, ang_t, ang, q_t, q, shift):
        ts(q, ang, 1.0 / TWO_PI, shift / TWO_PI, ALU.mult, ALU.add, [ang_t], [q_t])
        ts(q, q, 12582912.0, 12582912.0, ALU.add, ALU.subtract, [q_t], [q_t])
        stt(dst, q, -6.28125, ang, ALU.mult, ALU.add, [q_t, ang_t], [dst_t])
        stt(dst, q, -(TWO_PI - 6.28125), dst, ALU.mult, ALU.add, [q_t, dst_t], [dst_t])
        if shift:
            ts(dst, dst, shift, None, ALU.add, None, [dst_t], [dst_t])
        ts(dst, dst, -3.1415925, 3.1415925, ALU.max, ALU.min, [dst_t], [dst_t])
        act(dst, dst, AF.Sin, [dst_t], [dst_t])

    def phase_s5(l):
        NSC = "tiny param vectors"
        with ExitStack() as lst:
            CLr = sbt(lst, "sCLr", [128, 16, 128], BF16)
            CLi = sbt(lst, "sCLi", [128, 16, 128], BF16)
            wglu = sbt(lst, "swglu", [128, 4, 2 * D], BF16)
            dsk = sbt(lst, "sdsk", [128, 4])
            for k in range(4):
                load_cast(wglu[:, k, :], W['w_glu'][l, k * 128:(k + 1) * 128, :], wglu, 2 * D)
            with nc.allow_non_contiguous_dma(reason=NSC):
                P.dma("sp", dsk[:, :], W['s5_d'][l].rearrange("(c g) w -> (g w) c", c=4), [], [dsk])
            with ExitStack() as st:
                Z = [sbt(st, "sZ%d" % i, [128, 4, 128]) for i in range(2)]
                for i, nm in enumerate(('s5_c_re', 's5_c_im')):
                    P.op("dve", lambda e: e.memset(Z[i][:], 0.0), [], [Z[i]])
                    for ch in range(4):
                        for jj in range(4):
                            for two in range(2):
                                g = 2 * (4 * ch + jj) + two
                                P.dma("sp" if two else "pool", Z[i][32 * jj + 16 * two:32 * jj + 16 * two + 16, ch, 64 * two:64 * two + 64],
                                      W[nm][l, g], [], [Z[i]])
                P.op("dve", lambda e: e.memset(CLr[:], 0.0), [], [CLr])
                P.op("dve", lambda e: e.memset(CLi[:], 0.0), [], [CLi])
                for i, CL in enumerate((CLr, CLi)):
                    for ch in range(4):
                        pb = ps[(2 * i + ch) % 4]
                        mm(pb[:, 0:128], Z[i][:, ch, :], ident[:], True, True, [Z[i], ident], [pb])
                        for jj in range(4):
                            act(CL[:, 4 * ch + jj, 32 * jj:32 * jj + 32], pb[:, 32 * jj:32 * jj + 32], AF.Copy, [pb], [CL],
                                scale=(1.0 if i == 0 else -1.0))
                P.barrier()
            for d in (0, 1):
                with ExitStack() as st:
                    sm = {n: sbt(st, "s5" + n, [128, 16]) for n in
                          ("are", "aim", "dt", "rmag", "th", "cs", "sn", "q", "abr", "abi", "nr", "ni", "den", "bsr", "bsi", "t")}
                    with nc.allow_non_contiguous_dma(reason=NSC):
                        for two in range(2):
                            prt = slice(two * 64, two * 64 + 64)
                            P.dma("sp", sm["are"][prt, :], W['s5_a_re'][l, d].rearrange("(j two) p -> two p j", two=2)[two], [], [sm["are"]])
                            P.dma("sp", sm["aim"][prt, :], W['s5_a_im'][l, d].rearrange("(j two) p -> two p j", two=2)[two], [], [sm["aim"]])
                            P.dma("sp", sm["dt"][prt, :], W['s5_log_dt'][l, d].rearrange("(j two) -> two j", two=2)[two].partition_broadcast(64),
                                  [], [sm["dt"]])
                    A = lambda n: sm[n][:, :]
                    act(A("dt"), A("dt"), AF.Exp, [sm["dt"]], [sm["dt"]])
                    tt(A("rmag"), A("are"), A("dt"), ALU.mult, [sm["are"], sm["dt"]], [sm["rmag"]])
                    act(A("rmag"), A("rmag"), AF.Exp, [sm["rmag"]], [sm["rmag"]])
                    tt(A("th"), A("aim"), A("dt"), ALU.mult, [sm["aim"], sm["dt"]], [sm["th"]])
                    emit_sin(sm["sn"], A("sn"), sm["th"], A("th"), sm["q"], A("q"), 0.0)
                    emit_sin(sm["cs"], A("cs"), sm["th"], A("th"), sm["q"], A("q"), math.pi / 2.0)
                    tt(A("abr"), A("rmag"), A("cs"), ALU.mult, [sm["rmag"], sm["cs"]], [sm["abr"]])
                    tt(A("abi"), A("rmag"), A("sn"), ALU.mult, [sm["rmag"], sm["sn"]], [sm["abi"]])
                    ts(A("abr"), A("abr"), -1.0, None, ALU.add, None, [sm["abr"]], [sm["abr"]])
                    tt(A("nr"), A("abr"), A("are"), ALU.mult, [sm["abr"], sm["are"]], [sm["nr"]])
                    tt(A("t"), A("abi"), A("aim"), ALU.mult, [sm["abi"], sm["aim"]], [sm["t"]])
                    tt(A("nr"), A("nr"), A("t"), ALU.add, [sm["nr"], sm["t"]], [sm["nr"]])
                    tt(A("ni"), A("abi"), A("are"), ALU.mult, [sm["abi"], sm["are"]], [sm["ni"]])
                    tt(A("t"), A("abr"), A("aim"), ALU.mult, [sm["abr"], sm["aim"]], [sm["t"]])
                    tt(A("ni"), A("ni"), A("t"), ALU.subtract, [sm["ni"], sm["t"]], [sm["ni"]])
                    tt(A("den"), A("are"), A("are"), ALU.mult, [sm["are"]], [sm["den"]])
                    tt(A("t"), A("aim"), A("aim"), ALU.mult, [sm["aim"]], [sm["t"]])
                    tt(A("den"), A("den"), A("t"), ALU.add, [sm["den"], sm["t"]], [sm["den"]])
                    P.op("dve", lambda e: e.reciprocal(out=A("den"), in_=A("den")), [sm["den"]], [sm["den"]])
                    tt(A("bsr"), A("nr"), A("den"), ALU.mult, [sm["nr"], sm["den"]], [sm["bsr"]])
                    tt(A("bsi"), A("ni"), A("den"), ALU.mult, [sm["ni"], sm["den"]], [sm["bsi"]])
                    BLr = sbt(st, "sBLr", [128, 16, 128], BF16)
                    BLi = sbt(st, "sBLi", [128, 16, 128], BF16)
                    cosT = sbt(st, "scosT", [128, 16, TT])
                    sinT = sbt(st, "ssinT", [128, 16, TT])
                    with ExitStack() as st2:
                        Bt = [sbt(st2, "sBt%d" % i, [128, 16, 16]) for i in range(2)]
                        Bp = [sbt(st2, "sBp%d" % i, [128, 16, 16]) for i in range(2)]
                        tmp = sbt(st2, "sBtmp", [128, 16, 16])
                        X = sbt(st2, "sX", [128, 16, 2, 16])
                        for i, nm in enumerate(('s5_b_re', 's5_b_im')):
                            for two in range(2):
                                P.dma("sp", Bt[i][two * 64:two * 64 + 64, :, :],
                                      W[nm][l].rearrange("(j two) p c -> two p j c", two=2)[two], [], [Bt[i]])
                        bc = lambda n: sm[n][:, :].unsqueeze(2).to_broadcast([128, 16, 16])
                        tt(Bp[0][:], Bt[0][:], bc("bsr"), ALU.mult, [Bt[0], sm["bsr"]], [Bp[0]])
                        tt(tmp[:], Bt[1][:], bc("bsi"), ALU.mult, [Bt[1], sm["bsi"]], [tmp])
                        tt(Bp[0][:], Bp[0][:], tmp[:], ALU.subtract, [Bp[0], tmp], [Bp[0]])
                        tt(Bp[1][:], Bt[1][:], bc("bsr"), ALU.mult, [Bt[1], sm["bsr"]], [Bp[1]])
                        tt(tmp[:], Bt[0][:], bc("bsi"), ALU.mult, [Bt[0], sm["bsi"]], [tmp])
                        tt(Bp[1][:], Bp[1][:], tmp[:], ALU.add, [Bp[1], tmp], [Bp[1]])
                        for i, BL in enumerate((BLr, BLi)):
                            P.op("dve", lambda e: e.memset(BL[:], 0.0), [], [BL])
                            P.op("dve", lambda e: e.memset(X[:], 0.0), [], [X])
                            P.op("dve", lambda e: e.tensor_copy(out=X[0:64, :, 0, :], in_=Bp[i][0:64, :, :]), [Bp[i]], [X])
                            P.op("dve", lambda e: e.tensor_copy(out=X[64:128, :, 1, :], in_=Bp[i][64:128, :, :]), [Bp[i]], [X])
                            for ch in range(4):
                                pb = ps[ch]
                                mm(pb[:, 0:128], X[:, 4 * ch:4 * ch + 4, :, :].rearrange("p a b c -> p (a b c)"), ident[:], True, True,
                                   [X, ident], [pb])
                                for jj in range(4):
                                    act(BL[32 * jj:32 * jj + 32, 4 * ch + jj, :], pb[32 * jj:32 * jj + 32, 0:128], AF.Copy, [pb], [BL])
                        ti32 = sbt(st2, "sti", [128, TT], I32)
                        tau = sbt(st2, "stau", [128, TT])
                        ang = sbt(st2, "sang", [128, 16, TT])
                        qq = sbt(st2, "sqq", [128, 16, TT])
                        if d == 0:
                            P.op("pool", lambda e: e.iota(ti32[:], pattern=[[1, TT]], base=1, channel_multiplier=0), [], [ti32])
                        else:
                            P.op("pool", lambda e: e.iota(ti32[:], pattern=[[-1, TT]], base=TT, channel_multiplier=0), [], [ti32])
                        P.op("dve", lambda e: e.tensor_copy(out=tau[:], in_=ti32[:]), [ti32], [tau])
                        tt(ang[:], sm["th"][:, :].unsqueeze(2).to_broadcast([128, 16, TT]),
                           tau[:].unsqueeze(1).to_broadcast([128, 16, TT]), ALU.mult, [sm["th"], tau], [ang])
                        fl = lambda t: t[:].rearrange("p a b -> p (a b)")
                        emit_sin(sinT, fl(sinT), ang, fl(ang), qq, fl(qq), 0.0)
                        emit_sin(cosT, fl(cosT), ang, fl(ang), qq, fl(qq), math.pi / 2.0)
                        P.barrier()
                    ufs = [sbt(st, "suf%d" % i, [128, 4, TT]) for i in range(2)]
                    ubs = [sbt(st, "sub%d" % i, [128, 4, TT], BF16) for i in range(2)]
                    wk = [[sbt(st, "sw%s%d" % (n, i), [128, TT]) for i in range(2)] for n in "abcdefgh"]
                    xb_ = [[sbt(st, "sx%s%d" % (n, i), [128, TT], BF16) for i in range(2)] for n in "ri"]
                    car = [sbt(st, "scar%d" % i, [128, 16]) for i in range(2)]
                    y1t = [sbt(st, "sy1%d" % i, [128, TT]) for i in range(2)]
                    yg = sbt(st, "syg", [128, 4, TT], BF16)
                    sg = [sbt(st, "ssg%d" % i, [128, TT]) for i in range(2)]
                    P.op("dve", lambda e: e.memset(car[0][:], 0.0), [], [car[0]])
                    P.op("dve", lambda e: e.memset(car[1][:], 0.0), [], [car[1]])
                    order = list(range(NT)) if d == 0 else list(range(NT - 1, -1, -1))
                    nblk = 0
                    for it, ti in enumerate(order):
                        t0 = ti * TT
                        tsl = slice(t0, t0 + TT)
                        uf, ub = ufs[it % 2], ubs[it % 2]
                        P.dma("sp", uf[:], US[:, tsl].rearrange("(c p) t -> p c t", p=128), [US.bs[ti]], [uf])
                        act(ub[:], uf[:], AF.Copy, [uf], [ub])
                        if it > 0:
                            bnd = t0 if d == 0 else t0 + TT
                            if bnd % SL == 0:
                                for cc in car:
                                    ts(cc[:], cc[:], linkc[:, 0:1], None, ALU.mult, None, [cc, linkc], [cc])
                        for ch in range(4):
                            psy = ps[4 + ch % 2]
                            for jj in range(4):
                                j = 4 * ch + jj
                                pr_ = slice(32 * jj, 32 * jj + 32)
                                pvr, pvi = ps[(2 * nblk) % 4], ps[(2 * nblk + 1) % 4]
                                w = [wk[i][nblk % 2] for i in range(8)]
                                xr_b, xi_b = xb_[0][nblk % 2], xb_[1][nblk % 2]
                                nblk += 1
                                mm(pvr[:, :], BLr[:, j, :], ub[:, ch, :], True, True, [BLr, ub], [pvr])
                                mm(pvi[:, :], BLi[:, j, :], ub[:, ch, :], True, True, [BLi, ub], [pvi])
                                c_, s_ = cosT[:, j, :], sinT[:, j, :]
                                tt(w[0][:], pvr[:, :], c_, ALU.mult, [pvr, cosT], [w[0]])
                                tt(w[1][:], pvi[:, :], s_, ALU.mult, [pvi, sinT], [w[1]])
                                tt(w[0][:], w[0][:], w[1][:], ALU.add, [w[0], w[1]], [w[0]])
                                tt(w[2][:], pvi[:, :], c_, ALU.mult, [pvi, cosT], [w[2]])
                                tt(w[3][:], pvr[:, :], s_, ALU.mult, [pvr, sinT], [w[3]])
                                tt(w[2][:], w[2][:], w[3][:], ALU.subtract, [w[2], w[3]], [w[2]])
                                rb = sm["rmag"][:, j:j + 1].to_broadcast([128, TT])
                                for (src, dst, cc) in ((w[0], w[4], car[0]), (w[2], w[5], car[1])):
                                    if d == 0:
                                        P.op("dve", lambda e: e.tensor_tensor_scan(out=dst[:], data0=rb, data1=src[:],
                                                                                   initial=cc[:, j:j + 1], op0=ALU.mult, op1=ALU.add),
                                             [src, cc, sm["rmag"]], [dst])
                                    else:
                                        P.op("dve", lambda e: e.tensor_tensor_scan(out=dst[:, ::-1], data0=rb, data1=src[:, ::-1],
                                                                                   initial=cc[:, j:j + 1], op0=ALU.mult, op1=ALU.add),
                                             [src, cc, sm["rmag"]], [dst])
                                tt(w[6][:], w[4][:], c_, ALU.mult, [w[4], cosT], [w[6]])
                                tt(w[1][:], w[5][:], s_, ALU.mult, [w[5], sinT], [w[1]])
                                tt(w[6][:], w[6][:], w[1][:], ALU.subtract, [w[6], w[1]], [w[6]])
                                tt(w[7][:], w[4][:], s_, ALU.mult, [w[4], sinT], [w[7]])
                                tt(w[3][:], w[5][:], c_, ALU.mult, [w[5], cosT], [w[3]])
                                tt(w[7][:], w[7][:], w[3][:], ALU.add, [w[7], w[3]], [w[7]])
                                lastc = slice(TT - 1, TT) if d == 0 else slice(0, 1)
                                act(car[0][:, j:j + 1], w[6][:, lastc], AF.Copy, [w[6]], [car[0]])
                                act(car[1][:, j:j + 1], w[7][:, lastc], AF.Copy, [w[7]], [car[1]])
                                act(xr_b[:], w[6][:], AF.Copy, [w[6]], [xr_b])
                                act(xi_b[:], w[7][:], AF.Copy, [w[7]], [xi_b])
                                mm(psy[:, :], CLr[:, j, :], xr_b[:], jj == 0, False, [CLr, xr_b], [psy])
                                mm(psy[:, :], CLi[:, j, :], xi_b[:], False, jj == 3, [CLi, xi_b], [psy])
                            y1 = y1t[ch % 2]
                            if d == 0:
                                stt(y1[:], uf[:, ch, :], dsk[:, ch:ch + 1], psy[:, :], ALU.mult, ALU.add, [uf, dsk, psy], [y1])
                                P.dma("pool", Y1[ch * 128:(ch + 1) * 128, tsl], y1[:], [y1], [Y1.bs[ti]])
                            else:
                                P.dma("pool", y1[:], Y1[ch * 128:(ch + 1) * 128, tsl], [Y1.bs[ti]], [y1])
                                tt(y1[:], y1[:], psy[:, :], ALU.add, [y1, psy], [y1])
                                act(yg[:, ch, :], y1[:], AF.Gelu_apprx_tanh, [y1], [yg])
                        if d == 1:
                            for o in range(8):
                                pv, pg = ps[6], ps[7]
                                for k in range(4):
                                    mm(pv[:, :], wglu[:, k, o * 128:(o + 1) * 128], yg[:, k, :], k == 0, k == 3, [wglu, yg], [pv])
                                for k in range(4):
                                    mm(pg[:, :], wglu[:, k, D + o * 128:D + (o + 1) * 128], yg[:, k, :], k == 0, k == 3, [wglu, yg], [pg])
                                s1, s2 = sg[0], sg[1]
                                act(s1[:], pg[:, :], AF.Sigmoid, [pg], [s1])
                                s3 = y1t[o % 2]
                                tt(s2[:], s1[:], pv[:, :], ALU.mult, [s1, pv], [s2])
                                P.dma("sp" if o % 2 else "pool", YC[o * 128:(o + 1) * 128, tsl], s2[:], [s2], [YC.bs[ti]])
                    P.barrier()


    MS = nc.dram_tensor("MSscr", [4, 4, NCH], F32).ap()
    MSb = [Buf("ms%d" % i) for i in range(4)]

    def phase_m(l):
        CW = max(1, NCH // 32)
        NBLK = NCH // CW
        R = 4 * NBLK
        WD = CW * 128
        NCS = SL // 128
        with ExitStack() as st:
            cw = sbt(st, "m0cw", [128, 3, 4])
            cb = sbt(st, "m0cb", [128, 4])
            with nc.allow_non_contiguous_dma(reason="tiny param vectors"):
                for w in range(3):
                    P.dma("sp", cw[:, w, :], W['conv_m_w'][l, w].rearrange("(c p) -> p c", p=128), [], [cw])
                P.dma("sp", cb[:, :], W['conv_m_b'][l].rearrange("(c p) -> p c", p=128), [], [cb])
            xts = [sbt(st, "m0x%d" % i, [128, TT + 2]) for i in range(3)]
            accs = [sbt(st, "m0a%d" % i, [128, TT]) for i in range(2)]
            xos = [sbt(st, "m0o%d" % i, [128, TT], BF16) for i in range(2)]
            n = 0
            for ti in range(NT):
                t0 = ti * TT
                for c in range(4):
                    xt, acc, xo = xts[n % 3], accs[n % 2], xos[n % 2]
                    n += 1
                    load_halo("sp" if c % 2 else "pool", xt, XM, c * 128, 128, ti, XM.bs)
                    ts(acc[:], xt[:, 0:TT], cw[:, 0, c:c + 1], None, ALU.mult, None, [xt, cw], [acc])
                    stt(acc[:], xt[:, 1:TT + 1], cw[:, 1, c:c + 1], acc[:], ALU.mult, ALU.add, [xt, cw, acc], [acc])
                    stt(acc[:], xt[:, 2:TT + 2], cw[:, 2, c:c + 1], acc[:], ALU.mult, ALU.add, [xt, cw, acc], [acc])
                    act(xo[:], acc[:], AF.Silu, [acc, cb], [xo], bias=cb[:, c:c + 1])
                    P.dma("sp", XCB[c * 128:(c + 1) * 128, t0:t0 + TT], xo[:], [xo], [XCB.bs[ti]])
            P.barrier()
        import os as _os
        MSTOP = _os.environ.get('MSTOP', '')
        if MSTOP == 'm0':
            return
        with ExitStack() as lst:
            wq = sbt(lst, "mwq", [128, 4, 128], BF16)
            wk = sbt(lst, "mwk", [128, 4, 128], BF16)
            for h in range(4):
                load_cast(wq[:, h, :], W['w_q_m'][l, h], wq, 128)
                load_cast(wk[:, h, :], W['w_k_m'][l, h], wk, 128, scale=128.0 ** -0.5)
            maskF = sbt(lst, "mmaskF", [128, 128])
            maskB = sbt(lst, "mmaskB", [128, 128])
            for mk, cm, stp in ((maskF, -1, 1), (maskB, 1, -1)):
                P.op("pool", lambda e: e.memset(mk[:], 1.0), [], [mk])
                P.op("pool", lambda e: e.affine_select(out=mk[:], in_=mk[:], compare_op=ALU.is_ge, fill=0.0, base=0,
                                                       pattern=[[stp, 128]], channel_multiplier=cm), [mk], [mk])
            rmask = sbt(lst, "mrmask", [128, CW, 128])
            nmask = sbt(lst, "mnmask", [128, CW, 128])
            P.op("dve", lambda e: e.memset(rmask[:], 1.0), [], [rmask])
            P.op("dve", lambda e: e.memset(rmask[:, :, 0:1], 0.0), [], [rmask])
            P.op("dve", lambda e: e.memset(nmask[:], 0.0), [], [nmask])
            P.op("dve", lambda e: e.memset(nmask[:, :, 0:1], -1.0e30), [], [nmask])
            for d in (0, 1):
                with ExitStack() as st:
                    rev = (lambda ap: ap) if d == 0 else (lambda ap: ap[:, ::-1])
                    g = {n: sbt(st, "mg" + n, [R, WD]) for n in ("I", "A", "b", "cb", "M", "nM", "wi", "en", "wg", "t")}
                    cl = {n: sbt(st, "mc" + n, [R, CW]) for n in ("ms", "Ml", "t")}
                    ch4 = {n: sbt(st, "m4" + n, [4, NCH]) for n in ("al", "bm", "mn", "ms")}
                    mini = sbt(st, "mmini", [4, 1])
                    G = lambda n: g[n][:, :]
                    G3 = lambda n: g[n][:, :].rearrange("r (c w) -> r c w", w=128)
                    for h in range(4):
                        rs_ = slice(h * NBLK, (h + 1) * NBLK)
                        P.dma("sp", g["I"][rs_, :], GM[d * 8 + h, :].rearrange("(b w) -> b w", w=WD), GM.bs, [g["I"]])
                        P.dma("pool", g["A"][rs_, :], GM[d * 8 + 4 + h, :].rearrange("(b w) -> b w", w=WD), GM.bs, [g["A"]])
                    act(G("A"), G("A"), AF.Exp, [g["A"]], [g["A"]], scale=-1.0)
                    act(G("A"), G("A"), AF.Ln, [g["A"]], [g["A"]], bias=1.0)
                    rm2 = rmask[0:R, :, :].rearrange("r c w -> r (c w)")
                    nm2 = nmask[0:R, :, :].rearrange("r c w -> r (c w)")
                    P.op("dve", lambda e: e.tensor_tensor_scan(out=rev(G("t")), data0=rm2, data1=rev(G("A")), initial=0.0,
                                                               op0=ALU.mult, op1=ALU.add), [g["A"], rmask], [g["t"]])
                    tt(G("b"), G("I"), G("t"), ALU.add, [g["I"], g["t"]], [g["b"]])
                    P.op("dve", lambda e: e.tensor_tensor_scan(out=rev(G("cb")), data0=nm2, data1=rev(G("b")), initial=-1.0e30,
                                                               op0=ALU.add, op1=ALU.max), [g["b"], nmask], [g["cb"]])
                    lastw = 127 if d == 0 else 0
                    ts(cl["t"][:, :], G3("t")[:, :, lastw], -1.0, None, ALU.mult, None, [g["t"]], [cl["t"]])
                    P.dma("sp", MS[0].rearrange("h (b c) -> (h b) c", c=CW), cl["t"][:, :], [cl["t"]], [MSb[0]])
                    P.dma("sp", MS[1].rearrange("h (b c) -> (h b) c", c=CW), G3("cb")[:, :, lastw], [g["cb"]], [MSb[1]])
                    P.dma("sp", ch4["al"][:, :], MS[0], [MSb[0]], [ch4["al"]])
                    P.dma("sp", ch4["bm"][:, :], MS[1], [MSb[1]], [ch4["bm"]])
                    P.op("dve", lambda e: e.memset(mini[:], 0.0), [], [mini])
                    for sg_ in (range(NSEG) if d == 0 else range(NSEG - 1, -1, -1)):
                        c0, c1 = sg_ * NCS, (sg_ + 1) * NCS
                        P.op("dve", lambda e: e.tensor_tensor_scan(out=rev(ch4["mn"][:, c0:c1]), data0=rev(ch4["bm"][:, c0:c1]),
                                                                   data1=rev(ch4["al"][:, c0:c1]), initial=mini[:, 0:1],
                                                                   op0=ALU.max, op1=ALU.add), [ch4["bm"], ch4["al"], mini], [ch4["mn"]])
                        if d == 0:
                            P.op("dve", lambda e: e.tensor_copy(out=ch4["ms"][:, c0:c0 + 1], in_=mini[:, :]), [mini], [ch4["ms"]])
                            if NCS > 1:
                                P.op("dve", lambda e: e.tensor_copy(out=ch4["ms"][:, c0 + 1:c1], in_=ch4["mn"][:, c0:c1 - 1]),
                                     [ch4["mn"]], [ch4["ms"]])
                            lastm = ch4["mn"][:, c1 - 1:c1]
                        else:
                            P.op("dve", lambda e: e.tensor_copy(out=ch4["ms"][:, c1 - 1:c1], in_=mini[:, :]), [mini], [ch4["ms"]])
                            if NCS > 1:
                                P.op("dve", lambda e: e.tensor_copy(out=ch4["ms"][:, c0:c1 - 1], in_=ch4["mn"][:, c0 + 1:c1]),
                                     [ch4["mn"]], [ch4["ms"]])
                            lastm = ch4["mn"][:, c0:c0 + 1]
                        ts(mini[:], lastm, linkc[0:4, 0:1], None, ALU.mult, None, [ch4["mn"], linkc], [mini])
                    P.dma("sp", MS[2], ch4["ms"][:, :], [ch4["ms"]], [MSb[2]])
                    P.dma("sp", cl["ms"][:, :], MS[2].rearrange("h (b c) -> (h b) c", c=CW), [MSb[2]], [cl["ms"]])
                    msb = cl["ms"][:, :].unsqueeze(2).to_broadcast([R, CW, 128])
                    tt(G3("M"), G3("cb"), msb, ALU.max, [g["cb"], cl["ms"]], [g["M"]])
                    ts(G("nM"), G("M"), -1.0, None, ALU.mult, None, [g["M"]], [g["nM"]])
                    tt(G3("wi"), G3("nM"), msb, ALU.add, [g["nM"], cl["ms"]], [g["wi"]])
                    act(G("wi"), G("wi"), AF.Exp, [g["wi"]], [g["wi"]])
                    tt(G("en"), G("t"), G("M"), ALU.subtract, [g["t"], g["M"]], [g["en"]])
                    act(G("en"), G("en"), AF.Exp, [g["en"]], [g["en"]])
                    P.op("dve", lambda e: e.tensor_copy(out=cl["Ml"][:, :], in_=G3("M")[:, :, lastw]), [g["M"]], [cl["Ml"]])
                    tt(G3("wg"), G3("b"), cl["Ml"][:, :].unsqueeze(2).to_broadcast([R, CW, 128]), ALU.subtract,
                       [g["b"], cl["Ml"]], [g["wg"]])
                    act(G("wg"), G("wg"), AF.Exp, [g["wg"]], [g["wg"]])
                    if MSTOP == 'm1':
                        P.barrier()
                        continue
                    xcs = [sbt(st, "mxc%d" % i, [128, 4, 128], BF16) for i in range(2)]
                    vas = [sbt(st, "mva%d" % i, [128, 4, 130], BF16) for i in range(2)]
                    for v in vas:
                        P.op("dve", lambda e: e.memset(v[:], 1.0), [], [v])
                    cols = [sbt(st, "mcol%d" % i, [128, 12]) for i in range(2)]
                    C32 = [sbt(st, "mC%d" % i, [128, 130]) for i in range(4)]
                    Cb = [sbt(st, "mCb%d" % i, [128, 130], BF16) for i in range(4)]
                    for h in range(4):
                        P.op("dve", lambda e: e.memset(C32[h][:], 0.0), [], [C32[h]])
                        P.op("dve", lambda e: e.memset(Cb[h][:], 0.0), [], [Cb[h]])
                    W2 = lambda nm, i: [sbt(st, "m%s%d" % (nm, k), [128, 128], BF16 if i else F32) for k in range(2)]
                    qTs, kTs, kts, STs, qts = W2("qT", 1), W2("kT", 1), W2("kt", 1), W2("ST", 1), W2("qt", 1)
                    ETs, wbs, sfs = W2("ET", 0), W2("wb", 0), W2("sf", 0)
                    dn = [sbt(st, "mdn%d" % k, [128, 2]) for k in range(2)]
                    hst = [sbt(st, "mhst%d" % k, [128, 512]) for k in range(2)]
                    hfs = [sbt(st, "mhf%d" % k, [128, 512]) for k in range(2)]
                    oms = [sbt(st, "moms%d" % k, [128, 512]) for k in range(2)]
                    bst6 = sbt(st, "mbst", [128, 4, 6])
                    mv = sbt(st, "mmv", [128, 4, 2])
                    rsd = sbt(st, "mrsd", [128, 4])
                    hbn = [sbt(st, "mhbn%d" % k, [128, 512], BF16) for k in range(2)]
                    hbt = [sbt(st, "mhbt%d" % k, [128, 4, 128], BF16) for k in range(2)]
                    mask = maskF if d == 0 else maskB
                    order = list(range(NCH)) if d == 0 else list(range(NCH - 1, -1, -1))
                    n = 0
                    for it, c in enumerate(order):
                        blk, cwi = c // CW, c % CW
                        csl = slice(cwi * 128, (cwi + 1) * 128)
                        tsl = slice(c * 128, (c + 1) * 128)
                        ti = (c * 128) // TT
                        xc, va, col = xcs[it % 2], vas[it % 2], cols[it % 2]
                        P.dma("sp", xc[:], XCB[:, tsl].rearrange("(k p) t -> p k t", p=128), [XCB.bs[ti]], [xc])
                        P.dma("pool", va[:, :, 0:128], VM[tsl, :].rearrange("t (h e) -> t h e", h=4), [VM.bs[ti]], [va])
                        if d == 1:
                            hf, om_ = hfs[it % 2], oms[it % 2]
                            P.dma("sp", hf[:], HF[tsl, :], [HF.bs[ti]], [hf])
                            P.dma("pool", om_[:], OMS[tsl, :], [OMS.bs[ti]], [om_])
                        bnd = c * 128 if d == 0 else (c + 1) * 128
                        if it > 0 and bnd % SL == 0:
                            for h in range(4):
                                ts(C32[h][:], C32[h][:], linkc[:, 0:1], None, ALU.mult, None, [C32[h], linkc], [C32[h]])
                                act(Cb[h][:], C32[h][:], AF.Copy, [C32[h]], [Cb[h]])
                        pc = ps[7]
                        selc = ident[0:R, blk:blk + 3 * NBLK + 1:NBLK]
                        for qi, nm in enumerate(("b", "wg", "en")):
                            mm(pc[:, 4 * qi:4 * qi + 4], g[nm][:, csl], selc, True, True, [g[nm], ident], [pc])
                        P.op("dve", lambda e: e.tensor_copy(out=col[:], in_=pc[:, 0:12]), [pc], [col])
                        hs = hst[it % 2]
                        for h in range(4):
                            r = h * NBLK + blk
                            k2 = n % 2
                            n += 1
                            qT, kT, kt, ST, qt, ET, wb, sf = qTs[k2], kTs[k2], kts[k2], STs[k2], qts[k2], ETs[k2], wbs[k2], sfs[k2]
                            pq, pk, pkt, pm, pw, pS = ps[0], ps[1], ps[2], ps[3], ps[4], ps[5]
                            pn = ps[6]
                            sel = ident[0:R, r:r + 1].to_broadcast([R, 128])
                            mm(pq[:, 0:128], wq[:, h, :], xc[:, h, :], True, True, [wq, xc], [pq])
                            mm(pk[:, 0:128], wk[:, h, :], xc[:, h, :], True, True, [wk, xc], [pk])
                            mm(pkt[:, 0:128], xc[:, h, :], wk[:, h, :], True, True, [wk, xc], [pkt])
                            mm(pm[:, 0:128], sel, g["nM"][:, csl], True, True, [ident, g["nM"]], [pm])
                            mm(pw[:, 0:128], sel, g["wi"][:, csl], True, True, [ident, g["wi"]], [pw])
                            act(qT[:], pq[:, 0:128], AF.Copy, [pq], [qT])
                            act(kT[:], pk[:, 0:128], AF.Copy, [pk], [kT])
                            ts(kt[:], pkt[:, 0:128], col[:, 4 + h:5 + h], None, ALU.mult, None, [pkt, col], [kt])
                            ts(ET[:], pm[:, 0:128], col[:, h:h + 1], 0.0, ALU.add, ALU.min, [pm, col], [ET])
                            act(ET[:], ET[:], AF.Exp, [ET], [ET])
                            tt(ET[:], ET[:], mask[:], ALU.mult, [ET, mask], [ET])
                            mm(pS[:, 0:128], kT[:], qT[:], True, True, [kT, qT], [pS])
                            tt(ST[:], pS[:, 0:128], ET[:], ALU.mult, [pS, ET], [ST])
                            act(wb[:], pw[:, 0:128], AF.Copy, [pw], [wb])
                            tt(qt[:], pq[:, 0:128], wb[:], ALU.mult, [pq, wb], [qt])
                            mm(pn[:, 0:130], ST[:], va[:, h, :], True, False, [ST, va], [pn])
                            mm(pn[:, 0:130], qt[:], Cb[h][:], False, True, [qt, Cb[h]], [pn])
                            mm(pkt[:, 0:130], kt[:], va[:, h, :], True, True, [kt, va], [pkt])
                            dcol = slice(127, 128) if d == 0 else slice(0, 1)
                            stt(C32[h][:], C32[h][:], wb[:, dcol], pkt[:, 0:130], ALU.mult, ALU.add, [C32[h], wb, pkt], [C32[h]])
                            act(Cb[h][:], C32[h][:], AF.Copy, [C32[h]], [Cb[h]])
                            dd = dn[k2]
                            act(dd[:, 0:1], pn[:, 128:129], AF.Abs, [pn], [dd])
                            ts(dd[:, 0:1], dd[:, 0:1], col[:, 8 + h:9 + h], None, ALU.max, None, [dd, col], [dd])
                            P.op("dve", lambda e: e.reciprocal(out=dd[:, 1:2], in_=dd[:, 0:1]), [dd], [dd])
                            hsl = slice(h * 128, (h + 1) * 128)
                            if d == 0:
                                ts(hs[:, hsl], pn[:, 0:128], dd[:, 1:2], None, ALU.mult, None, [pn, dd], [hs])
                            else:
                                stt(hs[:, hsl], pn[:, 0:128], dd[:, 1:2], hf[:, hsl], ALU.mult, ALU.add, [pn, dd, hf], [hs])
                        if d == 0:
                            P.dma("sp", HF[tsl, :], hs[:], [hs], [HF.bs[ti]])
                        else:
                            for h in range(4):
                                P.op("dve", lambda e: e.bn_stats(out=bst6[:, h, :], in_=hs[:, h * 128:(h + 1) * 128]), [hs], [bst6])
                                P.op("dve", lambda e: e.bn_aggr(out=mv[:, h, :], in_=bst6[:, h, :]), [bst6], [mv])
                            ts(rsd[:], mv[:, :, 1], LN_EPS, None, ALU.add, None, [mv], [rsd])
                            act(rsd[:], rsd[:], AF.Ln, [rsd], [rsd])
                            act(rsd[:], rsd[:], AF.Exp, [rsd], [rsd], scale=-0.5)
                            hb = hbn[it % 2]
                            for h in range(4):
                                hsl = slice(h * 128, (h + 1) * 128)
                                ts(hs[:, hsl], hs[:, hsl], mv[:, h, 0:1], rsd[:, h:h + 1], ALU.subtract, ALU.mult, [hs, mv, rsd], [hs])
                            tt(hb[:], hs[:], om_[:], ALU.mult, [hs, om_], [hb])
                            ht = hbt[it % 2]
                            pt_ = ps[7]
                            for h in range(4):
                                mm(pt_[:, h * 128:(h + 1) * 128], hb[:, h * 128:(h + 1) * 128], identb[:], True, True, [hb, identb], [pt_])
                            act(ht[:], pt_[:, :].rearrange("p (k t) -> p k t", k=4), AF.Copy, [pt_], [ht])
                            P.dma("sp", HBT[:, tsl].rearrange("(k p) t -> p k t", p=128), ht[:], [ht], [HBT.bs[ti]])
                    P.barrier()


    def gen_att(l, st):
        SCALE = 96.0 ** -0.5
        kT = sbt(st, "xk", [98, T], BF16)
        vh = sbt(st, "xv", [128, NCH, 128], BF16)
        qTs = [sbt(st, "xq%d" % i, [98, TT], BF16) for i in range(2)]
        pts = [sbt(st, "xp%d" % i, [128, TT], BF16) for i in range(3)]
        rlt = sbt(st, "xrl", [128, TT])
        osb = [sbt(st, "xo%d" % i, [64, TT], BF16) for i in range(2)]
        sbank = [ps[0], ps[1], ps[2]]
        abank = [ps[3], ps[4]]
        n = 0
        units = [(h, i) for h in range(8) for i in range(NT)]
        pend_evac = []

        def load_q(u):
            h, i = units[u]
            P.dma("sp", qTs[u % 2][:], QT[h, :, i * TT:(i + 1) * TT], [QT.bs[i]], [qTs[u % 2]])
        load_q(0)
        for u, (h, i) in enumerate(units):
            if i == 0:
                P.dma("sp", kT[:], KT[h, :, :], KT.bs, [kT])
                P.dma("sp", vh[:], VA[:, h, :].rearrange("(n p) d -> p n d", p=128), VA.bs, [vh])
            qs = slice(i * TT, (i + 1) * TT)
            qT = qTs[u % 2]
            acc = abank[u % 2]
            segq = (i * TT) // SL

            def score(kb):
                R = 97 if (kb * 128) // SL == segq else 98
                pb = sbank[(n + kb) % 3]
                mm(pb[:, :], kT[0:R, kb * 128:(kb + 1) * 128], qT[0:R, :], True, True, [kT, qT], [pb])
                return pb
            pend = [score(0)]
            if NCH > 1:
                pend.append(score(1))
            for kb in range(NCH):
                pb = pend.pop(0)
                pt = pts[(n + kb) % 3]
                act(pt[:], pb[:, :], AF.Exp, [pb], [pt], scale=SCALE)
                if kb + 2 < NCH:
                    pend.append(score(kb + 2))
                mm(acc[:, :], vh[:, kb, :], pt[:], kb == 0, kb == NCH - 1, [vh, pt], [acc])
                if kb == min(6, NCH - 1):
                    while pend_evac:
                        pend_evac.pop(0)()
                if kb == NCH // 2 and u + 1 < len(units):
                    load_q(u + 1)
                yield
            n += NCH

            def evac(acc=acc, h=h, i=i, qs=qs):
                P.op("dve", lambda e: e.reciprocal(out=rlt[64:128, :], in_=acc[64:128, :]), [acc], [rlt])
                o = osb[i % 2]
                tt(o[:], acc[0:64, :], rlt[64:128, :], ALU.mult, [acc, rlt], [o])
                P.dma("pool", OT[h * 64:(h + 1) * 64, qs], o[:], [o], [OT.bs[i]])
            pend_evac.append(evac)
        while pend_evac:
            pend_evac.pop(0)()

    import os as _os2
    PENG = _os2.environ.get('PENG', 'dve')

    def gen_s5(l, st):
        pending = []
        rnd = [0]

        def later(k, fn):
            pending.append((rnd[0] + k, fn))

        def tick():
            rnd[0] += 1
            due = [p for p in pending if p[0] <= rnd[0]]
            for p in due:
                pending.remove(p)
            for p in due:
                p[1]()

        CLr = sbt(st, "sCLr", [128, 16, 128], BF16)
        CLi = sbt(st, "sCLi", [128, 16, 128], BF16)
        wglu = sbt(st, "swglu", [128, 4, 2 * D], BF16)
        dsk = sbt(st, "sdsk", [128, 4])
        pvr, pvi, psy = ps[5], ps[6], ps[7]
        NW = 2
        wk = [[sbt(st, "sw%s%d" % (n, i), [128, TT]) for i in range(NW)] for n in "abcdef"]

        class Al:
            def __init__(self, par, view):
                self.t = view
                self.b = par.b

            def __getitem__(self, k):
                return self.t[k]
        v3 = lambda tl: tl[:, 0:256].rearrange("p (a b) -> p a b", a=16)
        Zt = Al(wk[0][0], wk[0][0][:, :].rearrange("p (a b) -> p a b", a=4))
        Bt = [Al(wk[1][i], v3(wk[1][i])) for i in range(2)]
        Bp = [Al(wk[2][i], v3(wk[2][i])) for i in range(2)]
        tmp = Al(wk[3][0], v3(wk[3][0]))
        X = Al(wk[4][0], wk[4][0][:, :].rearrange("p (a b c) -> p a b c", a=16, b=2))
        for k in range(4):
            load_cast(wglu[:, k, :], W['w_glu'][l, k * 128:(k + 1) * 128, :], wglu, 2 * D)
            yield 2.0
        P.dma("sp", dsk[:, :], W['s5_d'][l].rearrange("(c g) w -> (g w) c", c=4), [], [dsk])
        for i, (nm, CL) in enumerate((('s5_c_re', CLr), ('s5_c_im', CLi))):
            P.op("dve", lambda e: e.memset(Zt[:], 0.0), [], [Zt])
            P.op("dve", lambda e: e.memset(CL[:], 0.0), [], [CL])
            for ch in range(4):
                for jj in range(4):
                    for two in range(2):
                        g = 2 * (4 * ch + jj) + two
                        P.dma("sp" if two else "pool", Zt[32 * jj + 16 * two:32 * jj + 16 * two + 16, ch, 64 * two:64 * two + 64],
                              W[nm][l, g], [], [Zt])
            yield 2.0
            for ch in range(4):
                mm(psy[:, 0:128], Zt[:, ch, :], ident[:], True, True, [Zt, ident], [psy])
                for jj in range(4):
                    P.op("dve", lambda e: e.tensor_scalar(out=CL[:, 4 * ch + jj, 32 * jj:32 * jj + 32], in0=psy[:, 32 * jj:32 * jj + 32],
                                                          scalar1=(1.0 if i == 0 else -1.0), scalar2=None, op0=ALU.mult), [psy], [CL])
                yield 2.0
        sm = {n: sbt(st, "s5" + n, [128, 16]) for n in
              ("are", "aim", "dt", "rmag", "th", "cs", "sn", "q", "abr", "abi", "nr", "ni", "den", "bsr", "bsi", "t")}
        A = lambda n: sm[n][:, :]
        BLr = sbt(st, "sBLr", [128, 16, 128], BF16)
        BLi = sbt(st, "sBLi", [128, 16, 128], BF16)
        cosT = sbt(st, "scosT", [128, 16, TT])
        sinT = sbt(st, "ssinT", [128, 16, TT])
        ti32 = sbt(st, "sti", [128, TT], I32)
        tau = sbt(st, "stau", [128, TT])
        ang = sbt(st, "sang", [128, TT])
        qq = sbt(st, "sqq", [128, TT])
        ub = sbt(st, "sub", [128, 4, TT], BF16)
        xb_ = [[sbt(st, "sx%s%d" % (n, i), [128, TT], BF16) for i in range(4)] for n in "ri"]
        tn = [sbt(st, "stn%d" % i, [128, 4]) for i in range(2)]
        car = [sbt(st, "scar%d" % i, [128, 16]) for i in range(2)]
        y1t = [sbt(st, "sy1%d" % i, [128, TT]) for i in range(2)]
        yg = sbt(st, "syg", [128, 4, TT], BF16)
        sg = y1t
        for d in (0, 1):
            for two in range(2):
                prt = slice(two * 64, two * 64 + 64)
                P.dma("sp", sm["are"][prt, :], W['s5_a_re'][l, d].rearrange("(j two) p -> two p j", two=2)[two], [], [sm["are"]])
                P.dma("sp", sm["aim"][prt, :], W['s5_a_im'][l, d].rearrange("(j two) p -> two p j", two=2)[two], [], [sm["aim"]])
                P.dma("sp", sm["dt"][prt, :], W['s5_log_dt'][l, d].rearrange("(j two) -> two j", two=2)[two].partition_broadcast(64),
                      [], [sm["dt"]])
            if True:
                for i, nm in enumerate(('s5_b_re', 's5_b_im')):
                    for two in range(2):
                        P.dma("pool", Bt[i][two * 64:two * 64 + 64, :, :],
                              W[nm][l].rearrange("(j two) p c -> two p j c", two=2)[two], [], [Bt[i]])
            yield 2.0
            act(A("dt"), A("dt"), AF.Exp, [sm["dt"]], [sm["dt"]])
            tt(A("rmag"), A("are"), A("dt"), ALU.mult, [sm["are"], sm["dt"]], [sm["rmag"]])
            act(A("rmag"), A("rmag"), AF.Exp, [sm["rmag"]], [sm["rmag"]])
            tt(A("th"), A("aim"), A("dt"), ALU.mult, [sm["aim"], sm["dt"]], [sm["th"]])
            yield 2.0
            emit_sin(sm["sn"], A("sn"), sm["th"], A("th"), sm["q"], A("q"), 0.0)
            yield 2.0
            emit_sin(sm["cs"], A("cs"), sm["th"], A("th"), sm["q"], A("q"), math.pi / 2.0)
            yield 2.0
            tt(A("abr"), A("rmag"), A("cs"), ALU.mult, [sm["rmag"], sm["cs"]], [sm["abr"]])
            tt(A("abi"), A("rmag"), A("sn"), ALU.mult, [sm["rmag"], sm["sn"]], [sm["abi"]])
            ts(A("abr"), A("abr"), -1.0, None, ALU.add, None, [sm["abr"]], [sm["abr"]])
            tt(A("nr"), A("abr"), A("are"), ALU.mult, [sm["abr"], sm["are"]], [sm["nr"]])
            tt(A("t"), A("abi"), A("aim"), ALU.mult, [sm["abi"], sm["aim"]], [sm["t"]])
            tt(A("nr"), A("nr"), A("t"), ALU.add, [sm["nr"], sm["t"]], [sm["nr"]])
            tt(A("ni"), A("abi"), A("are"), ALU.mult, [sm["abi"], sm["are"]], [sm["ni"]])
            tt(A("t"), A("abr"), A("aim"), ALU.mult, [sm["abr"], sm["aim"]], [sm["t"]])
            tt(A("ni"), A("ni"), A("t"), ALU.subtract, [sm["ni"], sm["t"]], [sm["ni"]])
            yield 2.0
            tt(A("den"), A("are"), A("are"), ALU.mult, [sm["are"]], [sm["den"]])
            tt(A("t"), A("aim"), A("aim"), ALU.mult, [sm["aim"]], [sm["t"]])
            tt(A("den"), A("den"), A("t"), ALU.add, [sm["den"], sm["t"]], [sm["den"]])
            P.op("dve", lambda e: e.reciprocal(out=A("den"), in_=A("den")), [sm["den"]], [sm["den"]])
            tt(A("bsr"), A("nr"), A("den"), ALU.mult, [sm["nr"], sm["den"]], [sm["bsr"]])
            tt(A("bsi"), A("ni"), A("den"), ALU.mult, [sm["ni"], sm["den"]], [sm["bsi"]])
            yield 2.0
            bc = lambda n: sm[n][:, :].unsqueeze(2).to_broadcast([128, 16, 16])
            tt(Bp[0][:], Bt[0][:], bc("bsr"), ALU.mult, [Bt[0], sm["bsr"]], [Bp[0]])
            tt(tmp[:], Bt[1][:], bc("bsi"), ALU.mult, [Bt[1], sm["bsi"]], [tmp])
            tt(Bp[0][:], Bp[0][:], tmp[:], ALU.subtract, [Bp[0], tmp], [Bp[0]])
            tt(Bp[1][:], Bt[1][:], bc("bsr"), ALU.mult, [Bt[1], sm["bsr"]], [Bp[1]])
            tt(tmp[:], Bt[0][:], bc("bsi"), ALU.mult, [Bt[0], sm["bsi"]], [tmp])
            tt(Bp[1][:], Bp[1][:], tmp[:], ALU.add, [Bp[1], tmp], [Bp[1]])
            yield 2.0
            for i, BL in enumerate((BLr, BLi)):
                P.op("dve", lambda e: e.memset(BL[:], 0.0), [], [BL])
                P.op("dve", lambda e: e.memset(X[:], 0.0), [], [X])
                P.op("dve", lambda e: e.tensor_copy(out=X[0:64, :, 0, :], in_=Bp[i][0:64, :, :]), [Bp[i]], [X])
                P.op("dve", lambda e: e.tensor_copy(out=X[64:128, :, 1, :], in_=Bp[i][64:128, :, :]), [Bp[i]], [X])
                yield 2.0
                for ch in range(4):
                    mm(psy[:, 0:128], X[:, 4 * ch:4 * ch + 4, :, :].rearrange("p a b c -> p (a b c)"), ident[:], True, True,
                       [X, ident], [psy])
                    yield 2.0
                    for jj in range(4):
                        P.op("dve", lambda e: e.tensor_copy(out=BL[32 * jj:32 * jj + 32, 4 * ch + jj, :], in_=psy[32 * jj:32 * jj + 32, 0:128]),
                             [psy], [BL])
            if d == 0:
                P.op("pool", lambda e: e.iota(ti32[:], pattern=[[1, TT]], base=1, channel_multiplier=0), [], [ti32])
            else:
                P.op("pool", lambda e: e.iota(ti32[:], pattern=[[-1, TT]], base=TT, channel_multiplier=0), [], [ti32])
            P.op("dve", lambda e: e.tensor_copy(out=tau[:], in_=ti32[:]), [ti32], [tau])
            for j in range(16):
                ts(ang[:], tau[:], sm["th"][:, j:j + 1], None, ALU.mult, None, [tau, sm["th"]], [ang])
                emit_sin(sinT, sinT[:, j, :], ang, ang[:], qq, qq[:], 0.0)
                yield 5.0
                emit_sin(cosT, cosT[:, j, :], ang, ang[:], qq, qq[:], math.pi / 2.0)
                yield 5.0
            P.op("dve", lambda e: e.memset(car[0][:], 0.0), [], [car[0]])
            P.op("dve", lambda e: e.memset(car[1][:], 0.0), [], [car[1]])
            order = list(range(NT)) if d == 0 else list(range(NT - 1, -1, -1))
            nblk = 0
            lastc = TT - 1 if d == 0 else 0
            for it, ti in enumerate(order):
                t0 = ti * TT
                tsl = slice(t0, t0 + TT)
                P.dma("sp", ub[:], USB[:, tsl].rearrange("(c p) t -> p c t", p=128), [USB.bs[ti]], [ub])
                if it > 0:
                    bnd = t0 if d == 0 else t0 + TT
                    if bnd % SL == 0:
                        for cc in car:
                            ts(cc[:], cc[:], linkc[:, 0:1], None, ALU.mult, None, [cc, linkc], [cc])
                for ch in range(4):
                    for jj in range(4):
                        j = 4 * ch + jj
                        w = [wk[i][nblk % NW] for i in range(6)]
                        xr_b, xi_b = xb_[0][nblk % 4], xb_[1][nblk % 4]
                        tnn = tn[nblk % 2]
                        nblk += 1
                        mm(pvr[:, :], BLr[:, j, :], ub[:, ch, :], True, True, [BLr, ub], [pvr])
                        mm(pvi[:, :], BLi[:, j, :], ub[:, ch, :], True, True, [BLi, ub], [pvi])
                        c_, s_ = cosT[:, j, :], sinT[:, j, :]
                        cl_, sl_ = cosT[:, j, lastc:lastc + 1], sinT[:, j, lastc:lastc + 1]
                        tt(w[0][:], pvr[:, :], c_, ALU.mult, [pvr, cosT], [w[0]])
                        tt(w[1][:], pvi[:, :], s_, ALU.mult, [pvi, sinT], [w[1]])
                        tt(w[2][:], pvi[:, :], c_, ALU.mult, [pvi, cosT], [w[2]])
                        tt(w[3][:], pvr[:, :], s_, ALU.mult, [pvr, sinT], [w[3]])
                        tt(w[0][:], w[0][:], w[1][:], ALU.add, [w[0], w[1]], [w[0]])
                        tt(w[2][:], w[2][:], w[3][:], ALU.subtract, [w[2], w[3]], [w[2]])
                        rb = sm["rmag"][:, j:j + 1].to_broadcast([128, TT])
                        for (src, dst, cc) in ((w[0], w[4], car[0]), (w[2], w[5], car[1])):
                            if d == 0:
                                P.op("dve", lambda e: e.tensor_tensor_scan(out=dst[:], data0=rb, data1=src[:],
                                                                           initial=cc[:, j:j + 1], op0=ALU.mult, op1=ALU.add),
                                     [src, cc, sm["rmag"]], [dst])
                            else:
                                P.op("dve", lambda e: e.tensor_tensor_scan(out=dst[:, ::-1], data0=rb, data1=src[:, ::-1],
                                                                           initial=cc[:, j:j + 1], op0=ALU.mult, op1=ALU.add),
                                     [src, cc, sm["rmag"]], [dst])
                        tt(w[0][:], w[4][:], c_, ALU.mult, [w[4], cosT], [w[0]], eng=PENG)
                        tt(w[1][:], w[4][:], s_, ALU.mult, [w[4], sinT], [w[1]], eng=PENG)
                        tt(w[2][:], w[5][:], s_, ALU.mult, [w[5], sinT], [w[2]], eng=PENG)
                        tt(w[3][:], w[5][:], c_, ALU.mult, [w[5], cosT], [w[3]], eng=PENG)
                        tt(xr_b[:], w[0][:], w[2][:], ALU.subtract, [w[0], w[2]], [xr_b])
                        tt(xi_b[:], w[1][:], w[3][:], ALU.add, [w[1], w[3]], [xi_b])
                        lc = slice(lastc, lastc + 1)
                        tt(car[0][:, j:j + 1], w[0][:, lc], w[2][:, lc], ALU.subtract, [w[0], w[2]], [car[0]])
                        tt(car[1][:, j:j + 1], w[1][:, lc], w[3][:, lc], ALU.add, [w[1], w[3]], [car[1]])

                        def cmm(j=j, jj=jj, xr_b=xr_b, xi_b=xi_b):
                            mm(psy[:, :], CLr[:, j, :], xr_b[:], jj == 0, False, [CLr, xr_b], [psy])
                            mm(psy[:, :], CLi[:, j, :], xi_b[:], False, jj == 3, [CLi, xi_b], [psy])
                        later(3, cmm)
                        if jj == 3:
                            y1 = y1t[ch % 2]
                            if d == 0:
                                P.dma("pool", y1[:], US[ch * 128:(ch + 1) * 128, tsl], [US.bs[ti]], [y1])

                                def fin(ch=ch, y1=y1, tsl=tsl, ti=ti):
                                    stt(y1[:], y1[:], dsk[:, ch:ch + 1], psy[:, :], ALU.mult, ALU.add, [y1, dsk, psy], [y1])
                                    P.dma("pool", Y1[ch * 128:(ch + 1) * 128, tsl], y1[:], [y1], [Y1.bs[ti]])
                                later(4, fin)
                            else:
                                P.dma("pool", y1[:], Y1[ch * 128:(ch + 1) * 128, tsl], [Y1.bs[ti]], [y1])

                                def fin(ch=ch, y1=y1):
                                    tt(y1[:], y1[:], psy[:, :], ALU.add, [y1, psy], [y1])
                                later(4, fin)

                                def fin2(ch=ch, y1=y1):
                                    act(yg[:, ch, :], y1[:], AF.Gelu_apprx_tanh, [y1], [yg])
                                later(5, fin2)
                        yield 11.0
                        tick()
                if d == 1:
                    for _ in range(6):
                        yield 0.3
                        tick()
                    for o in range(8):
                        for k in range(4):
                            mm(pvr[:, :], wglu[:, k, o * 128:(o + 1) * 128], yg[:, k, :], k == 0, k == 3, [wglu, yg], [pvr])
                        for k in range(4):
                            mm(pvi[:, :], wglu[:, k, D + o * 128:D + (o + 1) * 128], yg[:, k, :], k == 0, k == 3, [wglu, yg], [pvi])
                        s1, s2 = sg[0], sg[1]
                        yield 0.8
                        tick()
                        act(s1[:], pvi[:, :], AF.Sigmoid, [pvi], [s1])
                        yield 0.8
                        tick()
                        tt(s2[:], s1[:], pvr[:, :], ALU.mult, [s1, pvr], [s2])
                        P.dma("sp" if o % 2 else "pool", YC[o * 128:(o + 1) * 128, tsl], s2[:], [s2], [YC.bs[ti]])
                else:
                    for _ in range(6):
                        yield 0.3
                        tick()
            for _ in range(6):
                yield 0.3
                tick()

    def phase_att_s5(l):
        with ExitStack() as st:
            ga = gen_att(l, st)
            gs = gen_s5(l, st)
            n_att = 8 * NT * NCH
            total_cost = 2 * NT * 16 * 11.0 + NT * 16 * 0.8 + 2 * NT * 4 * 0.3 + 2 * (32 * 5.0 + 40 * 2.0) + 30.0
            per_us = n_att / total_cost
            acc_ = 0.0
            a_done = s_done = False
            while not (a_done and s_done):
                if not s_done:
                    try:
                        c = next(gs)
                        acc_ += (c if c else 1.0) * per_us
                    except StopIteration:
                        s_done = True
                if s_done:
                    acc_ += 64
                while acc_ >= 1.0 and not a_done:
                    acc_ -= 1.0
                    try:
                        next(ga)
                    except StopIteration:
                        a_done = True
                if a_done:
                    acc_ = 0.0
            P.barrier()

    with nc.allow_non_contiguous_dma(reason="small strided parameter / layout-conversion DMAs"):
        phase0()
        if branches:
            phase_init()
        for l in range(DEPTH):
            if branches:
                phase_a(l)
            if stop_after == 'a':
                break
            if "b" in branches:
                phase_m(l)
            if "a" in branches and "c" in branches:
                phase_att_s5(l)
            else:
                if "a" in branches:
                    phase_att(l)
                if "c" in branches:
                    phase_s5(l)
            phase_mrg(l)
            phase_f1(l)
            phase_f2(l)
        phase_out()
    P.close()
    return nc, P


NCORES = 8
T_CORE = 8192
SEGLEN = 2048


def kernel(**inputs):
    xp = np.ascontiguousarray(inputs['x_prompt'], dtype=np.float32)
    xs = np.ascontiguousarray(inputs['x_sample'], dtype=np.float32)
    nc, _ = build(T_CORE, SEGLEN, 4)
    wts = {n: np.ascontiguousarray(inputs[n], dtype=np.float32) for n in WNAMES}
    in_maps = []
    for c in range(NCORES):
        if c < 4:
            x = xp[4 * c:4 * c + 4].reshape(T_CORE, D)
            link = np.zeros((1, 1), np.float32)
        else:
            x = xs[c - 4].reshape(T_CORE, D)
            link = np.ones((1, 1), np.float32)
        m = {"x": x, "link": link}
        m.update(wts)
        in_maps.append(m)
    res = run_bass_kernel_spmd(nc, in_maps, core_ids=list(range(NCORES)))
    ys = [np.asarray(r["y"], dtype=np.float32) for r in res.results]
    y_prompt = np.concatenate([ys[c].reshape(4, SEGLEN, D) for c in range(4)], axis=0)
    y_sample = np.stack([ys[c].reshape(T_CORE, D) for c in range(4, 8)], axis=0)
    return (y_prompt, y_sample)
```

```python
import math
import numpy as np
import concourse.bass as bass
import concourse.mybir as mybir
from concourse.bass_utils import run_bass_kernel_spmd
from contextlib import ExitStack

F32 = mybir.dt.float32
BF16 = mybir.dt.bfloat16
AF = mybir.ActivationFunctionType
ALU = mybir.AluOpType
AX = mybir.AxisListType

D = 1024
KD = 8
H_A = 8
NIN = 5552
DFF = 2816
NFC = 22
ALPHA = 8 ** 0.25
LN_EPS = 1e-5
TT = 512
O_CQ, O_CKV, O_KR, O_XM, O_VM, O_OM, O_GM, O_US, O_GP = 0, 256, 384, 416, 928, 1440, 1952, 1968, 2480
WNAMES = ['ln0_g', 'ln0_b', 'w_in', 'b_mlstm_gate', 'b_merge', 'q_norm_g', 'kv_norm_g', 'w_uq', 'w_ukv', 'w_proj_a',
          'conv_m_w', 'conv_m_b', 'w_q_m', 'w_k_m', 'mh_norm_g', 'w_proj_b', 's5_a_re', 's5_a_im', 's5_log_dt',
          's5_b_re', 's5_b_im', 's5_c_re', 's5_c_im', 's5_d', 'w_glu', 'w_o', 'ln1_g', 'ln1_b', 'w_up', 'conv_f_w',
          'conv_f_b', 'w_down', 'ln2_g', 'ln2_b']


class Buf:
    __slots__ = ("name", "lw", "rd")

    def __init__(self, name=""):
        self.name = name
        self.lw = None
        self.rd = {}


class Tl:
    def __init__(self, t, name=""):
        self.t = t
        self.b = Buf(name)

    def __getitem__(self, k):
        return self.t[k]


class DT:
    def __init__(self, ap, name, ntile):
        self.ap = ap
        self.bs = [Buf(name + str(i)) for i in range(ntile)]

    def __getitem__(self, k):
        return self.ap[k]


def _bl(xs):
    out = []
    for x in xs:
        if isinstance(x, Buf):
            out.append(x)
        elif isinstance(x, (list, tuple)):
            out.extend(_bl(x))
        else:
            out.append(x.b)
    return out


class Prog:
    COMPUTE = ("pe", "act", "dve", "pool")

    def __init__(self, nc, ndma=12):
        self.nc = nc
        self.es = ExitStack()
        self.eng = {"pe": nc.tensor, "act": nc.scalar, "dve": nc.vector, "pool": nc.gpsimd, "sp": nc.sync}
        self.sem = {}
        for e in self.COMPUTE:
            self.sem[e] = self.es.enter_context(nc.semaphore("s_" + e))
        self.cnt = {e: 0 for e in self.COMPUTE}
        self.dq = {}
        for q in ("sp", "pool"):
            sems = [self.es.enter_context(nc.semaphore("d_%s%d" % (q, i))) for i in range(ndma)]
            self.dq[q] = {"sems": sems, "tgt": [0] * ndma, "i": 0}
        self.waited = {e: {} for e in self.eng}
        self.semobj = {}
        self.ninstr = 0

    def _semkey(self, s):
        k = id(s)
        self.semobj[k] = s
        return k

    def _need(self, e, tok, deps):
        if tok is None:
            return
        k, v = tok
        if self.waited[e].get(k, 0) >= v:
            return
        if deps.get(k, 0) < v:
            deps[k] = v

    def _collect(self, e, reads, writes, is_dma=False):
        deps = {}
        own = self._semkey(self.sem[e]) if (e in self.COMPUTE and not is_dma) else None
        for b in reads:
            self._need(e, b.lw, deps)
        for b in writes:
            if b.lw is not None and b.lw[0] != own:
                self._need(e, b.lw, deps)
            for key, tok in b.rd.items():
                if tok[0] != own:
                    self._need(e, tok, deps)
        if e == "pe" and own in deps:
            del deps[own]
        for k, v in deps.items():
            self.eng[e].wait_ge(self.semobj[k], v)
            self.waited[e][k] = v
            self.ninstr += 1

    def _update(self, tok, reads, writes, rkey):
        for b in writes:
            b.lw = tok
            b.rd = {}
        for b in reads:
            b.rd[rkey] = tok

    def op(self, e, fn, reads=(), writes=()):
        reads = _bl(reads)
        writes = _bl(writes)
        self._collect(e, reads, writes)
        ins = fn(self.eng[e])
        self.cnt[e] += 1
        ins.then_inc(self.sem[e], 1)
        tok = (self._semkey(self.sem[e]), self.cnt[e])
        self._update(tok, reads, writes, e)
        self.ninstr += 1
        return tok

    def dma(self, q, out, in_, reads=(), writes=(), **kw):
        reads = _bl(reads)
        writes = _bl(writes)
        q = "pool" if type(out.tensor).__name__.startswith("DRam") else "sp"
        d = self.dq[q]
        i = d["i"]
        d["i"] = (i + 1) % len(d["sems"])
        s = d["sems"][i]
        k = self._semkey(s)
        if d["tgt"][i] > 0 and self.waited[q].get(k, 0) < d["tgt"][i]:
            self.eng[q].wait_ge(s, d["tgt"][i])
            self.waited[q][k] = d["tgt"][i]
        self._collect(q, reads, writes, is_dma=True)
        ins = self.eng[q].dma_start(out=out, in_=in_, **kw)
        d["tgt"][i] += 16
        ins.then_inc(s, 16)
        tok = (k, d["tgt"][i])
        self._update(tok, reads, writes, ("dma", k))
        self.ninstr += 1
        return tok

    def barrier(self):
        toks = [(self._semkey(self.sem[e]), self.cnt[e]) for e in self.COMPUTE if self.cnt[e] > 0]
        for q, d in self.dq.items():
            for s, t in zip(d["sems"], d["tgt"]):
                if t > 0:
                    toks.append((self._semkey(s), t))
        for e in self.eng:
            for k, v in toks:
                if e in self.COMPUTE and k == self._semkey(self.sem[e]):
                    continue
                if self.waited[e].get(k, 0) < v:
                    self.eng[e].wait_ge(self.semobj[k], v)
                    self.waited[e][k] = v
                    self.ninstr += 1

    def close(self):
        self.barrier()
        self.es.close()


def wshapes(L):
    return {
        'ln0_g': (D,), 'ln0_b': (D,), 'w_in': (L, D, NIN), 'b_mlstm_gate': (L, 2, 2, 4), 'b_merge': (L, 3, D),
        'q_norm_g': (L, 256), 'kv_norm_g': (L, 128), 'w_uq': (L, 256, 768), 'w_ukv': (L, 128, 1024),
        'w_proj_a': (L, 512, D), 'conv_m_w': (L, 3, 512), 'conv_m_b': (L, 512), 'w_q_m': (L, 4, 128, 128),
        'w_k_m': (L, 4, 128, 128), 'mh_norm_g': (L, 512), 'w_proj_b': (L, 512, D), 's5_a_re': (L, 2, 32, 64),
        's5_a_im': (L, 2, 32, 64), 's5_log_dt': (L, 2, 32), 's5_b_re': (L, 32, 64, 16), 's5_b_im': (L, 32, 64, 16),
        's5_c_re': (L, 32, 16, 64), 's5_c_im': (L, 32, 16, 64), 's5_d': (L, 32, 16), 'w_glu': (L, 512, 2 * D),
        'w_o': (L, D, D), 'ln1_g': (L, D), 'ln1_b': (L, D), 'w_up': (L, D, 2 * DFF), 'conv_f_w': (L, 3, DFF),
        'conv_f_b': (L, DFF), 'w_down': (L, DFF, D), 'ln2_g': (L, D), 'ln2_b': (L, D)}


def build(T, SL, DEPTH, dbg=(), branches=("a", "b", "c"), stop_after=None):
    NSEG = T // SL
    NT = T // TT
    NCH = T // 128
    nc = bass.Bass("TRN2", target_bir_lowering=False)
    x_in = nc.dram_tensor("x", [T, D], F32, kind="ExternalInput").ap()
    link_in = nc.dram_tensor("link", [1, 1], F32, kind="ExternalInput").ap()
    W = {n: nc.dram_tensor(n, list(s), F32, kind="ExternalInput").ap() for n, s in wshapes(DEPTH).items()}
    y_out = nc.dram_tensor("y", [T, D], F32, kind="ExternalOutput").ap()
    P = Prog(nc)
    es = P.es
    NSL = "allow_slow_non_contiguous"

    def scratch(name, shape, dt):
        kind = "ExternalOutput" if name in dbg else "Internal"
        return DT(nc.dram_tensor(name, list(shape), dt, kind=kind).ap(), name, NT)

    XT = scratch("XT", [D, T], F32)
    XTB = scratch("XTB", [D, T], BF16)
    AT = scratch("AT", [DFF, T], F32)
    OT = scratch("OT", [512, T], BF16)
    HBT = scratch("HBT", [512, T], BF16)
    YC = scratch("YC", [D, T], F32)
    QT = scratch("QT", [8, 98, T], BF16)
    KT = scratch("KT", [8, 98, T], BF16)
    VA = scratch("VA", [T, 8, 128], BF16)
    XM = scratch("XM", [512, T], F32)
    XCB = scratch("XCB", [512, T], BF16)
    VM = scratch("VM", [T, 512], BF16)
    OMS = scratch("OMS", [T, 512], F32)
    GM = scratch("GM", [16, T], F32)
    US = scratch("US", [512, T], F32)
    USB = scratch("USB", [512, T], BF16)
    HF = scratch("HF", [T, 512], F32)
    Y1 = scratch("Y1", [512, T], F32)
    RC = scratch("RC", [32, T], F32)
    RS = scratch("RS", [32, T], F32)

    uid = [0]

    def sbt(stack, name, shape, dt=F32):
        uid[0] += 1
        name = "%s_%d" % (name, uid[0])
        return Tl(stack.enter_context(nc.sbuf_tensor(name, list(shape), dt)), name)

    ps = [Tl(es.enter_context(nc.psum_tensor("ps%d" % i, [128, 512], F32)), "ps%d" % i) for i in range(8)]
    ident = sbt(es, "ident", [128, 128])
    identb = sbt(es, "identb", [128, 128], BF16)
    ones32 = sbt(es, "ones32", [128, 128])
    linkc = sbt(es, "linkc", [128, 1])
    lng = sbt(es, "lng", [128, 2 * DEPTH + 1, 8])
    lnb = sbt(es, "lnb", [128, 2 * DEPTH + 1, 8])
    stg = [sbt(es, "stg%d" % i, [128, 1024]) for i in range(3)]
    stgi = [0]

    def mm(out, lhsT, rhs, start, stop, reads, writes):
        return P.op("pe", lambda e: e.matmul(out, lhsT=lhsT, rhs=rhs, start=start, stop=stop), reads, writes)

    def act(out, in_, func, reads, writes, **kw):
        return P.op("act", lambda e: e.activation(out=out, in_=in_, func=func, **kw), reads, writes)

    def tt(out, in0, in1, op, reads, writes, eng="dve"):
        return P.op(eng, lambda e: e.tensor_tensor(out=out, in0=in0, in1=in1, op=op), reads, writes)

    def ts(out, in0, s1, s2, op0, op1, reads, writes, eng="dve"):
        if s2 is None:
            return P.op(eng, lambda e: e.tensor_scalar(out=out, in0=in0, scalar1=s1, scalar2=None, op0=op0), reads, writes)
        return P.op(eng, lambda e: e.tensor_scalar(out=out, in0=in0, scalar1=s1, scalar2=s2, op0=op0, op1=op1), reads, writes)

    def stt(out, in0, scalar, in1, op0, op1, reads, writes):
        return P.op("dve", lambda e: e.scalar_tensor_tensor(out=out, in0=in0, scalar=scalar, in1=in1, op0=op0, op1=op1),
                    reads, writes)

    def load_cast(dst, src, dtl, n, scale=1.0, sreads=()):
        o = 0
        while o < n:
            w = min(1024, n - o)
            s = stg[stgi[0] % 3]
            stgi[0] += 1
            np_ = dst.shape[0]
            P.dma("sp", s[0:np_, 0:w], src[:, o:o + w], [], [s])
            act(dst[:, o:o + w], s[0:np_, 0:w], AF.Copy if isinstance(scale, float) else AF.Identity,
                [s] + list(sreads), [dtl], scale=scale)
            o += w

    P.op("pool", lambda e: e.memset(ident[:], 0.0), [], [ident])
    P.op("pool", lambda e: e.affine_select(out=ident[:], in_=ident[:], compare_op=ALU.not_equal, fill=1.0, base=0,
                                           pattern=[[-1, 128]], channel_multiplier=1), [ident], [ident])
    P.op("dve", lambda e: e.tensor_copy(out=identb[:], in_=ident[:]), [ident], [identb])
    P.op("dve", lambda e: e.memset(ones32[:], 1.0), [], [ones32])
    P.dma("sp", linkc[:], link_in.partition_broadcast(128), [], [linkc])
    with nc.allow_non_contiguous_dma(reason="tiny param vectors"):
        P.dma("sp", lng[:, 0, :], W['ln0_g'].rearrange("(c p) -> p c", p=128), [], [lng])
        P.dma("sp", lnb[:, 0, :], W['ln0_b'].rearrange("(c p) -> p c", p=128), [], [lnb])
        for l in range(DEPTH):
            for j, nm in ((1, 'ln1'), (2, 'ln2')):
                P.dma("sp", lng[:, j + 2 * l, :], W[nm + '_g'][l].rearrange("(c p) -> p c", p=128), [], [lng])
                P.dma("sp", lnb[:, j + 2 * l, :], W[nm + '_b'][l].rearrange("(c p) -> p c", p=128), [], [lnb])

    def ln_tile(y, sq, xb, sm, li, ti, pA, pB):
        mean, var = sm
        t0 = ti * TT
        act(sq[:], y[:], AF.Square, [y], [sq])
        for c in range(8):
            mm(pA[:, :], ones32[:], y[:, c, :], c == 0, c == 7, [ones32, y], [pA])
        for c in range(8):
            mm(pB[:, :], ones32[:], sq[:, c, :], c == 0, c == 7, [ones32, sq], [pB])
        P.op("act", lambda e: e.mul(out=mean[:], in_=pA[:, :], mul=1.0 / D), [pA], [mean])
        tt(var[:], mean[:], mean[:], ALU.mult, [mean], [var])
        stt(var[:], pB[:, :], 1.0 / D, var[:], ALU.mult, ALU.subtract, [pB, var], [var])
        ts(var[:], var[:], LN_EPS, None, ALU.add, None, [var], [var])
        act(var[:], var[:], AF.Ln, [var], [var])
        act(var[:], var[:], AF.Exp, [var], [var], scale=-0.5)
        bc = lambda a: a[:].unsqueeze(1).to_broadcast([128, 8, TT])
        tt(y[:], y[:], bc(mean), ALU.subtract, [y, mean], [y])
        tt(y[:], y[:], bc(var), ALU.mult, [y, var], [y])
        tt(y[:], y[:], lng[:, li, :].unsqueeze(2).to_broadcast([128, 8, TT]), ALU.mult, [y, lng], [y])
        tt(y[:], y[:], lnb[:, li, :].unsqueeze(2).to_broadcast([128, 8, TT]), ALU.add, [y, lnb], [y])
        act(xb[:], y[:], AF.Copy, [y], [xb])
        P.dma("sp", XT[:, t0:t0 + TT].rearrange("(c p) t -> p c t", p=128), y[:], [y], [XT.bs[ti]])
        P.dma("pool", XTB[:, t0:t0 + TT].rearrange("(c p) t -> p c t", p=128), xb[:], [xb], [XTB.bs[ti]])

    def phase0():
        with ExitStack() as st:
            xin = [sbt(st, "p0x%d" % i, [128, D]) for i in range(2)]
            y = sbt(st, "p0y", [128, 8, TT])
            sq = sbt(st, "p0sq", [128, 8, TT])
            xb = sbt(st, "p0xb", [128, 8, TT], BF16)
            sm = (sbt(st, "p0m", [128, TT]), sbt(st, "p0v", [128, TT]))
            k = 0
            for ti in range(NT):
                for b in range(4):
                    xi = xin[k % 2]
                    k += 1
                    r0 = ti * TT + b * 128
                    P.dma("sp", xi[:], x_in[r0:r0 + 128, :], [], [xi])
                    for c in range(8):
                        mm(ps[c][:, b * 128:(b + 1) * 128], xi[:, c * 128:(c + 1) * 128], ident[:], True, True,
                           [xi, ident], [ps[c]])
                for c in range(8):
                    if c % 2 == 0:
                        P.op("dve", lambda e: e.tensor_copy(out=y[:, c, :], in_=ps[c][:, :]), [ps[c]], [y])
                    else:
                        act(y[:, c, :], ps[c][:, :], AF.Copy, [ps[c]], [y])
                ln_tile(y, sq, xb, sm, 0, ti, ps[0], ps[1])
            P.barrier()

    def phase_out():
        with ExitStack() as st:
            xs = [sbt(st, "pox%d" % i, [128, 8, TT]) for i in range(2)]
            yo = [sbt(st, "poy%d" % i, [128, D]) for i in range(2)]
            k = 0
            for ti in range(NT):
                t0 = ti * TT
                x = xs[ti % 2]
                P.dma("sp", x[:], XT[:, t0:t0 + TT].rearrange("(c p) t -> p c t", p=128), [XT.bs[ti]], [x])
                for b in range(4):
                    o = yo[k % 2]
                    k += 1
                    for c in range(8):
                        pb = ps[(c // 4) + 2 * (b % 2)]
                        mm(pb[:, (c % 4) * 128:(c % 4 + 1) * 128], x[:, c, b * 128:(b + 1) * 128], ident[:], True, True,
                           [x, ident], [pb])
                    for hlf in range(2):
                        pb = ps[hlf + 2 * (b % 2)]
                        if hlf == 0:
                            P.op("dve", lambda e: e.tensor_copy(out=o[:, 0:512], in_=pb[:, :]), [pb], [o])
                        else:
                            act(o[:, 512:1024], pb[:, :], AF.Copy, [pb], [o])
                    r0 = t0 + b * 128
                    P.dma("pool", y_out[r0:r0 + 128, :], o[:], [o], [])
            P.barrier()

    def seg_edge(t):
        if t <= 0 or t >= T:
            return 2
        return 1 if t % SL == 0 else 0

    def load_halo(q, tl, src, r0, nrow, ti, dtbufs):
        t0 = ti * TT
        le, re = seg_edge(t0), seg_edge(t0 + TT)
        lo = t0 - 1 if le != 2 else t0
        hi = t0 + TT + 1 if re != 2 else t0 + TT
        rd = [dtbufs[j] for j in (ti - 1, ti, ti + 1) if 0 <= j < NT]
        P.dma(q, tl[0:nrow, (lo - t0 + 1):(hi - t0 + 1)], src[r0:r0 + nrow, lo:hi], rd, [tl])
        if le == 2:
            P.op("dve", lambda e: e.memset(tl[0:nrow, 0:1], 0.0), [], [tl])
        elif le == 1:
            ts(tl[0:nrow, 0:1], tl[0:nrow, 0:1], linkc[0:nrow, 0:1], None, ALU.mult, None, [tl, linkc], [tl])
        if re == 2:
            P.op("dve", lambda e: e.memset(tl[0:nrow, TT + 1:TT + 2], 0.0), [], [tl])
        elif re == 1:
            ts(tl[0:nrow, TT + 1:TT + 2], tl[0:nrow, TT + 1:TT + 2], linkc[0:nrow, 0:1], None, ALU.mult, None,
               [tl, linkc], [tl])

    def phase_f1(l):
        with ExitStack() as st:
            wa = sbt(st, "f1wa", [128, 8, DFF], BF16)
            for k in range(8):
                load_cast(wa[:, k, :], W['w_up'][l, k * 128:(k + 1) * 128, 0:DFF], wa, DFF)
            xbs = [sbt(st, "f1xb%d" % i, [128, 8, TT], BF16) for i in range(2)]
            ao = [sbt(st, "f1ao%d" % i, [128, TT]) for i in range(4)]
            n = 0
            for ti in range(NT):
                t0 = ti * TT
                xb = xbs[ti % 2]
                P.dma("sp", xb[:], XTB[:, t0:t0 + TT].rearrange("(c p) t -> p c t", p=128), [XTB.bs[ti]], [xb])
                for j in range(NFC):
                    pb = ps[n % 4]
                    a = ao[n % 4]
                    for k in range(8):
                        mm(pb[:, :], wa[:, k, j * 128:(j + 1) * 128], xb[:, k, :], k == 0, k == 7, [wa, xb], [pb])
                    if n % 2 == 0:
                        act(a[:], pb[:, :], AF.Copy, [pb], [a])
                    else:
                        P.op("dve", lambda e: e.tensor_copy(out=a[:], in_=pb[:, :]), [pb], [a])
                    P.dma("pool" if n % 2 else "sp", AT[j * 128:(j + 1) * 128, t0:t0 + TT], a[:], [a], [AT.bs[ti]])
                    n += 1
            P.barrier()

    def phase_f2(l):
        with ExitStack() as st:
            wb = sbt(st, "f2wb", [128, 8, DFF], BF16)
            wd = sbt(st, "f2wd", [128, NFC, D], BF16)
            cw = sbt(st, "f2cw", [128, 3, NFC])
            cb = sbt(st, "f2cb", [128, NFC])
            for k in range(8):
                load_cast(wb[:, k, :], W['w_up'][l, k * 128:(k + 1) * 128, DFF:2 * DFF], wb, DFF)
            for j in range(NFC):
                load_cast(wd[:, j, :], W['w_down'][l, j * 128:(j + 1) * 128, :], wd, D)
            with nc.allow_non_contiguous_dma(reason="tiny param vectors"):
                for w in range(3):
                    P.dma("sp", cw[:, w, :], W['conv_f_w'][l, w].rearrange("(c p) -> p c", p=128), [], [cw])
                P.dma("sp", cb[:, :], W['conv_f_b'][l].rearrange("(c p) -> p c", p=128), [], [cb])
            xb1 = sbt(st, "f2xb", [128, 8, TT], BF16)
            y = sbt(st, "f2y", [128, 8, TT])
            hh = sbt(st, "f2hh", [128, NFC, TT], BF16)
            sq = sbt(st, "f2sq", [128, 8, TT])
            xbo = sbt(st, "f2xbo", [128, 8, TT], BF16)
            sm = (sbt(st, "f2m", [128, TT]), sbt(st, "f2v", [128, TT]))
            ats = [sbt(st, "f2at%d" % i, [128, TT + 2]) for i in range(3)]
            accs = [sbt(st, "f2ac%d" % i, [128, TT]) for i in range(2)]
            n = 0
            for ti in range(NT):
                t0 = ti * TT
                P.dma("sp", xb1[:], XTB[:, t0:t0 + TT].rearrange("(c p) t -> p c t", p=128), [XTB.bs[ti]], [xb1])
                for j in range(NFC):
                    at = ats[n % 3]
                    acc = accs[n % 2]
                    pb = ps[n % 3]
                    n += 1
                    load_halo("sp" if j % 2 else "pool", at, AT, j * 128, 128, ti, AT.bs)
                    for k in range(8):
                        mm(pb[:, :], wb[:, k, j * 128:(j + 1) * 128], xb1[:, k, :], k == 0, k == 7, [wb, xb1], [pb])
                    ts(acc[:], at[:, 0:TT], cw[:, 0, j:j + 1], None, ALU.mult, None, [at, cw], [acc])
                    stt(acc[:], at[:, 1:TT + 1], cw[:, 1, j:j + 1], acc[:], ALU.mult, ALU.add, [at, cw, acc], [acc])
                    stt(acc[:], at[:, 2:TT + 2], cw[:, 2, j:j + 1], acc[:], ALU.mult, ALU.add, [at, cw, acc], [acc])
                    act(acc[:], acc[:], AF.Gelu_apprx_tanh, [acc, cb], [acc], bias=cb[:, j:j + 1])
                    tt(hh[:, j, :], acc[:], pb[:, :], ALU.mult, [acc, pb], [hh])
                P.dma("sp", y[:], XT[:, t0:t0 + TT].rearrange("(c p) t -> p c t", p=128), [XT.bs[ti]], [y])
                for o in range(8):
                    pb = ps[4 + o % 2]
                    for j in range(NFC):
                        mm(pb[:, :], wd[:, j, o * 128:(o + 1) * 128], hh[:, j, :], j == 0, j == NFC - 1, [wd, hh], [pb])
                    stt(y[:, o, :], y[:, o, :], ALPHA, pb[:, :], ALU.mult, ALU.add, [y, pb], [y])
                ln_tile(y, sq, xbo, sm, 2 + 2 * l, ti, ps[6], ps[7])
            P.barrier()

    def phase_mrg(l):
        with ExitStack() as st:
            wg = sbt(st, "mgwg", [128, 8, 3 * D], BF16)
            wo = sbt(st, "mgwo", [128, 8, D], BF16)
            wpa = sbt(st, "mgwpa", [128, 4, D], BF16)
            wpb = sbt(st, "mgwpb", [128, 4, D], BF16)
            bm = sbt(st, "mgbm", [128, 3, 8])
            mhg = sbt(st, "mgmhg", [128, 4])
            with nc.allow_non_contiguous_dma(reason="tiny param vectors"):
                for br in range(3):
                    P.dma("sp", bm[:, br, :], W['b_merge'][l, br].rearrange("(c p) -> p c", p=128), [], [bm])
                P.dma("sp", mhg[:, :], W['mh_norm_g'][l].rearrange("(c p) -> p c", p=128), [], [mhg])
            for k in range(8):
                load_cast(wg[:, k, :], W['w_in'][l, k * 128:(k + 1) * 128, O_GP:NIN], wg, 3 * D)
                load_cast(wo[:, k, :], W['w_o'][l, k * 128:(k + 1) * 128, :], wo, D)
            for k in range(4):
                load_cast(wpa[:, k, :], W['w_proj_a'][l, k * 128:(k + 1) * 128, :], wpa, D)
                load_cast(wpb[:, k, :], W['w_proj_b'][l, k * 128:(k + 1) * 128, :], wpb, D, scale=mhg[:, k:k + 1],
                          sreads=[mhg])
            xb = sbt(st, "mgxb", [128, 8, TT], BF16)
            y = sbt(st, "mgy", [128, 8, TT])
            ot = sbt(st, "mgot", [128, 4, TT], BF16)
            hb = sbt(st, "mghb", [128, 4, TT], BF16)
            yc = sbt(st, "mgyc", [128, 8, TT])
            mg = sbt(st, "mgmg", [128, 8, TT], BF16)
            sq = sbt(st, "mgsq", [128, 8, TT])
            xbo = sbt(st, "mgxbo", [128, 8, TT], BF16)
            sm = (sbt(st, "mgm", [128, TT]), sbt(st, "mgv", [128, TT]))
            g = [sbt(st, "mgg%d" % i, [128, TT]) for i in range(3)]
            t1 = sbt(st, "mgt1", [128, TT])
            t2 = sbt(st, "mgt2", [128, TT])
            fm = lambda a: a.rearrange("(c p) t -> p c t", p=128)
            for ti in range(NT):
                t0 = ti * TT
                P.dma("sp", xb[:], fm(XTB[:, t0:t0 + TT]), [XTB.bs[ti]], [xb])
                if "a" in branches:
                    P.dma("sp", ot[:], fm(OT[:, t0:t0 + TT]), [OT.bs[ti]], [ot])
                if "b" in branches:
                    P.dma("pool", hb[:], fm(HBT[:, t0:t0 + TT]), [HBT.bs[ti]], [hb])
                if "c" in branches:
                    P.dma("sp", yc[:], fm(YC[:, t0:t0 + TT]), [YC.bs[ti]], [yc])
                for c in range(8):
                    cs = slice(c * 128, (c + 1) * 128)
                    terms = []
                    for bi, br in enumerate("abc"):
                        if br not in branches:
                            continue
                        pb = ps[bi]
                        for k in range(8):
                            mm(pb[:, :], wg[:, k, bi * D + c * 128: bi * D + (c + 1) * 128], xb[:, k, :], k == 0, k == 7,
                               [wg, xb], [pb])
                        act(g[bi][:], pb[:, :], AF.Sigmoid, [pb, bm], [g[bi]], bias=bm[:, bi, c:c + 1])
                        if br == "a":
                            for k in range(4):
                                mm(ps[3][:, :], wpa[:, k, cs], ot[:, k, :], k == 0, k == 3, [wpa, ot], [ps[3]])
                            terms.append((g[bi], ps[3], ps[3][:, :]))
                        elif br == "b":
                            for k in range(4):
                                mm(ps[4][:, :], wpb[:, k, cs], hb[:, k, :], k == 0, k == 3, [wpb, hb], [ps[4]])
                            terms.append((g[bi], ps[4], ps[4][:, :]))
                        else:
                            terms.append((g[bi], yc, yc[:, c, :]))
                    if not terms:
                        P.op("dve", lambda e: e.memset(mg[:, c, :], 0.0), [], [mg])
                    for i, (gt, src, sap) in enumerate(terms):
                        last = i == len(terms) - 1
                        if i == 0:
                            tt(mg[:, c, :] if last else t1[:], gt[:], sap, ALU.mult, [gt, src], [mg if last else t1])
                        else:
                            tt(t2[:], gt[:], sap, ALU.mult, [gt, src], [t2])
                            tt(mg[:, c, :] if last else t1[:], t1[:], t2[:], ALU.add, [t1, t2], [mg if last else t1])
                P.dma("sp", y[:], fm(XT[:, t0:t0 + TT]), [XT.bs[ti]], [y])
                for o in range(8):
                    pb = ps[5 + o % 2]
                    for k in range(8):
                        mm(pb[:, :], wo[:, k, o * 128:(o + 1) * 128], mg[:, k, :], k == 0, k == 7, [wo, mg], [pb])
                    stt(y[:, o, :], y[:, o, :], ALPHA, pb[:, :], ALU.mult, ALU.add, [y, pb], [y])
                ln_tile(y, sq, xbo, sm, 1 + 2 * l, ti, ps[6], ps[7])
            P.barrier()


    I32 = mybir.dt.int32
    TWO_PI = 2.0 * math.pi

    def phase_init():
        with ExitStack() as st:
            CB = min(T, 2048)
            ji = sbt(st, "inji", [32, 1], I32)
            jf = sbt(st, "injf", [32, 1])
            jt = sbt(st, "injt", [32, 1])
            invf = sbt(st, "ininvf", [32, 1])
            sgn = sbt(st, "insgn", [32, 1])
            oml = sbt(st, "inoml", [32, 1])
            ti32 = sbt(st, "inti", [32, CB], I32)
            si32 = sbt(st, "insi", [32, CB], I32)
            tf = sbt(st, "intf", [32, CB])
            sf = sbt(st, "insf", [32, CB])
            ang = sbt(st, "inang", [32, CB])
            q = sbt(st, "inq", [32, CB])
            red = sbt(st, "inred", [32, CB])
            P.op("pool", lambda e: e.iota(ji[:], pattern=[[0, 1]], base=0, channel_multiplier=1), [], [ji])
            P.op("dve", lambda e: e.tensor_copy(out=jf[:], in_=ji[:]), [ji], [jf])
            ts(jt[:], jf[:], 16.0, 16.0, ALU.is_ge, ALU.mult, [jf], [jt])
            tt(invf[:], jf[:], jt[:], ALU.subtract, [jf, jt], [invf])
            act(invf[:], invf[:], AF.Exp, [invf], [invf], scale=-math.log(10000.0) / 16.0)
            ts(sgn[:], jt[:], 1.0 / 8.0, -1.0, ALU.mult, ALU.add, [jt], [sgn])
            ts(oml[:], linkc[0:32, :], -1.0, 1.0, ALU.mult, ALU.add, [linkc], [oml])
            for b0 in range(0, T, CB):
                P.op("pool", lambda e: e.iota(ti32[:], pattern=[[1, CB]], base=b0, channel_multiplier=0), [], [ti32])
                if SL >= CB:
                    P.op("pool", lambda e: e.iota(si32[:], pattern=[[0, CB]], base=(b0 // SL) * SL, channel_multiplier=0),
                         [], [si32])
                else:
                    P.op("pool", lambda e: e.iota(si32[:], pattern=[[SL, CB // SL], [0, SL]], base=b0,
                                                  channel_multiplier=0), [], [si32])
                P.op("dve", lambda e: e.tensor_copy(out=tf[:], in_=ti32[:]), [ti32], [tf])
                P.op("dve", lambda e: e.tensor_copy(out=sf[:], in_=si32[:]), [si32], [sf])
                stt(tf[:], sf[:], oml[:, 0:1], tf[:], ALU.mult, ALU.subtract, [sf, oml, tf], [tf])
                ts(ang[:], tf[:], invf[:, 0:1], -1.0, ALU.mult, ALU.mult, [tf, invf], [ang])
                for which, dst in ((0, RS), (1, RC)):
                    if which == 1:
                        ts(ang[:], ang[:], math.pi / 2.0, None, ALU.add, None, [ang], [ang])
                    ts(q[:], ang[:], 1.0 / TWO_PI, None, ALU.mult, None, [ang], [q])
                    ts(q[:], q[:], 12582912.0, 12582912.0, ALU.add, ALU.subtract, [q], [q])
                    stt(red[:], q[:], -6.28125, ang[:], ALU.mult, ALU.add, [q, ang], [red])
                    stt(red[:], q[:], -(TWO_PI - 6.28125), red[:], ALU.mult, ALU.add, [q, red], [red])
                    ts(red[:], red[:], -3.1415925, 3.1415925, ALU.max, ALU.min, [red], [red])
                    act(red[:], red[:], AF.Sin, [red], [red])
                    if which == 0:
                        ts(red[:], red[:], sgn[:, 0:1], None, ALU.mult, None, [red, sgn], [red])
                    P.dma("sp", dst[:, b0:b0 + CB], red[:], [red], dst.bs[b0 // TT:(b0 + CB) // TT])
            onesr = sbt(st, "inones", [2, 8, TT], BF16)
            mrow = sbt(st, "inmrow", [8, TT], BF16)
            lk8 = sbt(st, "inlk8", [8, 1])
            P.op("dve", lambda e: e.memset(onesr[:], 1.0), [], [onesr])
            ts(lk8[:], linkc[0:8, :], -1.0, 30000.0, ALU.add, ALU.mult, [linkc], [lk8])
            P.op("dve", lambda e: e.memset(mrow[:], 1.0), [], [mrow])
            ts(mrow[:], mrow[:], lk8[:, 0:1], None, ALU.mult, None, [mrow, lk8], [mrow])
            for ti in range(NT):
                t0 = ti * TT
                P.dma("sp", KT[:, 96:98, t0:t0 + TT].rearrange("h r t -> r h t"), onesr[:], [onesr], [KT.bs[ti]])
                P.dma("sp", QT[:, 97, t0:t0 + TT], mrow[:], [mrow], [QT.bs[ti]])
            P.barrier()

    def phase_a(l):
        with ExitStack() as st:
            wA = sbt(st, "awA", [128, 8, O_GP], BF16)
            wkrs = sbt(st, "awkrs", [128, 8, 32], BF16)
            wq = sbt(st, "awq", [128, 2, 8, 96], BF16)
            wqs = sbt(st, "awqs", [128, 2, 8, 96], BF16)
            wkn = sbt(st, "awkn", [128, 8, 64], BF16)
            wv = sbt(st, "awv", [128, 8, 64], BF16)
            qg = sbt(st, "aqg", [128, 2])
            kvg = sbt(st, "akvg", [128, 1])
            gmb = sbt(st, "agmb", [16, 1])
            indq = sbt(st, "aindq", [96, 8, 8], BF16)
            indk = sbt(st, "aindk", [128, 4, 8], BF16)
            onr = sbt(st, "aonr", [32, 8], BF16)
            with nc.allow_non_contiguous_dma(reason="tiny param vectors"):
                P.dma("sp", qg[:, :], W['q_norm_g'][l].rearrange("(c p) -> p c", p=128), [], [qg])
                P.dma("sp", kvg[:, :], W['kv_norm_g'][l].rearrange("(c p) -> p c", p=128), [], [kvg])
                P.dma("sp", gmb[:, :], W['b_mlstm_gate'][l].rearrange("a b (c o) -> (a b c) o", o=1), [], [gmb])
            for k in range(8):
                load_cast(wA[:, k, :], W['w_in'][l, k * 128:(k + 1) * 128, 0:O_GP], wA, O_GP)
                act(wkrs[:, k, 0:16], wA[:, k, O_KR + 16:O_KR + 32], AF.Copy, [wA], [wkrs])
                act(wkrs[:, k, 16:32], wA[:, k, O_KR:O_KR + 16], AF.Copy, [wA], [wkrs])
            for c2 in range(2):
                s = stg[stgi[0] % 3]
                stgi[0] += 1
                P.dma("sp", s[:, 0:768], W['w_uq'][l, c2 * 128:(c2 + 1) * 128, :], [], [s])
                sv = s[:, 0:768].rearrange("p (h d) -> p h d", h=8)
                sc = qg[:, c2:c2 + 1]
                act(wq[:, c2, :, :], sv[:, :, :], AF.Identity, [s, qg], [wq], scale=sc)
                P.op("dve", lambda e: e.memset(wqs[:, c2, :, 0:64], 0.0), [], [wqs])
                act(wqs[:, c2, :, 64:80], sv[:, :, 80:96], AF.Identity, [s, qg], [wqs], scale=sc)
                act(wqs[:, c2, :, 80:96], sv[:, :, 64:80], AF.Identity, [s, qg], [wqs], scale=sc)
            s = stg[stgi[0] % 3]
            stgi[0] += 1
            P.dma("sp", s[:, 0:1024], W['w_ukv'][l, :, :], [], [s])
            sv = s[:, 0:1024].rearrange("p (h d) -> p h d", h=8)
            act(wkn[:, :, :], sv[:, :, 0:64], AF.Identity, [s, kvg], [wkn], scale=kvg[:, 0:1])
            act(wv[:, :, :], sv[:, :, 64:128], AF.Identity, [s, kvg], [wv], scale=kvg[:, 0:1])
            P.op("dve", lambda e: e.memset(indq[:], 0.0), [], [indq])
            P.op("dve", lambda e: e.memset(indk[:], 0.0), [], [indk])
            P.op("dve", lambda e: e.memset(onr[:], 1.0), [], [onr])
            for h in range(8):
                P.op("dve", lambda e: e.memset(indq[:, h, h:h + 1], 1.0), [], [indq])
                P.op("dve", lambda e: e.memset(indk[(h % 2) * 64:(h % 2) * 64 + 64, h // 2, h:h + 1], 1.0), [], [indk])
            xbs = [sbt(st, "axb%d" % i, [128, 8, TT], BF16) for i in range(2)]
            cqf = sbt(st, "acqf", [128, 2, TT])
            sqt = sbt(st, "asq", [128, 2, TT])
            rstd = sbt(st, "arstd", [128, TT])
            cqn = sbt(st, "acqn", [128, 2, TT], BF16)
            ckf = sbt(st, "ackf", [128, TT])
            ckvn = sbt(st, "ackvn", [128, TT], BF16)
            rct = sbt(st, "arc", [96, TT])
            rst = sbt(st, "ars", [96, TT])
            r1 = sbt(st, "ar1", [96, TT])
            r2 = sbt(st, "ar2", [96, TT])
            krr = sbt(st, "akrr", [32, TT], BF16)
            fst = [sbt(st, "afst%d" % i, [128, TT]) for i in range(4)]
            bst = [sbt(st, "abst%d" % i, [128, TT], BF16) for i in range(4)]
            qos = [sbt(st, "aqo%d" % i, [96, TT], BF16) for i in range(2)]
            sqb = [sbt(st, "asqb%d" % i, [128, TT], BF16) for i in range(2)]
            sqb3 = sbt(st, "asqb3", [32, TT], BF16)
            vas = [sbt(st, "ava%d" % i, [128, 8, 128], BF16) for i in range(2)]
            qn2 = sbt(st, "aqn2", [8, T])
            km2 = sbt(st, "akm2", [8, 1])
            kmx = sbt(st, "akmx", [8, 1])
            mrw = sbt(st, "amrw", [8, T], BF16)
            for v in vas:
                P.op("dve", lambda e: e.memset(v[:], 1.0), [], [v])
            P.op("dve", lambda e: e.memset(km2[:], 0.0), [], [km2])
            cnt = {"b": 0, "f": 0, "s": 0, "q": 0}

            def nb():
                cnt["b"] += 1
                return ps[cnt["b"] % 6]

            def nf():
                cnt["f"] += 1
                return fst[cnt["f"] % 4]

            def nbs():
                cnt["s"] += 1
                return bst[cnt["s"] % 4]

            def dq():
                cnt["q"] += 1
                return "sp" if cnt["q"] % 2 else "pool"

            def rstd_from(pss, n):
                ts(rstd[:], pss[:, :], 1.0 / n, LN_EPS, ALU.mult, ALU.add, [pss], [rstd])
                act(rstd[:], rstd[:], AF.Ln, [rstd], [rstd])
                act(rstd[:], rstd[:], AF.Exp, [rstd], [rstd], scale=-0.5)

            for ti in range(NT):
                t0 = ti * TT
                tsl = slice(t0, t0 + TT)
                xb = xbs[ti % 2]
                P.dma("sp", xb[:], XTB[:, tsl].rearrange("(c p) t -> p c t", p=128), [XTB.bs[ti]], [xb])
                P.dma("pool", rct[0:32, :], RC[:, tsl], [RC.bs[ti]], [rct])
                P.dma("pool", rst[0:32, :], RS[:, tsl], [RS.bs[ti]], [rst])
                P.dma("pool", rct[64:96, :], RC[:, tsl], [RC.bs[ti]], [rct])
                P.dma("pool", rst[64:96, :], RS[:, tsl], [RS.bs[ti]], [rst])

                def fm_out(c0, m):
                    pb = nb()
                    for k in range(8):
                        mm(pb[0:m, :], wA[:, k, c0:c0 + m], xb[:, k, :], k == 0, k == 7, [wA, xb], [pb])
                    return pb
                for c2 in range(2):
                    pb = fm_out(O_CQ + c2 * 128, 128)
                    act(cqf[:, c2, :], pb[:, :], AF.Copy, [pb], [cqf])
                    act(sqt[:, c2, :], pb[:, :], AF.Square, [pb], [sqt])
                pss = nb()
                for c2 in range(2):
                    mm(pss[:, :], ones32[:], sqt[:, c2, :], c2 == 0, c2 == 1, [ones32, sqt], [pss])
                rstd_from(pss, 256.0)
                tt(cqn[:], cqf[:], rstd[:].unsqueeze(1).to_broadcast([128, 2, TT]), ALU.mult, [cqf, rstd], [cqn])
                pb = fm_out(O_CKV, 128)
                act(ckf[:], pb[:, :], AF.Copy, [pb], [ckf])
                act(sqt[:, 0, :], pb[:, :], AF.Square, [pb], [sqt])
                pss = nb()
                mm(pss[:, :], ones32[:], sqt[:, 0, :], True, True, [ones32, sqt], [pss])
                rstd_from(pss, 128.0)
                tt(ckvn[:], ckf[:], rstd[:], ALU.mult, [ckf, rstd], [ckvn])
                pb = fm_out(O_KR, 32)
                pb2 = nb()
                for k in range(8):
                    mm(pb2[0:32, :], wkrs[:, k, :], xb[:, k, :], k == 0, k == 7, [wkrs, xb], [pb2])
                tt(r1[0:32, :], pb2[0:32, :], rst[0:32, :], ALU.mult, [pb2, rst], [r1])
                tt(r2[0:32, :], pb[0:32, :], rct[0:32, :], ALU.mult, [pb, rct], [r2])
                tt(krr[:], r1[0:32, :], r2[0:32, :], ALU.add, [r1, r2], [krr])
                for h in range(8):
                    P.dma(dq(), KT[h, 64:96, tsl], krr[:], [krr], [KT.bs[ti]])
                for (c0, dst) in ((O_XM, XM), (O_US, US)):
                    for c in range(4):
                        pb = fm_out(c0 + c * 128, 128)
                        f = nf()
                        if c % 2:
                            act(f[:], pb[:, :], AF.Copy, [pb], [f])
                        else:
                            P.op("dve", lambda e: e.tensor_copy(out=f[:], in_=pb[:, :]), [pb], [f])
                        P.dma(dq(), dst[c * 128:(c + 1) * 128, tsl], f[:], [f], [dst.bs[ti]])
                        if dst is US:
                            bs_ = nbs()
                            act(bs_[:], pb[:, :], AF.Copy, [pb], [bs_])
                            P.dma(dq(), USB[c * 128:(c + 1) * 128, tsl], bs_[:], [bs_], [USB.bs[ti]])
                pb = fm_out(O_GM, 16)
                f = nf()
                act(f[0:16, :], pb[0:16, :], AF.Identity, [pb, gmb], [f], bias=gmb[:, 0:1])
                P.dma(dq(), GM[:, tsl], f[0:16, :], [f], [GM.bs[ti]])
                for b in range(4):
                    r0 = t0 + b * 128
                    pb = nb()
                    for k in range(8):
                        mm(pb[:, :], xb[:, k, b * 128:(b + 1) * 128], wA[:, k, O_VM:O_VM + 512], k == 0, k == 7, [wA, xb], [pb])
                    bs_ = nbs()
                    P.op("dve", lambda e: e.tensor_copy(out=bs_[:], in_=pb[:, :]), [pb], [bs_])
                    P.dma(dq(), VM[r0:r0 + 128, :], bs_[:], [bs_], [VM.bs[ti]])
                    pb = nb()
                    for k in range(8):
                        mm(pb[:, :], xb[:, k, b * 128:(b + 1) * 128], wA[:, k, O_OM:O_OM + 512], k == 0, k == 7, [wA, xb], [pb])
                    f = nf()
                    act(f[:], pb[:, :], AF.Sigmoid, [pb], [f])
                    P.dma(dq(), OMS[r0:r0 + 128, :], f[:], [f], [OMS.bs[ti]])
                pend_q = []
                for h in range(8):
                    pq = nb()
                    for c2 in range(2):
                        mm(pq[0:96, :], wq[:, c2, h, :], cqn[:, c2, :], c2 == 0, c2 == 1, [wq, cqn], [pq])
                    pqs = nb()
                    for c2 in range(2):
                        mm(pqs[0:96, :], wqs[:, c2, h, :], cqn[:, c2, :], c2 == 0, c2 == 1, [wqs, cqn], [pqs])
                    while pend_q:
                        pend_q.pop(0)()
                    qo = qos[h % 2]
                    tt(r1[64:96, :], pqs[64:96, :], rst[64:96, :], ALU.mult, [pqs, rst], [r1])
                    tt(r2[64:96, :], pq[64:96, :], rct[64:96, :], ALU.mult, [pq, rct], [r2])
                    tt(qo[64:96, :], r1[64:96, :], r2[64:96, :], ALU.add, [r1, r2], [qo])
                    act(qo[0:64, :], pq[0:64, :], AF.Copy, [pq], [qo])
                    sb_ = sqb[h % 2]
                    act(sb_[0:96, :], qo[:, :], AF.Square, [qo], [sb_])
                    pend_q.append(lambda h=h, sb_=sb_: mm(ps[6][0:8, :], indq[:, h, :], sb_[0:96, :], h == 0, h == 7, [indq, sb_], [ps[6]]))
                    P.dma(dq(), QT[h, 0:96, tsl], qo[:, :], [qo], [QT.bs[ti]])
                pend_k = list(pend_q)
                sb_ = sqb3
                act(sb_[0:32, :], krr[:, :], AF.Square, [krr], [sb_])
                pend_k.append(lambda sb_=sb_: mm(ps[7][0:8, :], onr[:, :], sb_[0:32, :], True, False, [onr, sb_], [ps[7]]))
                for j in range(4):
                    pb = nb()
                    mm(pb[:, :], wkn[:, 2 * j:2 * j + 2, :].rearrange("p h d -> p (h d)"), ckvn[:], True, True, [wkn, ckvn], [pb])
                    while pend_k:
                        pend_k.pop(0)()
                    if j == 0:
                        P.op("dve", lambda e: e.tensor_copy(out=qn2[:, tsl], in_=ps[6][0:8, :]), [ps[6]], [qn2])
                    ko = nbs()
                    P.op("dve", lambda e: e.tensor_copy(out=ko[:], in_=pb[:, :]), [pb], [ko])
                    sb_ = sqb[1 - j % 2]
                    act(sb_[:, :], ko[:, :], AF.Square, [ko], [sb_])
                    pend_k.append(lambda j=j, sb_=sb_: mm(ps[7][0:8, :], indk[:, j, :], sb_[:, :], False, j == 3, [indk, sb_], [ps[7]]))
                    P.dma(dq(), KT[2 * j, 0:64, tsl], ko[0:64, :], [ko], [KT.bs[ti]])
                    P.dma(dq(), KT[2 * j + 1, 0:64, tsl], ko[64:128, :], [ko], [KT.bs[ti]])
                pend_v = list(pend_k)
                for b in range(4):
                    r0 = t0 + b * 128
                    pb = nb()
                    mm(pb[:, :], ckvn[:, b * 128:(b + 1) * 128], wv[:].rearrange("p h d -> p (h d)"), True, True, [ckvn, wv], [pb])
                    while pend_v:
                        pend_v.pop(0)()
                    if b == 0:
                        P.op("dve", lambda e: e.reduce_max(out=kmx[:], in_=ps[7][0:8, :], axis=AX.X), [ps[7]], [kmx])
                        tt(km2[:], km2[:], kmx[:], ALU.max, [km2, kmx], [km2])
                    va = vas[b % 2]
                    P.op("dve", lambda e: e.tensor_copy(out=va[:, :, 0:64], in_=pb[:, :].rearrange("p (h d) -> p h d", h=8)),
                         [pb], [va])
                    P.dma(dq(), VA[r0:r0 + 128, :, :], va[:], [va], [VA.bs[ti]])
            act(qn2[:], qn2[:], AF.Sqrt, [qn2, km2], [qn2], scale=km2[:, 0:1])
            ts(mrw[:], qn2[:], -1.0, None, ALU.mult, None, [qn2], [mrw])
            P.dma("sp", QT[:, 96, :], mrw[:], [mrw], QT.bs)
            P.barrier()


    def phase_att(l):
        SCALE = 96.0 ** -0.5
        with ExitStack() as st:
            kTs = [sbt(st, "tk%d" % i, [98, T], BF16) for i in range(2)]
            qTs = [sbt(st, "tq%d" % i, [98, T], BF16) for i in range(2)]
            vhs = [sbt(st, "tv%d" % i, [128, NCH, 128], BF16) for i in range(2)]
            pts = [sbt(st, "tp%d" % i, [128, TT], BF16) for i in range(3)]
            rlt = sbt(st, "trl", [128, TT])
            osb = [sbt(st, "to%d" % i, [64, TT], BF16) for i in range(2)]
            n = 0
            for h in range(8):
                kT, qT, vh = kTs[h % 2], qTs[h % 2], vhs[h % 2]
                P.dma("sp", kT[:], KT[h, :, :], KT.bs, [kT])
                P.dma("pool", qT[:], QT[h, :, :], QT.bs, [qT])
                P.dma("sp", vh[:], VA[:, h, :].rearrange("(n p) d -> p n d", p=128), VA.bs, [vh])
                for i in range(NT):
                    qs = slice(i * TT, (i + 1) * TT)
                    acc = ps[4 + i % 2]
                    segq = (i * TT) // SL

                    def score(kb):
                        R = 97 if (kb * 128) // SL == segq else 98
                        pb = ps[(n + kb) % 3]
                        mm(pb[:, :], kT[0:R, kb * 128:(kb + 1) * 128], qT[0:R, qs], True, True, [kT, qT], [pb])
                        return pb
                    pbn = score(0)
                    for kb in range(NCH):
                        pb = pbn
                        pt = pts[(n + kb) % 3]
                        if kb + 1 < NCH:
                            pbn = score(kb + 1)
                        act(pt[:], pb[:, :], AF.Exp, [pb], [pt], scale=SCALE)
                        mm(acc[:, :], vh[:, kb, :], pt[:], kb == 0, kb == NCH - 1, [vh, pt], [acc])
                    n += NCH
                    P.op("dve", lambda e: e.reciprocal(out=rlt[64:128, :], in_=acc[64:128, :]), [acc], [rlt])
                    o = osb[i % 2]
                    tt(o[:], acc[0:64, :], rlt[64:128, :], ALU.mult, [acc, rlt], [o])
                    P.dma("pool" if i % 2 else "sp", OT[h * 64:(h + 1) * 64, qs], o[:], [o], [OT.bs[i]])
            P.barrier()


    def emit_sin(dst_t, dst, ang_t, ang, q_t, q, shift):
        ts(q, ang, 1.0 / TWO_PI, shift / TWO_PI, ALU.mult, ALU.add, [ang_t], [q_t])
        ts(q, q, 12582912.0, 12582912.0, ALU.add, ALU.subtract, [q_t], [q_t])
        stt(dst, q, -6.28125, ang, ALU.mult, ALU.add, [q_t, ang_t], [dst_t])
        stt(dst, q, -(TWO_PI - 6.28125), dst, ALU.mult, ALU.add, [q_t, dst_t], [dst_t])
        if shift:
            ts(dst, dst, shift, None, ALU.add, None, [dst_t], [dst_t])
        ts(dst, dst, -3.1415925, 3.1415925, ALU.max, ALU.min, [dst_t], [dst_t])
        act(dst, dst, AF.Sin, [dst_t], [dst_t])

    def phase_s5(l):
        NSC = "tiny param vectors"
        with ExitStack() as lst:
            CLr = sbt(lst, "sCLr", [128, 16, 128], BF16)
            CLi = sbt(lst, "sCLi", [128, 16, 128], BF16)
            wglu = sbt(lst, "swglu", [128, 4, 2 * D], BF16)
            dsk = sbt(lst, "sdsk", [128, 4])
            for k in range(4):
                load_cast(wglu[:, k, :], W['w_glu'][l, k * 128:(k + 1) * 128, :], wglu, 2 * D)
            with nc.allow_non_contiguous_dma(reason=NSC):
                P.dma("sp", dsk[:, :], W['s5_d'][l].rearrange("(c g) w -> (g w) c", c=4), [], [dsk])
            with ExitStack() as st:
                Z = [sbt(st, "sZ%d" % i, [128, 4, 128]) for i in range(2)]
                for i, nm in enumerate(('s5_c_re', 's5_c_im')):
                    P.op("dve", lambda e: e.memset(Z[i][:], 0.0), [], [Z[i]])
                    for ch in range(4):
                        for jj in range(4):
                            for two in range(2):
                                g = 2 * (4 * ch + jj) + two
                                P.dma("sp" if two else "pool", Z[i][32 * jj + 16 * two:32 * jj + 16 * two + 16, ch, 64 * two:64 * two + 64],
                                      W[nm][l, g], [], [Z[i]])
                P.op("dve", lambda e: e.memset(CLr[:], 0.0), [], [CLr])
                P.op("dve", lambda e: e.memset(CLi[:], 0.0), [], [CLi])
                for i, CL in enumerate((CLr, CLi)):
                    for ch in range(4):
                        pb = ps[(2 * i + ch) % 4]
                        mm(pb[:, 0:128], Z[i][:, ch, :], ident[:], True, True, [Z[i], ident], [pb])
                        for jj in range(4):
                            act(CL[:, 4 * ch + jj, 32 * jj:32 * jj + 32], pb[:, 32 * jj:32 * jj + 32], AF.Copy, [pb], [CL],
                                scale=(1.0 if i == 0 else -1.0))
                P.barrier()
            for d in (0, 1):
                with ExitStack() as st:
                    sm = {n: sbt(st, "s5" + n, [128, 16]) for n in
                          ("are", "aim", "dt", "rmag", "th", "cs", "sn", "q", "abr", "abi", "nr", "ni", "den", "bsr", "bsi", "t")}
                    with nc.allow_non_contiguous_dma(reason=NSC):
                        for two in range(2):
                            prt = slice(two * 64, two * 64 + 64)
                            P.dma("sp", sm["are"][prt, :], W['s5_a_re'][l, d].rearrange("(j two) p -> two p j", two=2)[two], [], [sm["are"]])
                            P.dma("sp", sm["aim"][prt, :], W['s5_a_im'][l, d].rearrange("(j two) p -> two p j", two=2)[two], [], [sm["aim"]])
                            P.dma("sp", sm["dt"][prt, :], W['s5_log_dt'][l, d].rearrange("(j two) -> two j", two=2)[two].partition_broadcast(64),
                                  [], [sm["dt"]])
                    A = lambda n: sm[n][:, :]
                    act(A("dt"), A("dt"), AF.Exp, [sm["dt"]], [sm["dt"]])
                    tt(A("rmag"), A("are"), A("dt"), ALU.mult, [sm["are"], sm["dt"]], [sm["rmag"]])
                    act(A("rmag"), A("rmag"), AF.Exp, [sm["rmag"]], [sm["rmag"]])
                    tt(A("th"), A("aim"), A("dt"), ALU.mult, [sm["aim"], sm["dt"]], [sm["th"]])
                    emit_sin(sm["sn"], A("sn"), sm["th"], A("th"), sm["q"], A("q"), 0.0)
                    emit_sin(sm["cs"], A("cs"), sm["th"], A("th"), sm["q"], A("q"), math.pi / 2.0)
                    tt(A("abr"), A("rmag"), A("cs"), ALU.mult, [sm["rmag"], sm["cs"]], [sm["abr"]])
                    tt(A("abi"), A("rmag"), A("sn"), ALU.mult, [sm["rmag"], sm["sn"]], [sm["abi"]])
                    ts(A("abr"), A("abr"), -1.0, None, ALU.add, None, [sm["abr"]], [sm["abr"]])
                    tt(A("nr"), A("abr"), A("are"), ALU.mult, [sm["abr"], sm["are"]], [sm["nr"]])
                    tt(A("t"), A("abi"), A("aim"), ALU.mult, [sm["abi"], sm["aim"]], [sm["t"]])
                    tt(A("nr"), A("nr"), A("t"), ALU.add, [sm["nr"], sm["t"]], [sm["nr"]])
                    tt(A("ni"), A("abi"), A("are"), ALU.mult, [sm["abi"], sm["are"]], [sm["ni"]])
                    tt(A("t"), A("abr"), A("aim"), ALU.mult, [sm["abr"], sm["aim"]], [sm["t"]])
                    tt(A("ni"), A("ni"), A("t"), ALU.subtract, [sm["ni"], sm["t"]], [sm["ni"]])
                    tt(A("den"), A("are"), A("are"), ALU.mult, [sm["are"]], [sm["den"]])
                    tt(A("t"), A("aim"), A("aim"), ALU.mult, [sm["aim"]], [sm["t"]])
                    tt(A("den"), A("den"), A("t"), ALU.add, [sm["den"], sm["t"]], [sm["den"]])
                    P.op("dve", lambda e: e.reciprocal(out=A("den"), in_=A("den")), [sm["den"]], [sm["den"]])
                    tt(A("bsr"), A("nr"), A("den"), ALU.mult, [sm["nr"], sm["den"]], [sm["bsr"]])
                    tt(A("bsi"), A("ni"), A("den"), ALU.mult, [sm["ni"], sm["den"]], [sm["bsi"]])
                    BLr = sbt(st, "sBLr", [128, 16, 128], BF16)
                    BLi = sbt(st, "sBLi", [128, 16, 128], BF16)
                    cosT = sbt(st, "scosT", [128, 16, TT])
                    sinT = sbt(st, "ssinT", [128, 16, TT])
                    with ExitStack() as st2:
                        Bt = [sbt(st2, "sBt%d" % i, [128, 16, 16]) for i in range(2)]
                        Bp = [sbt(st2, "sBp%d" % i, [128, 16, 16]) for i in range(2)]
                        tmp = sbt(st2, "sBtmp", [128, 16, 16])
                        X = sbt(st2, "sX", [128, 16, 2, 16])
                        for i, nm in enumerate(('s5_b_re', 's5_b_im')):
                            for two in range(2):
                                P.dma("sp", Bt[i][two * 64:two * 64 + 64, :, :],
                                      W[nm][l].rearrange("(j two) p c -> two p j c", two=2)[two], [], [Bt[i]])
                        bc = lambda n: sm[n][:, :].unsqueeze(2).to_broadcast([128, 16, 16])
                        tt(Bp[0][:], Bt[0][:], bc("bsr"), ALU.mult, [Bt[0], sm["bsr"]], [Bp[0]])
                        tt(tmp[:], Bt[1][:], bc("bsi"), ALU.mult, [Bt[1], sm["bsi"]], [tmp])
                        tt(Bp[0][:], Bp[0][:], tmp[:], ALU.subtract, [Bp[0], tmp], [Bp[0]])
                        tt(Bp[1][:], Bt[1][:], bc("bsr"), ALU.mult, [Bt[1], sm["bsr"]], [Bp[1]])
                        tt(tmp[:], Bt[0][:], bc("bsi"), ALU.mult, [Bt[0], sm["bsi"]], [tmp])
                        tt(Bp[1][:], Bp[1][:], tmp[:], ALU.add, [Bp[1], tmp], [Bp[1]])
                        for i, BL in enumerate((BLr, BLi)):
                            P.op("dve", lambda e: e.memset(BL[:], 0.0), [], [BL])
                            P.op("dve", lambda e: e.memset(X[:], 0.0), [], [X])
                            P.op("dve", lambda e: e.tensor_copy(out=X[0:64, :, 0, :], in_=Bp[i][0:64, :, :]), [Bp[i]], [X])
                            P.op("dve", lambda e: e.tensor_copy(out=X[64:128, :, 1, :], in_=Bp[i][64:128, :, :]), [Bp[i]], [X])
                            for ch in range(4):
                                pb = ps[ch]
                                mm(pb[:, 0:128], X[:, 4 * ch:4 * ch + 4, :, :].rearrange("p a b c -> p (a b c)"), ident[:], True, True,
                                   [X, ident], [pb])
                                for jj in range(4):
                                    act(BL[32 * jj:32 * jj + 32, 4 * ch + jj, :], pb[32 * jj:32 * jj + 32, 0:128], AF.Copy, [pb], [BL])
                        ti32 = sbt(st2, "sti", [128, TT], I32)
                        tau = sbt(st2, "stau", [128, TT])
                        ang = sbt(st2, "sang", [128, 16, TT])
                        qq = sbt(st2, "sqq", [128, 16, TT])
                        if d == 0:
                            P.op("pool", lambda e: e.iota(ti32[:], pattern=[[1, TT]], base=1, channel_multiplier=0), [], [ti32])
                        else:
                            P.op("pool", lambda e: e.iota(ti32[:], pattern=[[-1, TT]], base=TT, channel_multiplier=0), [], [ti32])
                        P.op("dve", lambda e: e.tensor_copy(out=tau[:], in_=ti32[:]), [ti32], [tau])
                        tt(ang[:], sm["th"][:, :].unsqueeze(2).to_broadcast([128, 16, TT]),
                           tau[:].unsqueeze(1).to_broadcast([128, 16, TT]), ALU.mult, [sm["th"], tau], [ang])
                        fl = lambda t: t[:].rearrange("p a b -> p (a b)")
                        emit_sin(sinT, fl(sinT), ang, fl(ang), qq, fl(qq), 0.0)
                        emit_sin(cosT, fl(cosT), ang, fl(ang), qq, fl(qq), math.pi / 2.0)
                        P.barrier()
                    ufs = [sbt(st, "suf%d" % i, [128, 4, TT]) for i in range(2)]
                    ubs = [sbt(st, "sub%d" % i, [128, 4, TT], BF16) for i in range(2)]
                    wk = [[sbt(st, "sw%s%d" % (n, i), [128, TT]) for i in range(2)] for n in "abcdefgh"]
                    xb_ = [[sbt(st, "sx%s%d" % (n, i), [128, TT], BF16) for i in range(2)] for n in "ri"]
                    car = [sbt(st, "scar%d" % i, [128, 16]) for i in range(2)]
                    y1t = [sbt(st, "sy1%d" % i, [128, TT]) for i in range(2)]
                    yg = sbt(st, "syg", [128, 4, TT], BF16)
                    sg = [sbt(st, "ssg%d" % i, [128, TT]) for i in range(2)]
                    P.op("dve", lambda e: e.memset(car[0][:], 0.0), [], [car[0]])
                    P.op("dve", lambda e: e.memset(car[1][:], 0.0), [], [car[1]])
                    order = list(range(NT)) if d == 0 else list(range(NT - 1, -1, -1))
                    nblk = 0
                    for it, ti in enumerate(order):
                        t0 = ti * TT
                        tsl = slice(t0, t0 + TT)
                        uf, ub = ufs[it % 2], ubs[it % 2]
                        P.dma("sp", uf[:], US[:, tsl].rearrange("(c p) t -> p c t", p=128), [US.bs[ti]], [uf])
                        act(ub[:], uf[:], AF.Copy, [uf], [ub])
                        if it > 0:
                            bnd = t0 if d == 0 else t0 + TT
                            if bnd % SL == 0:
                                for cc in car:
                                    ts(cc[:], cc[:], linkc[:, 0:1], None, ALU.mult, None, [cc, linkc], [cc])
                        for ch in range(4):
                            psy = ps[4 + ch % 2]
                            for jj in range(4):
                                j = 4 * ch + jj
                                pr_ = slice(32 * jj, 32 * jj + 32)
                                pvr, pvi = ps[(2 * nblk) % 4], ps[(2 * nblk + 1) % 4]
                                w = [wk[i][nblk % 2] for i in range(8)]
                                xr_b, xi_b = xb_[0][nblk % 2], xb_[1][nblk % 2]
                                nblk += 1
                                mm(pvr[:, :], BLr[:, j, :], ub[:, ch, :], True, True, [BLr, ub], [pvr])
                                mm(pvi[:, :], BLi[:, j, :], ub[:, ch, :], True, True, [BLi, ub], [pvi])
                                c_, s_ = cosT[:, j, :], sinT[:, j, :]
                                tt(w[0][:], pvr[:, :], c_, ALU.mult, [pvr, cosT], [w[0]])
                                tt(w[1][:], pvi[:, :], s_, ALU.mult, [pvi, sinT], [w[1]])
                                tt(w[0][:], w[0][:], w[1][:], ALU.add, [w[0], w[1]], [w[0]])
                                tt(w[2][:], pvi[:, :], c_, ALU.mult, [pvi, cosT], [w[2]])
                                tt(w[3][:], pvr[:, :], s_, ALU.mult, [pvr, sinT], [w[3]])
                                tt(w[2][:], w[2][:], w[3][:], ALU.subtract, [w[2], w[3]], [w[2]])
                                rb = sm["rmag"][:, j:j + 1].to_broadcast([128, TT])
                                for (src, dst, cc) in ((w[0], w[4], car[0]), (w[2], w[5], car[1])):
                                    if d == 0:
                                        P.op("dve", lambda e: e.tensor_tensor_scan(out=dst[:], data0=rb, data1=src[:],
                                                                                   initial=cc[:, j:j + 1], op0=ALU.mult, op1=ALU.add),
                                             [src, cc, sm["rmag"]], [dst])
                                    else:
                                        P.op("dve", lambda e: e.tensor_tensor_scan(out=dst[:, ::-1], data0=rb, data1=src[:, ::-1],
                                                                                   initial=cc[:, j:j + 1], op0=ALU.mult, op1=ALU.add),
                                             [src, cc, sm["rmag"]], [dst])
                                tt(w[6][:], w[4][:], c_, ALU.mult, [w[4], cosT], [w[6]])
                                tt(w[1][:], w[5][:], s_, ALU.mult, [w[5], sinT], [w[1]])
                                tt(w[6][:], w[6][:], w[1][:], ALU.subtract, [w[6], w[1]], [w[6]])
                                tt(w[7][:], w[4][:], s_, ALU.mult, [w[4], sinT], [w[7]])
                                tt(w[3][:], w[5][:], c_, ALU.mult, [w[5], cosT], [w[3]])
                                tt(w[7][:], w[7][:], w[3][:], ALU.add, [w[7], w[3]], [w[7]])
                                lastc = slice(TT - 1, TT) if d == 0 else slice(0, 1)
                                act(car[0][:, j:j + 1], w[6][:, lastc], AF.Copy, [w[6]], [car[0]])
                                act(car[1][:, j:j + 1], w[7][:, lastc], AF.Copy, [w[7]], [car[1]])
                                act(xr_b[:], w[6][:], AF.Copy, [w[6]], [xr_b])
                                act(xi_b[:], w[7][:], AF.Copy, [w[7]], [xi_b])
                                mm(psy[:, :], CLr[:, j, :], xr_b[:], jj == 0, False, [CLr, xr_b], [psy])
                                mm(psy[:, :], CLi[:, j, :], xi_b[:], False, jj == 3, [CLi, xi_b], [psy])
                            y1 = y1t[ch % 2]
                            if d == 0:
                                stt(y1[:], uf[:, ch, :], dsk[:, ch:ch + 1], psy[:, :], ALU.mult, ALU.add, [uf, dsk, psy], [y1])
                                P.dma("pool", Y1[ch * 128:(ch + 1) * 128, tsl], y1[:], [y1], [Y1.bs[ti]])
                            else:
                                P.dma("pool", y1[:], Y1[ch * 128:(ch + 1) * 128, tsl], [Y1.bs[ti]], [y1])
                                tt(y1[:], y1[:], psy[:, :], ALU.add, [y1, psy], [y1])
                                act(yg[:, ch, :], y1[:], AF.Gelu_apprx_tanh, [y1], [yg])
                        if d == 1:
                            for o in range(8):
                                pv, pg = ps[6], ps[7]
                                for k in range(4):
                                    mm(pv[:, :], wglu[:, k, o * 128:(o + 1) * 128], yg[:, k, :], k == 0, k == 3, [wglu, yg], [pv])
                                for k in range(4):
                                    mm(pg[:, :], wglu[:, k, D + o * 128:D + (o + 1) * 128], yg[:, k, :], k == 0, k == 3, [wglu, yg], [pg])
                                s1, s2 = sg[0], sg[1]
                                act(s1[:], pg[:, :], AF.Sigmoid, [pg], [s1])
                                s3 = y1t[o % 2]
                                tt(s2[:], s1[:], pv[:, :], ALU.mult, [s1, pv], [s2])
                                P.dma("sp" if o % 2 else "pool", YC[o * 128:(o + 1) * 128, tsl], s2[:], [s2], [YC.bs[ti]])
                    P.barrier()


    MS = nc.dram_tensor("MSscr", [4, 4, NCH], F32).ap()
    MSb = [Buf("ms%d" % i) for i in range(4)]

    def phase_m(l):
        CW = max(1, NCH // 32)
        NBLK = NCH // CW
        R = 4 * NBLK
        WD = CW * 128
        NCS = SL // 128
        with ExitStack() as st:
            cw = sbt(st, "m0cw", [128, 3, 4])
            cb = sbt(st, "m0cb", [128, 4])
            with nc.allow_non_contiguous_dma(reason="tiny param vectors"):
                for w in range(3):
                    P.dma("sp", cw[:, w, :], W['conv_m_w'][l, w].rearrange("(c p) -> p c", p=128), [], [cw])
                P.dma("sp", cb[:, :], W['conv_m_b'][l].rearrange("(c p) -> p c", p=128), [], [cb])
            xts = [sbt(st, "m0x%d" % i, [128, TT + 2]) for i in range(3)]
            accs = [sbt(st, "m0a%d" % i, [128, TT]) for i in range(2)]
            xos = [sbt(st, "m0o%d" % i, [128, TT], BF16) for i in range(2)]
            n = 0
            for ti in range(NT):
                t0 = ti * TT
                for c in range(4):
                    xt, acc, xo = xts[n % 3], accs[n % 2], xos[n % 2]
                    n += 1
                    load_halo("sp" if c % 2 else "pool", xt, XM, c * 128, 128, ti, XM.bs)
                    ts(acc[:], xt[:, 0:TT], cw[:, 0, c:c + 1], None, ALU.mult, None, [xt, cw], [acc])
                    stt(acc[:], xt[:, 1:TT + 1], cw[:, 1, c:c + 1], acc[:], ALU.mult, ALU.add, [xt, cw, acc], [acc])
                    stt(acc[:], xt[:, 2:TT + 2], cw[:, 2, c:c + 1], acc[:], ALU.mult, ALU.add, [xt, cw, acc], [acc])
                    act(xo[:], acc[:], AF.Silu, [acc, cb], [xo], bias=cb[:, c:c + 1])
                    P.dma("sp", XCB[c * 128:(c + 1) * 128, t0:t0 + TT], xo[:], [xo], [XCB.bs[ti]])
            P.barrier()
        import os as _os
        MSTOP = _os.environ.get('MSTOP', '')
        if MSTOP == 'm0':
            return
        with ExitStack() as lst:
            wq = sbt(lst, "mwq", [128, 4, 128], BF16)
            wk = sbt(lst, "mwk", [128, 4, 128], BF16)
            for h in range(4):
                load_cast(wq[:, h, :], W['w_q_m'][l, h], wq, 128)
                load_cast(wk[:, h, :], W['w_k_m'][l, h], wk, 128, scale=128.0 ** -0.5)
            maskF = sbt(lst, "mmaskF", [128, 128])
            maskB = sbt(lst, "mmaskB", [128, 128])
            for mk, cm, stp in ((maskF, -1, 1), (maskB, 1, -1)):
                P.op("pool", lambda e: e.memset(mk[:], 1.0), [], [mk])
                P.op("pool", lambda e: e.affine_select(out=mk[:], in_=mk[:], compare_op=ALU.is_ge, fill=0.0, base=0,
                                                       pattern=[[stp, 128]], channel_multiplier=cm), [mk], [mk])
            rmask = sbt(lst, "mrmask", [128, CW, 128])
            nmask = sbt(lst, "mnmask", [128, CW, 128])
            P.op("dve", lambda e: e.memset(rmask[:], 1.0), [], [rmask])
            P.op("dve", lambda e: e.memset(rmask[:, :, 0:1], 0.0), [], [rmask])
            P.op("dve", lambda e: e.memset(nmask[:], 0.0), [], [nmask])
            P.op("dve", lambda e: e.memset(nmask[:, :, 0:1], -1.0e30), [], [nmask])
            for d in (0, 1):
                with ExitStack() as st:
                    rev = (lambda ap: ap) if d == 0 else (lambda ap: ap[:, ::-1])
                    g = {n: sbt(st, "mg" + n, [R, WD]) for n in ("I", "A", "b", "cb", "M", "nM", "wi", "en", "wg", "t")}
                    cl = {n: sbt(st, "mc" + n, [R, CW]) for n in ("ms", "Ml", "t")}
                    ch4 = {n: sbt(st, "m4" + n, [4, NCH]) for n in ("al", "bm", "mn", "ms")}
                    mini = sbt(st, "mmini", [4, 1])
                    G = lambda n: g[n][:, :]
                    G3 = lambda n: g[n][:, :].rearrange("r (c w) -> r c w", w=128)
                    for h in range(4):
                        rs_ = slice(h * NBLK, (h + 1) * NBLK)
                        P.dma("sp", g["I"][rs_, :], GM[d * 8 + h, :].rearrange("(b w) -> b w", w=WD), GM.bs, [g["I"]])
                        P.dma("pool", g["A"][rs_, :], GM[d * 8 + 4 + h, :].rearrange("(b w) -> b w", w=WD), GM.bs, [g["A"]])
                    act(G("A"), G("A"), AF.Exp, [g["A"]], [g["A"]], scale=-1.0)
                    act(G("A"), G("A"), AF.Ln, [g["A"]], [g["A"]], bias=1.0)
                    rm2 = rmask[0:R, :, :].rearrange("r c w -> r (c w)")
                    nm2 = nmask[0:R, :, :].rearrange("r c w -> r (c w)")
                    P.op("dve", lambda e: e.tensor_tensor_scan(out=rev(G("t")), data0=rm2, data1=rev(G("A")), initial=0.0,
                                                               op0=ALU.mult, op1=ALU.add), [g["A"], rmask], [g["t"]])
                    tt(G("b"), G("I"), G("t"), ALU.add, [g["I"], g["t"]], [g["b"]])
                    P.op("dve", lambda e: e.tensor_tensor_scan(out=rev(G("cb")), data0=nm2, data1=rev(G("b")), initial=-1.0e30,
                                                               op0=ALU.add, op1=ALU.max), [g["b"], nmask], [g["cb"]])
                    lastw = 127 if d == 0 else 0
                    ts(cl["t"][:, :], G3("t")[:, :, lastw], -1.0, None, ALU.mult, None, [g["t"]], [cl["t"]])
                    P.dma("sp", MS[0].rearrange("h (b c) -> (h b) c", c=CW), cl["t"][:, :], [cl["t"]], [MSb[0]])
                    P.dma("sp", MS[1].rearrange("h (b c) -> (h b) c", c=CW), G3("cb")[:, :, lastw], [g["cb"]], [MSb[1]])
                    P.dma("sp", ch4["al"][:, :], MS[0], [MSb[0]], [ch4["al"]])
                    P.dma("sp", ch4["bm"][:, :], MS[1], [MSb[1]], [ch4["bm"]])
                    P.op("dve", lambda e: e.memset(mini[:], 0.0), [], [mini])
                    for sg_ in (range(NSEG) if d == 0 else range(NSEG - 1, -1, -1)):
                        c0, c1 = sg_ * NCS, (sg_ + 1) * NCS
                        P.op("dve", lambda e: e.tensor_tensor_scan(out=rev(ch4["mn"][:, c0:c1]), data0=rev(ch4["bm"][:, c0:c1]),
                                                                   data1=rev(ch4["al"][:, c0:c1]), initial=mini[:, 0:1],
                                                                   op0=ALU.max, op1=ALU.add), [ch4["bm"], ch4["al"], mini], [ch4["mn"]])
                        if d == 0:
                            P.op("dve", lambda e: e.tensor_copy(out=ch4["ms"][:, c0:c0 + 1], in_=mini[:, :]), [mini], [ch4["ms"]])
                            if NCS > 1:
                                P.op("dve", lambda e: e.tensor_copy(out=ch4["ms"][:, c0 + 1:c1], in_=ch4["mn"][:, c0:c1 - 1]),
                                     [ch4["mn"]], [ch4["ms"]])
                            lastm = ch4["mn"][:, c1 - 1:c1]
                        else:
                            P.op("dve", lambda e: e.tensor_copy(out=ch4["ms"][:, c1 - 1:c1], in_=mini[:, :]), [mini], [ch4["ms"]])
                            if NCS > 1:
                                P.op("dve", lambda e: e.tensor_copy(out=ch4["ms"][:, c0:c1 - 1], in_=ch4["mn"][:, c0 + 1:c1]),
                                     [ch4["mn"]], [ch4["ms"]])
                            lastm = ch4["mn"][:, c0:c0 + 1]
                        ts(mini[:], lastm, linkc[0:4, 0:1], None, ALU.mult, None, [ch4["mn"], linkc], [mini])
                    P.dma("sp", MS[2], ch4["ms"][:, :], [ch4["ms"]], [MSb[2]])
                    P.dma("sp", cl["ms"][:, :], MS[2].rearrange("h (b c) -> (h b) c", c=CW), [MSb[2]], [cl["ms"]])
                    msb = cl["ms"][:, :].unsqueeze(2).to_broadcast([R, CW, 128])
                    tt(G3("M"), G3("cb"), msb, ALU.max, [g["cb"], cl["ms"]], [g["M"]])
                    ts(G("nM"), G("M"), -1.0, None, ALU.mult, None, [g["M"]], [g["nM"]])
                    tt(G3("wi"), G3("nM"), msb, ALU.add, [g["nM"], cl["ms"]], [g["wi"]])
                    act(G("wi"), G("wi"), AF.Exp, [g["wi"]], [g["wi"]])
                    tt(G("en"), G("t"), G("M"), ALU.subtract, [g["t"], g["M"]], [g["en"]])
                    act(G("en"), G("en"), AF.Exp, [g["en"]], [g["en"]])
                    P.op("dve", lambda e: e.tensor_copy(out=cl["Ml"][:, :], in_=G3("M")[:, :, lastw]), [g["M"]], [cl["Ml"]])
                    tt(G3("wg"), G3("b"), cl["Ml"][:, :].unsqueeze(2).to_broadcast([R, CW, 128]), ALU.subtract,
                       [g["b"], cl["Ml"]], [g["wg"]])
                    act(G("wg"), G("wg"), AF.Exp, [g["wg"]], [g["wg"]])
                    if MSTOP == 'm1':
                        P.barrier()
                        continue
                    xcs = [sbt(st, "mxc%d" % i, [128, 4, 128], BF16) for i in range(2)]
                    vas = [sbt(st, "mva%d" % i, [128, 4, 130], BF16) for i in range(2)]
                    for v in vas:
                        P.op("dve", lambda e: e.memset(v[:], 1.0), [], [v])
                    cols = [sbt(st, "mcol%d" % i, [128, 12]) for i in range(2)]
                    C32 = [sbt(st, "mC%d" % i, [128, 130]) for i in range(4)]
                    Cb = [sbt(st, "mCb%d" % i, [128, 130], BF16) for i in range(4)]
                    for h in range(4):
                        P.op("dve", lambda e: e.memset(C32[h][:], 0.0), [], [C32[h]])
                        P.op("dve", lambda e: e.memset(Cb[h][:], 0.0), [], [Cb[h]])
                    W2 = lambda nm, i: [sbt(st, "m%s%d" % (nm, k), [128, 128], BF16 if i else F32) for k in range(2)]
                    qTs, kTs, kts, STs, qts = W2("qT", 1), W2("kT", 1), W2("kt", 1), W2("ST", 1), W2("qt", 1)
                    ETs, wbs, sfs = W2("ET", 0), W2("wb", 0), W2("sf", 0)
                    dn = [sbt(st, "mdn%d" % k, [128, 2]) for k in range(2)]
                    hst = [sbt(st, "mhst%d" % k, [128, 512]) for k in range(2)]
                    hfs = [sbt(st, "mhf%d" % k, [128, 512]) for k in range(2)]
                    oms = [sbt(st, "moms%d" % k, [128, 512]) for k in range(2)]
                    bst6 = sbt(st, "mbst", [128, 4, 6])
                    mv = sbt(st, "mmv", [128, 4, 2])
                    rsd = sbt(st, "mrsd", [128, 4])
                    hbn = [sbt(st, "mhbn%d" % k, [128, 512], BF16) for k in range(2)]
                    hbt = [sbt(st, "mhbt%d" % k, [128, 4, 128], BF16) for k in range(2)]
                    mask = maskF if d == 0 else maskB
                    order = list(range(NCH)) if d == 0 else list(range(NCH - 1, -1, -1))
                    n = 0
                    for it, c in enumerate(order):
                        blk, cwi = c // CW, c % CW
                        csl = slice(cwi * 128, (cwi + 1) * 128)
                        tsl = slice(c * 128, (c + 1) * 128)
                        ti = (c * 128) // TT
                        xc, va, col = xcs[it % 2], vas[it % 2], cols[it % 2]
                        P.dma("sp", xc[:], XCB[:, tsl].rearrange("(k p) t -> p k t", p=128), [XCB.bs[ti]], [xc])
                        P.dma("pool", va[:, :, 0:128], VM[tsl, :].rearrange("t (h e) -> t h e", h=4), [VM.bs[ti]], [va])
                        if d == 1:
                            hf, om_ = hfs[it % 2], oms[it % 2]
                            P.dma("sp", hf[:], HF[tsl, :], [HF.bs[ti]], [hf])
                            P.dma("pool", om_[:], OMS[tsl, :], [OMS.bs[ti]], [om_])
                        bnd = c * 128 if d == 0 else (c + 1) * 128
                        if it > 0 and bnd % SL == 0:
                            for h in range(4):
                                ts(C32[h][:], C32[h][:], linkc[:, 0:1], None, ALU.mult, None, [C32[h], linkc], [C32[h]])
                                act(Cb[h][:], C32[h][:], AF.Copy, [C32[h]], [Cb[h]])
                        pc = ps[7]
                        selc = ident[0:R, blk:blk + 3 * NBLK + 1:NBLK]
                        for qi, nm in enumerate(("b", "wg", "en")):
                            mm(pc[:, 4 * qi:4 * qi + 4], g[nm][:, csl], selc, True, True, [g[nm], ident], [pc])
                        P.op("dve", lambda e: e.tensor_copy(out=col[:], in_=pc[:, 0:12]), [pc], [col])
                        hs = hst[it % 2]
                        for h in range(4):
                            r = h * NBLK + blk
                            k2 = n % 2
                            n += 1
                            qT, kT, kt, ST, qt, ET, wb, sf = qTs[k2], kTs[k2], kts[k2], STs[k2], qts[k2], ETs[k2], wbs[k2], sfs[k2]
                            pq, pk, pkt, pm, pw, pS = ps[0], ps[1], ps[2], ps[3], ps[4], ps[5]
                            pn = ps[6]
                            sel = ident[0:R, r:r + 1].to_broadcast([R, 128])
                            mm(pq[:, 0:128], wq[:, h, :], xc[:, h, :], True, True, [wq, xc], [pq])
                            mm(pk[:, 0:128], wk[:, h, :], xc[:, h, :], True, True, [wk, xc], [pk])
                            mm(pkt[:, 0:128], xc[:, h, :], wk[:, h, :], True, True, [wk, xc], [pkt])
                            mm(pm[:, 0:128], sel, g["nM"][:, csl], True, True, [ident, g["nM"]], [pm])
                            mm(pw[:, 0:128], sel, g["wi"][:, csl], True, True, [ident, g["wi"]], [pw])
                            act(qT[:], pq[:, 0:128], AF.Copy, [pq], [qT])
                            act(kT[:], pk[:, 0:128], AF.Copy, [pk], [kT])
                            ts(kt[:], pkt[:, 0:128], col[:, 4 + h:5 + h], None, ALU.mult, None, [pkt, col], [kt])
                            ts(ET[:], pm[:, 0:128], col[:, h:h + 1], 0.0, ALU.add, ALU.min, [pm, col], [ET])
                            act(ET[:], ET[:], AF.Exp, [ET], [ET])
                            tt(ET[:], ET[:], mask[:], ALU.mult, [ET, mask], [ET])
                            mm(pS[:, 0:128], kT[:], qT[:], True, True, [kT, qT], [pS])
                            tt(ST[:], pS[:, 0:128], ET[:], ALU.mult, [pS, ET], [ST])
                            act(wb[:], pw[:, 0:128], AF.Copy, [pw], [wb])
                            tt(qt[:], pq[:, 0:128], wb[:], ALU.mult, [pq, wb], [qt])
                            mm(pn[:, 0:130], ST[:], va[:, h, :], True, False, [ST, va], [pn])
                            mm(pn[:, 0:130], qt[:], Cb[h][:], False, True, [qt, Cb[h]], [pn])
                            mm(pkt[:, 0:130], kt[:], va[:, h, :], True, True, [kt, va], [pkt])
                            dcol = slice(127, 128) if d == 0 else slice(0, 1)
                            stt(C32[h][:], C32[h][:], wb[:, dcol], pkt[:, 0:130], ALU.mult, ALU.add, [C32[h], wb, pkt], [C32[h]])
                            act(Cb[h][:], C32[h][:], AF.Copy, [C32[h]], [Cb[h]])
                            dd = dn[k2]
                            act(dd[:, 0:1], pn[:, 128:129], AF.Abs, [pn], [dd])
                            ts(dd[:, 0:1], dd[:, 0:1], col[:, 8 + h:9 + h], None, ALU.max, None, [dd, col], [dd])
                            P.op("dve", lambda e: e.reciprocal(out=dd[:, 1:2], in_=dd[:, 0:1]), [dd], [dd])
                            hsl = slice(h * 128, (h + 1) * 128)
                            if d == 0:
                                ts(hs[:, hsl], pn[:, 0:128], dd[:, 1:2], None, ALU.mult, None, [pn, dd], [hs])
                            else:
                                stt(hs[:, hsl], pn[:, 0:128], dd[:, 1:2], hf[:, hsl], ALU.mult, ALU.add, [pn, dd, hf], [hs])
                        if d == 0:
                            P.dma("sp", HF[tsl, :], hs[:], [hs], [HF.bs[ti]])
                        else:
                            for h in range(4):
                                P.op("dve", lambda e: e.bn_stats(out=bst6[:, h, :], in_=hs[:, h * 128:(h + 1) * 128]), [hs], [bst6])
                                P.op("dve", lambda e: e.bn_aggr(out=mv[:, h, :], in_=bst6[:, h, :]), [bst6], [mv])
                            ts(rsd[:], mv[:, :, 1], LN_EPS, None, ALU.add, None, [mv], [rsd])
                            act(rsd[:], rsd[:], AF.Ln, [rsd], [rsd])
                            act(rsd[:], rsd[:], AF.Exp, [rsd], [rsd], scale=-0.5)
                            hb = hbn[it % 2]
                            for h in range(4):
                                hsl = slice(h * 128, (h + 1) * 128)
                                ts(hs[:, hsl], hs[:, hsl], mv[:, h, 0:1], rsd[:, h:h + 1], ALU.subtract, ALU.mult, [hs, mv, rsd], [hs])
                            tt(hb[:], hs[:], om_[:], ALU.mult, [hs, om_], [hb])
                            ht = hbt[it % 2]
                            pt_ = ps[7]
                            for h in range(4):
                                mm(pt_[:, h * 128:(h + 1) * 128], hb[:, h * 128:(h + 1) * 128], identb[:], True, True, [hb, identb], [pt_])
                            act(ht[:], pt_[:, :].rearrange("p (k t) -> p k t", k=4), AF.Copy, [pt_], [ht])
                            P.dma("sp", HBT[:, tsl].rearrange("(k p) t -> p k t", p=128), ht[:], [ht], [HBT.bs[ti]])
                    P.barrier()


    def gen_att(l, st):
        SCALE = 96.0 ** -0.5
        kT = sbt(st, "xk", [98, T], BF16)
        vh = sbt(st, "xv", [128, NCH, 128], BF16)
        qTs = [sbt(st, "xq%d" % i, [98, TT], BF16) for i in range(2)]
        pts = [sbt(st, "xp%d" % i, [128, TT], BF16) for i in range(3)]
        rlt = sbt(st, "xrl", [128, TT])
        osb = [sbt(st, "xo%d" % i, [64, TT], BF16) for i in range(2)]
        sbank = [ps[0], ps[1], ps[2]]
        abank = [ps[3], ps[4]]
        n = 0
        nq = 0
        for h in range(8):
            P.dma("sp", kT[:], KT[h, :, :], KT.bs, [kT])
            P.dma("pool", vh[:], VA[:, h, :].rearrange("(n p) d -> p n d", p=128), VA.bs, [vh])
            for i in range(NT):
                qs = slice(i * TT, (i + 1) * TT)
                qT = qTs[nq % 2]
                acc = abank[nq % 2]
                nq += 1
                P.dma("sp", qT[:], QT[h, :, qs], [QT.bs[i]], [qT])
                segq = (i * TT) // SL

                def score(kb):
                    R = 97 if (kb * 128) // SL == segq else 98
                    pb = sbank[(n + kb) % 3]
                    mm(pb[:, :], kT[0:R, kb * 128:(kb + 1) * 128], qT[0:R, :], True, True, [kT, qT], [pb])
                    return pb
                pend = [score(0)]
                if NCH > 1:
                    pend.append(score(1))
                for kb in range(NCH):
                    pb = pend.pop(0)
                    pt = pts[(n + kb) % 3]
                    act(pt[:], pb[:, :], AF.Exp, [pb], [pt], scale=SCALE)
                    if kb + 2 < NCH:
                        pend.append(score(kb + 2))
                    mm(acc[:, :], vh[:, kb, :], pt[:], kb == 0, kb == NCH - 1, [vh, pt], [acc])
                    yield
                n += NCH
                P.op("dve", lambda e: e.reciprocal(out=rlt[64:128, :], in_=acc[64:128, :]), [acc], [rlt])
                o = osb[i % 2]
                tt(o[:], acc[0:64, :], rlt[64:128, :], ALU.mult, [acc, rlt], [o])
                P.dma("pool", OT[h * 64:(h + 1) * 64, qs], o[:], [o], [OT.bs[i]])

    import os as _os2
    PENG = _os2.environ.get('PENG', 'dve')

    def gen_s5(l, st):
        pending = []
        rnd = [0]

        def later(k, fn):
            pending.append((rnd[0] + k, fn))

        def tick():
            rnd[0] += 1
            due = [p for p in pending if p[0] <= rnd[0]]
            for p in due:
                pending.remove(p)
            for p in due:
                p[1]()

        CLr = sbt(st, "sCLr", [128, 16, 128], BF16)
        CLi = sbt(st, "sCLi", [128, 16, 128], BF16)
        wglu = sbt(st, "swglu", [128, 4, 2 * D], BF16)
        dsk = sbt(st, "sdsk", [128, 4])
        pvr, pvi, psy = ps[5], ps[6], ps[7]
        NW = 2
        wk = [[sbt(st, "sw%s%d" % (n, i), [128, TT]) for i in range(NW)] for n in "abcdef"]

        class Al:
            def __init__(self, par, view):
                self.t = view
                self.b = par.b

            def __getitem__(self, k):
                return self.t[k]
        v3 = lambda tl: tl[:, 0:256].rearrange("p (a b) -> p a b", a=16)
        Zt = Al(wk[0][0], wk[0][0][:, :].rearrange("p (a b) -> p a b", a=4))
        Bt = [Al(wk[1][i], v3(wk[1][i])) for i in range(2)]
        Bp = [Al(wk[2][i], v3(wk[2][i])) for i in range(2)]
        tmp = Al(wk[3][0], v3(wk[3][0]))
        X = Al(wk[4][0], wk[4][0][:, :].rearrange("p (a b c) -> p a b c", a=16, b=2))
        for k in range(4):
            load_cast(wglu[:, k, :], W['w_glu'][l, k * 128:(k + 1) * 128, :], wglu, 2 * D)
            yield 2.0
        P.dma("sp", dsk[:, :], W['s5_d'][l].rearrange("(c g) w -> (g w) c", c=4), [], [dsk])
        for i, (nm, CL) in enumerate((('s5_c_re', CLr), ('s5_c_im', CLi))):
            P.op("dve", lambda e: e.memset(Zt[:], 0.0), [], [Zt])
            P.op("dve", lambda e: e.memset(CL[:], 0.0), [], [CL])
            for ch in range(4):
                for jj in range(4):
                    for two in range(2):
                        g = 2 * (4 * ch + jj) + two
                        P.dma("sp" if two else "pool", Zt[32 * jj + 16 * two:32 * jj + 16 * two + 16, ch, 64 * two:64 * two + 64],
                              W[nm][l, g], [], [Zt])
            yield 2.0
            for ch in range(4):
                mm(psy[:, 0:128], Zt[:, ch, :], ident[:], True, True, [Zt, ident], [psy])
                for jj in range(4):
                    P.op("dve", lambda e: e.tensor_scalar(out=CL[:, 4 * ch + jj, 32 * jj:32 * jj + 32], in0=psy[:, 32 * jj:32 * jj + 32],
                                                          scalar1=(1.0 if i == 0 else -1.0), scalar2=None, op0=ALU.mult), [psy], [CL])
                yield 2.0
        sm = {n: sbt(st, "s5" + n, [128, 16]) for n in
              ("are", "aim", "dt", "rmag", "th", "cs", "sn", "q", "abr", "abi", "nr", "ni", "den", "bsr", "bsi", "t")}
        A = lambda n: sm[n][:, :]
        BLr = sbt(st, "sBLr", [128, 16, 128], BF16)
        BLi = sbt(st, "sBLi", [128, 16, 128], BF16)
        cosT = sbt(st, "scosT", [128, 16, TT])
        sinT = sbt(st, "ssinT", [128, 16, TT])
        ti32 = sbt(st, "sti", [128, TT], I32)
        tau = sbt(st, "stau", [128, TT])
        ang = sbt(st, "sang", [128, TT])
        qq = sbt(st, "sqq", [128, TT])
        ub = sbt(st, "sub", [128, 4, TT], BF16)
        xb_ = [[sbt(st, "sx%s%d" % (n, i), [128, TT], BF16) for i in range(4)] for n in "ri"]
        tn = [sbt(st, "stn%d" % i, [128, 4]) for i in range(2)]
        car = [sbt(st, "scar%d" % i, [128, 16]) for i in range(2)]
        y1t = [sbt(st, "sy1%d" % i, [128, TT]) for i in range(2)]
        yg = sbt(st, "syg", [128, 4, TT], BF16)
        sg = y1t
        for d in (0, 1):
            for two in range(2):
                prt = slice(two * 64, two * 64 + 64)
                P.dma("sp", sm["are"][prt, :], W['s5_a_re'][l, d].rearrange("(j two) p -> two p j", two=2)[two], [], [sm["are"]])
                P.dma("sp", sm["aim"][prt, :], W['s5_a_im'][l, d].rearrange("(j two) p -> two p j", two=2)[two], [], [sm["aim"]])
                P.dma("sp", sm["dt"][prt, :], W['s5_log_dt'][l, d].rearrange("(j two) -> two j", two=2)[two].partition_broadcast(64),
                      [], [sm["dt"]])
            if True:
                for i, nm in enumerate(('s5_b_re', 's5_b_im')):
                    for two in range(2):
                        P.dma("pool", Bt[i][two * 64:two * 64 + 64, :, :],
                              W[nm][l].rearrange("(j two) p c -> two p j c", two=2)[two], [], [Bt[i]])
            yield 2.0
            act(A("dt"), A("dt"), AF.Exp, [sm["dt"]], [sm["dt"]])
            tt(A("rmag"), A("are"), A("dt"), ALU.mult, [sm["are"], sm["dt"]], [sm["rmag"]])
            act(A("rmag"), A("rmag"), AF.Exp, [sm["rmag"]], [sm["rmag"]])
            tt(A("th"), A("aim"), A("dt"), ALU.mult, [sm["aim"], sm["dt"]], [sm["th"]])
            yield 2.0
            emit_sin(sm["sn"], A("sn"), sm["th"], A("th"), sm["q"], A("q"), 0.0)
            yield 2.0
            emit_sin(sm["cs"], A("cs"), sm["th"], A("th"), sm["q"], A("q"), math.pi / 2.0)
            yield 2.0
            tt(A("abr"), A("rmag"), A("cs"), ALU.mult, [sm["rmag"], sm["cs"]], [sm["abr"]])
            tt(A("abi"), A("rmag"), A("sn"), ALU.mult, [sm["rmag"], sm["sn"]], [sm["abi"]])
            ts(A("abr"), A("abr"), -1.0, None, ALU.add, None, [sm["abr"]], [sm["abr"]])
            tt(A("nr"), A("abr"), A("are"), ALU.mult, [sm["abr"], sm["are"]], [sm["nr"]])
            tt(A("t"), A("abi"), A("aim"), ALU.mult, [sm["abi"], sm["aim"]], [sm["t"]])
            tt(A("nr"), A("nr"), A("t"), ALU.add, [sm["nr"], sm["t"]], [sm["nr"]])
            tt(A("ni"), A("abi"), A("are"), ALU.mult, [sm["abi"], sm["are"]], [sm["ni"]])
            tt(A("t"), A("abr"), A("aim"), ALU.mult, [sm["abr"], sm["aim"]], [sm["t"]])
            tt(A("ni"), A("ni"), A("t"), ALU.subtract, [sm["ni"], sm["t"]], [sm["ni"]])
            yield 2.0
            tt(A("den"), A("are"), A("are"), ALU.mult, [sm["are"]], [sm["den"]])
            tt(A("t"), A("aim"), A("aim"), ALU.mult, [sm["aim"]], [sm["t"]])
            tt(A("den"), A("den"), A("t"), ALU.add, [sm["den"], sm["t"]], [sm["den"]])
            P.op("dve", lambda e: e.reciprocal(out=A("den"), in_=A("den")), [sm["den"]], [sm["den"]])
            tt(A("bsr"), A("nr"), A("den"), ALU.mult, [sm["nr"], sm["den"]], [sm["bsr"]])
            tt(A("bsi"), A("ni"), A("den"), ALU.mult, [sm["ni"], sm["den"]], [sm["bsi"]])
            yield 2.0
            bc = lambda n: sm[n][:, :].unsqueeze(2).to_broadcast([128, 16, 16])
            tt(Bp[0][:], Bt[0][:], bc("bsr"), ALU.mult, [Bt[0], sm["bsr"]], [Bp[0]])
            tt(tmp[:], Bt[1][:], bc("bsi"), ALU.mult, [Bt[1], sm["bsi"]], [tmp])
            tt(Bp[0][:], Bp[0][:], tmp[:], ALU.subtract, [Bp[0], tmp], [Bp[0]])
            tt(Bp[1][:], Bt[1][:], bc("bsr"), ALU.mult, [Bt[1], sm["bsr"]], [Bp[1]])
            tt(tmp[:], Bt[0][:], bc("bsi"), ALU.mult, [Bt[0], sm["bsi"]], [tmp])
            tt(Bp[1][:], Bp[1][:], tmp[:], ALU.add, [Bp[1], tmp], [Bp[1]])
            yield 2.0
            for i, BL in enumerate((BLr, BLi)):
                P.op("dve", lambda e: e.memset(BL[:], 0.0), [], [BL])
                P.op("dve", lambda e: e.memset(X[:], 0.0), [], [X])
                P.op("dve", lambda e: e.tensor_copy(out=X[0:64, :, 0, :], in_=Bp[i][0:64, :, :]), [Bp[i]], [X])
                P.op("dve", lambda e: e.tensor_copy(out=X[64:128, :, 1, :], in_=Bp[i][64:128, :, :]), [Bp[i]], [X])
                yield 2.0
                for ch in range(4):
                    mm(psy[:, 0:128], X[:, 4 * ch:4 * ch + 4, :, :].rearrange("p a b c -> p (a b c)"), ident[:], True, True,
                       [X, ident], [psy])
                    yield 2.0
                    for jj in range(4):
                        P.op("dve", lambda e: e.tensor_copy(out=BL[32 * jj:32 * jj + 32, 4 * ch + jj, :], in_=psy[32 * jj:32 * jj + 32, 0:128]),
                             [psy], [BL])
            if d == 0:
                P.op("pool", lambda e: e.iota(ti32[:], pattern=[[1, TT]], base=1, channel_multiplier=0), [], [ti32])
            else:
                P.op("pool", lambda e: e.iota(ti32[:], pattern=[[-1, TT]], base=TT, channel_multiplier=0), [], [ti32])
            P.op("dve", lambda e: e.tensor_copy(out=tau[:], in_=ti32[:]), [ti32], [tau])
            for j in range(16):
                ts(ang[:], tau[:], sm["th"][:, j:j + 1], None, ALU.mult, None, [tau, sm["th"]], [ang])
                emit_sin(sinT, sinT[:, j, :], ang, ang[:], qq, qq[:], 0.0)
                yield 5.0
                emit_sin(cosT, cosT[:, j, :], ang, ang[:], qq, qq[:], math.pi / 2.0)
                yield 5.0
            P.op("dve", lambda e: e.memset(car[0][:], 0.0), [], [car[0]])
            P.op("dve", lambda e: e.memset(car[1][:], 0.0), [], [car[1]])
            order = list(range(NT)) if d == 0 else list(range(NT - 1, -1, -1))
            nblk = 0
            lastc = TT - 1 if d == 0 else 0
            for it, ti in enumerate(order):
                t0 = ti * TT
                tsl = slice(t0, t0 + TT)
                P.dma("sp", ub[:], USB[:, tsl].rearrange("(c p) t -> p c t", p=128), [USB.bs[ti]], [ub])
                if it > 0:
                    bnd = t0 if d == 0 else t0 + TT
                    if bnd % SL == 0:
                        for cc in car:
                            ts(cc[:], cc[:], linkc[:, 0:1], None, ALU.mult, None, [cc, linkc], [cc])
                for ch in range(4):
                    for jj in range(4):
                        j = 4 * ch + jj
                        w = [wk[i][nblk % NW] for i in range(6)]
                        xr_b, xi_b = xb_[0][nblk % 4], xb_[1][nblk % 4]
                        tnn = tn[nblk % 2]
                        nblk += 1
                        mm(pvr[:, :], BLr[:, j, :], ub[:, ch, :], True, True, [BLr, ub], [pvr])
                        mm(pvi[:, :], BLi[:, j, :], ub[:, ch, :], True, True, [BLi, ub], [pvi])
                        c_, s_ = cosT[:, j, :], sinT[:, j, :]
                        cl_, sl_ = cosT[:, j, lastc:lastc + 1], sinT[:, j, lastc:lastc + 1]
                        tt(w[0][:], pvr[:, :], c_, ALU.mult, [pvr, cosT], [w[0]])
                        tt(w[1][:], pvi[:, :], s_, ALU.mult, [pvi, sinT], [w[1]])
                        tt(w[2][:], pvi[:, :], c_, ALU.mult, [pvi, cosT], [w[2]])
                        tt(w[3][:], pvr[:, :], s_, ALU.mult, [pvr, sinT], [w[3]])
                        tt(w[0][:], w[0][:], w[1][:], ALU.add, [w[0], w[1]], [w[0]])
                        tt(w[2][:], w[2][:], w[3][:], ALU.subtract, [w[2], w[3]], [w[2]])
                        rb = sm["rmag"][:, j:j + 1].to_broadcast([128, TT])
                        for (src, dst, cc) in ((w[0], w[4], car[0]), (w[2], w[5], car[1])):
                            if d == 0:
                                P.op("dve", lambda e: e.tensor_tensor_scan(out=dst[:], data0=rb, data1=src[:],
                                                                           initial=cc[:, j:j + 1], op0=ALU.mult, op1=ALU.add),
                                     [src, cc, sm["rmag"]], [dst])
                            else:
                                P.op("dve", lambda e: e.tensor_tensor_scan(out=dst[:, ::-1], data0=rb, data1=src[:, ::-1],
                                                                           initial=cc[:, j:j + 1], op0=ALU.mult, op1=ALU.add),
                                     [src, cc, sm["rmag"]], [dst])
                        tt(w[0][:], w[4][:], c_, ALU.mult, [w[4], cosT], [w[0]], eng=PENG)
                        tt(w[1][:], w[4][:], s_, ALU.mult, [w[4], sinT], [w[1]], eng=PENG)
                        tt(w[2][:], w[5][:], s_, ALU.mult, [w[5], sinT], [w[2]], eng=PENG)
                        tt(w[3][:], w[5][:], c_, ALU.mult, [w[5], cosT], [w[3]], eng=PENG)
                        tt(xr_b[:], w[0][:], w[2][:], ALU.subtract, [w[0], w[2]], [xr_b])
                        tt(xi_b[:], w[1][:], w[3][:], ALU.add, [w[1], w[3]], [xi_b])
                        lc = slice(lastc, lastc + 1)
                        tt(car[0][:, j:j + 1], w[0][:, lc], w[2][:, lc], ALU.subtract, [w[0], w[2]], [car[0]])
                        tt(car[1][:, j:j + 1], w[1][:, lc], w[3][:, lc], ALU.add, [w[1], w[3]], [car[1]])

                        def cmm(j=j, jj=jj, xr_b=xr_b, xi_b=xi_b):
                            mm(psy[:, :], CLr[:, j, :], xr_b[:], jj == 0, False, [CLr, xr_b], [psy])
                            mm(psy[:, :], CLi[:, j, :], xi_b[:], False, jj == 3, [CLi, xi_b], [psy])
                        later(3, cmm)
                        if jj == 3:
                            y1 = y1t[ch % 2]
                            if d == 0:
                                P.dma("pool", y1[:], US[ch * 128:(ch + 1) * 128, tsl], [US.bs[ti]], [y1])

                                def fin(ch=ch, y1=y1, tsl=tsl, ti=ti):
                                    stt(y1[:], y1[:], dsk[:, ch:ch + 1], psy[:, :], ALU.mult, ALU.add, [y1, dsk, psy], [y1])
                                    P.dma("pool", Y1[ch * 128:(ch + 1) * 128, tsl], y1[:], [y1], [Y1.bs[ti]])
                                later(4, fin)
                            else:
                                P.dma("pool", y1[:], Y1[ch * 128:(ch + 1) * 128, tsl], [Y1.bs[ti]], [y1])

                                def fin(ch=ch, y1=y1):
                                    tt(y1[:], y1[:], psy[:, :], ALU.add, [y1, psy], [y1])
                                later(4, fin)

                                def fin2(ch=ch, y1=y1):
                                    act(yg[:, ch, :], y1[:], AF.Gelu_apprx_tanh, [y1], [yg])
                                later(5, fin2)
                        yield 11.0
                        tick()
                if d == 1:
                    for _ in range(6):
                        yield 0.3
                        tick()
                    for o in range(8):
                        for k in range(4):
                            mm(pvr[:, :], wglu[:, k, o * 128:(o + 1) * 128], yg[:, k, :], k == 0, k == 3, [wglu, yg], [pvr])
                        for k in range(4):
                            mm(pvi[:, :], wglu[:, k, D + o * 128:D + (o + 1) * 128], yg[:, k, :], k == 0, k == 3, [wglu, yg], [pvi])
                        s1, s2 = sg[0], sg[1]
                        yield 0.8
                        tick()
                        act(s1[:], pvi[:, :], AF.Sigmoid, [pvi], [s1])
                        yield 0.8
                        tick()
                        tt(s2[:], s1[:], pvr[:, :], ALU.mult, [s1, pvr], [s2])
                        P.dma("sp" if o % 2 else "pool", YC[o * 128:(o + 1) * 128, tsl], s2[:], [s2], [YC.bs[ti]])
                else:
                    for _ in range(6):
                        yield 0.3
                        tick()
            for _ in range(6):
                yield 0.3
                tick()

    def phase_att_s5(l):
        with ExitStack() as st:
            ga = gen_att(l, st)
            gs = gen_s5(l, st)
            n_att = 8 * NT * NCH
            total_cost = 2 * NT * 16 * 11.0 + NT * 16 * 0.8 + 2 * NT * 4 * 0.3 + 2 * (32 * 5.0 + 40 * 2.0) + 30.0
            per_us = n_att / total_cost
            acc_ = 0.0
            a_done = s_done = False
            while not (a_done and s_done):
                if not s_done:
                    try:
                        c = next(gs)
                        acc_ += (c if c else 1.0) * per_us
                    except StopIteration:
                        s_done = True
                if s_done:
                    acc_ += 64
                while acc_ >= 1.0 and not a_done:
                    acc_ -= 1.0
                    try:
                        next(ga)
                    except StopIteration:
                        a_done = True
                if a_done:
                    acc_ = 0.0
            P.barrier()

    with nc.allow_non_contiguous_dma(reason="small strided parameter / layout-conversion DMAs"):
        phase0()
        if branches:
            phase_init()
        for l in range(DEPTH):
            if branches:
                phase_a(l)
            if stop_after == 'a':
                break
            if "b" in branches:
                phase_m(l)
            if "a" in branches and "c" in branches:
                phase_att_s5(l)
            else:
                if "a" in branches:
                    phase_att(l)
                if "c" in branches:
                    phase_s5(l)
            phase_mrg(l)
            phase_f1(l)
            phase_f2(l)
        phase_out()
    P.close()
    return nc, P


NCORES = 8
T_CORE = 8192
SEGLEN = 2048


def kernel(**inputs):
    xp = np.ascontiguousarray(inputs['x_prompt'], dtype=np.float32)
    xs = np.ascontiguousarray(inputs['x_sample'], dtype=np.float32)
    nc, _ = build(T_CORE, SEGLEN, 4)
    wts = {n: np.ascontiguousarray(inputs[n], dtype=np.float32) for n in WNAMES}
    in_maps = []
    for c in range(NCORES):
        if c < 4:
            x = xp[4 * c:4 * c + 4].reshape(T_CORE, D)
            link = np.zeros((1, 1), np.float32)
        else:
            x = xs[c - 4].reshape(T_CORE, D)
            link = np.ones((1, 1), np.float32)
        m = {"x": x, "link": link}
        m.update(wts)
        in_maps.append(m)
    res = run_bass_kernel_spmd(nc, in_maps, core_ids=list(range(NCORES)))
    ys = [np.asarray(r["y"], dtype=np.float32) for r in res.results]
    y_prompt = np.concatenate([ys[c].reshape(4, SEGLEN, D) for c in range(4)], axis=0)
    y_sample = np.stack([ys[c].reshape(T_CORE, D) for c in range(4, 8)], axis=0)
    return (y_prompt, y_sample)
```
